# Optimizing a Trainium2 kernel written in Bass

```python
import jax, jax.numpy as jnp
from jax import lax
import numpy as np

D_MODEL = 2048
BATCH = 2
SEQ = 4096
DEPTH = 1
DEC_BATCH = 32
DEC_SEQ = 8
PAST_LEN = 16384
PAGE_SIZE = 128

HEAD_DIM = 128
DIL_GROUPS = ((128, 1), (512, 4), (2048, 16))
N_GROUPS = 3
HEADS_PER_GROUP = 4
ATTN_HEADS = N_GROUPS * HEADS_PER_GROUP
ATTN_WIDTH = ATTN_HEADS * HEAD_DIM
ATTN_OUT_WIDTH = HEADS_PER_GROUP * HEAD_DIM
D_CONV = D_MODEL // 2
CONV_WIDTH = 31
MEM_TOKENS = 256
MEM_HEADS = 4
MEM_HEAD_DIM = D_MODEL // 8
MEM_WIDTH = MEM_HEADS * MEM_HEAD_DIM
D_FF = 11 * D_MODEL // 4
FFN_CONV_WIDTH = 3
N_BRANCHES = 3
IN_SPLIT_SIZES = (D_CONV, D_CONV, ATTN_WIDTH, ATTN_WIDTH, ATTN_WIDTH, MEM_WIDTH, D_MODEL, D_MODEL, D_MODEL)
IN_WIDTH = 2 * D_CONV + 3 * ATTN_WIDTH + MEM_WIDTH + N_BRANCHES * D_MODEL
EPS = 1e-6

kernel_name = 'hybrid_conformer_dilated_memory_decoder_step'


def _rmsnorm(x, g):
    xf = x.astype(jnp.float32)
    y = xf * lax.rsqrt(jnp.mean(xf * xf, axis=-1, keepdims=True) + EPS)
    return (y * g.astype(jnp.float32)).astype(x.dtype)


def _layernorm(x, g, b):
    xf = x.astype(jnp.float32)
    mu = jnp.mean(xf, axis=-1, keepdims=True)
    var = jnp.mean(jnp.square(xf - mu), axis=-1, keepdims=True)
    y = (xf - mu) * lax.rsqrt(var + EPS)
    return (y * g.astype(jnp.float32) + b.astype(jnp.float32)).astype(x.dtype)


def _causal_dwconv(x_ext, w, b):
    y = lax.conv_general_dilated(x_ext, w[:, None, :].astype(x_ext.dtype), window_strides=(1,), padding='VALID',
                                 dimension_numbers=('NWC', 'WIO', 'NWC'), feature_group_count=x_ext.shape[-1])
    return y + b.astype(y.dtype)


def _memory_kv(mem, g_mem, w_mem_k, w_mem_v, g_mk):
    B, M, _ = mem.shape
    hm = _rmsnorm(mem, g_mem)
    k = _rmsnorm((hm @ w_mem_k).reshape(B, M, MEM_HEADS, MEM_HEAD_DIM), g_mk)
    v = (hm @ w_mem_v).reshape(B, M, MEM_HEADS, MEM_HEAD_DIM)
    return k, v


def _dilated_prompt(q, k, v, dil, window):
    B, S, H, E = q.shape
    wd = window // dil
    L = S // dil
    nb = -(-L // wd)
    Lp = nb * wd

    def blocks(t):
        t = t.reshape(B, L, dil, H, E)
        t = jnp.pad(t, ((0, 0), (0, Lp - L), (0, 0), (0, 0), (0, 0)))
        return t.reshape(B, nb, wd, dil, H, E)

    def with_prev(t):
        prev = jnp.pad(t[:, :-1], ((0, 0), (1, 0), (0, 0), (0, 0), (0, 0), (0, 0)))
        return jnp.concatenate([prev, t], axis=2)

    qb = blocks(q)
    kk = with_prev(blocks(k))
    vv = with_prev(blocks(v))
    s = jnp.einsum('bnqrhe,bnkrhe->bnrhqk', qb.astype(jnp.float32), kk.astype(jnp.float32)) * (E ** -0.5)
    qi = jnp.arange(wd)[:, None]
    ki = jnp.arange(2 * wd)[None, :]
    dist = qi + wd - ki
    band = (dist >= 0) & (dist <= wd)
    has_prev = (jnp.arange(nb) > 0)[:, None, None] | (ki >= wd)[None]
    mask = band[None] & has_prev
    s = jnp.where(mask[None, :, None, None], s, -jnp.inf)
    lse = jax.nn.logsumexp(s, axis=-1)
    o = jnp.einsum('bnrhqk,bnkrhe->bnqrhe', jnp.exp(s - lse[..., None]), vv.astype(jnp.float32))
    o = o.reshape(B, Lp, dil, H, E)[:, :L].reshape(B, S, H, E)
    lse = jnp.transpose(lse, (0, 1, 4, 2, 3)).reshape(B, Lp, dil, H)[:, :L].reshape(B, S, H)
    return o, lse


def _dilated_sample(q, kf, vf, dil, window):
    B, T, H, E = q.shape
    Lk = kf.shape[1]
    nk = window // dil + 1
    idx = (Lk - T + jnp.arange(T))[:, None] - dil * jnp.arange(nk)[None, :]
    valid = idx >= 0
    idx = jnp.maximum(idx, 0)
    kg = kf[:, idx]
    vg = vf[:, idx]
    s = jnp.einsum('bthe,btkhe->bthk', q.astype(jnp.float32), kg.astype(jnp.float32)) * (E ** -0.5)
    s = jnp.where(valid[None, :, None, :], s, -jnp.inf)
    lse = jax.nn.logsumexp(s, axis=-1)
    o = jnp.einsum('bthk,btkhe->bthe', jnp.exp(s - lse[..., None]), vg.astype(jnp.float32))
    return o, lse


def _dilated_mixture(q, k, v, win_bufs):
    outs, lses, new_bufs = [], [], []
    for g, (window, dil) in enumerate(DIL_GROUPS):
        sl = slice(g * HEADS_PER_GROUP, (g + 1) * HEADS_PER_GROUP)
        qg, kg, vg = q[:, :, sl], k[:, :, sl], v[:, :, sl]
        if win_bufs is None:
            kf, vf = kg, vg
            o, lse = _dilated_prompt(qg, kg, vg, dil, window)
        else:
            kb, vb = win_bufs[g]
            kf = jnp.concatenate([kb.astype(kg.dtype), kg], axis=1)
            vf = jnp.concatenate([vb.astype(vg.dtype), vg], axis=1)
            o, lse = _dilated_sample(qg, kf, vf, dil, window)
        keep = min(window, kf.shape[1])
        new_bufs += [kf[:, -keep:], vf[:, -keep:]]
        outs.append(o)
        lses.append(lse)
    alpha = jax.nn.softmax(jnp.stack(lses, axis=0), axis=0)
    o = jnp.sum(alpha[..., None] * jnp.stack(outs, axis=0), axis=0)
    return o, new_bufs


def _layer(x, conv_prev, ffn_prev, mem_k, mem_v, win_bufs, p):
    B, T, _ = x.shape
    h = _rmsnorm(x, p['g_mix'])
    z = h @ p['w_in']
    a, b, q, k, v, qm, gc, ga, gm = jnp.split(z, np.cumsum(IN_SPLIT_SIZES)[:-1].tolist(), axis=-1)

    u = a * jax.nn.sigmoid(b)
    u_ext = jnp.concatenate([conv_prev.astype(u.dtype), u], axis=1)
    c = _causal_dwconv(u_ext, p['w_dw'], p['b_dw'])
    c = jax.nn.silu(_layernorm(c, p['g_cln'], p['b_cln']))
    br_conv = c @ p['w_conv_out']
    conv_new = u_ext[:, -(CONV_WIDTH - 1):]

    q = _rmsnorm(q.reshape(B, T, ATTN_HEADS, HEAD_DIM), p['g_q'])
    k = _rmsnorm(k.reshape(B, T, ATTN_HEADS, HEAD_DIM), p['g_k'])
    v = v.reshape(B, T, ATTN_HEADS, HEAD_DIM)
    o_attn, win_new = _dilated_mixture(q, k, v, win_bufs)
    br_attn = o_attn.reshape(B, T, ATTN_OUT_WIDTH).astype(x.dtype) @ p['w_attn_out']

    qm = _rmsnorm(qm.reshape(B, T, MEM_HEADS, MEM_HEAD_DIM), p['g_mq'])
    s = jnp.einsum('bthe,bmhe->bhtm', qm.astype(jnp.float32), mem_k.astype(jnp.float32)) * (MEM_HEAD_DIM ** -0.5)
    o_m = jnp.einsum('bhtm,bmhe->bthe', jax.nn.softmax(s, axis=-1), mem_v.astype(jnp.float32))
    br_mem = o_m.reshape(B, T, MEM_WIDTH).astype(x.dtype) @ p['w_mem_out']

    m = jax.nn.sigmoid(gc) * br_conv + jax.nn.sigmoid(ga) * br_attn + jax.nn.sigmoid(gm) * br_mem
    x = x + m @ p['w_o']

    h2 = _rmsnorm(x, p['g_ffn'])
    up = h2 @ p['w_up']
    up_ext = jnp.concatenate([ffn_prev.astype(up.dtype), up], axis=1)
    c2 = _causal_dwconv(up_ext, p['w_ffn_dw'], p['b_ffn_dw'])
    gate, val = jnp.split(c2, 2, axis=-1)
    x = x + (jax.nn.silu(gate) * val) @ p['w_down']
    ffn_new = up_ext[:, -(FFN_CONV_WIDTH - 1):]
    return x, (conv_new, *win_new, ffn_new)


def setup_inputs(seed: int = 0) -> dict:
    key = jax.random.key(seed)
    ks = iter(jax.random.split(key, 64))

    def nrm(shape, scale):
        return scale * jax.random.normal(next(ks), shape, jnp.float32)

    def gain(shape):
        return 1.0 + nrm(shape, 0.02)

    L = DEPTH
    wb = [min(w, PAST_LEN) for w, _ in DIL_GROUPS]
    return {
        'x_prompt': nrm((BATCH, SEQ, D_MODEL), 1.0),
        'x_sample': nrm((DEC_BATCH, DEC_SEQ, D_MODEL), 1.0),
        'cache_mem_k': nrm((L, DEC_BATCH, MEM_TOKENS, MEM_HEADS, MEM_HEAD_DIM), 1.0),
        'cache_mem_v': nrm((L, DEC_BATCH, MEM_TOKENS, MEM_HEADS, MEM_HEAD_DIM), 1.0),
        'state_conv': nrm((L, DEC_BATCH, CONV_WIDTH - 1, D_CONV), 0.5),
        'state_win1_k': nrm((L, DEC_BATCH, wb[0], HEADS_PER_GROUP, HEAD_DIM), 1.0),
        'state_win1_v': nrm((L, DEC_BATCH, wb[0], HEADS_PER_GROUP, HEAD_DIM), 1.0),
        'state_win2_k': nrm((L, DEC_BATCH, wb[1], HEADS_PER_GROUP, HEAD_DIM), 1.0),
        'state_win2_v': nrm((L, DEC_BATCH, wb[1], HEADS_PER_GROUP, HEAD_DIM), 1.0),
        'state_win3_k': nrm((L, DEC_BATCH, wb[2], HEADS_PER_GROUP, HEAD_DIM), 1.0),
        'state_win3_v': nrm((L, DEC_BATCH, wb[2], HEADS_PER_GROUP, HEAD_DIM), 1.0),
        'state_ffn_conv': nrm((L, DEC_BATCH, FFN_CONV_WIDTH - 1, 2 * D_FF), 1.0),
        'mem_prompt': nrm((BATCH, MEM_TOKENS, D_MODEL), 1.0),
        'g_mix': gain((L, D_MODEL)),
        'w_in': nrm((L, D_MODEL, IN_WIDTH), D_MODEL ** -0.5),
        'w_dw': nrm((L, CONV_WIDTH, D_CONV), CONV_WIDTH ** -0.5),
        'b_dw': nrm((L, D_CONV), 0.02),
        'g_cln': gain((L, D_CONV)),
        'b_cln': nrm((L, D_CONV), 0.02),
        'w_conv_out': nrm((L, D_CONV, D_MODEL), D_CONV ** -0.5),
        'g_q': gain((L, HEAD_DIM)),
        'g_k': gain((L, HEAD_DIM)),
        'w_attn_out': nrm((L, ATTN_OUT_WIDTH, D_MODEL), ATTN_OUT_WIDTH ** -0.5),
        'g_mem': gain((L, D_MODEL)),
        'w_mem_k': nrm((L, D_MODEL, MEM_WIDTH), D_MODEL ** -0.5),
        'w_mem_v': nrm((L, D_MODEL, MEM_WIDTH), D_MODEL ** -0.5),
        'g_mq': gain((L, MEM_HEAD_DIM)),
        'g_mk': gain((L, MEM_HEAD_DIM)),
        'w_mem_out': nrm((L, MEM_WIDTH, D_MODEL), MEM_WIDTH ** -0.5),
        'w_o': nrm((L, D_MODEL, D_MODEL), D_MODEL ** -0.5),
        'g_ffn': gain((L, D_MODEL)),
        'w_up': nrm((L, D_MODEL, 2 * D_FF), D_MODEL ** -0.5),
        'w_ffn_dw': nrm((L, FFN_CONV_WIDTH, 2 * D_FF), FFN_CONV_WIDTH ** -0.5),
        'b_ffn_dw': nrm((L, 2 * D_FF), 0.02),
        'w_down': nrm((L, D_FF, D_MODEL), D_FF ** -0.5),
    }


def reference(x_prompt, x_sample, cache_mem_k, cache_mem_v, state_conv, state_win1_k, state_win1_v,
              state_win2_k, state_win2_v, state_win3_k, state_win3_v, state_ffn_conv, mem_prompt,
              g_mix, w_in, w_dw, b_dw, g_cln, b_cln, w_conv_out, g_q, g_k, w_attn_out,
              g_mem, w_mem_k, w_mem_v, g_mq, g_mk, w_mem_out, w_o, g_ffn, w_up, w_ffn_dw, b_ffn_dw, w_down):
    B = x_prompt.shape[0]
    conv0 = jnp.zeros((B, CONV_WIDTH - 1, D_CONV), x_prompt.dtype)
    ffn0 = jnp.zeros((B, FFN_CONV_WIDTH - 1, 2 * D_FF), x_prompt.dtype)
    xp, xs = x_prompt, x_sample
    new_p, new_s = [], []
    for l in range(DEPTH):
        p = {'g_mix': g_mix[l], 'w_in': w_in[l], 'w_dw': w_dw[l], 'b_dw': b_dw[l], 'g_cln': g_cln[l],
             'b_cln': b_cln[l], 'w_conv_out': w_conv_out[l], 'g_q': g_q[l], 'g_k': g_k[l],
             'w_attn_out': w_attn_out[l], 'g_mq': g_mq[l], 'w_mem_out': w_mem_out[l], 'w_o': w_o[l],
             'g_ffn': g_ffn[l], 'w_up': w_up[l], 'w_ffn_dw': w_ffn_dw[l], 'b_ffn_dw': b_ffn_dw[l],
             'w_down': w_down[l]}
        mk_p, mv_p = _memory_kv(mem_prompt, g_mem[l], w_mem_k[l], w_mem_v[l], g_mk[l])
        xp, st_p = _layer(xp, conv0, ffn0, mk_p, mv_p, None, p)
        bufs = ((state_win1_k[l], state_win1_v[l]), (state_win2_k[l], state_win2_v[l]),
                (state_win3_k[l], state_win3_v[l]))
        xs, st_s = _layer(xs, state_conv[l], state_ffn_conv[l], cache_mem_k[l], cache_mem_v[l], bufs, p)
        new_p.append(st_p + (mk_p, mv_p))
        new_s.append(st_s)
    sp = [jnp.stack(t, axis=0) for t in zip(*new_p)]
    ss = [jnp.stack(t, axis=0) for t in zip(*new_s)]
    return (xp, xs, sp[0], sp[1], sp[2], sp[3], sp[4], sp[5], sp[6], sp[7], sp[8], sp[9],
            ss[0], ss[1], ss[2], ss[3], ss[4], ss[5], ss[6], ss[7])
```

```python
import numpy as np
from contextlib import ExitStack
import concourse.bass as bass
import concourse.mybir as mybir
from concourse.bass_utils import run_bass_kernel_spmd

F32 = mybir.dt.float32
BF16 = mybir.dt.bfloat16
AF = mybir.ActivationFunctionType
ALU = mybir.AluOpType
AX = mybir.AxisListType

NEG = -30000.0
EPS = 1e-6
D = 2048
KC = 16
NT = 1058
TOKB = [(0, 512), (512, 512), (1024, 34)]
TOKE = [(0, 352), (352, 354), (706, 352)]
GROUPS = [(1, 128), (4, 512), (16, 2048)]
ERES = {1: [0], 4: [2, 3], 16: [14, 15]}
C_A, C_B, C_Q, C_K, C_V, C_QM, C_GC, C_GA, C_GM = 0, 1024, 2048, 3584, 5120, 6656, 7680, 9728, 11776
DFF = 5632
RING_UNITS = 12
UNIT = 2048


def halo_tiles():
    out = []
    for g, (d, W) in enumerate(GROUPS):
        for r in range(d):
            if r in ERES[d]:
                out.append((g, r, 1))
                out.append((g, r, 2))
            out.append((g, r, 0))
    return out


HT_LIST = halo_tiles()
HT_IDX = {t: i for i, t in enumerate(HT_LIST)}
NHB = len(HT_LIST) + 1


class Builder:
    def __init__(self):
        self.nc = bass.Bass("TRN2", target_bir_lowering=False)
        nc = self.nc
        self.root = ExitStack()
        self.eng = {'pe': nc.tensor, 'act': nc.scalar, 'dve': nc.vector, 'pool': nc.gpsimd, 'sp': nc.sync}
        self.esem = {}
        self.ecount = {}
        for e in self.eng:
            self.esem[e] = self.root.enter_context(nc.semaphore("e_" + e))
            self.ecount[e] = 0
        self.waited = {e: {} for e in self.eng}
        self.last_w = {}
        self.readers = {}
        self.dsems = {}
        self.dram = {}
        self.ring_pos = 0
        self.n_inst = 0

    def din(self, name, shape, dtype=F32):
        t = self.nc.dram_tensor(name, list(shape), dtype, kind="ExternalInput")
        self.dram[name] = t.ap()
        return self.dram[name]

    def dout(self, name, shape, dtype=F32):
        t = self.nc.dram_tensor(name, list(shape), dtype, kind="ExternalOutput")
        self.dram[name] = t.ap()
        return self.dram[name]

    def sb(self, stack, name, shape, dtype):
        return stack.enter_context(self.nc.sbuf_tensor(name, list(shape), dtype))

    def ps(self, stack, name, shape, dtype):
        return stack.enter_context(self.nc.psum_tensor(name, list(shape), dtype))

    def _deps(self, reads, writes):
        toks = []
        for k in list(reads) + list(writes):
            if k in self.last_w:
                toks.append(self.last_w[k])
        for k in writes:
            toks.extend(self.readers.get(k, ()))
        return toks

    def _wait(self, e, toks):
        w = self.waited[e]
        need = {}
        for (sem, val, sid) in toks:
            if w.get(sid, 0) >= val:
                continue
            if sid == 'e_' + e and (e == 'pe' or val > self.ecount[e]):
                continue
            if need.get(sid, (None, 0))[1] < val:
                need[sid] = (sem, val)
        for sid, (sem, val) in need.items():
            self.eng[e].wait_ge(sem, val)
            w[sid] = val

    def _record(self, tok, reads, writes):
        for k in writes:
            self.last_w[k] = tok
            self.readers[k] = []
        for k in reads:
            self.readers.setdefault(k, []).append(tok)
            if len(self.readers[k]) > 24:
                best = {}
                for t in self.readers[k]:
                    if t[2] not in best or best[t[2]][1] < t[1]:
                        best[t[2]] = t
                self.readers[k] = list(best.values())

    def op(self, e, fn, R=(), W=(), inc=True):
        self._wait(e, self._deps(R, W))
        ins = fn(self.eng[e])
        self.n_inst += 1
        if inc:
            ins.then_inc(self.esem[e], 1)
            self.ecount[e] += 1
            tok = (self.esem[e], self.ecount[e], 'e_' + e)
        else:
            tok = (self.esem[e], self.ecount[e] + 1, 'e_' + e)
        self._record(tok, R, W)
        return ins

    def dsem(self, name):
        if name not in self.dsems:
            self.dsems[name] = [self.root.enter_context(self.nc.semaphore("d_" + name)), 0]
        return self.dsems[name]

    def dma(self, q, out, in_, R=(), W=(), sem=None, **kw):
        self._wait(q, self._deps(R, W))
        if sem is None:
            sem = (list(W) + list(R))[0]
        s = self.dsem(sem)
        ins = self.eng[q].dma_start(out=out, in_=in_, **kw)
        ins.then_inc(s[0], 16)
        s[1] += 16
        self.n_inst += 1
        tok = (s[0], s[1], 'd_' + sem)
        self._record(tok, R, W)
        return ins

    def barrier(self):
        toks = [(self.esem[e], self.ecount[e], 'e_' + e) for e in self.eng if self.ecount[e] > 0]
        toks += [(s[0], s[1], 'd_' + n) for n, s in self.dsems.items() if s[1] > 0]
        for e in self.eng:
            self._wait(e, toks)
        self.last_w.clear()
        self.readers.clear()

    def finish(self):
        toks = [(s[0], s[1], 'd_' + n) for n, s in self.dsems.items() if s[1] > 0]
        toks += [(self.esem[e], self.ecount[e], 'e_' + e) for e in self.eng if self.ecount[e] > 0]
        self._wait('sp', toks)


class _Stop(Exception):
    pass


def _ck(n):
    import os
    if int(os.environ.get('KSTOP', '999')) == n:
        raise _Stop()


def build_program():
    B = Builder()
    try:
        _build_body(B)
    except _Stop:
        pass
    B.finish()
    return B


def _build_body(B):
    nc = B.nc
    root = B.root
    x_main = B.din("x_main", [1024, D])
    x_halo = B.din("x_halo", [2048, D])
    x_far = B.din("x_far", [256, D])
    x_misc = B.din("x_misc", [34, D])
    hbias_d = B.din("hbias", [128, NHB])
    eflag_d = B.din("eflag", [128, 1])
    mem_p = B.din("mem_p", [256, D])
    cmk = B.din("cmk", [4, 256, 1024])
    cmv = B.din("cmv", [4, 256, 1024])
    st_conv = B.din("st_conv", [4, 30, 1024])
    st_ffn = B.din("st_ffn", [8, 2 * DFF])
    st_wk = [B.din("st_wk%d" % g, [4, W, 512]) for g, (d, W) in enumerate(GROUPS)]
    st_wv = [B.din("st_wv%d" % g, [4, W, 512]) for g, (d, W) in enumerate(GROUPS)]
    c_ident = B.din("c_ident", [128, 128])
    c_mask = B.din("c_mask", [128, 256])
    c_smask = B.din("c_smask", [34, 96])
    g_mix = B.din("g_mix", [D]); g_ffn = B.din("g_ffn", [D]); g_mem = B.din("g_mem", [D])
    w_in = B.din("w_in", [D, 13824])
    w_dw = B.din("w_dw", [31, 1024]); b_dw = B.din("b_dw", [1024]); g_cln = B.din("g_cln", [1024]); b_cln = B.din("b_cln", [1024])
    w_conv_out = B.din("w_conv_out", [1024, D])
    g_q = B.din("g_q", [128]); g_k = B.din("g_k", [128])
    w_attn_out = B.din("w_attn_out", [512, D])
    w_mem_k = B.din("w_mem_k", [D, 1024]); w_mem_v = B.din("w_mem_v", [D, 1024])
    g_mq = B.din("g_mq", [256]); g_mk = B.din("g_mk", [256])
    w_mem_out = B.din("w_mem_out", [1024, D])
    w_o = B.din("w_o", [D, D])
    w_up = B.din("w_up", [D, 2 * DFF]); w_ffn_dw = B.din("w_ffn_dw", [3, 2 * DFF]); b_ffn_dw = B.din("b_ffn_dw", [2 * DFF])
    w_down = B.din("w_down", [DFF, D])

    y_main = B.dout("y_main", [1024, D])
    y_misc = B.dout("y_misc", [34, D])
    o_pconv = B.dout("o_pconv", [32, 1024])
    o_k = [B.dout("o_k%d" % g, [n, 512]) for g, n in enumerate([128, 512, 1024])]
    o_v = [B.dout("o_v%d" % g, [n, 512]) for g, n in enumerate([128, 512, 1024])]
    o_pffn = B.dout("o_pffn", [10, 2 * DFF])
    o_pmk = B.dout("o_pmk", [256, 1024]); o_pmv = B.dout("o_pmv", [256, 1024])
    o_sconv = B.dout("o_sconv", [4, 30, 1024])
    o_swk = [B.dout("o_swk%d" % g, [4, W, 512]) for g, (d, W) in enumerate(GROUPS)]
    o_swv = [B.dout("o_swv%d" % g, [4, W, 512]) for g, (d, W) in enumerate(GROUPS)]

    ident_f = B.sb(root, "ident_f", [128, 128], F32)
    ident_b = B.sb(root, "ident_b", [128, 128], BF16)
    ones_b = B.sb(root, "ones_b", [128, 128], BF16)
    mask_b = B.sb(root, "mask_b", [128, 256], BF16)
    smask_b = B.sb(root, "smask_b", [34, 96], BF16)
    hbias = B.sb(root, "hbias_s", [128, NHB], F32)
    eflag = B.sb(root, "eflag_s", [128, 1], F32)
    epsc = B.sb(root, "epsc", [128, 1], F32)
    gT_mix = B.sb(root, "gT_mix", [128, KC], F32)
    gT_ffn = B.sb(root, "gT_ffn", [128, KC], F32)
    gT_mem = B.sb(root, "gT_mem", [128, KC], F32)
    gq_t = B.sb(root, "gq_t", [128, 128], F32)
    gk_t = B.sb(root, "gk_t", [128, 128], F32)
    gmq_t = B.sb(root, "gmq_t", [128, 256], F32)
    gmk_t = B.sb(root, "gmk_t", [128, 256], F32)
    stat = B.sb(root, "stat", [128, 192], F32)
    cpar = B.sb(root, "cpar", [128, 3, 8], F32)
    junk = B.sb(root, "junk", [128, 2048], BF16)
    psb = [B.ps(root, "ps%d" % i, [128, 512], F32) for i in range(8)]

    ARENA_KB = 196
    arena = B.sb(root, "arena", [128, ARENA_KB * 256], F32)
    apos = {'p': 0}

    def set_ptr(kb):
        apos['p'] = int(kb * 1024)

    def al(name, shape, dtype, at=None):
        esz = 4 if dtype == F32 else 2
        n = 1
        for d_ in shape[1:]:
            n *= d_
        nbytes = (n * esz + 31) // 32 * 32
        off = apos['p'] if at is None else int(at * 1024)
        if at is None:
            apos['p'] = off + nbytes
        assert off + nbytes <= ARENA_KB * 1024, (name, off, nbytes)
        v = arena[:, off // 4:(off + nbytes) // 4]
        if dtype != F32:
            v = v.bitcast(dtype)
        v = v[0:shape[0], 0:n]
        if len(shape) == 3:
            v = v.rearrange("p (a b) -> p a b", a=shape[1])
        return v

    ph = {'stat_i': 0}

    def stat_slot(n=4):
        i = ph['stat_i']
        ph['stat_i'] = (i + 12) % 192
        return i

    B.dma('sp', ident_f[:], c_ident[:], W=['ident_f'], sem='const')
    B.dma('pool', ident_b[:], c_ident[:], W=['ident_b'], sem='constp')
    B.dma('pool', mask_b[:], c_mask[:], W=['mask_b'], sem='constp')
    B.dma('pool', smask_b[:], c_smask[:], W=['smask_b'], sem='constp')
    B.dma('sp', hbias[:], hbias_d[:], W=['hbias'], sem='const')
    B.dma('sp', eflag[:], eflag_d[:], W=['eflag'], sem='const')
    import os
    SK = os.environ.get('KSKIP', '')
    for (t, src, key) in [(gT_mix, g_mix, 'gT_mix'), (gT_ffn, g_ffn, 'gT_ffn'), (gT_mem, g_mem, 'gT_mem')]:
        if 'g' in SK: break
        B.dma('sp', t[:], src.rearrange("(k p) -> p k", p=128), W=[key], sem='const', allow_slow_non_contiguous=True)
    for (t, src, key, n) in [(gq_t, g_q, 'gq', 128), (gk_t, g_k, 'gk', 128), (gmq_t, g_mq, 'gmq', 256), (gmk_t, g_mk, 'gmk', 256)]:
        if 'b' in SK: break
        B.dma('sp', t[:], src.partition_broadcast(128), W=[key], sem='const')
    for i_, src_ in enumerate([b_dw, g_cln, b_cln]):
        B.dma('sp', cpar[:, i_, :], src_.rearrange("(j p) -> p j", p=128), W=['cpar'], sem='const', allow_slow_non_contiguous=True)
    B.op('dve', lambda e: e.memset(ones_b[:], 1.0), W=['ones_b'])
    B.op('dve', lambda e: e.memset(epsc[:], EPS), W=['epsc'])

    _ck(1)
    def new_ring(stack, units, tag):
        ph['ring'] = al("ring_" + tag, [128, units * UNIT], BF16)
        ph['ring_units'] = units
        B.ring_pos = 0

    def ring_alloc(nunits):
        p = B.ring_pos
        if p + nunits > ph['ring_units']:
            p = 0
        B.ring_pos = p + nunits
        return p, ['ring%d' % u for u in range(p, p + nunits)]

    def load_panel(Wd, k0, nk, c0, ncol):
        nun = -(-(nk * ncol) // UNIT)
        p, keys = ring_alloc(nun)
        view = ph['ring'][:, p * UNIT: p * UNIT + nk * ncol].rearrange("p (k n) -> p k n", k=nk)
        src = Wd[k0 * 128:(k0 + nk) * 128, c0:c0 + ncol].rearrange("(k p) n -> p k n", p=128)
        B.dma('pool', view, src, W=keys, sem=keys[0])
        return view, keys

    def load_w16(Wd, c0, ncol=512):
        return [load_panel(Wd, 0, 8, c0, ncol) + (0,), load_panel(Wd, 8, 8, c0, ncol) + (8,)]

    def prepA(stack_bufs, src, rows, from_sbuf=None, extra_scale=None):
        xt_list, pst = stack_bufs
        slot = ph.get('xt_i', 0)
        ph['xt_i'] = (slot + 1) % len(xt_list)
        xt = xt_list[slot]
        xk = 'xt%d' % slot
        if from_sbuf is None:
            B.dma('sp', xt[0:rows, :], src, W=[xk])
            xin = xt
            rk = [xk]
        else:
            xin, rk0 = from_sbuf
            rk = [rk0]
        si = stat_slot(4)
        sk = 'stat%d' % si
        B.op('act', lambda e: e.activation(out=junk[0:rows, :], in_=xin[0:rows, :], func=AF.Square,
                                           accum_out=stat[0:rows, si:si + 1]), R=rk, W=[sk])
        B.op('act', lambda e: e.activation(out=stat[0:rows, si + 1:si + 2], in_=stat[0:rows, si:si + 1], func=AF.Ln,
                                           bias=epsc[0:rows, :], scale=1.0 / D), R=[sk, 'epsc'], W=[sk])
        B.op('act', lambda e: e.activation(out=stat[0:rows, si + 2:si + 3], in_=stat[0:rows, si + 1:si + 2], func=AF.Exp,
                                           scale=-0.5), R=[sk], W=[sk])
        if extra_scale is not None:
            B.op('dve', lambda e: e.tensor_tensor(out=stat[0:rows, si + 2:si + 3], in0=stat[0:rows, si + 2:si + 3],
                                                  in1=extra_scale[0:rows, :], op=ALU.mult), R=[sk, 'esc'], W=[sk])
        B.op('dve', lambda e: e.tensor_scalar(out=xt[0:rows, :], in0=xin[0:rows, :], scalar1=stat[0:rows, si + 2:si + 3],
                                              scalar2=None, op0=ALU.mult), R=rk + [sk], W=[xk])
        return (xt, xk, rows, pst)

    def prepT(pa, gT, gkey, dst_fn, dst_keys):
        xt, xk, rows, pst = pa
        for q in range(4):
            pi = ph.get('pst_i', 0)
            ph['pst_i'] = (pi + 1) % len(pst)
            bank, bkey = pst[pi]
            for j in range(4):
                kc = q * 4 + j
                B.op('pe', lambda e, kc=kc, j=j: e.transpose(out=bank[:, j * 128:j * 128 + rows],
                                                              in_=xt[0:rows, kc * 128:(kc + 1) * 128],
                                                              identity=ident_f[0:rows, 0:rows]),
                     R=[xk, 'ident_f'], W=[bkey], inc=(j == 3))
            B.op('dve', lambda e, q=q: e.tensor_tensor(
                out=dst_fn(q),
                in0=bank[:, :].rearrange("p (j t) -> p j t", j=4)[:, :, 0:rows],
                in1=gT[:, q * 4:q * 4 + 4].unsqueeze(2).broadcast_to([128, 4, rows]), op=ALU.mult),
                R=[bkey, gkey], W=dst_keys)

    def prep(stack_bufs, src, rows, gT, gkey, dst_fn, dst_keys, from_sbuf=None, extra_scale=None):
        pa = prepA(stack_bufs, src, rows, from_sbuf=from_sbuf, extra_scale=extra_scale)
        prepT(pa, gT, gkey, dst_fn, dst_keys)

    def proj_tm(lhs_fn, M, panels, bank, bkey, rkeys, ncol=512, col_off=0):
        n = 0
        for (view, keys, kb) in panels:
            for k in range(8):
                kc = kb + k
                B.op('pe', lambda e, view=view, k=k, kc=kc, n=n: e.matmul(
                    bank[0:M, col_off:col_off + ncol], lhs_fn(kc), view[:, k, 0:ncol], start=(n == 0), stop=(n == 15)),
                    R=rkeys + keys, W=[bkey], inc=(n == 15))
                n += 1

    def head_norm(bank, bkey, M, nh, hd, g_tile, gkey, kf, kfk, out_f32, of_keys, out_bf, ob_keys):
        w = nh * hd
        B.op('act', lambda e: e.activation(out=kf[0:M, 0:w], in_=bank[0:M, 0:w], func=AF.Copy), R=[bkey], W=[kfk])
        si = stat_slot(12)
        sk = 'stat%d' % si
        sqs = ph['sqs']
        B.op('act', lambda e: e.activation(out=sqs[0:M, 0:w], in_=kf[0:M, 0:w], func=AF.Square), R=[kfk], W=['sqs'])
        B.op('dve', lambda e: e.tensor_reduce(out=stat[0:M, si:si + nh], in_=sqs[0:M, 0:w].rearrange("p (h d) -> p h d", h=nh),
                                              axis=AX.X, op=ALU.add), R=['sqs'], W=[sk])
        B.op('act', lambda e: e.activation(out=stat[0:M, si + 4:si + 4 + nh], in_=stat[0:M, si:si + nh], func=AF.Ln,
                                           bias=epsc[0:M, :], scale=1.0 / hd), R=[sk, 'epsc'], W=[sk])
        B.op('act', lambda e: e.activation(out=stat[0:M, si + 8:si + 8 + nh], in_=stat[0:M, si + 4:si + 4 + nh], func=AF.Exp,
                                           scale=-0.5), R=[sk], W=[sk])
        dst = out_f32 if out_f32 is not None else out_bf
        dk = of_keys if out_f32 is not None else ob_keys
        for h in range(nh):
            B.op('dve', lambda e, h=h: e.scalar_tensor_tensor(
                out=dst[0:M, h * hd:(h + 1) * hd], in0=kf[0:M, h * hd:(h + 1) * hd],
                scalar=stat[0:M, si + 8 + h:si + 9 + h], in1=g_tile[0:M, 0:hd], op0=ALU.mult, op1=ALU.mult),
                R=[kfk, sk, gkey], W=dk)
        if out_f32 is not None and out_bf is not None:
            B.op('act', lambda e: e.activation(out=out_bf[0:M, 0:w], in_=out_f32[0:M, 0:w], func=AF.Copy), R=of_keys, W=ob_keys)

    st_mix = ExitStack()
    st1 = ExitStack()
    hT = al("hT", [128, KC, NT], BF16, at=0)
    o_attn = al("o_attn", [128, 4, NT], BF16, at=34)
    hT_h1 = al("hT_h1", [128, KC, 32], BF16, at=42.5)
    c_act = al("c_act", [128, 8, NT], BF16, at=43.5)
    o_mem = al("o_mem", [128, 8, NT], BF16, at=60.5)
    set_ptr(43.5)
    new_ring(st1, 12, "a")
    xt_list = [al("xt%d" % i, [128, D], F32) for i in range(2)]
    pst = [(psb[6], 'ps6'), (psb[7], 'ps7')]
    prep_bufs = (xt_list, pst)
    hTh = [al("hTh%d" % i, [128, KC, 128], BF16) for i in range(2)]
    acc_O = al("acc_O", [128, 4, NT], F32)
    acc_D = al("acc_D", [128, 4, NT], F32)
    kf_l = [al("kf%d" % i, [128, 512], F32) for i in range(2)]
    ko_l = [al("ko%d" % i, [128, 512], F32) for i in range(3)]
    vo_l = [al("vo%d" % i, [128, 512], F32) for i in range(3)]
    sqs = al("sqs", [128, 512], F32)
    qb_l = [al("qb%d" % i, [128, 512], BF16) for i in range(2)]
    kb_l = [al("kb%d" % i, [128, 512], BF16) for i in range(2)]
    kT_l = [al("kT%d" % i, [128, 4, 128], BF16) for i in range(3)]
    vS_l = [al("vS%d" % i, [128, 512], BF16) for i in range(4)]
    qT_l = [al("qT%d" % i, [128, 4, 128], BF16) for i in range(2)]
    pA_l = [al("pA%d" % i, [128, 512], BF16) for i in range(2)]
    pB_l = [al("pB%d" % i, [128, 512], BF16) for i in range(2)]
    kTs = al("kTs", [128, 4, 34], BF16)
    qTs = al("qTs", [128, 4, 34], BF16)
    vSs = al("vSs", [34, 512], BF16)
    pBs = al("pBs", [34, 4, 32], BF16)
    ksb_l = [al("ksb%d" % i, [128, 512], BF16) for i in range(3)]
    vsb_l = [al("vsb%d" % i, [128, 512], BF16) for i in range(3)]
    kTA_l = [al("kTA%d" % i, [128, 4, 128], BF16) for i in range(2)]
    pAs_l = [al("pAs%d" % i, [128, 4, 8], BF16) for i in range(2)]
    rot = {}
    ph['sqs'] = sqs

    def nxt(name, lst):
        i = rot.get(name, 0)
        rot[name] = (i + 1) % len(lst)
        return lst[i], '%s%d' % (name, i)

    PQ, PK, PV, PT, PSA, PSB, PO, PD = range(8)
    pT_bf = psb[PT][:, :].bitcast(BF16)

    B.barrier()
    a_tiles = [(x_misc[:, :], 34, 1024)] + [(x_main[t * 128:(t + 1) * 128, :], 128, t * 128) for t in range(8)]
    pa_cur = prepA(prep_bufs, a_tiles[0][0], a_tiles[0][1])
    for ti_, (src_, rows_, c0_) in enumerate(a_tiles):
        pa_nxt = prepA(prep_bufs, a_tiles[ti_ + 1][0], a_tiles[ti_ + 1][1]) if ti_ + 1 < len(a_tiles) else None
        prepT(pa_cur, gT_mix, 'gT_mix', lambda q, c0_=c0_, rows_=rows_: hT[:, q * 4:q * 4 + 4, c0_:c0_ + rows_], ['hT'])
        pa_cur = pa_nxt
    _ck(2)

    def transpose4(src_bf, skey, M, half, dst, dkeys, perm=None):
        pk = 'ps%d' % PT
        base = half * 512
        for h in range(4):
            B.op('pe', lambda e, h=h: e.transpose(out=pT_bf[:, base + h * 128: base + h * 128 + M],
                                                  in_=src_bf[0:M, h * 128:(h + 1) * 128], identity=ident_b[0:M, 0:M]),
                 R=[skey, 'ident_b'], W=[pk], inc=(h == 3))
        src = pT_bf[:, base:base + 512].rearrange("p (h t) -> p h t", h=4)[:, :, 0:M]
        if perm is None:
            B.op('act', lambda e: e.activation(out=dst, in_=src, func=AF.Copy), R=[pk], W=dkeys)
        else:
            perm(src, pk)

    scale_att = 1.0 / np.sqrt(128.0)

    def attend_scores(QB, qc0, nq, kTA, kTAk, vA, vAk, kTB, kTBk, vB, vBk, qT, qTk, hb_col, acc_cols_fn, first, hb_colB=None):
        pA, pAk = nxt('pA', pA_l)
        pB, pBk = nxt('pB', pB_l)
        sA, sB = psb[PSA], psb[PSB]
        mA = mask_b[:, qc0:qc0 + nq].unsqueeze(1).broadcast_to([128, 4, nq])
        B.op('pe', lambda e: e.matmul(sA[:, 0:4 * nq], ident_b[:, :], mA, start=True, stop=False),
             R=['ident_b', 'mask_b'], W=['ps%d' % PSA], inc=False)
        for h in range(4):
            B.op('pe', lambda e, h=h: e.matmul(sA[:, h * nq:(h + 1) * nq], kTA[:, h, 0:128], qT[:, h, qc0:qc0 + nq],
                                               start=False, stop=(h == 3)), R=[kTAk, qTk], W=['ps%d' % PSA], inc=(h == 3))
        B.op('act', lambda e: e.activation(out=pA[:, 0:4 * nq], in_=sA[:, 0:4 * nq], func=AF.Exp,
                                           bias=hbias[:, hb_col:hb_col + 1], scale=scale_att),
             R=['ps%d' % PSA, 'hbias'], W=[pAk])
        mB = mask_b[0:QB, 128 + qc0:128 + qc0 + nq].unsqueeze(1).broadcast_to([QB, 4, nq])
        B.op('pe', lambda e: e.matmul(sB[0:QB, 0:4 * nq], ident_b[0:QB, 0:QB], mB, start=True, stop=False),
             R=['ident_b', 'mask_b'], W=['ps%d' % PSB], inc=False)
        for h in range(4):
            B.op('pe', lambda e, h=h: e.matmul(sB[0:QB, h * nq:(h + 1) * nq], kTB[:, h, 0:QB], qT[:, h, qc0:qc0 + nq],
                                               start=False, stop=(h == 3)), R=[kTBk, qTk], W=['ps%d' % PSB], inc=(h == 3))
        cb = NHB - 1 if hb_colB is None else hb_colB
        B.op('act', lambda e: e.activation(out=pB[0:QB, 0:4 * nq], in_=sB[0:QB, 0:4 * nq], func=AF.Exp,
                                           bias=hbias[0:QB, cb:cb + 1], scale=scale_att),
             R=['ps%d' % PSB, 'hbias'], W=[pBk])
        return (pA, pAk, pB, pBk)

    def attend_pv(st, QB, qc0, nq, kTA, kTAk, vA, vAk, kTB, kTBk, vB, vBk, qT, qTk, hb_col, acc_cols_fn, first, hb_colB=None):
        pA, pAk, pB, pBk = st
        oT, dn = psb[PO], psb[PD]
        for h in range(4):
            B.op('pe', lambda e, h=h: e.matmul(oT[:, h * nq:(h + 1) * nq], vA[:, h * 128:(h + 1) * 128], pA[:, h * nq:(h + 1) * nq],
                                               start=True, stop=False), R=[vAk, pAk], W=['ps%d' % PO], inc=False)
            B.op('pe', lambda e, h=h: e.matmul(oT[:, h * nq:(h + 1) * nq], vB[0:QB, h * 128:(h + 1) * 128], pB[0:QB, h * nq:(h + 1) * nq],
                                               start=False, stop=True), R=[vBk, pBk], W=['ps%d' % PO], inc=(h == 3))
        B.op('pe', lambda e: e.matmul(dn[:, 0:4 * nq], ones_b[:, :], pA[:, 0:4 * nq], start=True, stop=False),
             R=['ones_b', pAk], W=['ps%d' % PD], inc=False)
        B.op('pe', lambda e: e.matmul(dn[:, 0:4 * nq], ones_b[0:QB, :], pB[0:QB, 0:4 * nq], start=False, stop=True),
             R=['ones_b', pBk], W=['ps%d' % PD], inc=True)
        for (acc, pbank, pk, ak) in [(acc_O, oT, 'ps%d' % PO, 'acc_O'), (acc_D, dn, 'ps%d' % PD, 'acc_D')]:
            dst = acc_cols_fn(acc)
            src = pbank[:, 0:4 * nq].rearrange("p (h t) -> p h t", h=4)
            if first:
                B.op('act', lambda e, dst=dst, src=src: e.activation(out=dst, in_=src, func=AF.Copy), R=[pk], W=[ak])
            else:
                B.op('dve', lambda e, dst=dst, src=src: e.tensor_tensor(out=dst, in0=src, in1=dst, op=ALU.add), R=[pk, ak], W=[ak])


    def attend(*args, **kw):
        st = attend_scores(*args, **kw)
        attend_pv(st, *args, **kw)

    for g, (d, W) in enumerate(GROUPS):
        QB = min(128, 1024 // d)
        nmb = 1024 // (d * QB)
        Wq = load_w16(w_in, C_Q + g * 512)
        Wk = load_w16(w_in, C_K + g * 512)
        Wv = load_w16(w_in, C_V + g * 512)

        def tile_proj(lhs_fn, lkeys, M, want_q):
            proj_tm(lhs_fn, M, Wk, psb[PK], 'ps%d' % PK, lkeys)
            proj_tm(lhs_fn, M, Wv, psb[PV], 'ps%d' % PV, lkeys)
            if want_q:
                proj_tm(lhs_fn, M, Wq, psb[PQ], 'ps%d' % PQ, lkeys)

        def tile_norm(M, want_q, out_rows):
            res = {}
            kf, kfk = nxt('kf', kf_l)
            kb, kbk = nxt('kb', kb_l)
            if out_rows is not None:
                ko, kok = nxt('ko', ko_l)
                head_norm(psb[PK], 'ps%d' % PK, M, 4, 128, gk_t, 'gk', kf, kfk, ko, [kok], kb, [kbk])
                B.dma('sp', out_rows(o_k[g]), ko[0:M, :], R=[kok])
            else:
                head_norm(psb[PK], 'ps%d' % PK, M, 4, 128, gk_t, 'gk', kf, kfk, None, None, kb, [kbk])
            vS, vSk = nxt('vS', vS_l)
            B.op('act', lambda e: e.activation(out=vS[0:M, :], in_=psb[PV][0:M, :], func=AF.Copy), R=['ps%d' % PV], W=[vSk])
            if out_rows is not None:
                vo, vok = nxt('vo', vo_l)
                B.op('dve', lambda e: e.tensor_copy(out=vo[0:M, :], in_=psb[PV][0:M, :]), R=['ps%d' % PV], W=[vok])
                B.dma('sp', out_rows(o_v[g]), vo[0:M, :], R=[vok])
            res.update(kb=kb, kbk=kbk, vS=vS, vSk=vSk)
            if want_q:
                kf2, kfk2 = nxt('kf', kf_l)
                qb, qbk = nxt('qb', qb_l)
                head_norm(psb[PQ], 'ps%d' % PQ, M, 4, 128, gq_t, 'gq', kf2, kfk2, None, None, qb, [qbk])
                res.update(qb=qb, qbk=qbk)
            return res

        def tile_tr(M, want_q, res):
            kT, kTk = nxt('kT', kT_l)
            transpose4(res['kb'], res['kbk'], M, 0, kT[:, :, 0:M], [kTk])
            res.update(kT=kT, kTk=kTk)
            if want_q:
                qT, qTk = nxt('qT', qT_l)
                transpose4(res['qb'], res['qbk'], M, 1, qT[:, :, 0:M], [qTk])
                res.update(qT=qT, qTk=qTk)
            return res

        proj_tm(lambda kc: hT[:, kc, 1024:1058], 34, Wk, psb[PK], 'ps%d' % PK, ['hT'])
        proj_tm(lambda kc: hT[:, kc, 1024:1058], 34, Wv, psb[PV], 'ps%d' % PV, ['hT'])
        proj_tm(lambda kc: hT[:, kc, 1024:1058], 34, Wq, psb[PQ], 'ps%d' % PQ, ['hT'])
        kf, kfk = nxt('kf', kf_l)
        kb, kbk = nxt('kb', kb_l)
        ko, kok = nxt('ko', ko_l)
        head_norm(psb[PK], 'ps%d' % PK, 34, 4, 128, gk_t, 'gk', kf, kfk, ko, [kok], kb, [kbk])
        for b in range(4):
            B.dma('sp', o_swk[g][b, W - 8:W, :], ko[b * 8:(b + 1) * 8, :], R=[kok])
        transpose4(kb, kbk, 34, 0, kTs[:, :, :], ['kTs'])
        B.op('act', lambda e: e.activation(out=vSs[0:34, :], in_=psb[PV][0:34, :], func=AF.Copy), R=['ps%d' % PV], W=['vSs'])
        vo, vok = nxt('vo', vo_l)
        B.op('dve', lambda e: e.tensor_copy(out=vo[0:34, :], in_=psb[PV][0:34, :]), R=['ps%d' % PV], W=[vok])
        for b in range(4):
            B.dma('sp', o_swv[g][b, W - 8:W, :], vo[b * 8:(b + 1) * 8, :], R=[vok])
        kf2, kfk2 = nxt('kf', kf_l)
        qb, qbk = nxt('qb', qb_l)
        head_norm(psb[PQ], 'ps%d' % PQ, 34, 4, 128, gq_t, 'gq', kf2, kfk2, None, None, qb, [qbk])
        ds = min(d, 8)
        QBs = 8 // ds

        def qperm(src, pk):
            for h in range(4):
                B.op('act', lambda e, h=h: e.activation(
                    out=qTs[:, h, 0:32].rearrange("p (b r q) -> p b q r", b=4, r=ds, q=QBs),
                    in_=src[:, h, 0:32].rearrange("p (b q r) -> p b q r", b=4, q=QBs, r=ds), func=AF.Copy),
                    R=[pk], W=['qTs'])
        transpose4(qb, qbk, 34, 1, None, None, perm=qperm)
        sB = psb[PSB]
        for h in range(4):
            B.op('pe', lambda e, h=h: e.matmul(sB[0:32, h * 32:(h + 1) * 32], kTs[:, h, 0:32], qTs[:, h, 0:32],
                                               start=True, stop=False), R=['kTs', 'qTs'], W=['ps%d' % PSB], inc=False)
            B.op('pe', lambda e, h=h: e.matmul(sB[0:32, h * 32:(h + 1) * 32], ident_b[0:32, 0:32], smask_b[0:32, g * 32:(g + 1) * 32],
                                               start=False, stop=True), R=['ident_b', 'smask_b'], W=['ps%d' % PSB], inc=(h == 3))
        B.op('act', lambda e: e.activation(out=pBs[0:32, :, :], in_=sB[0:32, 0:128].rearrange("p (h t) -> p h t", h=4),
                                           func=AF.Exp, scale=scale_att), R=['ps%d' % PSB], W=['pBs'])
        oTs, dns = psb[PO], psb[PD]
        combos = [(b, r) for b in range(4) for r in range(ds)]

        def s_stageA(b, r):
            ksb, ksbk = nxt('ksb', ksb_l)
            vsb, vsbk = nxt('vsb', vsb_l)
            B.dma('pool', ksb[:, :], st_wk[g][b, r:W:d, :], W=[ksbk])
            B.dma('pool', vsb[:, :], st_wv[g][b, r:W:d, :], W=[vsbk])
            kTA, kTAk = nxt('kTA', kTA_l)
            transpose4(ksb, ksbk, 128, 0, kTA[:, :, :], [kTAk])
            return (kTA, kTAk, vsb, vsbk)

        def s_stageB(b, r, st):
            kTA, kTAk, vsb, vsbk = st
            off = b * 8 + r * QBs
            pAs, pAsk = nxt('pAs', pAs_l)
            sA = psb[PSA]
            for h in range(4):
                B.op('pe', lambda e, h=h: e.matmul(sA[:, h * QBs:(h + 1) * QBs], kTA[:, h, :], qTs[:, h, off:off + QBs],
                                                   start=True, stop=False), R=[kTAk, 'qTs'], W=['ps%d' % PSA], inc=False)
                B.op('pe', lambda e, h=h: e.matmul(sA[:, h * QBs:(h + 1) * QBs], ident_b[:, :], mask_b[:, 0:QBs],
                                                   start=False, stop=True), R=['ident_b', 'mask_b'], W=['ps%d' % PSA], inc=(h == 3))
            B.op('act', lambda e: e.activation(out=pAs[:, :, 0:QBs], in_=sA[:, 0:4 * QBs].rearrange("p (h t) -> p h t", h=4),
                                               func=AF.Exp, scale=scale_att), R=['ps%d' % PSA], W=[pAsk])
            for h in range(4):
                B.op('pe', lambda e, h=h: e.matmul(oTs[:, h * 32 + off:h * 32 + off + QBs], vSs[0:32, h * 128:(h + 1) * 128],
                                                   pBs[0:32, h, off:off + QBs], start=True, stop=False),
                     R=['vSs', 'pBs'], W=['ps%d' % PO], inc=False)
                B.op('pe', lambda e, h=h: e.matmul(oTs[:, h * 32 + off:h * 32 + off + QBs], vsb[:, h * 128:(h + 1) * 128],
                                                   pAs[:, h, 0:QBs], start=False, stop=True),
                     R=[vsbk, pAsk], W=['ps%d' % PO], inc=(h == 3))
            for h in range(4):
                B.op('pe', lambda e, h=h: e.matmul(dns[:, h * 32 + off:h * 32 + off + QBs], ones_b[0:32, :],
                                                   pBs[0:32, h, off:off + QBs], start=True, stop=False),
                     R=['ones_b', 'pBs'], W=['ps%d' % PD], inc=False)
                B.op('pe', lambda e, h=h: e.matmul(dns[:, h * 32 + off:h * 32 + off + QBs], ones_b[:, :],
                                                   pAs[:, h, 0:QBs], start=False, stop=True),
                     R=['ones_b', pAsk], W=['ps%d' % PD], inc=(h == 3))

        st_cur = s_stageA(*combos[0])
        for ci, (b, r) in enumerate(combos):
            st_nxt = s_stageA(*combos[ci + 1]) if ci + 1 < len(combos) else None
            s_stageB(b, r, st_cur)
            st_cur = st_nxt
        for (acc, pbank, pk, ak) in [(acc_O, oTs, 'ps%d' % PO, 'acc_O'), (acc_D, dns, 'ps%d' % PD, 'acc_D')]:
            for h in range(4):
                dst = acc[:, h, 1024:1056].rearrange("p (b q r) -> p b q r", b=4, q=QBs, r=ds)
                src = pbank[:, h * 32:(h + 1) * 32].rearrange("p (b r q) -> p b q r", b=4, r=ds, q=QBs)
                if g == 0:
                    B.op('act', lambda e, dst=dst, src=src: e.activation(out=dst, in_=src, func=AF.Copy), R=[pk], W=[ak])
                else:
                    B.op('dve', lambda e, dst=dst, src=src: e.tensor_tensor(out=dst, in0=src, in1=dst, op=ALU.add), R=[pk, ak], W=[ak])

        _ck(3 + 2 * g)
        jobs = []
        for r in range(d):
            if r in ERES[d]:
                jobs.append(dict(kind='x', r=r, first=True))
            jobs.append(dict(kind='h', r=r, first=(r not in ERES[d])))
            for bm in range(nmb):
                jobs.append(dict(kind='m', r=r, bm=bm, first=False))

        def job_prepA(job):
            r = job['r']
            if job['kind'] == 'h':
                src = x_halo[2048 - 128 * d + r:2048:d, :]
            elif d == 16:
                src = x_far[(r - 14) * 128:(r - 13) * 128, :]
            else:
                src = x_halo[2048 - 256 * d + r:2048 - 128 * d:d, :]
            return prepA(prep_bufs, src, 128)

        def job_prepT(job, pa):
            hh, hhk = nxt('hTh', hTh)
            prepT(pa, gT_mix, 'gT_mix', lambda q, hh=hh: hh[:, q * 4:q * 4 + 4, :], [hhk])
            if job['kind'] == 'h' and d == 1:
                B.op('pool', lambda e, hh=hh: e.tensor_copy(out=hT_h1[:, :, :], in_=hh[:, :, 96:128]), R=[hhk], W=['hT_h1'])
            job['hh'] = (hh, hhk)

        prev = None
        T1 = None
        T2 = None
        if jobs[0]['kind'] != 'm':
            job_prepT(jobs[0], job_prepA(jobs[0]))

        def do_tr(T):
            job_, res_, M_, wq_, t0_ = T
            r_ = job_['r']
            prev_ = None if job_['first'] else do_tr.prev
            cur = tile_tr(M_, wq_, res_)
            args = None
            if job_['kind'] != 'm' and wq_:
                nq = 2 if d == 1 else 1
                qc0 = 128 - nq
                ecol = 1056 + (0 if d == 1 else (r_ - ERES[d][0]))
                args = ((128, qc0, nq, prev_['kT'], prev_['kTk'], prev_['vS'], prev_['vSk'],
                         cur['kT'], cur['kTk'], cur['vS'], cur['vSk'], cur['qT'], cur['qTk'],
                         HT_IDX[(g, r_, 1)], (lambda acc, ecol=ecol, nq=nq: acc[:, :, ecol:ecol + nq]), g == 0),
                        dict(hb_colB=HT_IDX[(g, r_, 2)]))
            elif job_['kind'] == 'm':
                hb = HT_IDX[(g, r_, 0)] if job_['bm'] == 0 else NHB - 1
                args = ((QB, 0, QB, prev_['kT'], prev_['kTk'], prev_['vS'], prev_['vSk'],
                         cur['kT'], cur['kTk'], cur['vS'], cur['vSk'], cur['qT'], cur['qTk'],
                         hb, (lambda acc, t0_=t0_: acc[:, :, t0_:t0_ + d * QB:d]), g == 0), {})
            do_tr.prev = cur
            return args
        do_tr.prev = None

        for i, job in enumerate(jobs):
            nj = jobs[i + 1] if i + 1 < len(jobs) else None
            pa = job_prepA(nj) if (nj is not None and nj['kind'] != 'm') else None
            if T2 is not None:
                dstate = attend_scores(*T2[0], **T2[1])
            r = job['r']
            t0 = None
            if job['kind'] != 'm':
                hh, hhk = job['hh']
                wq = (job['kind'] == 'h' and r in ERES[d])
                M = 128
                tile_proj(lambda kc, hh=hh: hh[:, kc, :], [hhk], M, wq)
                res = tile_norm(M, wq, None)
            else:
                bm = job['bm']
                t0 = r + d * QB * bm
                if g == 0:
                    orow = (lambda o: o[0:128, :]) if bm == 7 else None
                elif g == 1:
                    orow = (lambda o, r=r: o[r:512:4, :]) if bm == 1 else None
                else:
                    orow = (lambda o, r=r: o[r:1024:16, :])
                wq = True
                M = QB
                tile_proj(lambda kc, t0=t0: hT[:, kc, t0:t0 + d * QB:d], ['hT'], M, True)
                res = tile_norm(M, True, orow)
            if T2 is not None:
                attend_pv(dstate, *T2[0], **T2[1])
                T2 = None
            if pa is not None:
                job_prepT(nj, pa)
            if T1 is not None:
                T2 = do_tr(T1)
            T1 = (job, res, M, wq, t0)
        if T2 is not None:
            attend(*T2[0], **T2[1])
            T2 = None
        T2 = do_tr(T1)
        if T2 is not None:
            attend(*T2[0], **T2[1])
            T2 = None
        _ck(4 + 2 * g)
    _ck(10)
    for (c0, n) in TOKB:
        for h in range(4):
            B.op('act', lambda e, h=h, c0=c0, n=n: e.activation(out=acc_D[:, h, c0:c0 + n], in_=acc_D[:, h, c0:c0 + n], func=AF.Ln),
                 R=['acc_D'], W=['acc_D'])
            B.op('act', lambda e, h=h, c0=c0, n=n: e.activation(out=acc_D[:, h, c0:c0 + n], in_=acc_D[:, h, c0:c0 + n], func=AF.Exp, scale=-1.0),
                 R=['acc_D'], W=['acc_D'])
            B.op('dve', lambda e, h=h, c0=c0, n=n: e.tensor_tensor(out=o_attn[:, h, c0:c0 + n], in0=acc_O[:, h, c0:c0 + n],
                                                                  in1=acc_D[:, h, c0:c0 + n], op=ALU.mult),
                 R=['acc_O', 'acc_D'], W=['o_attn'])
    import os
    if os.environ.get('KDEBUG'):
        dbg = B.dout("dbg_oattn", [128, 4 * NT], BF16)
        B.dma('sp', dbg[:, :], o_attn[:, :, :].rearrange("p h t -> p (h t)"), R=['o_attn'], sem='dbg')
    B.barrier()


    _ck(20)
    st2 = ExitStack()
    set_ptr(60.5)
    for g, (d, W) in enumerate(GROUPS):
        for b in range(4):
            if 'c' in SK: break
            B.dma('sp', o_swk[g][b, 0:W - 8, :].rearrange("(a r) c -> a (r c)", r=8),
                  st_wk[g][b, 8:W, :].rearrange("(a r) c -> a (r c)", r=8), sem='copy')
            B.dma('sp', o_swv[g][b, 0:W - 8, :].rearrange("(a r) c -> a (r c)", r=8),
                  st_wv[g][b, 8:W, :].rearrange("(a r) c -> a (r c)", r=8), sem='copy')
    for b in range(4):
        B.dma('sp', o_sconv[b, 0:22, :], st_conv[b, 8:30, :], sem='copy')

    new_ring(st2, 8, "c")
    u_p = al("u_p", [128, 8, 1056], BF16)
    u_s = al("u_s", [128, 8, 152], BF16)
    utail = al("utail", [128, 8, 64], F32)
    wdwT = al("wdwT", [128, 8, 31], F32)
    dg_l = [al("dg%d" % i, [128, 31, 128], BF16) for i in range(2)]
    c_f = al("c_f", [128, 8, NT], F32)
    tmpA = [al("tmpA%d" % i, [128, 512], F32) for i in range(2)]
    tmpB = [al("tmpB%d" % i, [128, 512], F32) for i in range(2)]
    mu_t = al("mu_t", [128, NT], F32)
    rs_t = al("rs_t", [128, NT], F32)
    ld32 = al("ld32", [128, 1024], F32)
    ones_f = al("ones_f", [128, 128], F32)
    uo = al("uo", [64, 1024], F32)
    bank_i = {'i': 0}

    def nbank():
        i = bank_i['i']
        bank_i['i'] = (i + 1) % 8
        return psb[i], 'ps%d' % i

    B.op('dve', lambda e: e.memset(ones_f[:], 1.0), W=['ones_f'])
    B.dma('sp', ld32[0:31, :], w_dw[:, :], W=['ld32'])
    bk, bkk = nbank()
    for j in range(8):
        B.op('pe', lambda e, j=j: e.transpose(out=bk[:, j * 31:(j + 1) * 31], in_=ld32[0:31, j * 128:(j + 1) * 128],
                                              identity=ident_f[0:31, 0:31]), R=['ld32', 'ident_f'], W=[bkk], inc=(j == 7))
    B.op('act', lambda e: e.activation(out=wdwT[:, :, :], in_=bk[:, 0:248].rearrange("p (j k) -> p j k", j=8), func=AF.Copy),
         R=[bkk], W=['wdwT'])
    B.dma('sp', ld32[0:120, :], st_conv.rearrange("b p c -> (b p) c"), W=['ld32'])
    for half in range(2):
        bk, bkk = nbank()
        for jj in range(4):
            j = half * 4 + jj
            B.op('pe', lambda e, j=j, jj=jj: e.transpose(out=bk[:, jj * 120:(jj + 1) * 120], in_=ld32[0:120, j * 128:(j + 1) * 128],
                                                         identity=ident_f[0:120, 0:120]), R=['ld32', 'ident_f'], W=[bkk], inc=(jj == 3))
        for jj in range(4):
            j = half * 4 + jj
            B.op('act', lambda e, j=j, jj=jj: e.activation(
                out=u_s[:, j, 0:120].rearrange("p (pos b) -> p b pos", b=4),
                in_=bk[:, jj * 120:(jj + 1) * 120].rearrange("p (b pos) -> p b pos", b=4), func=AF.Copy), R=[bkk], W=['u_s'])

    def proj_fm(panels, jcol, rhs_fn, rkeys, N, nk_total=16):
        bk, bkk = nbank()
        n = 0
        for (view, keys, kb) in panels:
            nk = view.shape[1]
            for k in range(nk):
                B.op('pe', lambda e, view=view, k=k, kc=kb + k, n=n: e.matmul(
                    bk[:, 0:N], view[:, k, jcol:jcol + 128], rhs_fn(kc), start=(n == 0), stop=(n == nk_total - 1)),
                    R=rkeys + keys, W=[bkk], inc=(n == nk_total - 1))
                n += 1
        return bk, bkk

    tblocks = [('m0', lambda kc: hT[:, kc, 0:512], ['hT'], 512), ('m1', lambda kc: hT[:, kc, 512:1024], ['hT'], 512),
               ('s', lambda kc: hT[:, kc, 1024:1056], ['hT'], 32), ('h', lambda kc: hT_h1[:, kc, :], ['hT_h1'], 32)]
    for cbk in range(2):
        Wa = load_w16(w_in, C_A + cbk * 512)
        Wb = load_w16(w_in, C_B + cbk * 512)
        for jj in range(4):
            J = cbk * 4 + jj
            for (nm, rf, rk, N) in tblocks:
                pa, pak = proj_fm(Wa, jj * 128, rf, rk, N)
                pb, pbk = proj_fm(Wb, jj * 128, rf, rk, N)
                tA, tAk = nxt('tmpA', tmpA)
                B.op('act', lambda e, pb=pb, tA=tA, N=N: e.activation(out=tA[:, 0:N], in_=pb[:, 0:N], func=AF.Sigmoid), R=[pbk], W=[tAk])
                if nm == 'm0':
                    dst = u_p[:, J, 32:544]
                elif nm == 'm1':
                    dst = u_p[:, J, 544:1056]
                elif nm == 'h':
                    dst = u_p[:, J, 0:32]
                else:
                    dst = u_s[:, J, 120:152].rearrange("p (t b) -> p b t", b=4)
                src_a = pa[:, 0:N] if nm != 's' else pa[:, 0:32].rearrange("p (b t) -> p b t", b=4)
                src_s = tA[:, 0:N] if nm != 's' else tA[:, 0:32].rearrange("p (b t) -> p b t", b=4)
                B.op('dve', lambda e, dst=dst, src_a=src_a, src_s=src_s: e.tensor_tensor(out=dst, in0=src_a, in1=src_s, op=ALU.mult),
                     R=[pak, tAk], W=['u_p' if nm != 's' else 'u_s'])
                if nm == 'm1':
                    B.op('dve', lambda e, pa=pa, tA=tA, J=J: e.tensor_tensor(out=utail[:, J, 0:32], in0=pa[:, 480:512], in1=tA[:, 480:512], op=ALU.mult),
                         R=[pak, tAk], W=['utail'])
                if nm == 's':
                    B.op('dve', lambda e, pa=pa, tA=tA, J=J: e.tensor_tensor(out=utail[:, J, 32:64], in0=pa[:, 0:32], in1=tA[:, 0:32], op=ALU.mult),
                         R=[pak, tAk], W=['utail'])
    for half in range(2):
        bk, bkk = nbank()
        for jj in range(4):
            J = half * 4 + jj
            B.op('pe', lambda e, J=J, jj=jj: e.transpose(out=bk[0:64, jj * 128:(jj + 1) * 128], in_=utail[:, J, :], identity=ident_f[:, :]),
                 R=['utail', 'ident_f'], W=[bkk], inc=(jj == 3))
        B.op('act', lambda e, half=half, bk=bk: e.activation(out=uo[:, half * 512:(half + 1) * 512], in_=bk[0:64, :], func=AF.Copy), R=[bkk], W=['uo'])
    B.dma('sp', o_pconv[:, :], uo[0:32, :], R=['uo'])
    for b in range(4):
        B.dma('sp', o_sconv[b, 22:30, :], uo[32 + b * 8:40 + b * 8, :], R=['uo'])
    def build_dg(J):
        dgb = dg_l[J % 2]
        for k in range(31):
            B.op('dve', lambda e, k=k, J=J, dgb=dgb: e.tensor_scalar(out=dgb[:, k, :], in0=ident_b[:, :], scalar1=wdwT[:, J, k:k + 1], scalar2=None, op0=ALU.mult),
                 R=['ident_b', 'wdwT'], W=['dg%d' % (J % 2)])

    build_dg(0)
    for J in range(8):
        if J + 1 < 8:
            build_dg(J + 1)
        dg = dg_l[J % 2]
        dgk = 'dg%d' % (J % 2)
        outs = [(lambda k: u_p[:, J, 2 + k:2 + k + 512], 512, c_f[:, J, 0:512], None),
                (lambda k: u_p[:, J, 514 + k:514 + k + 512], 512, c_f[:, J, 512:1024], None),
                (lambda k: u_p[:, J, k:k + 2], 2, c_f[:, J, 1056:1058], None),
                (lambda k: u_s[:, J, k * 4:k * 4 + 32], 32, c_f[:, J, 1024:1056].rearrange("p (b t) -> p b t", b=4), 's')]
        for (rf, N, dst, kind) in outs:
            bk, bkk = nbank()
            for k in range(31):
                B.op('pe', lambda e, k=k, rf=rf, bk=bk, N=N, dg=dg: e.matmul(bk[:, 0:N], dg[:, k, :], rf(k), start=(k == 0), stop=(k == 30)),
                     R=[dgk, 'u_p', 'u_s'], W=[bkk], inc=(k == 30))
            src = bk[:, 0:N] if kind is None else bk[:, 0:32].rearrange("p (t b) -> p b t", b=4)
            B.op('act', lambda e, dst=dst, src=src, J=J: e.activation(out=dst, in_=src, func=AF.Identity, bias=cpar[:, 0, J:J + 1], scale=1.0),
                 R=[bkk, 'cpar'], W=['c_f'])
    for (c0, N) in TOKE:
        b1, b1k = nbank()
        b2, b2k = nbank()
        for J in range(8):
            tA, tAk = nxt('tmpA', tmpA)
            B.op('act', lambda e, tA=tA, J=J: e.activation(out=tA[:, 0:N], in_=c_f[:, J, c0:c0 + N], func=AF.Square), R=['c_f'], W=[tAk])
            B.op('pe', lambda e, J=J: e.matmul(b1[:, 0:N], ones_f[:, :], c_f[:, J, c0:c0 + N], start=(J == 0), stop=(J == 7)),
                 R=['ones_f', 'c_f'], W=[b1k], inc=(J == 7))
            B.op('pe', lambda e, J=J, tA=tA: e.matmul(b2[:, 0:N], ones_f[:, :], tA[:, 0:N], start=(J == 0), stop=(J == 7)),
                 R=['ones_f', tAk], W=[b2k], inc=True)
        B.op('act', lambda e: e.activation(out=mu_t[:, c0:c0 + N], in_=b1[:, 0:N], func=AF.Copy, scale=1.0 / 1024), R=[b1k], W=['mu_t'])
        tB, tBk = nxt('tmpB', tmpB)
        B.op('dve', lambda e, tB=tB: e.tensor_tensor(out=tB[:, 0:N], in0=mu_t[:, c0:c0 + N], in1=mu_t[:, c0:c0 + N], op=ALU.mult), R=['mu_t'], W=[tBk])
        B.op('dve', lambda e, tB=tB: e.scalar_tensor_tensor(out=tB[:, 0:N], in0=b2[:, 0:N], scalar=1.0 / 1024, in1=tB[:, 0:N],
                                                           op0=ALU.mult, op1=ALU.subtract), R=[b2k, tBk], W=[tBk])
        B.op('act', lambda e, tB=tB: e.activation(out=tB[:, 0:N], in_=tB[:, 0:N], func=AF.Ln, bias=epsc[:, :], scale=1.0), R=[tBk, 'epsc'], W=[tBk])
        B.op('act', lambda e, tB=tB: e.activation(out=rs_t[:, c0:c0 + N], in_=tB[:, 0:N], func=AF.Exp, scale=-0.5), R=[tBk], W=['rs_t'])
    for (c0, N) in TOKE:
        for J in range(8):
            tA, tAk = nxt('tmpA', tmpA)
            B.op('dve', lambda e, tA=tA, J=J: e.tensor_tensor(out=tA[:, 0:N], in0=c_f[:, J, c0:c0 + N], in1=mu_t[:, c0:c0 + N], op=ALU.subtract),
                 R=['c_f', 'mu_t'], W=[tAk])
            B.op('dve', lambda e, tA=tA: e.tensor_tensor(out=tA[:, 0:N], in0=tA[:, 0:N], in1=rs_t[:, c0:c0 + N], op=ALU.mult), R=[tAk, 'rs_t'], W=[tAk])
            B.op('act', lambda e, tA=tA, J=J: e.activation(out=c_act[:, J, c0:c0 + N], in_=tA[:, 0:N], func=AF.Silu,
                                                           bias=cpar[:, 2, J:J + 1], scale=cpar[:, 1, J:J + 1]), R=[tAk, 'cpar'], W=['c_act'])
    B.barrier()
    _ck(21)

    st3 = ExitStack()
    set_ptr(77.5)
    new_ring(st3, 8, "d")
    xt_list = [al("xtd%d" % i, [128, D], F32) for i in range(2)]
    ph['xt_i'] = 0
    prep_bufs = (xt_list, [(psb[6], 'ps6'), (psb[7], 'ps7')])
    hmT = al("hmT", [128, KC, 256], BF16)
    kmT = al("kmT", [128, 8, 256], BF16)
    vmb = al("vmb", [128, 2, 1024], BF16)
    qmT = al("qmT", [128, 8, NT], BF16)
    kf_l = [al("kfd%d" % i, [128, 512], F32) for i in range(2)]
    ko_l = [al("kod%d" % i, [128, 512], F32) for i in range(2)]
    kb_l = [al("kbd%d" % i, [128, 512], BF16) for i in range(2)]
    sqs3 = al("sqs3", [128, 512], F32)
    ph['sqs'] = sqs3
    pm_l = [al("pm%d" % i, [128, 512], BF16) for i in range(4)]
    rd_l = [al("rd%d" % i, [128, 512], F32) for i in range(2)]
    kms = [al("kms%d" % i, [128, 2, 1024], BF16) for i in range(1)]
    vms = [al("vms%d" % i, [128, 2, 1024], BF16) for i in range(1)]
    kmTs = [al("kmTs%d" % i, [128, 8, 256], BF16) for i in range(1)]
    rot.clear()
    pT_bf3 = psb[5][:, :].bitcast(BF16)

    def transpose4b(src_bf, skey, M, half, dst, dkeys):
        pk = 'ps5'
        base = half * 512
        for h in range(4):
            B.op('pe', lambda e, h=h: e.transpose(out=pT_bf3[:, base + h * 128: base + h * 128 + M],
                                                  in_=src_bf[0:M, h * 128:(h + 1) * 128], identity=ident_b[0:M, 0:M]),
                 R=[skey, 'ident_b'], W=[pk], inc=(h == 3))
        src = pT_bf3[:, base:base + 512].rearrange("p (h t) -> p h t", h=4)[:, :, 0:M]
        B.op('act', lambda e: e.activation(out=dst, in_=src, func=AF.Copy), R=[pk], W=dkeys)

    for mt in range(2):
        prep(prep_bufs, mem_p[mt * 128:(mt + 1) * 128, :], 128, gT_mem, 'gT_mem',
             lambda q, mt=mt: hmT[:, q * 4:q * 4 + 4, mt * 128:(mt + 1) * 128], ['hmT'])
    for cb in range(2):
        Wmk = load_w16(w_mem_k, cb * 512)
        Wmv = load_w16(w_mem_v, cb * 512)
        for mt in range(2):
            proj_tm(lambda kc, mt=mt: hmT[:, kc, mt * 128:(mt + 1) * 128], 128, Wmk, psb[0], 'ps0', ['hmT'])
            proj_tm(lambda kc, mt=mt: hmT[:, kc, mt * 128:(mt + 1) * 128], 128, Wmv, psb[1], 'ps1', ['hmT'])
            kf, kfk = nxt('kfd', kf_l); ko, kok = nxt('kod', ko_l); kb, kbk = nxt('kbd', kb_l)
            head_norm(psb[0], 'ps0', 128, 2, 256, gmk_t, 'gmk', kf, kfk, ko, [kok], kb, [kbk])
            B.dma('sp', o_pmk[mt * 128:(mt + 1) * 128, cb * 512:(cb + 1) * 512], ko[:, :], R=[kok])
            transpose4b(kb, kbk, 128, 0, kmT[:, cb * 4:cb * 4 + 4, mt * 128:(mt + 1) * 128], ['kmT'])
            B.op('act', lambda e, mt=mt, cb=cb: e.activation(out=vmb[:, mt, cb * 512:(cb + 1) * 512], in_=psb[1][:, :], func=AF.Copy), R=['ps1'], W=['vmb'])
            ko2, kok2 = nxt('kod', ko_l)
            B.op('dve', lambda e, ko2=ko2: e.tensor_copy(out=ko2[:, :], in_=psb[1][:, :]), R=['ps1'], W=[kok2])
            B.dma('sp', o_pmv[mt * 128:(mt + 1) * 128, cb * 512:(cb + 1) * 512], ko2[:, :], R=[kok2])
    tiles9 = [(t * 128, 128) for t in range(8)] + [(1024, 34)]
    qdef = [None]
    for cb in range(2):
        Wqm = load_w16(w_in, C_QM + cb * 512)
        for (t0, M) in tiles9:
            proj_tm(lambda kc, t0=t0, M=M: hT[:, kc, t0:t0 + M], M, Wqm, psb[2], 'ps2', ['hT'])
            kf, kfk = nxt('kfd', kf_l); kb, kbk = nxt('kbd', kb_l)
            head_norm(psb[2], 'ps2', M, 2, 256, gmq_t, 'gmq', kf, kfk, None, None, kb, [kbk])
            if qdef[0] is not None:
                qdef[0]()
            qdef[0] = (lambda kb=kb, kbk=kbk, M=M, cb=cb, t0=t0: transpose4b(kb, kbk, M, 1, qmT[:, cb * 4:cb * 4 + 4, t0:t0 + M], ['qmT']))
    qdef[0]()
    scale_mem = 1.0 / 16.0

    mdef = [None]

    def mem_attend(kT_, kTk, v_, vk, c0, N):
        for h in range(4):
            pms = []
            for mt in range(2):
                bk, bkk = psb[mt], 'ps%d' % mt
                for dc in range(2):
                    B.op('pe', lambda e, mt=mt, dc=dc, bk=bk: e.matmul(bk[:, 0:N], kT_[:, h * 2 + dc, mt * 128:(mt + 1) * 128],
                                                                      qmT[:, h * 2 + dc, c0:c0 + N], start=(dc == 0), stop=(dc == 1)),
                         R=[kTk, 'qmT'], W=[bkk], inc=(dc == 1))
                pm, pmk = nxt('pm', pm_l)
                B.op('act', lambda e, pm=pm, bk=bk: e.activation(out=pm[:, 0:N], in_=bk[:, 0:N], func=AF.Exp, scale=scale_mem), R=[bkk], W=[pmk])
                pms.append((pm, pmk))
            B.op('pe', lambda e: e.matmul(psb[4][:, 0:N], ones_b[:, :], pms[0][0][:, 0:N], start=True, stop=False), R=['ones_b', pms[0][1]], W=['ps4'], inc=False)
            B.op('pe', lambda e: e.matmul(psb[4][:, 0:N], ones_b[:, :], pms[1][0][:, 0:N], start=False, stop=True), R=['ones_b', pms[1][1]], W=['ps4'], inc=True)
            rd, rdk = nxt('rd', rd_l)
            B.op('act', lambda e, rd=rd: e.activation(out=rd[:, 0:N], in_=psb[4][:, 0:N], func=AF.Ln), R=['ps4'], W=[rdk])
            B.op('act', lambda e, rd=rd: e.activation(out=rd[:, 0:N], in_=rd[:, 0:N], func=AF.Exp, scale=-1.0), R=[rdk], W=[rdk])

            def stage2(h=h, pms=pms, rd=rd, rdk=rdk):
                for dc in range(2):
                    bk, bkk = psb[2 + dc], 'ps%d' % (2 + dc)
                    for mt in range(2):
                        B.op('pe', lambda e, mt=mt, dc=dc, bk=bk: e.matmul(bk[:, 0:N], v_[:, mt, h * 256 + dc * 128:h * 256 + (dc + 1) * 128],
                                                                          pms[mt][0][:, 0:N], start=(mt == 0), stop=(mt == 1)),
                             R=[vk, pms[mt][1]], W=[bkk], inc=(mt == 1))
                    B.op('dve', lambda e, dc=dc, bk=bk: e.tensor_tensor(out=o_mem[:, h * 2 + dc, c0:c0 + N], in0=bk[:, 0:N], in1=rd[:, 0:N], op=ALU.mult),
                         R=[bkk, rdk], W=['o_mem'])
            if mdef[0] is not None:
                mdef[0]()
            mdef[0] = stage2
        mdef[0]()
        mdef[0] = None

    for (c0, N) in [(0, 512), (512, 512), (1056, 2)]:
        mem_attend(kmT, 'kmT', vmb, 'vmb', c0, N)
    for b in range(4):
        km, kmk = nxt('kms', kms); vm, vmk = nxt('vms', vms); kt, ktk = nxt('kmTs', kmTs)
        B.dma('pool', km[:, :, :], cmk[b].rearrange("(mt p) c -> p mt c", p=128), W=[kmk])
        B.dma('pool', vm[:, :, :], cmv[b].rearrange("(mt p) c -> p mt c", p=128), W=[vmk])
        for mt in range(2):
            for cb in range(2):
                transpose4b(km[:, mt, cb * 512:(cb + 1) * 512], kmk, 128, (mt * 2 + cb) % 2, kt[:, cb * 4:cb * 4 + 4, mt * 128:(mt + 1) * 128], [ktk])
        mem_attend(kt, ktk, vm, vmk, 1024 + b * 8, 8)
    if mdef[0] is not None:
        mdef[0]()
        mdef[0] = None
    B.barrier()
    _ck(22)

    st4 = ExitStack()
    st_x2 = ExitStack()
    set_ptr(77.5)
    m_t = al("m_t", [128, KC, NT], BF16, at=162)
    new_ring(st4, 16, "e")
    sg_l = [al("sg%d" % i, [128, 512], F32) for i in range(6)]
    tt_l = [al("tt%d" % i, [128, 512], F32) for i in range(4)]
    for fp in range(8):
        f0 = fp * 256
        Wg = [[load_panel(w_in, 0, 8, cg + f0, 256) + (0,), load_panel(w_in, 8, 8, cg + f0, 256) + (8,)] for cg in (C_GC, C_GA, C_GM)]
        Wco = [load_panel(w_conv_out, 0, 8, f0, 256) + (0,)]
        Wao = [load_panel(w_attn_out, 0, 4, f0, 256) + (0,)]
        Wmo = [load_panel(w_mem_out, 0, 8, f0, 256) + (0,)]
        for fi in range(2):
            f = fp * 2 + fi
            for (c0, N) in TOKE:
                sgs = []
                for gi in range(3):
                    bk, bkk = proj_fm(Wg[gi], fi * 128, lambda kc: hT[:, kc, c0:c0 + N], ['hT'], N)
                    sg, sgk = nxt('sg', sg_l)
                    B.op('act', lambda e, sg=sg, bk=bk: e.activation(out=sg[:, 0:N], in_=bk[:, 0:N], func=AF.Sigmoid), R=[bkk], W=[sgk])
                    sgs.append((sg, sgk))
                brs = [(Wco, lambda kc: c_act[:, kc, c0:c0 + N], ['c_act'], 8),
                       (Wao, lambda kc: o_attn[:, kc, c0:c0 + N], ['o_attn'], 4),
                       (Wmo, lambda kc: o_mem[:, kc, c0:c0 + N], ['o_mem'], 8)]
                tts = []
                for bi, (Wp, rf, rk, nk) in enumerate(brs):
                    bk, bkk = proj_fm(Wp, fi * 128, rf, rk, N, nk_total=nk)
                    tt, ttk = nxt('tt', tt_l)
                    B.op('dve', lambda e, tt=tt, bk=bk, sg=sgs[bi][0]: e.tensor_tensor(out=tt[:, 0:N], in0=bk[:, 0:N], in1=sg[:, 0:N], op=ALU.mult),
                         R=[bkk, sgs[bi][1]], W=[ttk])
                    tts.append((tt, ttk))
                B.op('dve', lambda e: e.tensor_tensor(out=tts[0][0][:, 0:N], in0=tts[0][0][:, 0:N], in1=tts[1][0][:, 0:N], op=ALU.add),
                     R=[tts[0][1], tts[1][1]], W=[tts[0][1]])
                B.op('dve', lambda e, f=f: e.tensor_tensor(out=m_t[:, f, c0:c0 + N], in0=tts[0][0][:, 0:N], in1=tts[2][0][:, 0:N], op=ALU.add),
                     R=[tts[0][1], tts[2][1]], W=['m_t'])
    B.barrier()
    _ck(23)
    x2 = al("x2", [128, 9, D], F32, at=0)
    set_ptr(72)
    new_ring(None, 12, "e2")
    for t in range(8):
        B.dma('sp', x2[:, t, :], x_main[t * 128:(t + 1) * 128, :], W=['x2_%d' % t])
    B.dma('sp', x2[0:34, 8, :], x_misc[:, :], W=['x2_8'])
    for cb in range(4):
        Wo = load_w16(w_o, cb * 512)
        for ti, (t0, M) in enumerate(tiles9):
            bk, bkk = nbank()
            proj_tm(lambda kc, t0=t0, M=M: m_t[:, kc, t0:t0 + M], M, Wo, bk, bkk, ['m_t'])
            B.op('dve', lambda e, bk=bk, ti=ti, M=M, cb=cb: e.tensor_tensor(out=x2[0:M, ti, cb * 512:(cb + 1) * 512], in0=bk[0:M, :],
                                                                         in1=x2[0:M, ti, cb * 512:(cb + 1) * 512], op=ALU.add),
                 R=[bkk, 'x2_%d' % ti], W=['x2_%d' % ti])
    B.barrier()
    _ck(24)

    st5 = ExitStack()
    set_ptr(72)
    new_ring(st5, 12, "f")
    h2T = al("h2T", [128, KC, NT], BF16)
    xs_full = al("xs0", [128, 2112], F32)
    xs_l = [xs_full[:, 0:D]]
    ph['xt_i'] = 0
    prep_bufs = (xs_l, [(psb[6], 'ps6'), (psb[7], 'ps7')])
    esc = al("esc", [128, 1], F32)
    wf = al("wf", [128, 88, 3], F32)
    bf_ = al("bf_", [128, 88], F32)
    sfc = al("sfc", [128, 88, 8], F32)
    uptail = al("uptail", [128, 88, 10], F32)
    ldf = al("ldf", [88, 128], F32)
    upraw = al("upraw", [128, 1026], F32)
    ups = al("ups", [128, 40], F32)
    acc_g = al("acc_g", [128, 1056], F32)
    acc_v = al("acc_v", [128, 1056], F32)
    act_t = al("act_t", [128, 4, 1056], BF16)
    act_alt = xs_full[:, 0:2112].bitcast(BF16).rearrange("p (a b) -> p a b", a=4)
    act_l = [act_t, act_alt]
    stg = xs_l[0]
    B.op('dve', lambda e: e.memset(esc[:, :], 1.0), W=['esc'])
    B.op('dve', lambda e: e.tensor_copy(out=esc[32:34, :], in_=eflag[32:34, :]), R=['eflag'], W=['esc'])
    for k3 in range(3):
        pass
    ldw = acc_g
    for part in range(11):
        B.dma('sp', ldw[0:3, 0:1024], w_ffn_dw[:, part * 1024:(part + 1) * 1024], W=['ldw'])
        bk, bkk = nbank()
        for j in range(8):
            B.op('pe', lambda e, j=j, bk=bk: e.transpose(out=bk[:, j * 3:(j + 1) * 3], in_=ldw[0:3, j * 128:(j + 1) * 128], identity=ident_f[0:3, 0:3]),
                 R=['ldw', 'ident_f'], W=[bkk], inc=(j == 7))
        B.op('act', lambda e, bk=bk, part=part: e.activation(out=wf[:, part * 8:(part + 1) * 8, :], in_=bk[:, 0:24].rearrange("p (j k) -> p j k", j=8), func=AF.Copy),
             R=[bkk], W=['wf'])
    B.dma('sp', ldf[:, :], b_ffn_dw.rearrange("(j p) -> j p", p=128), W=['ldf'])
    bk, bkk = nbank()
    B.op('pe', lambda e: e.transpose(out=bk[:, 0:88], in_=ldf[0:88, :], identity=ident_f[0:88, 0:88]), R=['ldf', 'ident_f'], W=[bkk])
    B.op('act', lambda e: e.activation(out=bf_[:, :], in_=bk[:, 0:88], func=AF.Copy), R=[bkk], W=['bf_'])
    for part in range(11):
        B.dma('sp', ldw[0:8, 0:1024], st_ffn[:, part * 1024:(part + 1) * 1024], W=['ldw'])
        bk, bkk = nbank()
        for j in range(8):
            B.op('pe', lambda e, j=j, bk=bk: e.transpose(out=bk[:, j * 8:(j + 1) * 8], in_=ldw[0:8, j * 128:(j + 1) * 128], identity=ident_f[0:8, 0:8]),
                 R=['ldw', 'ident_f'], W=[bkk], inc=(j == 7))
        B.op('act', lambda e, bk=bk, part=part: e.activation(out=sfc[:, part * 8:(part + 1) * 8, :], in_=bk[:, 0:64].rearrange("p (j k) -> p j k", j=8), func=AF.Copy),
             R=[bkk], W=['sfc'])
    B.barrier()
    for ti, (t0, M) in enumerate(tiles9):
        prep(prep_bufs, None, M, gT_ffn, 'gT_ffn', lambda q, t0=t0, M=M: h2T[:, q * 4:q * 4 + 4, t0:t0 + M], ['h2T'],
             from_sbuf=(x2[:, ti, :], 'x2_%d' % ti), extra_scale=(esc if ti == 8 else None))
    B.barrier()

    def up_chunk(j0, jj, Wug, Wuv, act_cur, actk):
            for iv, (Wp, acc) in enumerate([(Wug, acc_g), (Wuv, acc_v)]):
                Jp = iv * 44 + j0 + jj
                for bi, (c0, N) in enumerate(TOKE):
                    bk, bkk = proj_fm(Wp, jj * 128, lambda kc: h2T[:, kc, c0:c0 + N], ['h2T'], N)
                    m1 = min(c0 + N, 1024)
                    if m1 > c0:
                        B.op('act', lambda e, bk=bk, c0=c0, m1=m1: e.activation(out=upraw[:, 2 + c0:2 + m1], in_=bk[:, 0:m1 - c0], func=AF.Copy), R=[bkk], W=['upraw'])
                    if c0 + N > 1024:
                        so = 1024 - c0
                        B.op('act', lambda e, bk=bk, so=so: e.activation(out=ups[:, 8:40].rearrange("p (t b) -> p b t", b=4),
                                                                         in_=bk[:, so:so + 32].rearrange("p (b t) -> p b t", b=4), func=AF.Copy), R=[bkk], W=['ups'])
                        B.op('act', lambda e, bk=bk, so=so: e.activation(out=upraw[:, 0:2], in_=bk[:, so + 32:so + 34], func=AF.Copy), R=[bkk], W=['upraw'])
                B.op('act', lambda e, Jp=Jp: e.activation(out=ups[:, 0:8].rearrange("p (pos b) -> p b pos", b=4),
                                                          in_=sfc[:, Jp, :].rearrange("p (b pos) -> p b pos", b=4), func=AF.Copy), R=['sfc'], W=['ups'])
                B.op('act', lambda e, Jp=Jp: e.activation(out=uptail[:, Jp, 0:2], in_=upraw[:, 1024:1026], func=AF.Copy), R=['upraw'], W=['uptail'])
                B.op('act', lambda e, Jp=Jp: e.activation(out=uptail[:, Jp, 2:10], in_=ups[:, 32:40], func=AF.Copy), R=['ups'], W=['uptail'])
                for is_s in (0, 1):
                    ak = 'acc_g' if iv == 0 else 'acc_v'
                    if not is_s:
                        dst = acc[:, 0:1024]
                        sk = 'upraw'
                        srcs = [upraw[:, 0:1024], upraw[:, 1:1025], upraw[:, 2:1026]]
                    else:
                        dst = acc[:, 1024:1056].rearrange("p (b t) -> p t b", b=4)
                        sk = 'ups'
                        srcs = [ups[:, o_:o_ + 32].rearrange("p (t b) -> p t b", b=4) for o_ in (0, 4, 8)]
                    B.op('act', lambda e, dst=dst, srcs=srcs, Jp=Jp: e.activation(out=dst, in_=srcs[2], func=AF.Identity,
                                                                               bias=bf_[:, Jp:Jp + 1], scale=wf[:, Jp, 2:3]), R=[sk, 'wf', 'bf_'], W=[ak])
                    for tap in (1, 0):
                        B.op('dve', lambda e, dst=dst, srcs=srcs, Jp=Jp, tap=tap: e.scalar_tensor_tensor(
                            out=dst, in0=srcs[tap], scalar=wf[:, Jp, tap:tap + 1], in1=dst, op0=ALU.mult, op1=ALU.add), R=[sk, 'wf', ak], W=[ak])
            B.op('act', lambda e: e.activation(out=acc_g[:, :], in_=acc_g[:, :], func=AF.Silu), R=['acc_g'], W=['acc_g'])
            B.op('dve', lambda e, jj=jj: e.tensor_tensor(out=act_cur[:, jj, :], in0=acc_g[:, :], in1=acc_v[:, :], op=ALU.mult), R=['acc_g', 'acc_v'], W=[actk])


    def down_group(Wd_, act_cur, actk):
        for ti, (t0, M) in enumerate(tiles9):
            tc0 = t0 if ti < 8 else 1024
            MM = M if ti < 8 else 32
            for cb in range(4):
                bk, bkk = nbank()
                view, keys = Wd_[cb]
                for jj in range(4):
                    B.op('pe', lambda e, jj=jj, bk=bk, view=view: e.matmul(bk[0:MM, :], act_cur[:, jj, tc0:tc0 + MM], view[:, jj, :], start=(jj == 0), stop=(jj == 3)),
                         R=[actk] + keys, W=[bkk], inc=(jj == 3))
                B.op('dve', lambda e, bk=bk, ti=ti, cb=cb: e.tensor_tensor(out=x2[0:MM, ti, cb * 512:(cb + 1) * 512], in0=bk[0:MM, :],
                                                                        in1=x2[0:MM, ti, cb * 512:(cb + 1) * 512], op=ALU.add),
                     R=[bkk, 'x2_%d' % ti], W=['x2_%d' % ti])


    prev_d = None
    for G in range(11):
        j0 = G * 4
        Wug = load_w16(w_up, j0 * 128)
        Wuv = load_w16(w_up, DFF + j0 * 128)
        act_cur = act_l[G % 2]
        actk = 'act%d' % (G % 2)
        up_chunk(j0, 0, Wug, Wuv, act_cur, actk)
        if prev_d is not None:
            down_group(*prev_d)
        Wd = [load_panel(w_down, j0, 4, cb * 512, 512) for cb in range(4)]
        for jj in range(1, 4):
            up_chunk(j0, jj, Wug, Wuv, act_cur, actk)
        prev_d = (Wd, act_cur, actk)
    down_group(*prev_d)
    for t in range(8):
        B.dma('sp', y_main[t * 128:(t + 1) * 128, :], x2[:, t, :], R=['x2_%d' % t])
    B.dma('sp', y_misc[:, :], x2[0:34, 8, :], R=['x2_8'])
    for rnd in range(6):
        nch = 16 if rnd < 5 else 8
        banks = []
        for q in range(nch // 4):
            bk, bkk = nbank()
            for j in range(4):
                Jp = rnd * 16 + q * 4 + j
                B.op('pe', lambda e, Jp=Jp, j=j, bk=bk: e.transpose(out=bk[0:10, j * 128:(j + 1) * 128], in_=uptail[:, Jp, :], identity=ident_f[:, :]),
                     R=['uptail', 'ident_f'], W=[bkk], inc=(j == 3))
            banks.append((bk, bkk))
        for q, (bk, bkk) in enumerate(banks):
            B.op('act', lambda e, bk=bk, q=q: e.activation(out=stg[0:10, q * 512:(q + 1) * 512], in_=bk[0:10, :], func=AF.Copy),
                 R=[bkk], W=['xt0'])
        B.dma('sp', o_pffn[:, rnd * 2048:rnd * 2048 + nch * 128], stg[0:10, 0:nch * 128], R=['xt0'])
    B.barrier()


_CACHE = {}


def _consts():
    ident = np.eye(128, dtype=np.float32)
    ki = np.arange(128)[:, None]
    qi = np.arange(128)[None, :]
    mask = np.zeros((128, 256), np.float32)
    mask[:, 0:128] = np.where(ki >= qi, 0.0, NEG)
    mask[:, 128:256] = np.where(ki <= qi, 0.0, NEG)
    smask = np.full((34, 96), NEG, np.float32)
    for g, (d, W) in enumerate(GROUPS):
        ds = min(d, 8)
        QBs = 8 // ds
        for b in range(4):
            for r in range(ds):
                for q in range(QBs):
                    t = r + d * q
                    col = g * 32 + b * 8 + r * QBs + q
                    for t2 in range(8):
                        if t2 <= t and (t - t2) % d == 0:
                            smask[b * 8 + t2, col] = 0.0
    return ident, mask, smask


def _hbias(c):
    P0 = 1024 * c
    hb = np.zeros((128, NHB), np.float32)
    i = np.arange(128)
    for (g, r, kind), col in HT_IDX.items():
        d, W = GROUPS[g]
        pos = P0 - 128 * d * (2 if kind == 1 else 1) + r + d * i
        hb[:, col] = np.where(pos >= 0, 0.0, NEG)
        if kind in (1, 2) and c == 0:
            hb[:, col] = 0.0
    return hb


def kernel(**inp):
    f = lambda a: np.ascontiguousarray(np.asarray(a, dtype=np.float32))
    if 'B' not in _CACHE:
        _CACHE['B'] = build_program()
    B = _CACHE['B']
    ident, mask, smask = _consts()
    xp = f(inp['x_prompt']); xs = f(inp['x_sample'])
    shared = {
        'c_ident': ident, 'c_mask': mask, 'c_smask': smask,
        'g_mix': f(inp['g_mix'][0]), 'g_ffn': f(inp['g_ffn'][0]), 'g_mem': f(inp['g_mem'][0]),
        'w_in': f(inp['w_in'][0]), 'w_dw': f(inp['w_dw'][0]), 'b_dw': f(inp['b_dw'][0]), 'g_cln': f(inp['g_cln'][0]),
        'b_cln': f(inp['b_cln'][0]), 'w_conv_out': f(inp['w_conv_out'][0]), 'g_q': f(inp['g_q'][0]), 'g_k': f(inp['g_k'][0]),
        'w_attn_out': f(inp['w_attn_out'][0]), 'w_mem_k': f(inp['w_mem_k'][0]), 'w_mem_v': f(inp['w_mem_v'][0]),
        'g_mq': f(inp['g_mq'][0]), 'g_mk': f(inp['g_mk'][0]), 'w_mem_out': f(inp['w_mem_out'][0]), 'w_o': f(inp['w_o'][0]),
        'w_up': f(inp['w_up'][0]), 'w_ffn_dw': f(inp['w_ffn_dw'][0]), 'b_ffn_dw': f(inp['b_ffn_dw'][0]), 'w_down': f(inp['w_down'][0]),
    }
    used = set(B.dram.keys())
    in_maps = []
    for core in range(8):
        bp, c = core // 4, core % 4
        P0 = 1024 * c
        m = dict(shared)
        m['x_main'] = xp[bp, P0:P0 + 1024]
        xh = np.zeros((4096, D), np.float32)
        lo = max(0, P0 - 4096)
        if P0 > 0:
            xh[4096 - (P0 - lo):] = xp[bp, lo:P0]
        m['x_halo'] = np.ascontiguousarray(xh[2048:])
        m['x_far'] = np.ascontiguousarray(np.concatenate([xh[14:2048:16], xh[15:2048:16]], axis=0))
        m['x_misc'] = np.ascontiguousarray(np.concatenate([xs[4 * core:4 * core + 4].reshape(32, D), xh[4094:4096]], axis=0))
        m['hbias'] = _hbias(c)
        m['eflag'] = np.full((128, 1), 1.0 if c > 0 else 0.0, np.float32)
        m['mem_p'] = f(inp['mem_prompt'][bp])
        sl = slice(4 * core, 4 * core + 4)
        m['cmk'] = f(inp['cache_mem_k'][0, sl]).reshape(4, 256, 1024)
        m['cmv'] = f(inp['cache_mem_v'][0, sl]).reshape(4, 256, 1024)
        m['st_conv'] = f(inp['state_conv'][0, sl])
        m['st_ffn'] = f(inp['state_ffn_conv'][0, sl]).reshape(8, 2 * DFF)
        swk = [inp['state_win1_k'], inp['state_win2_k'], inp['state_win3_k']]
        swv = [inp['state_win1_v'], inp['state_win2_v'], inp['state_win3_v']]
        for g, (d, W) in enumerate(GROUPS):
            m['st_wk%d' % g] = f(swk[g][0, sl]).reshape(4, W, 512)
            m['st_wv%d' % g] = f(swv[g][0, sl]).reshape(4, W, 512)
        in_maps.append({k: v for k, v in m.items() if k in used})
    res = run_bass_kernel_spmd(B.nc, in_maps, core_ids=list(range(8)))
    R = res.results
    y_p = np.zeros((2, 4096, D), np.float32)
    y_s = np.zeros((32, 8, D), np.float32)
    for core in range(8):
        bp, c = core // 4, core % 4
        y_p[bp, 1024 * c:1024 * (c + 1)] = R[core]['y_main']
        y_s[4 * core:4 * core + 4] = R[core]['y_misc'][0:32].reshape(4, 8, D)
    last = [3, 7]
    p_conv = np.stack([R[k]['o_pconv'][2:32] for k in last])[None]
    pk = []; pv = []
    for g, (d, W) in enumerate(GROUPS):
        if g < 2:
            kk = np.stack([R[k]['o_k%d' % g] for k in last]); vv = np.stack([R[k]['o_v%d' % g] for k in last])
        else:
            kk = np.stack([np.concatenate([R[k - 1]['o_k2'], R[k]['o_k2']], 0) for k in last])
            vv = np.stack([np.concatenate([R[k - 1]['o_v2'], R[k]['o_v2']], 0) for k in last])
        pk.append(kk.reshape(1, 2, W, 4, 128)); pv.append(vv.reshape(1, 2, W, 4, 128))
    p_ffn = np.stack([R[k]['o_pffn'][0:2] for k in last])[None]
    p_mk = np.stack([R[k]['o_pmk'] for k in last]).reshape(1, 2, 256, 4, 256)
    p_mv = np.stack([R[k]['o_pmv'] for k in last]).reshape(1, 2, 256, 4, 256)
    s_conv = np.concatenate([R[k]['o_sconv'] for k in range(8)], 0)[None]
    sk = []; sv = []
    for g, (d, W) in enumerate(GROUPS):
        sk.append(np.concatenate([R[k]['o_swk%d' % g] for k in range(8)], 0).reshape(1, 32, W, 4, 128))
        sv.append(np.concatenate([R[k]['o_swv%d' % g] for k in range(8)], 0).reshape(1, 32, W, 4, 128))
    s_ffn = np.concatenate([R[k]['o_pffn'][2:10].reshape(2, 4, 2 * DFF).transpose(1, 0, 2) for k in range(8)], 0)[None]
    return (y_p, y_s, p_conv, pk[0], pv[0], pk[1], pv[1], pk[2], pv[2], p_ffn, p_mk, p_mv,
            s_conv, sk[0], sv[0], sk[1], sv[1], sk[2], sv[2], s_ffn)
```

```python
import numpy as np
from contextlib import ExitStack
import concourse.bass as bass
import concourse.mybir as mybir
from concourse.bass_utils import run_bass_kernel_spmd

F32 = mybir.dt.float32
BF16 = mybir.dt.bfloat16
AF = mybir.ActivationFunctionType
ALU = mybir.AluOpType
AX = mybir.AxisListType

NEG = -30000.0
EPS = 1e-6
D = 2048
KC = 16
NT = 1058
TOKB = [(0, 512), (512, 512), (1024, 34)]
TOKE = [(0, 352), (352, 354), (706, 352)]
GROUPS = [(1, 128), (4, 512), (16, 2048)]
ERES = {1: [0], 4: [2, 3], 16: [14, 15]}
C_A, C_B, C_Q, C_K, C_V, C_QM, C_GC, C_GA, C_GM = 0, 1024, 2048, 3584, 5120, 6656, 7680, 9728, 11776
DFF = 5632
RING_UNITS = 12
UNIT = 2048


def halo_tiles():
    out = []
    for g, (d, W) in enumerate(GROUPS):
        for r in range(d):
            if r in ERES[d]:
                out.append((g, r, 1))
                out.append((g, r, 2))
            out.append((g, r, 0))
    return out


HT_LIST = halo_tiles()
HT_IDX = {t: i for i, t in enumerate(HT_LIST)}
NHB = len(HT_LIST) + 1


class Builder:
    def __init__(self):
        self.nc = bass.Bass("TRN2", target_bir_lowering=False)
        nc = self.nc
        self.root = ExitStack()
        self.eng = {'pe': nc.tensor, 'act': nc.scalar, 'dve': nc.vector, 'pool': nc.gpsimd, 'sp': nc.sync}
        self.esem = {}
        self.ecount = {}
        for e in self.eng:
            self.esem[e] = self.root.enter_context(nc.semaphore("e_" + e))
            self.ecount[e] = 0
        self.waited = {e: {} for e in self.eng}
        self.last_w = {}
        self.readers = {}
        self.dsems = {}
        self.dram = {}
        self.ring_pos = 0
        self.n_inst = 0

    def din(self, name, shape, dtype=F32):
        t = self.nc.dram_tensor(name, list(shape), dtype, kind="ExternalInput")
        self.dram[name] = t.ap()
        return self.dram[name]

    def dout(self, name, shape, dtype=F32):
        t = self.nc.dram_tensor(name, list(shape), dtype, kind="ExternalOutput")
        self.dram[name] = t.ap()
        return self.dram[name]

    def sb(self, stack, name, shape, dtype):
        return stack.enter_context(self.nc.sbuf_tensor(name, list(shape), dtype))

    def ps(self, stack, name, shape, dtype):
        return stack.enter_context(self.nc.psum_tensor(name, list(shape), dtype))

    def _deps(self, reads, writes):
        toks = []
        for k in list(reads) + list(writes):
            if k in self.last_w:
                toks.append(self.last_w[k])
        for k in writes:
            toks.extend(self.readers.get(k, ()))
        return toks

    def _wait(self, e, toks):
        w = self.waited[e]
        need = {}
        for (sem, val, sid) in toks:
            if w.get(sid, 0) >= val:
                continue
            if sid == 'e_' + e and (e == 'pe' or val > self.ecount[e]):
                continue
            if need.get(sid, (None, 0))[1] < val:
                need[sid] = (sem, val)
        for sid, (sem, val) in need.items():
            self.eng[e].wait_ge(sem, val)
            w[sid] = val

    def _record(self, tok, reads, writes):
        for k in writes:
            self.last_w[k] = tok
            self.readers[k] = []
        for k in reads:
            self.readers.setdefault(k, []).append(tok)
            if len(self.readers[k]) > 24:
                best = {}
                for t in self.readers[k]:
                    if t[2] not in best or best[t[2]][1] < t[1]:
                        best[t[2]] = t
                self.readers[k] = list(best.values())

    def op(self, e, fn, R=(), W=(), inc=True):
        self._wait(e, self._deps(R, W))
        ins = fn(self.eng[e])
        self.n_inst += 1
        if inc:
            ins.then_inc(self.esem[e], 1)
            self.ecount[e] += 1
            tok = (self.esem[e], self.ecount[e], 'e_' + e)
        else:
            tok = (self.esem[e], self.ecount[e] + 1, 'e_' + e)
        self._record(tok, R, W)
        return ins

    def dsem(self, name):
        if name not in self.dsems:
            self.dsems[name] = [self.root.enter_context(self.nc.semaphore("d_" + name)), 0]
        return self.dsems[name]

    def dma(self, q, out, in_, R=(), W=(), sem=None, **kw):
        self._wait(q, self._deps(R, W))
        if sem is None:
            sem = (list(W) + list(R))[0]
        s = self.dsem(sem)
        ins = self.eng[q].dma_start(out=out, in_=in_, **kw)
        ins.then_inc(s[0], 16)
        s[1] += 16
        self.n_inst += 1
        tok = (s[0], s[1], 'd_' + sem)
        self._record(tok, R, W)
        return ins

    def barrier(self):
        toks = [(self.esem[e], self.ecount[e], 'e_' + e) for e in self.eng if self.ecount[e] > 0]
        toks += [(s[0], s[1], 'd_' + n) for n, s in self.dsems.items() if s[1] > 0]
        for e in self.eng:
            self._wait(e, toks)
        self.last_w.clear()
        self.readers.clear()

    def finish(self):
        toks = [(s[0], s[1], 'd_' + n) for n, s in self.dsems.items() if s[1] > 0]
        toks += [(self.esem[e], self.ecount[e], 'e_' + e) for e in self.eng if self.ecount[e] > 0]
        self._wait('sp', toks)


class _Stop(Exception):
    pass


def _ck(n):
    import os
    if int(os.environ.get('KSTOP', '999')) == n:
        raise _Stop()


def build_program():
    B = Builder()
    try:
        _build_body(B)
    except _Stop:
        pass
    B.finish()
    return B


def _build_body(B):
    nc = B.nc
    root = B.root
    x_main = B.din("x_main", [1024, D])
    x_halo = B.din("x_halo", [2048, D])
    x_far = B.din("x_far", [256, D])
    x_misc = B.din("x_misc", [34, D])
    hbias_d = B.din("hbias", [128, NHB])
    eflag_d = B.din("eflag", [128, 1])
    mem_p = B.din("mem_p", [256, D])
    cmk = B.din("cmk", [4, 256, 1024])
    cmv = B.din("cmv", [4, 256, 1024])
    st_conv = B.din("st_conv", [4, 30, 1024])
    st_ffn = B.din("st_ffn", [8, 2 * DFF])
    st_wk = [B.din("st_wk%d" % g, [4, W, 512]) for g, (d, W) in enumerate(GROUPS)]
    st_wv = [B.din("st_wv%d" % g, [4, W, 512]) for g, (d, W) in enumerate(GROUPS)]
    c_ident = B.din("c_ident", [128, 128])
    c_mask = B.din("c_mask", [128, 256])
    c_smask = B.din("c_smask", [34, 96])
    g_mix = B.din("g_mix", [D]); g_ffn = B.din("g_ffn", [D]); g_mem = B.din("g_mem", [D])
    w_in = B.din("w_in", [D, 13824])
    w_dw = B.din("w_dw", [31, 1024]); b_dw = B.din("b_dw", [1024]); g_cln = B.din("g_cln", [1024]); b_cln = B.din("b_cln", [1024])
    w_conv_out = B.din("w_conv_out", [1024, D])
    g_q = B.din("g_q", [128]); g_k = B.din("g_k", [128])
    w_attn_out = B.din("w_attn_out", [512, D])
    w_mem_k = B.din("w_mem_k", [D, 1024]); w_mem_v = B.din("w_mem_v", [D, 1024])
    g_mq = B.din("g_mq", [256]); g_mk = B.din("g_mk", [256])
    w_mem_out = B.din("w_mem_out", [1024, D])
    w_o = B.din("w_o", [D, D])
    w_up = B.din("w_up", [D, 2 * DFF]); w_ffn_dw = B.din("w_ffn_dw", [3, 2 * DFF]); b_ffn_dw = B.din("b_ffn_dw", [2 * DFF])
    w_down = B.din("w_down", [DFF, D])

    y_main = B.dout("y_main", [1024, D])
    y_misc = B.dout("y_misc", [34, D])
    o_pconv = B.dout("o_pconv", [32, 1024])
    o_k = [B.dout("o_k%d" % g, [n, 512]) for g, n in enumerate([128, 512, 1024])]
    o_v = [B.dout("o_v%d" % g, [n, 512]) for g, n in enumerate([128, 512, 1024])]
    o_pffn = B.dout("o_pffn", [10, 2 * DFF])
    o_pmk = B.dout("o_pmk", [256, 1024]); o_pmv = B.dout("o_pmv", [256, 1024])
    o_sconv = B.dout("o_sconv", [4, 30, 1024])
    o_swk = [B.dout("o_swk%d" % g, [4, W, 512]) for g, (d, W) in enumerate(GROUPS)]
    o_swv = [B.dout("o_swv%d" % g, [4, W, 512]) for g, (d, W) in enumerate(GROUPS)]

    ident_f = B.sb(root, "ident_f", [128, 128], F32)
    ident_b = B.sb(root, "ident_b", [128, 128], BF16)
    ones_b = B.sb(root, "ones_b", [128, 128], BF16)
    mask_b = B.sb(root, "mask_b", [128, 256], BF16)
    smask_b = B.sb(root, "smask_b", [34, 96], BF16)
    hbias = B.sb(root, "hbias_s", [128, NHB], F32)
    eflag = B.sb(root, "eflag_s", [128, 1], F32)
    epsc = B.sb(root, "epsc", [128, 1], F32)
    gT_mix = B.sb(root, "gT_mix", [128, KC], F32)
    gT_ffn = B.sb(root, "gT_ffn", [128, KC], F32)
    gT_mem = B.sb(root, "gT_mem", [128, KC], F32)
    gq_t = B.sb(root, "gq_t", [128, 128], F32)
    gk_t = B.sb(root, "gk_t", [128, 128], F32)
    gmq_t = B.sb(root, "gmq_t", [128, 256], F32)
    gmk_t = B.sb(root, "gmk_t", [128, 256], F32)
    stat = B.sb(root, "stat", [128, 192], F32)
    cpar = B.sb(root, "cpar", [128, 3, 8], F32)
    junk = B.sb(root, "junk", [128, 2048], BF16)
    psb = [B.ps(root, "ps%d" % i, [128, 512], F32) for i in range(8)]

    ARENA_KB = 196
    arena = B.sb(root, "arena", [128, ARENA_KB * 256], F32)
    apos = {'p': 0}

    def set_ptr(kb):
        apos['p'] = int(kb * 1024)

    def al(name, shape, dtype, at=None):
        esz = 4 if dtype == F32 else 2
        n = 1
        for d_ in shape[1:]:
            n *= d_
        nbytes = (n * esz + 31) // 32 * 32
        off = apos['p'] if at is None else int(at * 1024)
        if at is None:
            apos['p'] = off + nbytes
        assert off + nbytes <= ARENA_KB * 1024, (name, off, nbytes)
        v = arena[:, off // 4:(off + nbytes) // 4]
        if dtype != F32:
            v = v.bitcast(dtype)
        v = v[0:shape[0], 0:n]
        if len(shape) == 3:
            v = v.rearrange("p (a b) -> p a b", a=shape[1])
        return v

    ph = {'stat_i': 0}

    def stat_slot(n=4):
        i = ph['stat_i']
        ph['stat_i'] = (i + 12) % 192
        return i

    B.dma('sp', ident_f[:], c_ident[:], W=['ident_f'], sem='const')
    B.dma('pool', ident_b[:], c_ident[:], W=['ident_b'], sem='constp')
    B.dma('pool', mask_b[:], c_mask[:], W=['mask_b'], sem='constp')
    B.dma('pool', smask_b[:], c_smask[:], W=['smask_b'], sem='constp')
    B.dma('sp', hbias[:], hbias_d[:], W=['hbias'], sem='const')
    B.dma('sp', eflag[:], eflag_d[:], W=['eflag'], sem='const')
    import os
    SK = os.environ.get('KSKIP', '')
    for (t, src, key) in [(gT_mix, g_mix, 'gT_mix'), (gT_ffn, g_ffn, 'gT_ffn'), (gT_mem, g_mem, 'gT_mem')]:
        if 'g' in SK: break
        B.dma('sp', t[:], src.rearrange("(k p) -> p k", p=128), W=[key], sem='const', allow_slow_non_contiguous=True)
    for (t, src, key, n) in [(gq_t, g_q, 'gq', 128), (gk_t, g_k, 'gk', 128), (gmq_t, g_mq, 'gmq', 256), (gmk_t, g_mk, 'gmk', 256)]:
        if 'b' in SK: break
        B.dma('sp', t[:], src.partition_broadcast(128), W=[key], sem='const')
    for i_, src_ in enumerate([b_dw, g_cln, b_cln]):
        B.dma('sp', cpar[:, i_, :], src_.rearrange("(j p) -> p j", p=128), W=['cpar'], sem='const', allow_slow_non_contiguous=True)
    B.op('dve', lambda e: e.memset(ones_b[:], 1.0), W=['ones_b'])
    B.op('dve', lambda e: e.memset(epsc[:], EPS), W=['epsc'])

    _ck(1)
    def new_ring(stack, units, tag):
        ph['ring'] = al("ring_" + tag, [128, units * UNIT], BF16)
        ph['ring_units'] = units
        B.ring_pos = 0

    def ring_alloc(nunits):
        p = B.ring_pos
        if p + nunits > ph['ring_units']:
            p = 0
        B.ring_pos = p + nunits
        return p, ['ring%d' % u for u in range(p, p + nunits)]

    def load_panel(Wd, k0, nk, c0, ncol):
        nun = -(-(nk * ncol) // UNIT)
        p, keys = ring_alloc(nun)
        view = ph['ring'][:, p * UNIT: p * UNIT + nk * ncol].rearrange("p (k n) -> p k n", k=nk)
        src = Wd[k0 * 128:(k0 + nk) * 128, c0:c0 + ncol].rearrange("(k p) n -> p k n", p=128)
        B.dma('pool', view, src, W=keys, sem=keys[0])
        return view, keys

    def load_w16(Wd, c0, ncol=512):
        return [load_panel(Wd, 0, 8, c0, ncol) + (0,), load_panel(Wd, 8, 8, c0, ncol) + (8,)]

    def prepA(stack_bufs, src, rows, from_sbuf=None, extra_scale=None):
        xt_list, pst = stack_bufs
        slot = ph.get('xt_i', 0)
        ph['xt_i'] = (slot + 1) % len(xt_list)
        xt = xt_list[slot]
        xk = 'xt%d' % slot
        if from_sbuf is None:
            B.dma('sp', xt[0:rows, :], src, W=[xk])
            xin = xt
            rk = [xk]
        else:
            xin, rk0 = from_sbuf
            rk = [rk0]
        si = stat_slot(4)
        sk = 'stat%d' % si
        B.op('act', lambda e: e.activation(out=junk[0:rows, :], in_=xin[0:rows, :], func=AF.Square,
                                           accum_out=stat[0:rows, si:si + 1]), R=rk, W=[sk])
        B.op('act', lambda e: e.activation(out=stat[0:rows, si + 1:si + 2], in_=stat[0:rows, si:si + 1], func=AF.Ln,
                                           bias=epsc[0:rows, :], scale=1.0 / D), R=[sk, 'epsc'], W=[sk])
        B.op('act', lambda e: e.activation(out=stat[0:rows, si + 2:si + 3], in_=stat[0:rows, si + 1:si + 2], func=AF.Exp,
                                           scale=-0.5), R=[sk], W=[sk])
        if extra_scale is not None:
            B.op('dve', lambda e: e.tensor_tensor(out=stat[0:rows, si + 2:si + 3], in0=stat[0:rows, si + 2:si + 3],
                                                  in1=extra_scale[0:rows, :], op=ALU.mult), R=[sk, 'esc'], W=[sk])
        B.op('dve', lambda e: e.tensor_scalar(out=xt[0:rows, :], in0=xin[0:rows, :], scalar1=stat[0:rows, si + 2:si + 3],
                                              scalar2=None, op0=ALU.mult), R=rk + [sk], W=[xk])
        return (xt, xk, rows, pst)

    def prepT(pa, gT, gkey, dst_fn, dst_keys):
        xt, xk, rows, pst = pa
        for q in range(4):
            pi = ph.get('pst_i', 0)
            ph['pst_i'] = (pi + 1) % len(pst)
            bank, bkey = pst[pi]
            for j in range(4):
                kc = q * 4 + j
                B.op('pe', lambda e, kc=kc, j=j: e.transpose(out=bank[:, j * 128:j * 128 + rows],
                                                              in_=xt[0:rows, kc * 128:(kc + 1) * 128],
                                                              identity=ident_f[0:rows, 0:rows]),
                     R=[xk, 'ident_f'], W=[bkey], inc=(j == 3))
            B.op('dve', lambda e, q=q: e.tensor_tensor(
                out=dst_fn(q),
                in0=bank[:, :].rearrange("p (j t) -> p j t", j=4)[:, :, 0:rows],
                in1=gT[:, q * 4:q * 4 + 4].unsqueeze(2).broadcast_to([128, 4, rows]), op=ALU.mult),
                R=[bkey, gkey], W=dst_keys)

    def prep(stack_bufs, src, rows, gT, gkey, dst_fn, dst_keys, from_sbuf=None, extra_scale=None):
        pa = prepA(stack_bufs, src, rows, from_sbuf=from_sbuf, extra_scale=extra_scale)
        prepT(pa, gT, gkey, dst_fn, dst_keys)

    def proj_tm(lhs_fn, M, panels, bank, bkey, rkeys, ncol=512, col_off=0):
        n = 0
        for (view, keys, kb) in panels:
            for k in range(8):
                kc = kb + k
                B.op('pe', lambda e, view=view, k=k, kc=kc, n=n: e.matmul(
                    bank[0:M, col_off:col_off + ncol], lhs_fn(kc), view[:, k, 0:ncol], start=(n == 0), stop=(n == 15)),
                    R=rkeys + keys, W=[bkey], inc=(n == 15))
                n += 1

    def head_norm(bank, bkey, M, nh, hd, g_tile, gkey, kf, kfk, out_f32, of_keys, out_bf, ob_keys):
        w = nh * hd
        B.op('act', lambda e: e.activation(out=kf[0:M, 0:w], in_=bank[0:M, 0:w], func=AF.Copy), R=[bkey], W=[kfk])
        si = stat_slot(12)
        sk = 'stat%d' % si
        sqs = ph['sqs']
        B.op('act', lambda e: e.activation(out=sqs[0:M, 0:w], in_=kf[0:M, 0:w], func=AF.Square), R=[kfk], W=['sqs'])
        B.op('dve', lambda e: e.tensor_reduce(out=stat[0:M, si:si + nh], in_=sqs[0:M, 0:w].rearrange("p (h d) -> p h d", h=nh),
                                              axis=AX.X, op=ALU.add), R=['sqs'], W=[sk])
        B.op('act', lambda e: e.activation(out=stat[0:M, si + 4:si + 4 + nh], in_=stat[0:M, si:si + nh], func=AF.Ln,
                                           bias=epsc[0:M, :], scale=1.0 / hd), R=[sk, 'epsc'], W=[sk])
        B.op('act', lambda e: e.activation(out=stat[0:M, si + 8:si + 8 + nh], in_=stat[0:M, si + 4:si + 4 + nh], func=AF.Exp,
                                           scale=-0.5), R=[sk], W=[sk])
        dst = out_f32 if out_f32 is not None else out_bf
        dk = of_keys if out_f32 is not None else ob_keys
        for h in range(nh):
            B.op('dve', lambda e, h=h: e.scalar_tensor_tensor(
                out=dst[0:M, h * hd:(h + 1) * hd], in0=kf[0:M, h * hd:(h + 1) * hd],
                scalar=stat[0:M, si + 8 + h:si + 9 + h], in1=g_tile[0:M, 0:hd], op0=ALU.mult, op1=ALU.mult),
                R=[kfk, sk, gkey], W=dk)
        if out_f32 is not None and out_bf is not None:
            B.op('act', lambda e: e.activation(out=out_bf[0:M, 0:w], in_=out_f32[0:M, 0:w], func=AF.Copy), R=of_keys, W=ob_keys)

    st_mix = ExitStack()
    st1 = ExitStack()
    hT = al("hT", [128, KC, NT], BF16, at=0)
    o_attn = al("o_attn", [128, 4, NT], BF16, at=34)
    hT_h1 = al("hT_h1", [128, KC, 32], BF16, at=42.5)
    c_act = al("c_act", [128, 8, NT], BF16, at=43.5)
    o_mem = al("o_mem", [128, 8, NT], BF16, at=60.5)
    set_ptr(43.5)
    new_ring(st1, 12, "a")
    xt_list = [al("xt%d" % i, [128, D], F32) for i in range(2)]
    pst = [(psb[6], 'ps6'), (psb[7], 'ps7')]
    prep_bufs = (xt_list, pst)
    hTh = [al("hTh%d" % i, [128, KC, 128], BF16) for i in range(2)]
    acc_O = al("acc_O", [128, 4, NT], F32)
    acc_D = al("acc_D", [128, 4, NT], F32)
    kf_l = [al("kf%d" % i, [128, 512], F32) for i in range(2)]
    ko_l = [al("ko%d" % i, [128, 512], F32) for i in range(3)]
    vo_l = [al("vo%d" % i, [128, 512], F32) for i in range(3)]
    sqs = al("sqs", [128, 512], F32)
    qb_l = [al("qb%d" % i, [128, 512], BF16) for i in range(2)]
    kb_l = [al("kb%d" % i, [128, 512], BF16) for i in range(2)]
    kT_l = [al("kT%d" % i, [128, 4, 128], BF16) for i in range(3)]
    vS_l = [al("vS%d" % i, [128, 512], BF16) for i in range(4)]
    qT_l = [al("qT%d" % i, [128, 4, 128], BF16) for i in range(2)]
    pA_l = [al("pA%d" % i, [128, 512], BF16) for i in range(2)]
    pB_l = [al("pB%d" % i, [128, 512], BF16) for i in range(2)]
    kTs = al("kTs", [128, 4, 34], BF16)
    qTs = al("qTs", [128, 4, 34], BF16)
    vSs = al("vSs", [34, 512], BF16)
    pBs = al("pBs", [34, 4, 32], BF16)
    ksb_l = [al("ksb%d" % i, [128, 512], BF16) for i in range(3)]
    vsb_l = [al("vsb%d" % i, [128, 512], BF16) for i in range(3)]
    kTA_l = [al("kTA%d" % i, [128, 4, 128], BF16) for i in range(2)]
    pAs_l = [al("pAs%d" % i, [128, 4, 8], BF16) for i in range(2)]
    rot = {}
    ph['sqs'] = sqs

    def nxt(name, lst):
        i = rot.get(name, 0)
        rot[name] = (i + 1) % len(lst)
        return lst[i], '%s%d' % (name, i)

    PQ, PK, PV, PT, PSA, PSB, PO, PD = range(8)
    pT_bf = psb[PT][:, :].bitcast(BF16)

    B.barrier()
    a_tiles = [(x_misc[:, :], 34, 1024)] + [(x_main[t * 128:(t + 1) * 128, :], 128, t * 128) for t in range(8)]
    pa_cur = prepA(prep_bufs, a_tiles[0][0], a_tiles[0][1])
    for ti_, (src_, rows_, c0_) in enumerate(a_tiles):
        pa_nxt = prepA(prep_bufs, a_tiles[ti_ + 1][0], a_tiles[ti_ + 1][1]) if ti_ + 1 < len(a_tiles) else None
        prepT(pa_cur, gT_mix, 'gT_mix', lambda q, c0_=c0_, rows_=rows_: hT[:, q * 4:q * 4 + 4, c0_:c0_ + rows_], ['hT'])
        pa_cur = pa_nxt
    _ck(2)

    def transpose4(src_bf, skey, M, half, dst, dkeys, perm=None):
        pk = 'ps%d' % PT
        base = half * 512
        for h in range(4):
            B.op('pe', lambda e, h=h: e.transpose(out=pT_bf[:, base + h * 128: base + h * 128 + M],
                                                  in_=src_bf[0:M, h * 128:(h + 1) * 128], identity=ident_b[0:M, 0:M]),
                 R=[skey, 'ident_b'], W=[pk], inc=(h == 3))
        src = pT_bf[:, base:base + 512].rearrange("p (h t) -> p h t", h=4)[:, :, 0:M]
        if perm is None:
            B.op('act', lambda e: e.activation(out=dst, in_=src, func=AF.Copy), R=[pk], W=dkeys)
        else:
            perm(src, pk)

    scale_att = 1.0 / np.sqrt(128.0)

    def attend_scores(QB, qc0, nq, kTA, kTAk, vA, vAk, kTB, kTBk, vB, vBk, qT, qTk, hb_col, acc_cols_fn, first, hb_colB=None):
        pA, pAk = nxt('pA', pA_l)
        pB, pBk = nxt('pB', pB_l)
        sA, sB = psb[PSA], psb[PSB]
        mA = mask_b[:, qc0:qc0 + nq].unsqueeze(1).broadcast_to([128, 4, nq])
        B.op('pe', lambda e: e.matmul(sA[:, 0:4 * nq], ident_b[:, :], mA, start=True, stop=False),
             R=['ident_b', 'mask_b'], W=['ps%d' % PSA], inc=False)
        for h in range(4):
            B.op('pe', lambda e, h=h: e.matmul(sA[:, h * nq:(h + 1) * nq], kTA[:, h, 0:128], qT[:, h, qc0:qc0 + nq],
                                               start=False, stop=(h == 3)), R=[kTAk, qTk], W=['ps%d' % PSA], inc=(h == 3))
        B.op('act', lambda e: e.activation(out=pA[:, 0:4 * nq], in_=sA[:, 0:4 * nq], func=AF.Exp,
                                           bias=hbias[:, hb_col:hb_col + 1], scale=scale_att),
             R=['ps%d' % PSA, 'hbias'], W=[pAk])
        mB = mask_b[0:QB, 128 + qc0:128 + qc0 + nq].unsqueeze(1).broadcast_to([QB, 4, nq])
        B.op('pe', lambda e: e.matmul(sB[0:QB, 0:4 * nq], ident_b[0:QB, 0:QB], mB, start=True, stop=False),
             R=['ident_b', 'mask_b'], W=['ps%d' % PSB], inc=False)
        for h in range(4):
            B.op('pe', lambda e, h=h: e.matmul(sB[0:QB, h * nq:(h + 1) * nq], kTB[:, h, 0:QB], qT[:, h, qc0:qc0 + nq],
                                               start=False, stop=(h == 3)), R=[kTBk, qTk], W=['ps%d' % PSB], inc=(h == 3))
        cb = NHB - 1 if hb_colB is None else hb_colB
        B.op('act', lambda e: e.activation(out=pB[0:QB, 0:4 * nq], in_=sB[0:QB, 0:4 * nq], func=AF.Exp,
                                           bias=hbias[0:QB, cb:cb + 1], scale=scale_att),
             R=['ps%d' % PSB, 'hbias'], W=[pBk])
        return (pA, pAk, pB, pBk)

    def attend_pv(st, QB, qc0, nq, kTA, kTAk, vA, vAk, kTB, kTBk, vB, vBk, qT, qTk, hb_col, acc_cols_fn, first, hb_colB=None):
        pA, pAk, pB, pBk = st
        oT, dn = psb[PO], psb[PD]
        for h in range(4):
            B.op('pe', lambda e, h=h: e.matmul(oT[:, h * nq:(h + 1) * nq], vA[:, h * 128:(h + 1) * 128], pA[:, h * nq:(h + 1) * nq],
                                               start=True, stop=False), R=[vAk, pAk], W=['ps%d' % PO], inc=False)
            B.op('pe', lambda e, h=h: e.matmul(oT[:, h * nq:(h + 1) * nq], vB[0:QB, h * 128:(h + 1) * 128], pB[0:QB, h * nq:(h + 1) * nq],
                                               start=False, stop=True), R=[vBk, pBk], W=['ps%d' % PO], inc=(h == 3))
        B.op('pe', lambda e: e.matmul(dn[:, 0:4 * nq], ones_b[:, :], pA[:, 0:4 * nq], start=True, stop=False),
             R=['ones_b', pAk], W=['ps%d' % PD], inc=False)
        B.op('pe', lambda e: e.matmul(dn[:, 0:4 * nq], ones_b[0:QB, :], pB[0:QB, 0:4 * nq], start=False, stop=True),
             R=['ones_b', pBk], W=['ps%d' % PD], inc=True)
        for (acc, pbank, pk, ak) in [(acc_O, oT, 'ps%d' % PO, 'acc_O'), (acc_D, dn, 'ps%d' % PD, 'acc_D')]:
            dst = acc_cols_fn(acc)
            src = pbank[:, 0:4 * nq].rearrange("p (h t) -> p h t", h=4)
            if first:
                B.op('act', lambda e, dst=dst, src=src: e.activation(out=dst, in_=src, func=AF.Copy), R=[pk], W=[ak])
            else:
                B.op('dve', lambda e, dst=dst, src=src: e.tensor_tensor(out=dst, in0=src, in1=dst, op=ALU.add), R=[pk, ak], W=[ak])


    def attend(*args, **kw):
        st = attend_scores(*args, **kw)
        attend_pv(st, *args, **kw)

    for g, (d, W) in enumerate(GROUPS):
        QB = min(128, 1024 // d)
        nmb = 1024 // (d * QB)
        Wq = load_w16(w_in, C_Q + g * 512)
        Wk = load_w16(w_in, C_K + g * 512)
        Wv = load_w16(w_in, C_V + g * 512)

        def tile_proj(lhs_fn, lkeys, M, want_q):
            proj_tm(lhs_fn, M, Wk, psb[PK], 'ps%d' % PK, lkeys)
            proj_tm(lhs_fn, M, Wv, psb[PV], 'ps%d' % PV, lkeys)
            if want_q:
                proj_tm(lhs_fn, M, Wq, psb[PQ], 'ps%d' % PQ, lkeys)

        def tile_norm(M, want_q, out_rows):
            res = {}
            kf, kfk = nxt('kf', kf_l)
            kb, kbk = nxt('kb', kb_l)
            if out_rows is not None:
                ko, kok = nxt('ko', ko_l)
                head_norm(psb[PK], 'ps%d' % PK, M, 4, 128, gk_t, 'gk', kf, kfk, ko, [kok], kb, [kbk])
                B.dma('sp', out_rows(o_k[g]), ko[0:M, :], R=[kok])
            else:
                head_norm(psb[PK], 'ps%d' % PK, M, 4, 128, gk_t, 'gk', kf, kfk, None, None, kb, [kbk])
            vS, vSk = nxt('vS', vS_l)
            B.op('act', lambda e: e.activation(out=vS[0:M, :], in_=psb[PV][0:M, :], func=AF.Copy), R=['ps%d' % PV], W=[vSk])
            if out_rows is not None:
                vo, vok = nxt('vo', vo_l)
                B.op('dve', lambda e: e.tensor_copy(out=vo[0:M, :], in_=psb[PV][0:M, :]), R=['ps%d' % PV], W=[vok])
                B.dma('sp', out_rows(o_v[g]), vo[0:M, :], R=[vok])
            res.update(kb=kb, kbk=kbk, vS=vS, vSk=vSk)
            if want_q:
                kf2, kfk2 = nxt('kf', kf_l)
                qb, qbk = nxt('qb', qb_l)
                head_norm(psb[PQ], 'ps%d' % PQ, M, 4, 128, gq_t, 'gq', kf2, kfk2, None, None, qb, [qbk])
                res.update(qb=qb, qbk=qbk)
            return res

        def tile_tr(M, want_q, res):
            kT, kTk = nxt('kT', kT_l)
            transpose4(res['kb'], res['kbk'], M, 0, kT[:, :, 0:M], [kTk])
            res.update(kT=kT, kTk=kTk)
            if want_q:
                qT, qTk = nxt('qT', qT_l)
                transpose4(res['qb'], res['qbk'], M, 1, qT[:, :, 0:M], [qTk])
                res.update(qT=qT, qTk=qTk)
            return res

        proj_tm(lambda kc: hT[:, kc, 1024:1058], 34, Wk, psb[PK], 'ps%d' % PK, ['hT'])
        proj_tm(lambda kc: hT[:, kc, 1024:1058], 34, Wv, psb[PV], 'ps%d' % PV, ['hT'])
        proj_tm(lambda kc: hT[:, kc, 1024:1058], 34, Wq, psb[PQ], 'ps%d' % PQ, ['hT'])
        kf, kfk = nxt('kf', kf_l)
        kb, kbk = nxt('kb', kb_l)
        ko, kok = nxt('ko', ko_l)
        head_norm(psb[PK], 'ps%d' % PK, 34, 4, 128, gk_t, 'gk', kf, kfk, ko, [kok], kb, [kbk])
        for b in range(4):
            B.dma('sp', o_swk[g][b, W - 8:W, :], ko[b * 8:(b + 1) * 8, :], R=[kok])
        transpose4(kb, kbk, 34, 0, kTs[:, :, :], ['kTs'])
        B.op('act', lambda e: e.activation(out=vSs[0:34, :], in_=psb[PV][0:34, :], func=AF.Copy), R=['ps%d' % PV], W=['vSs'])
        vo, vok = nxt('vo', vo_l)
        B.op('dve', lambda e: e.tensor_copy(out=vo[0:34, :], in_=psb[PV][0:34, :]), R=['ps%d' % PV], W=[vok])
        for b in range(4):
            B.dma('sp', o_swv[g][b, W - 8:W, :], vo[b * 8:(b + 1) * 8, :], R=[vok])
        kf2, kfk2 = nxt('kf', kf_l)
        qb, qbk = nxt('qb', qb_l)
        head_norm(psb[PQ], 'ps%d' % PQ, 34, 4, 128, gq_t, 'gq', kf2, kfk2, None, None, qb, [qbk])
        ds = min(d, 8)
        QBs = 8 // ds

        def qperm(src, pk):
            for h in range(4):
                B.op('act', lambda e, h=h: e.activation(
                    out=qTs[:, h, 0:32].rearrange("p (b r q) -> p b q r", b=4, r=ds, q=QBs),
                    in_=src[:, h, 0:32].rearrange("p (b q r) -> p b q r", b=4, q=QBs, r=ds), func=AF.Copy),
                    R=[pk], W=['qTs'])
        transpose4(qb, qbk, 34, 1, None, None, perm=qperm)
        sB = psb[PSB]
        for h in range(4):
            B.op('pe', lambda e, h=h: e.matmul(sB[0:32, h * 32:(h + 1) * 32], kTs[:, h, 0:32], qTs[:, h, 0:32],
                                               start=True, stop=False), R=['kTs', 'qTs'], W=['ps%d' % PSB], inc=False)
            B.op('pe', lambda e, h=h: e.matmul(sB[0:32, h * 32:(h + 1) * 32], ident_b[0:32, 0:32], smask_b[0:32, g * 32:(g + 1) * 32],
                                               start=False, stop=True), R=['ident_b', 'smask_b'], W=['ps%d' % PSB], inc=(h == 3))
        B.op('act', lambda e: e.activation(out=pBs[0:32, :, :], in_=sB[0:32, 0:128].rearrange("p (h t) -> p h t", h=4),
                                           func=AF.Exp, scale=scale_att), R=['ps%d' % PSB], W=['pBs'])
        oTs, dns = psb[PO], psb[PD]
        combos = [(b, r) for b in range(4) for r in range(ds)]

        def s_stageA(b, r):
            ksb, ksbk = nxt('ksb', ksb_l)
            vsb, vsbk = nxt('vsb', vsb_l)
            B.dma('pool', ksb[:, :], st_wk[g][b, r:W:d, :], W=[ksbk])
            B.dma('pool', vsb[:, :], st_wv[g][b, r:W:d, :], W=[vsbk])
            kTA, kTAk = nxt('kTA', kTA_l)
            transpose4(ksb, ksbk, 128, 0, kTA[:, :, :], [kTAk])
            return (kTA, kTAk, vsb, vsbk)

        def s_stageB(b, r, st):
            kTA, kTAk, vsb, vsbk = st
            off = b * 8 + r * QBs
            pAs, pAsk = nxt('pAs', pAs_l)
            sA = psb[PSA]
            for h in range(4):
                B.op('pe', lambda e, h=h: e.matmul(sA[:, h * QBs:(h + 1) * QBs], kTA[:, h, :], qTs[:, h, off:off + QBs],
                                                   start=True, stop=False), R=[kTAk, 'qTs'], W=['ps%d' % PSA], inc=False)
                B.op('pe', lambda e, h=h: e.matmul(sA[:, h * QBs:(h + 1) * QBs], ident_b[:, :], mask_b[:, 0:QBs],
                                                   start=False, stop=True), R=['ident_b', 'mask_b'], W=['ps%d' % PSA], inc=(h == 3))
            B.op('act', lambda e: e.activation(out=pAs[:, :, 0:QBs], in_=sA[:, 0:4 * QBs].rearrange("p (h t) -> p h t", h=4),
                                               func=AF.Exp, scale=scale_att), R=['ps%d' % PSA], W=[pAsk])
            for h in range(4):
                B.op('pe', lambda e, h=h: e.matmul(oTs[:, h * 32 + off:h * 32 + off + QBs], vSs[0:32, h * 128:(h + 1) * 128],
                                                   pBs[0:32, h, off:off + QBs], start=True, stop=False),
                     R=['vSs', 'pBs'], W=['ps%d' % PO], inc=False)
                B.op('pe', lambda e, h=h: e.matmul(oTs[:, h * 32 + off:h * 32 + off + QBs], vsb[:, h * 128:(h + 1) * 128],
                                                   pAs[:, h, 0:QBs], start=False, stop=True),
                     R=[vsbk, pAsk], W=['ps%d' % PO], inc=(h == 3))
            for h in range(4):
                B.op('pe', lambda e, h=h: e.matmul(dns[:, h * 32 + off:h * 32 + off + QBs], ones_b[0:32, :],
                                                   pBs[0:32, h, off:off + QBs], start=True, stop=False),
                     R=['ones_b', 'pBs'], W=['ps%d' % PD], inc=False)
                B.op('pe', lambda e, h=h: e.matmul(dns[:, h * 32 + off:h * 32 + off + QBs], ones_b[:, :],
                                                   pAs[:, h, 0:QBs], start=False, stop=True),
                     R=['ones_b', pAsk], W=['ps%d' % PD], inc=(h == 3))

        st_cur = s_stageA(*combos[0])
        for ci, (b, r) in enumerate(combos):
            st_nxt = s_stageA(*combos[ci + 1]) if ci + 1 < len(combos) else None
            s_stageB(b, r, st_cur)
            st_cur = st_nxt
        for (acc, pbank, pk, ak) in [(acc_O, oTs, 'ps%d' % PO, 'acc_O'), (acc_D, dns, 'ps%d' % PD, 'acc_D')]:
            for h in range(4):
                dst = acc[:, h, 1024:1056].rearrange("p (b q r) -> p b q r", b=4, q=QBs, r=ds)
                src = pbank[:, h * 32:(h + 1) * 32].rearrange("p (b r q) -> p b q r", b=4, r=ds, q=QBs)
                if g == 0:
                    B.op('act', lambda e, dst=dst, src=src: e.activation(out=dst, in_=src, func=AF.Copy), R=[pk], W=[ak])
                else:
                    B.op('dve', lambda e, dst=dst, src=src: e.tensor_tensor(out=dst, in0=src, in1=dst, op=ALU.add), R=[pk, ak], W=[ak])

        _ck(3 + 2 * g)
        jobs = []
        for r in range(d):
            if r in ERES[d]:
                jobs.append(dict(kind='x', r=r, first=True))
            jobs.append(dict(kind='h', r=r, first=(r not in ERES[d])))
            for bm in range(nmb):
                jobs.append(dict(kind='m', r=r, bm=bm, first=False))

        def job_prepA(job):
            r = job['r']
            if job['kind'] == 'h':
                src = x_halo[2048 - 128 * d + r:2048:d, :]
            elif d == 16:
                src = x_far[(r - 14) * 128:(r - 13) * 128, :]
            else:
                src = x_halo[2048 - 256 * d + r:2048 - 128 * d:d, :]
            return prepA(prep_bufs, src, 128)

        def job_prepT(job, pa):
            hh, hhk = nxt('hTh', hTh)
            prepT(pa, gT_mix, 'gT_mix', lambda q, hh=hh: hh[:, q * 4:q * 4 + 4, :], [hhk])
            if job['kind'] == 'h' and d == 1:
                B.op('pool', lambda e, hh=hh: e.tensor_copy(out=hT_h1[:, :, :], in_=hh[:, :, 96:128]), R=[hhk], W=['hT_h1'])
            job['hh'] = (hh, hhk)

        prev = None
        T1 = None
        T2 = None
        if jobs[0]['kind'] != 'm':
            job_prepT(jobs[0], job_prepA(jobs[0]))

        def do_tr(T):
            job_, res_, M_, wq_, t0_ = T
            r_ = job_['r']
            prev_ = None if job_['first'] else do_tr.prev
            cur = tile_tr(M_, wq_, res_)
            args = None
            if job_['kind'] != 'm' and wq_:
                nq = 2 if d == 1 else 1
                qc0 = 128 - nq
                ecol = 1056 + (0 if d == 1 else (r_ - ERES[d][0]))
                args = ((128, qc0, nq, prev_['kT'], prev_['kTk'], prev_['vS'], prev_['vSk'],
                         cur['kT'], cur['kTk'], cur['vS'], cur['vSk'], cur['qT'], cur['qTk'],
                         HT_IDX[(g, r_, 1)], (lambda acc, ecol=ecol, nq=nq: acc[:, :, ecol:ecol + nq]), g == 0),
                        dict(hb_colB=HT_IDX[(g, r_, 2)]))
            elif job_['kind'] == 'm':
                hb = HT_IDX[(g, r_, 0)] if job_['bm'] == 0 else NHB - 1
                args = ((QB, 0, QB, prev_['kT'], prev_['kTk'], prev_['vS'], prev_['vSk'],
                         cur['kT'], cur['kTk'], cur['vS'], cur['vSk'], cur['qT'], cur['qTk'],
                         hb, (lambda acc, t0_=t0_: acc[:, :, t0_:t0_ + d * QB:d]), g == 0), {})
            do_tr.prev = cur
            return args
        do_tr.prev = None

        for i, job in enumerate(jobs):
            nj = jobs[i + 1] if i + 1 < len(jobs) else None
            pa = job_prepA(nj) if (nj is not None and nj['kind'] != 'm') else None
            if T2 is not None:
                dstate = attend_scores(*T2[0], **T2[1])
            r = job['r']
            t0 = None
            if job['kind'] != 'm':
                hh, hhk = job['hh']
                wq = (job['kind'] == 'h' and r in ERES[d])
                M = 128
                tile_proj(lambda kc, hh=hh: hh[:, kc, :], [hhk], M, wq)
                res = tile_norm(M, wq, None)
            else:
                bm = job['bm']
                t0 = r + d * QB * bm
                if g == 0:
                    orow = (lambda o: o[0:128, :]) if bm == 7 else None
                elif g == 1:
                    orow = (lambda o, r=r: o[r:512:4, :]) if bm == 1 else None
                else:
                    orow = (lambda o, r=r: o[r:1024:16, :])
                wq = True
                M = QB
                tile_proj(lambda kc, t0=t0: hT[:, kc, t0:t0 + d * QB:d], ['hT'], M, True)
                res = tile_norm(M, True, orow)
            if T2 is not None:
                attend_pv(dstate, *T2[0], **T2[1])
                T2 = None
            if pa is not None:
                job_prepT(nj, pa)
            if T1 is not None:
                T2 = do_tr(T1)
            T1 = (job, res, M, wq, t0)
        if T2 is not None:
            attend(*T2[0], **T2[1])
            T2 = None
        T2 = do_tr(T1)
        if T2 is not None:
            attend(*T2[0], **T2[1])
            T2 = None
        _ck(4 + 2 * g)
    _ck(10)
    for (c0, n) in TOKB:
        for h in range(4):
            B.op('act', lambda e, h=h, c0=c0, n=n: e.activation(out=acc_D[:, h, c0:c0 + n], in_=acc_D[:, h, c0:c0 + n], func=AF.Ln),
                 R=['acc_D'], W=['acc_D'])
            B.op('act', lambda e, h=h, c0=c0, n=n: e.activation(out=acc_D[:, h, c0:c0 + n], in_=acc_D[:, h, c0:c0 + n], func=AF.Exp, scale=-1.0),
                 R=['acc_D'], W=['acc_D'])
            B.op('dve', lambda e, h=h, c0=c0, n=n: e.tensor_tensor(out=o_attn[:, h, c0:c0 + n], in0=acc_O[:, h, c0:c0 + n],
                                                                  in1=acc_D[:, h, c0:c0 + n], op=ALU.mult),
                 R=['acc_O', 'acc_D'], W=['o_attn'])
    import os
    if os.environ.get('KDEBUG'):
        dbg = B.dout("dbg_oattn", [128, 4 * NT], BF16)
        B.dma('sp', dbg[:, :], o_attn[:, :, :].rearrange("p h t -> p (h t)"), R=['o_attn'], sem='dbg')
    B.barrier()


    _ck(20)
    st2 = ExitStack()
    set_ptr(60.5)
    new_ring(st2, 8, "c")
    u_p = al("u_p", [128, 8, 1056], BF16)
    u_s = al("u_s", [128, 8, 152], BF16)
    utail = al("utail", [128, 8, 64], F32)
    wdwT = al("wdwT", [128, 8, 31], F32)
    dg_l = [al("dg%d" % i, [128, 31, 128], BF16) for i in range(2)]
    c_f = al("c_f", [128, 8, NT], F32)
    tmpA = [al("tmpA%d" % i, [128, 512], F32) for i in range(2)]
    tmpB = [al("tmpB%d" % i, [128, 512], F32) for i in range(2)]
    mu_t = al("mu_t", [128, NT], F32)
    rs_t = al("rs_t", [128, NT], F32)
    ld32 = al("ld32", [128, 1024], F32)
    ones_f = al("ones_f", [128, 128], F32)
    uo = al("uo", [64, 1024], F32)
    bank_i = {'i': 0}

    def nbank():
        i = bank_i['i']
        bank_i['i'] = (i + 1) % 8
        return psb[i], 'ps%d' % i

    B.op('dve', lambda e: e.memset(ones_f[:], 1.0), W=['ones_f'])
    B.dma('sp', ld32[0:31, :], w_dw[:, :], W=['ld32'])
    bk, bkk = nbank()
    for j in range(8):
        B.op('pe', lambda e, j=j: e.transpose(out=bk[:, j * 31:(j + 1) * 31], in_=ld32[0:31, j * 128:(j + 1) * 128],
                                              identity=ident_f[0:31, 0:31]), R=['ld32', 'ident_f'], W=[bkk], inc=(j == 7))
    B.op('act', lambda e: e.activation(out=wdwT[:, :, :], in_=bk[:, 0:248].rearrange("p (j k) -> p j k", j=8), func=AF.Copy),
         R=[bkk], W=['wdwT'])
    B.dma('sp', ld32[0:120, :], st_conv.rearrange("b p c -> (b p) c"), W=['ld32'])
    for half in range(2):
        bk, bkk = nbank()
        for jj in range(4):
            j = half * 4 + jj
            B.op('pe', lambda e, j=j, jj=jj: e.transpose(out=bk[:, jj * 120:(jj + 1) * 120], in_=ld32[0:120, j * 128:(j + 1) * 128],
                                                         identity=ident_f[0:120, 0:120]), R=['ld32', 'ident_f'], W=[bkk], inc=(jj == 3))
        for jj in range(4):
            j = half * 4 + jj
            B.op('act', lambda e, j=j, jj=jj: e.activation(
                out=u_s[:, j, 0:120].rearrange("p (pos b) -> p b pos", b=4),
                in_=bk[:, jj * 120:(jj + 1) * 120].rearrange("p (b pos) -> p b pos", b=4), func=AF.Copy), R=[bkk], W=['u_s'])

    def proj_fm(panels, jcol, rhs_fn, rkeys, N, nk_total=16):
        bk, bkk = nbank()
        n = 0
        for (view, keys, kb) in panels:
            nk = view.shape[1]
            for k in range(nk):
                B.op('pe', lambda e, view=view, k=k, kc=kb + k, n=n: e.matmul(
                    bk[:, 0:N], view[:, k, jcol:jcol + 128], rhs_fn(kc), start=(n == 0), stop=(n == nk_total - 1)),
                    R=rkeys + keys, W=[bkk], inc=(n == nk_total - 1))
                n += 1
        return bk, bkk

    tblocks = [('m0', lambda kc: hT[:, kc, 0:512], ['hT'], 512), ('m1', lambda kc: hT[:, kc, 512:1024], ['hT'], 512),
               ('s', lambda kc: hT[:, kc, 1024:1056], ['hT'], 32), ('h', lambda kc: hT_h1[:, kc, :], ['hT_h1'], 32)]
    for cbk in range(2):
        Wa = load_w16(w_in, C_A + cbk * 512)
        Wb = load_w16(w_in, C_B + cbk * 512)
        for jj in range(4):
            J = cbk * 4 + jj
            for (nm, rf, rk, N) in tblocks:
                pa, pak = proj_fm(Wa, jj * 128, rf, rk, N)
                pb, pbk = proj_fm(Wb, jj * 128, rf, rk, N)
                tA, tAk = nxt('tmpA', tmpA)
                B.op('act', lambda e, pb=pb, tA=tA, N=N: e.activation(out=tA[:, 0:N], in_=pb[:, 0:N], func=AF.Sigmoid), R=[pbk], W=[tAk])
                if nm == 'm0':
                    dst = u_p[:, J, 32:544]
                elif nm == 'm1':
                    dst = u_p[:, J, 544:1056]
                elif nm == 'h':
                    dst = u_p[:, J, 0:32]
                else:
                    dst = u_s[:, J, 120:152].rearrange("p (t b) -> p b t", b=4)
                src_a = pa[:, 0:N] if nm != 's' else pa[:, 0:32].rearrange("p (b t) -> p b t", b=4)
                src_s = tA[:, 0:N] if nm != 's' else tA[:, 0:32].rearrange("p (b t) -> p b t", b=4)
                B.op('dve', lambda e, dst=dst, src_a=src_a, src_s=src_s: e.tensor_tensor(out=dst, in0=src_a, in1=src_s, op=ALU.mult),
                     R=[pak, tAk], W=['u_p' if nm != 's' else 'u_s'])
                if nm == 'm1':
                    B.op('dve', lambda e, pa=pa, tA=tA, J=J: e.tensor_tensor(out=utail[:, J, 0:32], in0=pa[:, 480:512], in1=tA[:, 480:512], op=ALU.mult),
                         R=[pak, tAk], W=['utail'])
                if nm == 's':
                    B.op('dve', lambda e, pa=pa, tA=tA, J=J: e.tensor_tensor(out=utail[:, J, 32:64], in0=pa[:, 0:32], in1=tA[:, 0:32], op=ALU.mult),
                         R=[pak, tAk], W=['utail'])
    for half in range(2):
        bk, bkk = nbank()
        for jj in range(4):
            J = half * 4 + jj
            B.op('pe', lambda e, J=J, jj=jj: e.transpose(out=bk[0:64, jj * 128:(jj + 1) * 128], in_=utail[:, J, :], identity=ident_f[:, :]),
                 R=['utail', 'ident_f'], W=[bkk], inc=(jj == 3))
        B.op('act', lambda e, half=half, bk=bk: e.activation(out=uo[:, half * 512:(half + 1) * 512], in_=bk[0:64, :], func=AF.Copy), R=[bkk], W=['uo'])
    B.dma('sp', o_pconv[:, :], uo[0:32, :], R=['uo'])
    for b in range(4):
        B.dma('sp', o_sconv[b, 22:30, :], uo[32 + b * 8:40 + b * 8, :], R=['uo'])
    def build_dg(J):
        dgb = dg_l[J % 2]
        for k in range(31):
            B.op('dve', lambda e, k=k, J=J, dgb=dgb: e.tensor_scalar(out=dgb[:, k, :], in0=ident_b[:, :], scalar1=wdwT[:, J, k:k + 1], scalar2=None, op0=ALU.mult),
                 R=['ident_b', 'wdwT'], W=['dg%d' % (J % 2)])

    build_dg(0)
    for J in range(8):
        if J + 1 < 8:
            build_dg(J + 1)
        dg = dg_l[J % 2]
        dgk = 'dg%d' % (J % 2)
        outs = [(lambda k: u_p[:, J, 2 + k:2 + k + 512], 512, c_f[:, J, 0:512], None),
                (lambda k: u_p[:, J, 514 + k:514 + k + 512], 512, c_f[:, J, 512:1024], None),
                (lambda k: u_p[:, J, k:k + 2], 2, c_f[:, J, 1056:1058], None),
                (lambda k: u_s[:, J, k * 4:k * 4 + 32], 32, c_f[:, J, 1024:1056].rearrange("p (b t) -> p b t", b=4), 's')]
        for (rf, N, dst, kind) in outs:
            bk, bkk = nbank()
            for k in range(31):
                B.op('pe', lambda e, k=k, rf=rf, bk=bk, N=N, dg=dg: e.matmul(bk[:, 0:N], dg[:, k, :], rf(k), start=(k == 0), stop=(k == 30)),
                     R=[dgk, 'u_p', 'u_s'], W=[bkk], inc=(k == 30))
            src = bk[:, 0:N] if kind is None else bk[:, 0:32].rearrange("p (t b) -> p b t", b=4)
            B.op('act', lambda e, dst=dst, src=src, J=J: e.activation(out=dst, in_=src, func=AF.Identity, bias=cpar[:, 0, J:J + 1], scale=1.0),
                 R=[bkk, 'cpar'], W=['c_f'])
    for (c0, N) in TOKE:
        b1, b1k = nbank()
        b2, b2k = nbank()
        for J in range(8):
            tA, tAk = nxt('tmpA', tmpA)
            B.op('act', lambda e, tA=tA, J=J: e.activation(out=tA[:, 0:N], in_=c_f[:, J, c0:c0 + N], func=AF.Square), R=['c_f'], W=[tAk])
            B.op('pe', lambda e, J=J: e.matmul(b1[:, 0:N], ones_f[:, :], c_f[:, J, c0:c0 + N], start=(J == 0), stop=(J == 7)),
                 R=['ones_f', 'c_f'], W=[b1k], inc=(J == 7))
            B.op('pe', lambda e, J=J, tA=tA: e.matmul(b2[:, 0:N], ones_f[:, :], tA[:, 0:N], start=(J == 0), stop=(J == 7)),
                 R=['ones_f', tAk], W=[b2k], inc=True)
        B.op('act', lambda e: e.activation(out=mu_t[:, c0:c0 + N], in_=b1[:, 0:N], func=AF.Copy, scale=1.0 / 1024), R=[b1k], W=['mu_t'])
        tB, tBk = nxt('tmpB', tmpB)
        B.op('dve', lambda e, tB=tB: e.tensor_tensor(out=tB[:, 0:N], in0=mu_t[:, c0:c0 + N], in1=mu_t[:, c0:c0 + N], op=ALU.mult), R=['mu_t'], W=[tBk])
        B.op('dve', lambda e, tB=tB: e.scalar_tensor_tensor(out=tB[:, 0:N], in0=b2[:, 0:N], scalar=1.0 / 1024, in1=tB[:, 0:N],
                                                           op0=ALU.mult, op1=ALU.subtract), R=[b2k, tBk], W=[tBk])
        B.op('act', lambda e, tB=tB: e.activation(out=tB[:, 0:N], in_=tB[:, 0:N], func=AF.Ln, bias=epsc[:, :], scale=1.0), R=[tBk, 'epsc'], W=[tBk])
        B.op('act', lambda e, tB=tB: e.activation(out=rs_t[:, c0:c0 + N], in_=tB[:, 0:N], func=AF.Exp, scale=-0.5), R=[tBk], W=['rs_t'])
    for (c0, N) in TOKE:
        for J in range(8):
            tA, tAk = nxt('tmpA', tmpA)
            B.op('dve', lambda e, tA=tA, J=J: e.tensor_tensor(out=tA[:, 0:N], in0=c_f[:, J, c0:c0 + N], in1=mu_t[:, c0:c0 + N], op=ALU.subtract),
                 R=['c_f', 'mu_t'], W=[tAk])
            B.op('dve', lambda e, tA=tA: e.tensor_tensor(out=tA[:, 0:N], in0=tA[:, 0:N], in1=rs_t[:, c0:c0 + N], op=ALU.mult), R=[tAk, 'rs_t'], W=[tAk])
            B.op('act', lambda e, tA=tA, J=J: e.activation(out=c_act[:, J, c0:c0 + N], in_=tA[:, 0:N], func=AF.Silu,
                                                           bias=cpar[:, 2, J:J + 1], scale=cpar[:, 1, J:J + 1]), R=[tAk, 'cpar'], W=['c_act'])
    B.barrier()
    _ck(21)

    st3 = ExitStack()
    set_ptr(77.5)
    new_ring(st3, 8, "d")
    xt_list = [al("xtd%d" % i, [128, D], F32) for i in range(2)]
    ph['xt_i'] = 0
    prep_bufs = (xt_list, [(psb[6], 'ps6'), (psb[7], 'ps7')])
    hmT = al("hmT", [128, KC, 256], BF16)
    kmT = al("kmT", [128, 8, 256], BF16)
    vmb = al("vmb", [128, 2, 1024], BF16)
    qmT = al("qmT", [128, 8, NT], BF16)
    kf_l = [al("kfd%d" % i, [128, 512], F32) for i in range(2)]
    ko_l = [al("kod%d" % i, [128, 512], F32) for i in range(2)]
    kb_l = [al("kbd%d" % i, [128, 512], BF16) for i in range(2)]
    sqs3 = al("sqs3", [128, 512], F32)
    ph['sqs'] = sqs3
    pm_l = [al("pm%d" % i, [128, 512], BF16) for i in range(4)]
    rd_l = [al("rd%d" % i, [128, 512], F32) for i in range(2)]
    kms = [al("kms%d" % i, [128, 2, 1024], BF16) for i in range(1)]
    vms = [al("vms%d" % i, [128, 2, 1024], BF16) for i in range(1)]
    kmTs = [al("kmTs%d" % i, [128, 8, 256], BF16) for i in range(1)]
    rot.clear()
    pT_bf3 = psb[5][:, :].bitcast(BF16)

    def transpose4b(src_bf, skey, M, half, dst, dkeys):
        pk = 'ps5'
        base = half * 512
        for h in range(4):
            B.op('pe', lambda e, h=h: e.transpose(out=pT_bf3[:, base + h * 128: base + h * 128 + M],
                                                  in_=src_bf[0:M, h * 128:(h + 1) * 128], identity=ident_b[0:M, 0:M]),
                 R=[skey, 'ident_b'], W=[pk], inc=(h == 3))
        src = pT_bf3[:, base:base + 512].rearrange("p (h t) -> p h t", h=4)[:, :, 0:M]
        B.op('act', lambda e: e.activation(out=dst, in_=src, func=AF.Copy), R=[pk], W=dkeys)

    for mt in range(2):
        prep(prep_bufs, mem_p[mt * 128:(mt + 1) * 128, :], 128, gT_mem, 'gT_mem',
             lambda q, mt=mt: hmT[:, q * 4:q * 4 + 4, mt * 128:(mt + 1) * 128], ['hmT'])
    for cb in range(2):
        Wmk = load_w16(w_mem_k, cb * 512)
        Wmv = load_w16(w_mem_v, cb * 512)
        for mt in range(2):
            proj_tm(lambda kc, mt=mt: hmT[:, kc, mt * 128:(mt + 1) * 128], 128, Wmk, psb[0], 'ps0', ['hmT'])
            proj_tm(lambda kc, mt=mt: hmT[:, kc, mt * 128:(mt + 1) * 128], 128, Wmv, psb[1], 'ps1', ['hmT'])
            kf, kfk = nxt('kfd', kf_l); ko, kok = nxt('kod', ko_l); kb, kbk = nxt('kbd', kb_l)
            head_norm(psb[0], 'ps0', 128, 2, 256, gmk_t, 'gmk', kf, kfk, ko, [kok], kb, [kbk])
            B.dma('sp', o_pmk[mt * 128:(mt + 1) * 128, cb * 512:(cb + 1) * 512], ko[:, :], R=[kok])
            transpose4b(kb, kbk, 128, 0, kmT[:, cb * 4:cb * 4 + 4, mt * 128:(mt + 1) * 128], ['kmT'])
            B.op('act', lambda e, mt=mt, cb=cb: e.activation(out=vmb[:, mt, cb * 512:(cb + 1) * 512], in_=psb[1][:, :], func=AF.Copy), R=['ps1'], W=['vmb'])
            ko2, kok2 = nxt('kod', ko_l)
            B.op('dve', lambda e, ko2=ko2: e.tensor_copy(out=ko2[:, :], in_=psb[1][:, :]), R=['ps1'], W=[kok2])
            B.dma('sp', o_pmv[mt * 128:(mt + 1) * 128, cb * 512:(cb + 1) * 512], ko2[:, :], R=[kok2])
    tiles9 = [(t * 128, 128) for t in range(8)] + [(1024, 34)]
    qdef = [None]
    for cb in range(2):
        Wqm = load_w16(w_in, C_QM + cb * 512)
        for (t0, M) in tiles9:
            proj_tm(lambda kc, t0=t0, M=M: hT[:, kc, t0:t0 + M], M, Wqm, psb[2], 'ps2', ['hT'])
            kf, kfk = nxt('kfd', kf_l); kb, kbk = nxt('kbd', kb_l)
            head_norm(psb[2], 'ps2', M, 2, 256, gmq_t, 'gmq', kf, kfk, None, None, kb, [kbk])
            if qdef[0] is not None:
                qdef[0]()
            qdef[0] = (lambda kb=kb, kbk=kbk, M=M, cb=cb, t0=t0: transpose4b(kb, kbk, M, 1, qmT[:, cb * 4:cb * 4 + 4, t0:t0 + M], ['qmT']))
    qdef[0]()
    scale_mem = 1.0 / 16.0

    mdef = [None]

    def mem_attend(kT_, kTk, v_, vk, c0, N):
        for h in range(4):
            pms = []
            for mt in range(2):
                bk, bkk = psb[mt], 'ps%d' % mt
                for dc in range(2):
                    B.op('pe', lambda e, mt=mt, dc=dc, bk=bk: e.matmul(bk[:, 0:N], kT_[:, h * 2 + dc, mt * 128:(mt + 1) * 128],
                                                                      qmT[:, h * 2 + dc, c0:c0 + N], start=(dc == 0), stop=(dc == 1)),
                         R=[kTk, 'qmT'], W=[bkk], inc=(dc == 1))
                pm, pmk = nxt('pm', pm_l)
                B.op('act', lambda e, pm=pm, bk=bk: e.activation(out=pm[:, 0:N], in_=bk[:, 0:N], func=AF.Exp, scale=scale_mem), R=[bkk], W=[pmk])
                pms.append((pm, pmk))
            B.op('pe', lambda e: e.matmul(psb[4][:, 0:N], ones_b[:, :], pms[0][0][:, 0:N], start=True, stop=False), R=['ones_b', pms[0][1]], W=['ps4'], inc=False)
            B.op('pe', lambda e: e.matmul(psb[4][:, 0:N], ones_b[:, :], pms[1][0][:, 0:N], start=False, stop=True), R=['ones_b', pms[1][1]], W=['ps4'], inc=True)
            rd, rdk = nxt('rd', rd_l)
            B.op('act', lambda e, rd=rd: e.activation(out=rd[:, 0:N], in_=psb[4][:, 0:N], func=AF.Ln), R=['ps4'], W=[rdk])
            B.op('act', lambda e, rd=rd: e.activation(out=rd[:, 0:N], in_=rd[:, 0:N], func=AF.Exp, scale=-1.0), R=[rdk], W=[rdk])

            def stage2(h=h, pms=pms, rd=rd, rdk=rdk):
                for dc in range(2):
                    bk, bkk = psb[2 + dc], 'ps%d' % (2 + dc)
                    for mt in range(2):
                        B.op('pe', lambda e, mt=mt, dc=dc, bk=bk: e.matmul(bk[:, 0:N], v_[:, mt, h * 256 + dc * 128:h * 256 + (dc + 1) * 128],
                                                                          pms[mt][0][:, 0:N], start=(mt == 0), stop=(mt == 1)),
                             R=[vk, pms[mt][1]], W=[bkk], inc=(mt == 1))
                    B.op('dve', lambda e, dc=dc, bk=bk: e.tensor_tensor(out=o_mem[:, h * 2 + dc, c0:c0 + N], in0=bk[:, 0:N], in1=rd[:, 0:N], op=ALU.mult),
                         R=[bkk, rdk], W=['o_mem'])
            if mdef[0] is not None:
                mdef[0]()
            mdef[0] = stage2
        mdef[0]()
        mdef[0] = None

    for (c0, N) in [(0, 512), (512, 512), (1056, 2)]:
        mem_attend(kmT, 'kmT', vmb, 'vmb', c0, N)
    for b in range(4):
        km, kmk = nxt('kms', kms); vm, vmk = nxt('vms', vms); kt, ktk = nxt('kmTs', kmTs)
        B.dma('pool', km[:, :, :], cmk[b].rearrange("(mt p) c -> p mt c", p=128), W=[kmk])
        B.dma('pool', vm[:, :, :], cmv[b].rearrange("(mt p) c -> p mt c", p=128), W=[vmk])
        for mt in range(2):
            for cb in range(2):
                transpose4b(km[:, mt, cb * 512:(cb + 1) * 512], kmk, 128, (mt * 2 + cb) % 2, kt[:, cb * 4:cb * 4 + 4, mt * 128:(mt + 1) * 128], [ktk])
        mem_attend(kt, ktk, vm, vmk, 1024 + b * 8, 8)
    if mdef[0] is not None:
        mdef[0]()
        mdef[0] = None
    B.barrier()
    _ck(22)

    st4 = ExitStack()
    st_x2 = ExitStack()
    set_ptr(77.5)
    m_t = al("m_t", [128, KC, NT], BF16, at=162)
    new_ring(st4, 16, "e")
    sg_l = [al("sg%d" % i, [128, 512], F32) for i in range(6)]
    tt_l = [al("tt%d" % i, [128, 512], F32) for i in range(4)]
    for fp in range(8):
        f0 = fp * 256
        Wg = [[load_panel(w_in, 0, 8, cg + f0, 256) + (0,), load_panel(w_in, 8, 8, cg + f0, 256) + (8,)] for cg in (C_GC, C_GA, C_GM)]
        Wco = [load_panel(w_conv_out, 0, 8, f0, 256) + (0,)]
        Wao = [load_panel(w_attn_out, 0, 4, f0, 256) + (0,)]
        Wmo = [load_panel(w_mem_out, 0, 8, f0, 256) + (0,)]
        for fi in range(2):
            f = fp * 2 + fi
            for (c0, N) in TOKE:
                sgs = []
                for gi in range(3):
                    bk, bkk = proj_fm(Wg[gi], fi * 128, lambda kc: hT[:, kc, c0:c0 + N], ['hT'], N)
                    sg, sgk = nxt('sg', sg_l)
                    B.op('act', lambda e, sg=sg, bk=bk: e.activation(out=sg[:, 0:N], in_=bk[:, 0:N], func=AF.Sigmoid), R=[bkk], W=[sgk])
                    sgs.append((sg, sgk))
                brs = [(Wco, lambda kc: c_act[:, kc, c0:c0 + N], ['c_act'], 8),
                       (Wao, lambda kc: o_attn[:, kc, c0:c0 + N], ['o_attn'], 4),
                       (Wmo, lambda kc: o_mem[:, kc, c0:c0 + N], ['o_mem'], 8)]
                tts = []
                for bi, (Wp, rf, rk, nk) in enumerate(brs):
                    bk, bkk = proj_fm(Wp, fi * 128, rf, rk, N, nk_total=nk)
                    tt, ttk = nxt('tt', tt_l)
                    B.op('dve', lambda e, tt=tt, bk=bk, sg=sgs[bi][0]: e.tensor_tensor(out=tt[:, 0:N], in0=bk[:, 0:N], in1=sg[:, 0:N], op=ALU.mult),
                         R=[bkk, sgs[bi][1]], W=[ttk])
                    tts.append((tt, ttk))
                B.op('dve', lambda e: e.tensor_tensor(out=tts[0][0][:, 0:N], in0=tts[0][0][:, 0:N], in1=tts[1][0][:, 0:N], op=ALU.add),
                     R=[tts[0][1], tts[1][1]], W=[tts[0][1]])
                B.op('dve', lambda e, f=f: e.tensor_tensor(out=m_t[:, f, c0:c0 + N], in0=tts[0][0][:, 0:N], in1=tts[2][0][:, 0:N], op=ALU.add),
                     R=[tts[0][1], tts[2][1]], W=['m_t'])
    B.barrier()
    _ck(23)
    x2 = al("x2", [128, 9, D], F32, at=0)
    set_ptr(72)
    new_ring(None, 12, "e2")
    for t in range(8):
        B.dma('sp', x2[:, t, :], x_main[t * 128:(t + 1) * 128, :], W=['x2_%d' % t])
    B.dma('sp', x2[0:34, 8, :], x_misc[:, :], W=['x2_8'])
    for cb in range(4):
        Wo = load_w16(w_o, cb * 512)
        for ti, (t0, M) in enumerate(tiles9):
            bk, bkk = nbank()
            proj_tm(lambda kc, t0=t0, M=M: m_t[:, kc, t0:t0 + M], M, Wo, bk, bkk, ['m_t'])
            B.op('dve', lambda e, bk=bk, ti=ti, M=M, cb=cb: e.tensor_tensor(out=x2[0:M, ti, cb * 512:(cb + 1) * 512], in0=bk[0:M, :],
                                                                         in1=x2[0:M, ti, cb * 512:(cb + 1) * 512], op=ALU.add),
                 R=[bkk, 'x2_%d' % ti], W=['x2_%d' % ti])
    B.barrier()
    _ck(24)

    st5 = ExitStack()
    set_ptr(72)
    for g, (d, W) in enumerate(GROUPS):
        for b in range(4):
            if 'c' in SK: break
            B.dma('act', o_swk[g][b, 0:W - 8, :].rearrange("(a r) c -> a (r c)", r=8),
                  st_wk[g][b, 8:W, :].rearrange("(a r) c -> a (r c)", r=8), sem='copy')
            B.dma('act', o_swv[g][b, 0:W - 8, :].rearrange("(a r) c -> a (r c)", r=8),
                  st_wv[g][b, 8:W, :].rearrange("(a r) c -> a (r c)", r=8), sem='copy')
    for b in range(4):
        B.dma('act', o_sconv[b, 0:22, :], st_conv[b, 8:30, :], sem='copy')

    new_ring(st5, 12, "f")
    h2T = al("h2T", [128, KC, NT], BF16)
    xs_full = al("xs0", [128, 2112], F32)
    xs_l = [xs_full[:, 0:D]]
    ph['xt_i'] = 0
    prep_bufs = (xs_l, [(psb[6], 'ps6'), (psb[7], 'ps7')])
    esc = al("esc", [128, 1], F32)
    wf = al("wf", [128, 88, 3], F32)
    bf_ = al("bf_", [128, 88], F32)
    sfc = al("sfc", [128, 88, 8], F32)
    uptail = al("uptail", [128, 88, 10], F32)
    ldf = al("ldf", [88, 128], F32)
    upraw = al("upraw", [128, 1026], F32)
    ups = al("ups", [128, 40], F32)
    acc_g = al("acc_g", [128, 1056], F32)
    acc_v = al("acc_v", [128, 1056], F32)
    act_t = al("act_t", [128, 4, 1056], BF16)
    act_alt = xs_full[:, 0:2112].bitcast(BF16).rearrange("p (a b) -> p a b", a=4)
    act_l = [act_t, act_alt]
    stg = xs_l[0]
    B.op('dve', lambda e: e.memset(esc[:, :], 1.0), W=['esc'])
    B.op('dve', lambda e: e.tensor_copy(out=esc[32:34, :], in_=eflag[32:34, :]), R=['eflag'], W=['esc'])
    for k3 in range(3):
        pass
    ldw = acc_g
    for part in range(11):
        B.dma('sp', ldw[0:3, 0:1024], w_ffn_dw[:, part * 1024:(part + 1) * 1024], W=['ldw'])
        bk, bkk = nbank()
        for j in range(8):
            B.op('pe', lambda e, j=j, bk=bk: e.transpose(out=bk[:, j * 3:(j + 1) * 3], in_=ldw[0:3, j * 128:(j + 1) * 128], identity=ident_f[0:3, 0:3]),
                 R=['ldw', 'ident_f'], W=[bkk], inc=(j == 7))
        B.op('act', lambda e, bk=bk, part=part: e.activation(out=wf[:, part * 8:(part + 1) * 8, :], in_=bk[:, 0:24].rearrange("p (j k) -> p j k", j=8), func=AF.Copy),
             R=[bkk], W=['wf'])
    B.dma('sp', ldf[:, :], b_ffn_dw.rearrange("(j p) -> j p", p=128), W=['ldf'])
    bk, bkk = nbank()
    B.op('pe', lambda e: e.transpose(out=bk[:, 0:88], in_=ldf[0:88, :], identity=ident_f[0:88, 0:88]), R=['ldf', 'ident_f'], W=[bkk])
    B.op('act', lambda e: e.activation(out=bf_[:, :], in_=bk[:, 0:88], func=AF.Copy), R=[bkk], W=['bf_'])
    for part in range(11):
        B.dma('sp', ldw[0:8, 0:1024], st_ffn[:, part * 1024:(part + 1) * 1024], W=['ldw'])
        bk, bkk = nbank()
        for j in range(8):
            B.op('pe', lambda e, j=j, bk=bk: e.transpose(out=bk[:, j * 8:(j + 1) * 8], in_=ldw[0:8, j * 128:(j + 1) * 128], identity=ident_f[0:8, 0:8]),
                 R=['ldw', 'ident_f'], W=[bkk], inc=(j == 7))
        B.op('act', lambda e, bk=bk, part=part: e.activation(out=sfc[:, part * 8:(part + 1) * 8, :], in_=bk[:, 0:64].rearrange("p (j k) -> p j k", j=8), func=AF.Copy),
             R=[bkk], W=['sfc'])
    B.barrier()
    for ti, (t0, M) in enumerate(tiles9):
        prep(prep_bufs, None, M, gT_ffn, 'gT_ffn', lambda q, t0=t0, M=M: h2T[:, q * 4:q * 4 + 4, t0:t0 + M], ['h2T'],
             from_sbuf=(x2[:, ti, :], 'x2_%d' % ti), extra_scale=(esc if ti == 8 else None))
    B.barrier()

    def up_chunk(j0, jj, Wug, Wuv, act_cur, actk):
            for iv, (Wp, acc) in enumerate([(Wug, acc_g), (Wuv, acc_v)]):
                Jp = iv * 44 + j0 + jj
                for bi, (c0, N) in enumerate(TOKE):
                    bk, bkk = proj_fm(Wp, jj * 128, lambda kc: h2T[:, kc, c0:c0 + N], ['h2T'], N)
                    m1 = min(c0 + N, 1024)
                    if m1 > c0:
                        B.op('act', lambda e, bk=bk, c0=c0, m1=m1: e.activation(out=upraw[:, 2 + c0:2 + m1], in_=bk[:, 0:m1 - c0], func=AF.Copy), R=[bkk], W=['upraw'])
                    if c0 + N > 1024:
                        so = 1024 - c0
                        B.op('act', lambda e, bk=bk, so=so: e.activation(out=ups[:, 8:40].rearrange("p (t b) -> p b t", b=4),
                                                                         in_=bk[:, so:so + 32].rearrange("p (b t) -> p b t", b=4), func=AF.Copy), R=[bkk], W=['ups'])
                        B.op('act', lambda e, bk=bk, so=so: e.activation(out=upraw[:, 0:2], in_=bk[:, so + 32:so + 34], func=AF.Copy), R=[bkk], W=['upraw'])
                B.op('act', lambda e, Jp=Jp: e.activation(out=ups[:, 0:8].rearrange("p (pos b) -> p b pos", b=4),
                                                          in_=sfc[:, Jp, :].rearrange("p (b pos) -> p b pos", b=4), func=AF.Copy), R=['sfc'], W=['ups'])
                B.op('act', lambda e, Jp=Jp: e.activation(out=uptail[:, Jp, 0:2], in_=upraw[:, 1024:1026], func=AF.Copy), R=['upraw'], W=['uptail'])
                B.op('act', lambda e, Jp=Jp: e.activation(out=uptail[:, Jp, 2:10], in_=ups[:, 32:40], func=AF.Copy), R=['ups'], W=['uptail'])
                for is_s in (0, 1):
                    ak = 'acc_g' if iv == 0 else 'acc_v'
                    if not is_s:
                        dst = acc[:, 0:1024]
                        sk = 'upraw'
                        srcs = [upraw[:, 0:1024], upraw[:, 1:1025], upraw[:, 2:1026]]
                    else:
                        dst = acc[:, 1024:1056].rearrange("p (b t) -> p t b", b=4)
                        sk = 'ups'
                        srcs = [ups[:, o_:o_ + 32].rearrange("p (t b) -> p t b", b=4) for o_ in (0, 4, 8)]
                    B.op('act', lambda e, dst=dst, srcs=srcs, Jp=Jp: e.activation(out=dst, in_=srcs[2], func=AF.Identity,
                                                                               bias=bf_[:, Jp:Jp + 1], scale=wf[:, Jp, 2:3]), R=[sk, 'wf', 'bf_'], W=[ak])
                    for tap in (1, 0):
                        B.op('dve', lambda e, dst=dst, srcs=srcs, Jp=Jp, tap=tap: e.scalar_tensor_tensor(
                            out=dst, in0=srcs[tap], scalar=wf[:, Jp, tap:tap + 1], in1=dst, op0=ALU.mult, op1=ALU.add), R=[sk, 'wf', ak], W=[ak])
            B.op('act', lambda e: e.activation(out=acc_g[:, :], in_=acc_g[:, :], func=AF.Silu), R=['acc_g'], W=['acc_g'])
            B.op('dve', lambda e, jj=jj: e.tensor_tensor(out=act_cur[:, jj, :], in0=acc_g[:, :], in1=acc_v[:, :], op=ALU.mult), R=['acc_g', 'acc_v'], W=[actk])


    def down_group(Wd_, act_cur, actk):
        for ti, (t0, M) in enumerate(tiles9):
            tc0 = t0 if ti < 8 else 1024
            MM = M if ti < 8 else 32
            for cb in range(4):
                bk, bkk = nbank()
                view, keys = Wd_[cb]
                for jj in range(4):
                    B.op('pe', lambda e, jj=jj, bk=bk, view=view: e.matmul(bk[0:MM, :], act_cur[:, jj, tc0:tc0 + MM], view[:, jj, :], start=(jj == 0), stop=(jj == 3)),
                         R=[actk] + keys, W=[bkk], inc=(jj == 3))
                B.op('dve', lambda e, bk=bk, ti=ti, cb=cb: e.tensor_tensor(out=x2[0:MM, ti, cb * 512:(cb + 1) * 512], in0=bk[0:MM, :],
                                                                        in1=x2[0:MM, ti, cb * 512:(cb + 1) * 512], op=ALU.add),
                     R=[bkk, 'x2_%d' % ti], W=['x2_%d' % ti])


    prev_d = None
    for G in range(11):
        j0 = G * 4
        Wug = load_w16(w_up, j0 * 128)
        Wuv = load_w16(w_up, DFF + j0 * 128)
        act_cur = act_l[G % 2]
        actk = 'act%d' % (G % 2)
        up_chunk(j0, 0, Wug, Wuv, act_cur, actk)
        if prev_d is not None:
            down_group(*prev_d)
        Wd = [load_panel(w_down, j0, 4, cb * 512, 512) for cb in range(4)]
        for jj in range(1, 4):
            up_chunk(j0, jj, Wug, Wuv, act_cur, actk)
        prev_d = (Wd, act_cur, actk)
    down_group(*prev_d)
    for t in range(8):
        B.dma('sp', y_main[t * 128:(t + 1) * 128, :], x2[:, t, :], R=['x2_%d' % t])
    B.dma('sp', y_misc[:, :], x2[0:34, 8, :], R=['x2_8'])
    for rnd in range(6):
        nch = 16 if rnd < 5 else 8
        banks = []
        for q in range(nch // 4):
            bk, bkk = nbank()
            for j in range(4):
                Jp = rnd * 16 + q * 4 + j
                B.op('pe', lambda e, Jp=Jp, j=j, bk=bk: e.transpose(out=bk[0:10, j * 128:(j + 1) * 128], in_=uptail[:, Jp, :], identity=ident_f[:, :]),
                     R=['uptail', 'ident_f'], W=[bkk], inc=(j == 3))
            banks.append((bk, bkk))
        for q, (bk, bkk) in enumerate(banks):
            B.op('act', lambda e, bk=bk, q=q: e.activation(out=stg[0:10, q * 512:(q + 1) * 512], in_=bk[0:10, :], func=AF.Copy),
                 R=[bkk], W=['xt0'])
        B.dma('sp', o_pffn[:, rnd * 2048:rnd * 2048 + nch * 128], stg[0:10, 0:nch * 128], R=['xt0'])
    B.barrier()


_CACHE = {}


def _consts():
    ident = np.eye(128, dtype=np.float32)
    ki = np.arange(128)[:, None]
    qi = np.arange(128)[None, :]
    mask = np.zeros((128, 256), np.float32)
    mask[:, 0:128] = np.where(ki >= qi, 0.0, NEG)
    mask[:, 128:256] = np.where(ki <= qi, 0.0, NEG)
    smask = np.full((34, 96), NEG, np.float32)
    for g, (d, W) in enumerate(GROUPS):
        ds = min(d, 8)
        QBs = 8 // ds
        for b in range(4):
            for r in range(ds):
                for q in range(QBs):
                    t = r + d * q
                    col = g * 32 + b * 8 + r * QBs + q
                    for t2 in range(8):
                        if t2 <= t and (t - t2) % d == 0:
                            smask[b * 8 + t2, col] = 0.0
    return ident, mask, smask


def _hbias(c):
    P0 = 1024 * c
    hb = np.zeros((128, NHB), np.float32)
    i = np.arange(128)
    for (g, r, kind), col in HT_IDX.items():
        d, W = GROUPS[g]
        pos = P0 - 128 * d * (2 if kind == 1 else 1) + r + d * i
        hb[:, col] = np.where(pos >= 0, 0.0, NEG)
        if kind in (1, 2) and c == 0:
            hb[:, col] = 0.0
    return hb


def kernel(**inp):
    f = lambda a: np.ascontiguousarray(np.asarray(a, dtype=np.float32))
    if 'B' not in _CACHE:
        _CACHE['B'] = build_program()
    B = _CACHE['B']
    ident, mask, smask = _consts()
    xp = f(inp['x_prompt']); xs = f(inp['x_sample'])
    shared = {
        'c_ident': ident, 'c_mask': mask, 'c_smask': smask,
        'g_mix': f(inp['g_mix'][0]), 'g_ffn': f(inp['g_ffn'][0]), 'g_mem': f(inp['g_mem'][0]),
        'w_in': f(inp['w_in'][0]), 'w_dw': f(inp['w_dw'][0]), 'b_dw': f(inp['b_dw'][0]), 'g_cln': f(inp['g_cln'][0]),
        'b_cln': f(inp['b_cln'][0]), 'w_conv_out': f(inp['w_conv_out'][0]), 'g_q': f(inp['g_q'][0]), 'g_k': f(inp['g_k'][0]),
        'w_attn_out': f(inp['w_attn_out'][0]), 'w_mem_k': f(inp['w_mem_k'][0]), 'w_mem_v': f(inp['w_mem_v'][0]),
        'g_mq': f(inp['g_mq'][0]), 'g_mk': f(inp['g_mk'][0]), 'w_mem_out': f(inp['w_mem_out'][0]), 'w_o': f(inp['w_o'][0]),
        'w_up': f(inp['w_up'][0]), 'w_ffn_dw': f(inp['w_ffn_dw'][0]), 'b_ffn_dw': f(inp['b_ffn_dw'][0]), 'w_down': f(inp['w_down'][0]),
    }
    used = set(B.dram.keys())
    in_maps = []
    for core in range(8):
        bp, c = core // 4, core % 4
        P0 = 1024 * c
        m = dict(shared)
        m['x_main'] = xp[bp, P0:P0 + 1024]
        xh = np.zeros((4096, D), np.float32)
        lo = max(0, P0 - 4096)
        if P0 > 0:
            xh[4096 - (P0 - lo):] = xp[bp, lo:P0]
        m['x_halo'] = np.ascontiguousarray(xh[2048:])
        m['x_far'] = np.ascontiguousarray(np.concatenate([xh[14:2048:16], xh[15:2048:16]], axis=0))
        m['x_misc'] = np.ascontiguousarray(np.concatenate([xs[4 * core:4 * core + 4].reshape(32, D), xh[4094:4096]], axis=0))
        m['hbias'] = _hbias(c)
        m['eflag'] = np.full((128, 1), 1.0 if c > 0 else 0.0, np.float32)
        m['mem_p'] = f(inp['mem_prompt'][bp])
        sl = slice(4 * core, 4 * core + 4)
        m['cmk'] = f(inp['cache_mem_k'][0, sl]).reshape(4, 256, 1024)
        m['cmv'] = f(inp['cache_mem_v'][0, sl]).reshape(4, 256, 1024)
        m['st_conv'] = f(inp['state_conv'][0, sl])
        m['st_ffn'] = f(inp['state_ffn_conv'][0, sl]).reshape(8, 2 * DFF)
        swk = [inp['state_win1_k'], inp['state_win2_k'], inp['state_win3_k']]
        swv = [inp['state_win1_v'], inp['state_win2_v'], inp['state_win3_v']]
        for g, (d, W) in enumerate(GROUPS):
            m['st_wk%d' % g] = f(swk[g][0, sl]).reshape(4, W, 512)
            m['st_wv%d' % g] = f(swv[g][0, sl]).reshape(4, W, 512)
        in_maps.append({k: v for k, v in m.items() if k in used})
    res = run_bass_kernel_spmd(B.nc, in_maps, core_ids=list(range(8)))
    R = res.results
    y_p = np.zeros((2, 4096, D), np.float32)
    y_s = np.zeros((32, 8, D), np.float32)
    for core in range(8):
        bp, c = core // 4, core % 4
        y_p[bp, 1024 * c:1024 * (c + 1)] = R[core]['y_main']
        y_s[4 * core:4 * core + 4] = R[core]['y_misc'][0:32].reshape(4, 8, D)
    last = [3, 7]
    p_conv = np.stack([R[k]['o_pconv'][2:32] for k in last])[None]
    pk = []; pv = []
    for g, (d, W) in enumerate(GROUPS):
        if g < 2:
            kk = np.stack([R[k]['o_k%d' % g] for k in last]); vv = np.stack([R[k]['o_v%d' % g] for k in last])
        else:
            kk = np.stack([np.concatenate([R[k - 1]['o_k2'], R[k]['o_k2']], 0) for k in last])
            vv = np.stack([np.concatenate([R[k - 1]['o_v2'], R[k]['o_v2']], 0) for k in last])
        pk.append(kk.reshape(1, 2, W, 4, 128)); pv.append(vv.reshape(1, 2, W, 4, 128))
    p_ffn = np.stack([R[k]['o_pffn'][0:2] for k in last])[None]
    p_mk = np.stack([R[k]['o_pmk'] for k in last]).reshape(1, 2, 256, 4, 256)
    p_mv = np.stack([R[k]['o_pmv'] for k in last]).reshape(1, 2, 256, 4, 256)
    s_conv = np.concatenate([R[k]['o_sconv'] for k in range(8)], 0)[None]
    sk = []; sv = []
    for g, (d, W) in enumerate(GROUPS):
        sk.append(np.concatenate([R[k]['o_swk%d' % g] for k in range(8)], 0).reshape(1, 32, W, 4, 128))
        sv.append(np.concatenate([R[k]['o_swv%d' % g] for k in range(8)], 0).reshape(1, 32, W, 4, 128))
    s_ffn = np.concatenate([R[k]['o_pffn'][2:10].reshape(2, 4, 2 * DFF).transpose(1, 0, 2) for k in range(8)], 0)[None]
    return (y_p, y_s, p_conv, pk[0], pv[0], pk[1], pv[1], pk[2], pv[2], p_ffn, p_mk, p_mv,
            s_conv, sk[0], sv[0], sk[1], sv[1], sk[2], sv[2], s_ffn)
```

```python
import numpy as np
from contextlib import ExitStack
import concourse.bass as bass
import concourse.mybir as mybir
from concourse.bass_utils import run_bass_kernel_spmd

F32 = mybir.dt.float32
BF16 = mybir.dt.bfloat16
AF = mybir.ActivationFunctionType
ALU = mybir.AluOpType
AX = mybir.AxisListType

NEG = -30000.0
EPS = 1e-6
D = 2048
KC = 16
NT = 1058
TOKB = [(0, 512), (512, 512), (1024, 34)]
TOKE = [(0, 352), (352, 354), (706, 352)]
GROUPS = [(1, 128), (4, 512), (16, 2048)]
ERES = {1: [0], 4: [2, 3], 16: [14, 15]}
C_A, C_B, C_Q, C_K, C_V, C_QM, C_GC, C_GA, C_GM = 0, 1024, 2048, 3584, 5120, 6656, 7680, 9728, 11776
DFF = 5632
RING_UNITS = 12
UNIT = 2048


def halo_tiles():
    out = []
    for g, (d, W) in enumerate(GROUPS):
        for r in range(d):
            if r in ERES[d]:
                out.append((g, r, 1))
                out.append((g, r, 2))
            out.append((g, r, 0))
    return out


HT_LIST = halo_tiles()
HT_IDX = {t: i for i, t in enumerate(HT_LIST)}
NHB = len(HT_LIST) + 1


class Builder:
    def __init__(self):
        self.nc = bass.Bass("TRN2", target_bir_lowering=False)
        nc = self.nc
        self.root = ExitStack()
        self.eng = {'pe': nc.tensor, 'act': nc.scalar, 'dve': nc.vector, 'pool': nc.gpsimd, 'sp': nc.sync}
        self.esem = {}
        self.ecount = {}
        for e in self.eng:
            self.esem[e] = self.root.enter_context(nc.semaphore("e_" + e))
            self.ecount[e] = 0
        self.waited = {e: {} for e in self.eng}
        self.last_w = {}
        self.readers = {}
        self.dsems = {}
        self.dram = {}
        self.ring_pos = 0
        self.n_inst = 0

    def din(self, name, shape, dtype=F32):
        t = self.nc.dram_tensor(name, list(shape), dtype, kind="ExternalInput")
        self.dram[name] = t.ap()
        return self.dram[name]

    def dout(self, name, shape, dtype=F32):
        t = self.nc.dram_tensor(name, list(shape), dtype, kind="ExternalOutput")
        self.dram[name] = t.ap()
        return self.dram[name]

    def sb(self, stack, name, shape, dtype):
        return stack.enter_context(self.nc.sbuf_tensor(name, list(shape), dtype))

    def ps(self, stack, name, shape, dtype):
        return stack.enter_context(self.nc.psum_tensor(name, list(shape), dtype))

    def _deps(self, reads, writes):
        toks = []
        for k in list(reads) + list(writes):
            if k in self.last_w:
                toks.append(self.last_w[k])
        for k in writes:
            toks.extend(self.readers.get(k, ()))
        return toks

    def _wait(self, e, toks):
        w = self.waited[e]
        need = {}
        for (sem, val, sid) in toks:
            if w.get(sid, 0) >= val:
                continue
            if sid == 'e_' + e and (e == 'pe' or val > self.ecount[e]):
                continue
            if need.get(sid, (None, 0))[1] < val:
                need[sid] = (sem, val)
        for sid, (sem, val) in need.items():
            self.eng[e].wait_ge(sem, val)
            w[sid] = val

    def _record(self, tok, reads, writes):
        for k in writes:
            self.last_w[k] = tok
            self.readers[k] = []
        for k in reads:
            self.readers.setdefault(k, []).append(tok)
            if len(self.readers[k]) > 24:
                best = {}
                for t in self.readers[k]:
                    if t[2] not in best or best[t[2]][1] < t[1]:
                        best[t[2]] = t
                self.readers[k] = list(best.values())

    def op(self, e, fn, R=(), W=(), inc=True):
        self._wait(e, self._deps(R, W))
        ins = fn(self.eng[e])
        self.n_inst += 1
        if inc:
            ins.then_inc(self.esem[e], 1)
            self.ecount[e] += 1
            tok = (self.esem[e], self.ecount[e], 'e_' + e)
        else:
            tok = (self.esem[e], self.ecount[e] + 1, 'e_' + e)
        self._record(tok, R, W)
        return ins

    def dsem(self, name):
        if name not in self.dsems:
            self.dsems[name] = [self.root.enter_context(self.nc.semaphore("d_" + name)), 0]
        return self.dsems[name]

    def dma(self, q, out, in_, R=(), W=(), sem=None, **kw):
        self._wait(q, self._deps(R, W))
        if sem is None:
            sem = (list(W) + list(R))[0]
        s = self.dsem(sem)
        ins = self.eng[q].dma_start(out=out, in_=in_, **kw)
        ins.then_inc(s[0], 16)
        s[1] += 16
        self.n_inst += 1
        tok = (s[0], s[1], 'd_' + sem)
        self._record(tok, R, W)
        return ins

    def barrier(self):
        toks = [(self.esem[e], self.ecount[e], 'e_' + e) for e in self.eng if self.ecount[e] > 0]
        toks += [(s[0], s[1], 'd_' + n) for n, s in self.dsems.items() if s[1] > 0]
        for e in self.eng:
            self._wait(e, toks)
        self.last_w.clear()
        self.readers.clear()

    def finish(self):
        toks = [(s[0], s[1], 'd_' + n) for n, s in self.dsems.items() if s[1] > 0]
        toks += [(self.esem[e], self.ecount[e], 'e_' + e) for e in self.eng if self.ecount[e] > 0]
        self._wait('sp', toks)


class _Stop(Exception):
    pass


def _ck(n):
    import os
    if int(os.environ.get('KSTOP', '999')) == n:
        raise _Stop()


def build_program():
    B = Builder()
    try:
        _build_body(B)
    except _Stop:
        pass
    B.finish()
    return B


def _build_body(B):
    nc = B.nc
    root = B.root
    x_main = B.din("x_main", [1024, D])
    x_halo = B.din("x_halo", [2048, D])
    x_far = B.din("x_far", [256, D])
    x_misc = B.din("x_misc", [34, D])
    hbias_d = B.din("hbias", [128, NHB])
    eflag_d = B.din("eflag", [128, 1])
    mem_p = B.din("mem_p", [256, D])
    cmk = B.din("cmk", [4, 256, 1024])
    cmv = B.din("cmv", [4, 256, 1024])
    st_conv = B.din("st_conv", [4, 30, 1024])
    st_ffn = B.din("st_ffn", [8, 2 * DFF])
    st_wk = [B.din("st_wk%d" % g, [4, W, 512]) for g, (d, W) in enumerate(GROUPS)]
    st_wv = [B.din("st_wv%d" % g, [4, W, 512]) for g, (d, W) in enumerate(GROUPS)]
    c_ident = B.din("c_ident", [128, 128])
    c_mask = B.din("c_mask", [128, 256])
    c_smask = B.din("c_smask", [34, 96])
    g_mix = B.din("g_mix", [D]); g_ffn = B.din("g_ffn", [D]); g_mem = B.din("g_mem", [D])
    w_in = B.din("w_in", [D, 13824])
    w_dw = B.din("w_dw", [31, 1024]); b_dw = B.din("b_dw", [1024]); g_cln = B.din("g_cln", [1024]); b_cln = B.din("b_cln", [1024])
    w_conv_out = B.din("w_conv_out", [1024, D])
    g_q = B.din("g_q", [128]); g_k = B.din("g_k", [128])
    w_attn_out = B.din("w_attn_out", [512, D])
    w_mem_k = B.din("w_mem_k", [D, 1024]); w_mem_v = B.din("w_mem_v", [D, 1024])
    g_mq = B.din("g_mq", [256]); g_mk = B.din("g_mk", [256])
    w_mem_out = B.din("w_mem_out", [1024, D])
    w_o = B.din("w_o", [D, D])
    w_up = B.din("w_up", [D, 2 * DFF]); w_ffn_dw = B.din("w_ffn_dw", [3, 2 * DFF]); b_ffn_dw = B.din("b_ffn_dw", [2 * DFF])
    w_down = B.din("w_down", [DFF, D])

    y_main = B.dout("y_main", [1024, D])
    y_misc = B.dout("y_misc", [34, D])
    o_pconv = B.dout("o_pconv", [32, 1024])
    o_k = [B.dout("o_k%d" % g, [n, 512]) for g, n in enumerate([128, 512, 1024])]
    o_v = [B.dout("o_v%d" % g, [n, 512]) for g, n in enumerate([128, 512, 1024])]
    o_pffn = B.dout("o_pffn", [10, 2 * DFF])
    o_pmk = B.dout("o_pmk", [256, 1024]); o_pmv = B.dout("o_pmv", [256, 1024])
    o_sconv = B.dout("o_sconv", [4, 30, 1024])
    o_swk = [B.dout("o_swk%d" % g, [4, W, 512]) for g, (d, W) in enumerate(GROUPS)]
    o_swv = [B.dout("o_swv%d" % g, [4, W, 512]) for g, (d, W) in enumerate(GROUPS)]

    ident_f = B.sb(root, "ident_f", [128, 128], F32)
    ident_b = B.sb(root, "ident_b", [128, 128], BF16)
    ones_b = B.sb(root, "ones_b", [128, 128], BF16)
    mask_b = B.sb(root, "mask_b", [128, 256], BF16)
    smask_b = B.sb(root, "smask_b", [34, 96], BF16)
    hbias = B.sb(root, "hbias_s", [128, NHB], F32)
    eflag = B.sb(root, "eflag_s", [128, 1], F32)
    epsc = B.sb(root, "epsc", [128, 1], F32)
    gT_mix = B.sb(root, "gT_mix", [128, KC], F32)
    gT_ffn = B.sb(root, "gT_ffn", [128, KC], F32)
    gT_mem = B.sb(root, "gT_mem", [128, KC], F32)
    gq_t = B.sb(root, "gq_t", [128, 128], F32)
    gk_t = B.sb(root, "gk_t", [128, 128], F32)
    gmq_t = B.sb(root, "gmq_t", [128, 256], F32)
    gmk_t = B.sb(root, "gmk_t", [128, 256], F32)
    stat = B.sb(root, "stat", [128, 192], F32)
    cpar = B.sb(root, "cpar", [128, 3, 8], F32)
    junk = B.sb(root, "junk", [128, 2048], BF16)
    psb = [B.ps(root, "ps%d" % i, [128, 512], F32) for i in range(8)]

    ARENA_KB = 196
    arena = B.sb(root, "arena", [128, ARENA_KB * 256], F32)
    apos = {'p': 0}

    def set_ptr(kb):
        apos['p'] = int(kb * 1024)

    def al(name, shape, dtype, at=None):
        esz = 4 if dtype == F32 else 2
        n = 1
        for d_ in shape[1:]:
            n *= d_
        nbytes = (n * esz + 31) // 32 * 32
        off = apos['p'] if at is None else int(at * 1024)
        if at is None:
            apos['p'] = off + nbytes
        assert off + nbytes <= ARENA_KB * 1024, (name, off, nbytes)
        v = arena[:, off // 4:(off + nbytes) // 4]
        if dtype != F32:
            v = v.bitcast(dtype)
        v = v[0:shape[0], 0:n]
        if len(shape) == 3:
            v = v.rearrange("p (a b) -> p a b", a=shape[1])
        return v

    ph = {'stat_i': 0}

    def stat_slot(n=4):
        i = ph['stat_i']
        ph['stat_i'] = (i + 12) % 192
        return i

    B.dma('sp', ident_f[:], c_ident[:], W=['ident_f'], sem='const')
    B.dma('pool', ident_b[:], c_ident[:], W=['ident_b'], sem='constp')
    B.dma('pool', mask_b[:], c_mask[:], W=['mask_b'], sem='constp')
    B.dma('pool', smask_b[:], c_smask[:], W=['smask_b'], sem='constp')
    B.dma('sp', hbias[:], hbias_d[:], W=['hbias'], sem='const')
    B.dma('sp', eflag[:], eflag_d[:], W=['eflag'], sem='const')
    import os
    SK = os.environ.get('KSKIP', '')
    for (t, src, key) in [(gT_mix, g_mix, 'gT_mix'), (gT_ffn, g_ffn, 'gT_ffn'), (gT_mem, g_mem, 'gT_mem')]:
        if 'g' in SK: break
        B.dma('sp', t[:], src.rearrange("(k p) -> p k", p=128), W=[key], sem='const', allow_slow_non_contiguous=True)
    for (t, src, key, n) in [(gq_t, g_q, 'gq', 128), (gk_t, g_k, 'gk', 128), (gmq_t, g_mq, 'gmq', 256), (gmk_t, g_mk, 'gmk', 256)]:
        if 'b' in SK: break
        B.dma('sp', t[:], src.partition_broadcast(128), W=[key], sem='const')
    B.op('dve', lambda e: e.memset(ones_b[:], 1.0), W=['ones_b'])
    B.op('dve', lambda e: e.memset(epsc[:], EPS), W=['epsc'])

    _ck(1)
    def new_ring(stack, units, tag):
        ph['ring'] = al("ring_" + tag, [128, units * UNIT], BF16)
        ph['ring_units'] = units
        B.ring_pos = 0

    def ring_alloc(nunits):
        p = B.ring_pos
        if p + nunits > ph['ring_units']:
            p = 0
        B.ring_pos = p + nunits
        return p, ['ring%d' % u for u in range(p, p + nunits)]

    def load_panel(Wd, k0, nk, c0, ncol):
        nun = -(-(nk * ncol) // UNIT)
        p, keys = ring_alloc(nun)
        view = ph['ring'][:, p * UNIT: p * UNIT + nk * ncol].rearrange("p (k n) -> p k n", k=nk)
        src = Wd[k0 * 128:(k0 + nk) * 128, c0:c0 + ncol].rearrange("(k p) n -> p k n", p=128)
        B.dma('pool', view, src, W=keys, sem=keys[0])
        return view, keys

    def load_w16(Wd, c0, ncol=512):
        return [load_panel(Wd, 0, 8, c0, ncol) + (0,), load_panel(Wd, 8, 8, c0, ncol) + (8,)]

    def prepA(stack_bufs, src, rows, from_sbuf=None, extra_scale=None):
        xt_list, pst = stack_bufs
        slot = ph.get('xt_i', 0)
        ph['xt_i'] = (slot + 1) % len(xt_list)
        xt = xt_list[slot]
        xk = 'xt%d' % slot
        if from_sbuf is None:
            B.dma('sp', xt[0:rows, :], src, W=[xk])
            xin = xt
            rk = [xk]
        else:
            xin, rk0 = from_sbuf
            rk = [rk0]
        si = stat_slot(4)
        sk = 'stat%d' % si
        B.op('act', lambda e: e.activation(out=junk[0:rows, :], in_=xin[0:rows, :], func=AF.Square,
                                           accum_out=stat[0:rows, si:si + 1]), R=rk, W=[sk])
        B.op('act', lambda e: e.activation(out=stat[0:rows, si + 1:si + 2], in_=stat[0:rows, si:si + 1], func=AF.Ln,
                                           bias=epsc[0:rows, :], scale=1.0 / D), R=[sk, 'epsc'], W=[sk])
        B.op('act', lambda e: e.activation(out=stat[0:rows, si + 2:si + 3], in_=stat[0:rows, si + 1:si + 2], func=AF.Exp,
                                           scale=-0.5), R=[sk], W=[sk])
        if extra_scale is not None:
            B.op('dve', lambda e: e.tensor_tensor(out=stat[0:rows, si + 2:si + 3], in0=stat[0:rows, si + 2:si + 3],
                                                  in1=extra_scale[0:rows, :], op=ALU.mult), R=[sk, 'esc'], W=[sk])
        B.op('dve', lambda e: e.tensor_scalar(out=xt[0:rows, :], in0=xin[0:rows, :], scalar1=stat[0:rows, si + 2:si + 3],
                                              scalar2=None, op0=ALU.mult), R=rk + [sk], W=[xk])
        return (xt, xk, rows, pst)

    def prepT(pa, gT, gkey, dst_fn, dst_keys):
        xt, xk, rows, pst = pa
        for q in range(4):
            pi = ph.get('pst_i', 0)
            ph['pst_i'] = (pi + 1) % len(pst)
            bank, bkey = pst[pi]
            for j in range(4):
                kc = q * 4 + j
                B.op('pe', lambda e, kc=kc, j=j: e.transpose(out=bank[:, j * 128:j * 128 + rows],
                                                              in_=xt[0:rows, kc * 128:(kc + 1) * 128],
                                                              identity=ident_f[0:rows, 0:rows]),
                     R=[xk, 'ident_f'], W=[bkey], inc=(j == 3))
            B.op('dve', lambda e, q=q: e.tensor_tensor(
                out=dst_fn(q),
                in0=bank[:, :].rearrange("p (j t) -> p j t", j=4)[:, :, 0:rows],
                in1=gT[:, q * 4:q * 4 + 4].unsqueeze(2).broadcast_to([128, 4, rows]), op=ALU.mult),
                R=[bkey, gkey], W=dst_keys)

    def prep(stack_bufs, src, rows, gT, gkey, dst_fn, dst_keys, from_sbuf=None, extra_scale=None):
        pa = prepA(stack_bufs, src, rows, from_sbuf=from_sbuf, extra_scale=extra_scale)
        prepT(pa, gT, gkey, dst_fn, dst_keys)

    def proj_tm(lhs_fn, M, panels, bank, bkey, rkeys, ncol=512, col_off=0):
        n = 0
        for (view, keys, kb) in panels:
            for k in range(8):
                kc = kb + k
                B.op('pe', lambda e, view=view, k=k, kc=kc, n=n: e.matmul(
                    bank[0:M, col_off:col_off + ncol], lhs_fn(kc), view[:, k, 0:ncol], start=(n == 0), stop=(n == 15)),
                    R=rkeys + keys, W=[bkey], inc=(n == 15))
                n += 1

    def head_norm(bank, bkey, M, nh, hd, g_tile, gkey, kf, kfk, out_f32, of_keys, out_bf, ob_keys):
        w = nh * hd
        B.op('act', lambda e: e.activation(out=kf[0:M, 0:w], in_=bank[0:M, 0:w], func=AF.Copy), R=[bkey], W=[kfk])
        si = stat_slot(12)
        sk = 'stat%d' % si
        sqs = ph['sqs']
        B.op('act', lambda e: e.activation(out=sqs[0:M, 0:w], in_=kf[0:M, 0:w], func=AF.Square), R=[kfk], W=['sqs'])
        B.op('dve', lambda e: e.tensor_reduce(out=stat[0:M, si:si + nh], in_=sqs[0:M, 0:w].rearrange("p (h d) -> p h d", h=nh),
                                              axis=AX.X, op=ALU.add), R=['sqs'], W=[sk])
        B.op('act', lambda e: e.activation(out=stat[0:M, si + 4:si + 4 + nh], in_=stat[0:M, si:si + nh], func=AF.Ln,
                                           bias=epsc[0:M, :], scale=1.0 / hd), R=[sk, 'epsc'], W=[sk])
        B.op('act', lambda e: e.activation(out=stat[0:M, si + 8:si + 8 + nh], in_=stat[0:M, si + 4:si + 4 + nh], func=AF.Exp,
                                           scale=-0.5), R=[sk], W=[sk])
        dst = out_f32 if out_f32 is not None else out_bf
        dk = of_keys if out_f32 is not None else ob_keys
        for h in range(nh):
            B.op('dve', lambda e, h=h: e.scalar_tensor_tensor(
                out=dst[0:M, h * hd:(h + 1) * hd], in0=kf[0:M, h * hd:(h + 1) * hd],
                scalar=stat[0:M, si + 8 + h:si + 9 + h], in1=g_tile[0:M, 0:hd], op0=ALU.mult, op1=ALU.mult),
                R=[kfk, sk, gkey], W=dk)
        if out_f32 is not None and out_bf is not None:
            B.op('act', lambda e: e.activation(out=out_bf[0:M, 0:w], in_=out_f32[0:M, 0:w], func=AF.Copy), R=of_keys, W=ob_keys)

    st_mix = ExitStack()
    st1 = ExitStack()
    hT = al("hT", [128, KC, NT], BF16, at=0)
    o_attn = al("o_attn", [128, 4, NT], BF16, at=34)
    hT_h1 = al("hT_h1", [128, KC, 32], BF16, at=42.5)
    c_act = al("c_act", [128, 8, NT], BF16, at=43.5)
    o_mem = al("o_mem", [128, 8, NT], BF16, at=60.5)
    set_ptr(43.5)
    new_ring(st1, 12, "a")
    xt_list = [al("xt%d" % i, [128, D], F32) for i in range(2)]
    pst = [(psb[6], 'ps6'), (psb[7], 'ps7')]
    prep_bufs = (xt_list, pst)
    hTh = [al("hTh%d" % i, [128, KC, 128], BF16) for i in range(2)]
    acc_O = al("acc_O", [128, 4, NT], F32)
    acc_D = al("acc_D", [128, 4, NT], F32)
    kf_l = [al("kf%d" % i, [128, 512], F32) for i in range(2)]
    ko_l = [al("ko%d" % i, [128, 512], F32) for i in range(3)]
    vo_l = [al("vo%d" % i, [128, 512], F32) for i in range(3)]
    sqs = al("sqs", [128, 512], F32)
    qb_l = [al("qb%d" % i, [128, 512], BF16) for i in range(2)]
    kb_l = [al("kb%d" % i, [128, 512], BF16) for i in range(2)]
    kT_l = [al("kT%d" % i, [128, 4, 128], BF16) for i in range(3)]
    vS_l = [al("vS%d" % i, [128, 512], BF16) for i in range(4)]
    qT_l = [al("qT%d" % i, [128, 4, 128], BF16) for i in range(2)]
    pA_l = [al("pA%d" % i, [128, 512], BF16) for i in range(2)]
    pB_l = [al("pB%d" % i, [128, 512], BF16) for i in range(2)]
    kTs = al("kTs", [128, 4, 34], BF16)
    qTs = al("qTs", [128, 4, 34], BF16)
    vSs = al("vSs", [34, 512], BF16)
    pBs = al("pBs", [34, 4, 32], BF16)
    ksb_l = [al("ksb%d" % i, [128, 512], BF16) for i in range(3)]
    vsb_l = [al("vsb%d" % i, [128, 512], BF16) for i in range(3)]
    kTA_l = [al("kTA%d" % i, [128, 4, 128], BF16) for i in range(2)]
    pAs_l = [al("pAs%d" % i, [128, 4, 8], BF16) for i in range(2)]
    rot = {}
    ph['sqs'] = sqs

    def nxt(name, lst):
        i = rot.get(name, 0)
        rot[name] = (i + 1) % len(lst)
        return lst[i], '%s%d' % (name, i)

    PQ, PK, PV, PT, PSA, PSB, PO, PD = range(8)
    pT_bf = psb[PT][:, :].bitcast(BF16)

    B.barrier()
    a_tiles = [(x_misc[:, :], 34, 1024)] + [(x_main[t * 128:(t + 1) * 128, :], 128, t * 128) for t in range(8)]
    pa_cur = prepA(prep_bufs, a_tiles[0][0], a_tiles[0][1])
    for ti_, (src_, rows_, c0_) in enumerate(a_tiles):
        pa_nxt = prepA(prep_bufs, a_tiles[ti_ + 1][0], a_tiles[ti_ + 1][1]) if ti_ + 1 < len(a_tiles) else None
        prepT(pa_cur, gT_mix, 'gT_mix', lambda q, c0_=c0_, rows_=rows_: hT[:, q * 4:q * 4 + 4, c0_:c0_ + rows_], ['hT'])
        pa_cur = pa_nxt
    for i_, src_ in enumerate([b_dw, g_cln, b_cln]):
        B.dma('sp', cpar[:, i_, :], src_.rearrange("(j p) -> p j", p=128), W=['cpar'], sem='cparl', allow_slow_non_contiguous=True)
    _ck(2)

    def transpose4(src_bf, skey, M, half, dst, dkeys, perm=None):
        pk = 'ps%d' % PT
        base = half * 512
        for h in range(4):
            B.op('pe', lambda e, h=h: e.transpose(out=pT_bf[:, base + h * 128: base + h * 128 + M],
                                                  in_=src_bf[0:M, h * 128:(h + 1) * 128], identity=ident_b[0:M, 0:M]),
                 R=[skey, 'ident_b'], W=[pk], inc=(h == 3))
        src = pT_bf[:, base:base + 512].rearrange("p (h t) -> p h t", h=4)[:, :, 0:M]
        if perm is None:
            B.op('act', lambda e: e.activation(out=dst, in_=src, func=AF.Copy), R=[pk], W=dkeys)
        else:
            perm(src, pk)

    scale_att = 1.0 / np.sqrt(128.0)

    def attend_scores(QB, qc0, nq, kTA, kTAk, vA, vAk, kTB, kTBk, vB, vBk, qT, qTk, hb_col, acc_cols_fn, first, hb_colB=None):
        pA, pAk = nxt('pA', pA_l)
        pB, pBk = nxt('pB', pB_l)
        sA, sB = psb[PSA], psb[PSB]
        mA = mask_b[:, qc0:qc0 + nq].unsqueeze(1).broadcast_to([128, 4, nq])
        B.op('pe', lambda e: e.matmul(sA[:, 0:4 * nq], ident_b[:, :], mA, start=True, stop=False),
             R=['ident_b', 'mask_b'], W=['ps%d' % PSA], inc=False)
        for h in range(4):
            B.op('pe', lambda e, h=h: e.matmul(sA[:, h * nq:(h + 1) * nq], kTA[:, h, 0:128], qT[:, h, qc0:qc0 + nq],
                                               start=False, stop=(h == 3)), R=[kTAk, qTk], W=['ps%d' % PSA], inc=(h == 3))
        B.op('act', lambda e: e.activation(out=pA[:, 0:4 * nq], in_=sA[:, 0:4 * nq], func=AF.Exp,
                                           bias=hbias[:, hb_col:hb_col + 1], scale=scale_att),
             R=['ps%d' % PSA, 'hbias'], W=[pAk])
        mB = mask_b[0:QB, 128 + qc0:128 + qc0 + nq].unsqueeze(1).broadcast_to([QB, 4, nq])
        B.op('pe', lambda e: e.matmul(sB[0:QB, 0:4 * nq], ident_b[0:QB, 0:QB], mB, start=True, stop=False),
             R=['ident_b', 'mask_b'], W=['ps%d' % PSB], inc=False)
        for h in range(4):
            B.op('pe', lambda e, h=h: e.matmul(sB[0:QB, h * nq:(h + 1) * nq], kTB[:, h, 0:QB], qT[:, h, qc0:qc0 + nq],
                                               start=False, stop=(h == 3)), R=[kTBk, qTk], W=['ps%d' % PSB], inc=(h == 3))
        cb = NHB - 1 if hb_colB is None else hb_colB
        B.op('act', lambda e: e.activation(out=pB[0:QB, 0:4 * nq], in_=sB[0:QB, 0:4 * nq], func=AF.Exp,
                                           bias=hbias[0:QB, cb:cb + 1], scale=scale_att),
             R=['ps%d' % PSB, 'hbias'], W=[pBk])
        return (pA, pAk, pB, pBk)

    def attend_pv(st, QB, qc0, nq, kTA, kTAk, vA, vAk, kTB, kTBk, vB, vBk, qT, qTk, hb_col, acc_cols_fn, first, hb_colB=None):
        pA, pAk, pB, pBk = st
        oT, dn = psb[PO], psb[PD]
        for h in range(4):
            B.op('pe', lambda e, h=h: e.matmul(oT[:, h * nq:(h + 1) * nq], vA[:, h * 128:(h + 1) * 128], pA[:, h * nq:(h + 1) * nq],
                                               start=True, stop=False), R=[vAk, pAk], W=['ps%d' % PO], inc=False)
            B.op('pe', lambda e, h=h: e.matmul(oT[:, h * nq:(h + 1) * nq], vB[0:QB, h * 128:(h + 1) * 128], pB[0:QB, h * nq:(h + 1) * nq],
                                               start=False, stop=True), R=[vBk, pBk], W=['ps%d' % PO], inc=(h == 3))
        B.op('pe', lambda e: e.matmul(dn[:, 0:4 * nq], ones_b[:, :], pA[:, 0:4 * nq], start=True, stop=False),
             R=['ones_b', pAk], W=['ps%d' % PD], inc=False)
        B.op('pe', lambda e: e.matmul(dn[:, 0:4 * nq], ones_b[0:QB, :], pB[0:QB, 0:4 * nq], start=False, stop=True),
             R=['ones_b', pBk], W=['ps%d' % PD], inc=True)
        for (acc, pbank, pk, ak) in [(acc_O, oT, 'ps%d' % PO, 'acc_O'), (acc_D, dn, 'ps%d' % PD, 'acc_D')]:
            dst = acc_cols_fn(acc)
            src = pbank[:, 0:4 * nq].rearrange("p (h t) -> p h t", h=4)
            if first:
                B.op('act', lambda e, dst=dst, src=src: e.activation(out=dst, in_=src, func=AF.Copy), R=[pk], W=[ak])
            else:
                B.op('dve', lambda e, dst=dst, src=src: e.tensor_tensor(out=dst, in0=src, in1=dst, op=ALU.add), R=[pk, ak], W=[ak])


    def attend(*args, **kw):
        st = attend_scores(*args, **kw)
        attend_pv(st, *args, **kw)

    for g, (d, W) in enumerate(GROUPS):
        QB = min(128, 1024 // d)
        nmb = 1024 // (d * QB)
        Wq = load_w16(w_in, C_Q + g * 512)
        Wk = load_w16(w_in, C_K + g * 512)
        Wv = load_w16(w_in, C_V + g * 512)

        def tile_proj(lhs_fn, lkeys, M, want_q):
            proj_tm(lhs_fn, M, Wk, psb[PK], 'ps%d' % PK, lkeys)
            proj_tm(lhs_fn, M, Wv, psb[PV], 'ps%d' % PV, lkeys)
            if want_q:
                proj_tm(lhs_fn, M, Wq, psb[PQ], 'ps%d' % PQ, lkeys)

        def tile_norm(M, want_q, out_rows):
            res = {}
            kf, kfk = nxt('kf', kf_l)
            kb, kbk = nxt('kb', kb_l)
            if out_rows is not None:
                ko, kok = nxt('ko', ko_l)
                head_norm(psb[PK], 'ps%d' % PK, M, 4, 128, gk_t, 'gk', kf, kfk, ko, [kok], kb, [kbk])
                B.dma('sp', out_rows(o_k[g]), ko[0:M, :], R=[kok])
            else:
                head_norm(psb[PK], 'ps%d' % PK, M, 4, 128, gk_t, 'gk', kf, kfk, None, None, kb, [kbk])
            vS, vSk = nxt('vS', vS_l)
            B.op('act', lambda e: e.activation(out=vS[0:M, :], in_=psb[PV][0:M, :], func=AF.Copy), R=['ps%d' % PV], W=[vSk])
            if out_rows is not None:
                vo, vok = nxt('vo', vo_l)
                B.op('dve', lambda e: e.tensor_copy(out=vo[0:M, :], in_=psb[PV][0:M, :]), R=['ps%d' % PV], W=[vok])
                B.dma('sp', out_rows(o_v[g]), vo[0:M, :], R=[vok])
            res.update(kb=kb, kbk=kbk, vS=vS, vSk=vSk)
            if want_q:
                kf2, kfk2 = nxt('kf', kf_l)
                qb, qbk = nxt('qb', qb_l)
                head_norm(psb[PQ], 'ps%d' % PQ, M, 4, 128, gq_t, 'gq', kf2, kfk2, None, None, qb, [qbk])
                res.update(qb=qb, qbk=qbk)
            return res

        def tile_tr(M, want_q, res):
            kT, kTk = nxt('kT', kT_l)
            transpose4(res['kb'], res['kbk'], M, 0, kT[:, :, 0:M], [kTk])
            res.update(kT=kT, kTk=kTk)
            if want_q:
                qT, qTk = nxt('qT', qT_l)
                transpose4(res['qb'], res['qbk'], M, 1, qT[:, :, 0:M], [qTk])
                res.update(qT=qT, qTk=qTk)
            return res

        proj_tm(lambda kc: hT[:, kc, 1024:1058], 34, Wk, psb[PK], 'ps%d' % PK, ['hT'])
        proj_tm(lambda kc: hT[:, kc, 1024:1058], 34, Wv, psb[PV], 'ps%d' % PV, ['hT'])
        proj_tm(lambda kc: hT[:, kc, 1024:1058], 34, Wq, psb[PQ], 'ps%d' % PQ, ['hT'])
        kf, kfk = nxt('kf', kf_l)
        kb, kbk = nxt('kb', kb_l)
        ko, kok = nxt('ko', ko_l)
        head_norm(psb[PK], 'ps%d' % PK, 34, 4, 128, gk_t, 'gk', kf, kfk, ko, [kok], kb, [kbk])
        for b in range(4):
            B.dma('sp', o_swk[g][b, W - 8:W, :], ko[b * 8:(b + 1) * 8, :], R=[kok])
        transpose4(kb, kbk, 34, 0, kTs[:, :, :], ['kTs'])
        B.op('act', lambda e: e.activation(out=vSs[0:34, :], in_=psb[PV][0:34, :], func=AF.Copy), R=['ps%d' % PV], W=['vSs'])
        vo, vok = nxt('vo', vo_l)
        B.op('dve', lambda e: e.tensor_copy(out=vo[0:34, :], in_=psb[PV][0:34, :]), R=['ps%d' % PV], W=[vok])
        for b in range(4):
            B.dma('sp', o_swv[g][b, W - 8:W, :], vo[b * 8:(b + 1) * 8, :], R=[vok])
        kf2, kfk2 = nxt('kf', kf_l)
        qb, qbk = nxt('qb', qb_l)
        head_norm(psb[PQ], 'ps%d' % PQ, 34, 4, 128, gq_t, 'gq', kf2, kfk2, None, None, qb, [qbk])
        ds = min(d, 8)
        QBs = 8 // ds

        def qperm(src, pk):
            for h in range(4):
                B.op('act', lambda e, h=h: e.activation(
                    out=qTs[:, h, 0:32].rearrange("p (b r q) -> p b q r", b=4, r=ds, q=QBs),
                    in_=src[:, h, 0:32].rearrange("p (b q r) -> p b q r", b=4, q=QBs, r=ds), func=AF.Copy),
                    R=[pk], W=['qTs'])
        transpose4(qb, qbk, 34, 1, None, None, perm=qperm)
        sB = psb[PSB]
        for h in range(4):
            B.op('pe', lambda e, h=h: e.matmul(sB[0:32, h * 32:(h + 1) * 32], kTs[:, h, 0:32], qTs[:, h, 0:32],
                                               start=True, stop=False), R=['kTs', 'qTs'], W=['ps%d' % PSB], inc=False)
            B.op('pe', lambda e, h=h: e.matmul(sB[0:32, h * 32:(h + 1) * 32], ident_b[0:32, 0:32], smask_b[0:32, g * 32:(g + 1) * 32],
                                               start=False, stop=True), R=['ident_b', 'smask_b'], W=['ps%d' % PSB], inc=(h == 3))
        B.op('act', lambda e: e.activation(out=pBs[0:32, :, :], in_=sB[0:32, 0:128].rearrange("p (h t) -> p h t", h=4),
                                           func=AF.Exp, scale=scale_att), R=['ps%d' % PSB], W=['pBs'])
        oTs, dns = psb[PO], psb[PD]
        combos = [(b, r) for b in range(4) for r in range(ds)]

        def s_stageA(b, r):
            ksb, ksbk = nxt('ksb', ksb_l)
            vsb, vsbk = nxt('vsb', vsb_l)
            B.dma('pool', ksb[:, :], st_wk[g][b, r:W:d, :], W=[ksbk])
            B.dma('pool', vsb[:, :], st_wv[g][b, r:W:d, :], W=[vsbk])
            kTA, kTAk = nxt('kTA', kTA_l)
            transpose4(ksb, ksbk, 128, 0, kTA[:, :, :], [kTAk])
            return (kTA, kTAk, vsb, vsbk)

        def s_stageB(b, r, st):
            kTA, kTAk, vsb, vsbk = st
            off = b * 8 + r * QBs
            pAs, pAsk = nxt('pAs', pAs_l)
            sA = psb[PSA]
            for h in range(4):
                B.op('pe', lambda e, h=h: e.matmul(sA[:, h * QBs:(h + 1) * QBs], kTA[:, h, :], qTs[:, h, off:off + QBs],
                                                   start=True, stop=False), R=[kTAk, 'qTs'], W=['ps%d' % PSA], inc=False)
                B.op('pe', lambda e, h=h: e.matmul(sA[:, h * QBs:(h + 1) * QBs], ident_b[:, :], mask_b[:, 0:QBs],
                                                   start=False, stop=True), R=['ident_b', 'mask_b'], W=['ps%d' % PSA], inc=(h == 3))
            B.op('act', lambda e: e.activation(out=pAs[:, :, 0:QBs], in_=sA[:, 0:4 * QBs].rearrange("p (h t) -> p h t", h=4),
                                               func=AF.Exp, scale=scale_att), R=['ps%d' % PSA], W=[pAsk])
            for h in range(4):
                B.op('pe', lambda e, h=h: e.matmul(oTs[:, h * 32 + off:h * 32 + off + QBs], vSs[0:32, h * 128:(h + 1) * 128],
                                                   pBs[0:32, h, off:off + QBs], start=True, stop=False),
                     R=['vSs', 'pBs'], W=['ps%d' % PO], inc=False)
                B.op('pe', lambda e, h=h: e.matmul(oTs[:, h * 32 + off:h * 32 + off + QBs], vsb[:, h * 128:(h + 1) * 128],
                                                   pAs[:, h, 0:QBs], start=False, stop=True),
                     R=[vsbk, pAsk], W=['ps%d' % PO], inc=(h == 3))
            for h in range(4):
                B.op('pe', lambda e, h=h: e.matmul(dns[:, h * 32 + off:h * 32 + off + QBs], ones_b[0:32, :],
                                                   pBs[0:32, h, off:off + QBs], start=True, stop=False),
                     R=['ones_b', 'pBs'], W=['ps%d' % PD], inc=False)
                B.op('pe', lambda e, h=h: e.matmul(dns[:, h * 32 + off:h * 32 + off + QBs], ones_b[:, :],
                                                   pAs[:, h, 0:QBs], start=False, stop=True),
                     R=['ones_b', pAsk], W=['ps%d' % PD], inc=(h == 3))

        st_cur = s_stageA(*combos[0])
        for ci, (b, r) in enumerate(combos):
            st_nxt = s_stageA(*combos[ci + 1]) if ci + 1 < len(combos) else None
            s_stageB(b, r, st_cur)
            st_cur = st_nxt
        for (acc, pbank, pk, ak) in [(acc_O, oTs, 'ps%d' % PO, 'acc_O'), (acc_D, dns, 'ps%d' % PD, 'acc_D')]:
            for h in range(4):
                dst = acc[:, h, 1024:1056].rearrange("p (b q r) -> p b q r", b=4, q=QBs, r=ds)
                src = pbank[:, h * 32:(h + 1) * 32].rearrange("p (b r q) -> p b q r", b=4, r=ds, q=QBs)
                if g == 0:
                    B.op('act', lambda e, dst=dst, src=src: e.activation(out=dst, in_=src, func=AF.Copy), R=[pk], W=[ak])
                else:
                    B.op('dve', lambda e, dst=dst, src=src: e.tensor_tensor(out=dst, in0=src, in1=dst, op=ALU.add), R=[pk, ak], W=[ak])

        _ck(3 + 2 * g)
        jobs = []
        for r in range(d):
            if r in ERES[d]:
                jobs.append(dict(kind='x', r=r, first=True))
            jobs.append(dict(kind='h', r=r, first=(r not in ERES[d])))
            for bm in range(nmb):
                jobs.append(dict(kind='m', r=r, bm=bm, first=False))

        def job_prepA(job):
            r = job['r']
            if job['kind'] == 'h':
                src = x_halo[2048 - 128 * d + r:2048:d, :]
            elif d == 16:
                src = x_far[(r - 14) * 128:(r - 13) * 128, :]
            else:
                src = x_halo[2048 - 256 * d + r:2048 - 128 * d:d, :]
            return prepA(prep_bufs, src, 128)

        def job_prepT(job, pa):
            hh, hhk = nxt('hTh', hTh)
            prepT(pa, gT_mix, 'gT_mix', lambda q, hh=hh: hh[:, q * 4:q * 4 + 4, :], [hhk])
            if job['kind'] == 'h' and d == 1:
                B.op('pool', lambda e, hh=hh: e.tensor_copy(out=hT_h1[:, :, :], in_=hh[:, :, 96:128]), R=[hhk], W=['hT_h1'])
            job['hh'] = (hh, hhk)

        prev = None
        T1 = None
        T2 = None
        if jobs[0]['kind'] != 'm':
            job_prepT(jobs[0], job_prepA(jobs[0]))

        def do_tr(T):
            job_, res_, M_, wq_, t0_ = T
            r_ = job_['r']
            prev_ = None if job_['first'] else do_tr.prev
            cur = tile_tr(M_, wq_, res_)
            args = None
            if job_['kind'] != 'm' and wq_:
                nq = 2 if d == 1 else 1
                qc0 = 128 - nq
                ecol = 1056 + (0 if d == 1 else (r_ - ERES[d][0]))
                args = ((128, qc0, nq, prev_['kT'], prev_['kTk'], prev_['vS'], prev_['vSk'],
                         cur['kT'], cur['kTk'], cur['vS'], cur['vSk'], cur['qT'], cur['qTk'],
                         HT_IDX[(g, r_, 1)], (lambda acc, ecol=ecol, nq=nq: acc[:, :, ecol:ecol + nq]), g == 0),
                        dict(hb_colB=HT_IDX[(g, r_, 2)]))
            elif job_['kind'] == 'm':
                hb = HT_IDX[(g, r_, 0)] if job_['bm'] == 0 else NHB - 1
                args = ((QB, 0, QB, prev_['kT'], prev_['kTk'], prev_['vS'], prev_['vSk'],
                         cur['kT'], cur['kTk'], cur['vS'], cur['vSk'], cur['qT'], cur['qTk'],
                         hb, (lambda acc, t0_=t0_: acc[:, :, t0_:t0_ + d * QB:d]), g == 0), {})
            do_tr.prev = cur
            return args
        do_tr.prev = None

        for i, job in enumerate(jobs):
            nj = jobs[i + 1] if i + 1 < len(jobs) else None
            pa = job_prepA(nj) if (nj is not None and nj['kind'] != 'm') else None
            if T2 is not None:
                dstate = attend_scores(*T2[0], **T2[1])
            r = job['r']
            t0 = None
            if job['kind'] != 'm':
                hh, hhk = job['hh']
                wq = (job['kind'] == 'h' and r in ERES[d])
                M = 128
                tile_proj(lambda kc, hh=hh: hh[:, kc, :], [hhk], M, wq)
                res = tile_norm(M, wq, None)
            else:
                bm = job['bm']
                t0 = r + d * QB * bm
                if g == 0:
                    orow = (lambda o: o[0:128, :]) if bm == 7 else None
                elif g == 1:
                    orow = (lambda o, r=r: o[r:512:4, :]) if bm == 1 else None
                else:
                    orow = (lambda o, r=r: o[r:1024:16, :])
                wq = True
                M = QB
                tile_proj(lambda kc, t0=t0: hT[:, kc, t0:t0 + d * QB:d], ['hT'], M, True)
                res = tile_norm(M, True, orow)
            if T2 is not None:
                attend_pv(dstate, *T2[0], **T2[1])
                T2 = None
            if pa is not None:
                job_prepT(nj, pa)
            if T1 is not None:
                T2 = do_tr(T1)
            T1 = (job, res, M, wq, t0)
        if T2 is not None:
            attend(*T2[0], **T2[1])
            T2 = None
        T2 = do_tr(T1)
        if T2 is not None:
            attend(*T2[0], **T2[1])
            T2 = None
        _ck(4 + 2 * g)
    _ck(10)
    for (c0, n) in TOKB:
        for h in range(4):
            B.op('act', lambda e, h=h, c0=c0, n=n: e.activation(out=acc_D[:, h, c0:c0 + n], in_=acc_D[:, h, c0:c0 + n], func=AF.Ln),
                 R=['acc_D'], W=['acc_D'])
            B.op('act', lambda e, h=h, c0=c0, n=n: e.activation(out=acc_D[:, h, c0:c0 + n], in_=acc_D[:, h, c0:c0 + n], func=AF.Exp, scale=-1.0),
                 R=['acc_D'], W=['acc_D'])
            B.op('dve', lambda e, h=h, c0=c0, n=n: e.tensor_tensor(out=o_attn[:, h, c0:c0 + n], in0=acc_O[:, h, c0:c0 + n],
                                                                  in1=acc_D[:, h, c0:c0 + n], op=ALU.mult),
                 R=['acc_O', 'acc_D'], W=['o_attn'])
    import os
    if os.environ.get('KDEBUG'):
        dbg = B.dout("dbg_oattn", [128, 4 * NT], BF16)
        B.dma('sp', dbg[:, :], o_attn[:, :, :].rearrange("p h t -> p (h t)"), R=['o_attn'], sem='dbg')
    B.barrier()


    _ck(20)
    st2 = ExitStack()
    set_ptr(60.5)
    new_ring(st2, 8, "c")
    u_p = al("u_p", [128, 8, 1056], BF16)
    u_s = al("u_s", [128, 8, 152], BF16)
    utail = al("utail", [128, 8, 64], F32)
    wdwT = al("wdwT", [128, 8, 31], F32)
    dg_l = [al("dg%d" % i, [128, 31, 128], BF16) for i in range(2)]
    c_f = al("c_f", [128, 8, NT], F32)
    tmpA = [al("tmpA%d" % i, [128, 512], F32) for i in range(2)]
    tmpB = [al("tmpB%d" % i, [128, 512], F32) for i in range(2)]
    mu_t = al("mu_t", [128, NT], F32)
    rs_t = al("rs_t", [128, NT], F32)
    ld32 = al("ld32", [128, 1024], F32)
    ones_f = al("ones_f", [128, 128], F32)
    uo = al("uo", [64, 1024], F32)
    bank_i = {'i': 0}

    def nbank():
        i = bank_i['i']
        bank_i['i'] = (i + 1) % 8
        return psb[i], 'ps%d' % i

    B.op('dve', lambda e: e.memset(ones_f[:], 1.0), W=['ones_f'])
    B.dma('sp', ld32[0:31, :], w_dw[:, :], W=['ld32'])
    bk, bkk = nbank()
    for j in range(8):
        B.op('pe', lambda e, j=j: e.transpose(out=bk[:, j * 31:(j + 1) * 31], in_=ld32[0:31, j * 128:(j + 1) * 128],
                                              identity=ident_f[0:31, 0:31]), R=['ld32', 'ident_f'], W=[bkk], inc=(j == 7))
    B.op('act', lambda e: e.activation(out=wdwT[:, :, :], in_=bk[:, 0:248].rearrange("p (j k) -> p j k", j=8), func=AF.Copy),
         R=[bkk], W=['wdwT'])
    B.dma('sp', ld32[0:120, :], st_conv.rearrange("b p c -> (b p) c"), W=['ld32'])
    for half in range(2):
        bk, bkk = nbank()
        for jj in range(4):
            j = half * 4 + jj
            B.op('pe', lambda e, j=j, jj=jj: e.transpose(out=bk[:, jj * 120:(jj + 1) * 120], in_=ld32[0:120, j * 128:(j + 1) * 128],
                                                         identity=ident_f[0:120, 0:120]), R=['ld32', 'ident_f'], W=[bkk], inc=(jj == 3))
        for jj in range(4):
            j = half * 4 + jj
            B.op('act', lambda e, j=j, jj=jj: e.activation(
                out=u_s[:, j, 0:120].rearrange("p (pos b) -> p b pos", b=4),
                in_=bk[:, jj * 120:(jj + 1) * 120].rearrange("p (b pos) -> p b pos", b=4), func=AF.Copy), R=[bkk], W=['u_s'])

    def proj_fm(panels, jcol, rhs_fn, rkeys, N, nk_total=16):
        bk, bkk = nbank()
        n = 0
        for (view, keys, kb) in panels:
            nk = view.shape[1]
            for k in range(nk):
                B.op('pe', lambda e, view=view, k=k, kc=kb + k, n=n: e.matmul(
                    bk[:, 0:N], view[:, k, jcol:jcol + 128], rhs_fn(kc), start=(n == 0), stop=(n == nk_total - 1)),
                    R=rkeys + keys, W=[bkk], inc=(n == nk_total - 1))
                n += 1
        return bk, bkk

    tblocks = [('m0', lambda kc: hT[:, kc, 0:512], ['hT'], 512), ('m1', lambda kc: hT[:, kc, 512:1024], ['hT'], 512),
               ('s', lambda kc: hT[:, kc, 1024:1056], ['hT'], 32), ('h', lambda kc: hT_h1[:, kc, :], ['hT_h1'], 32)]
    for cbk in range(2):
        Wa = load_w16(w_in, C_A + cbk * 512)
        Wb = load_w16(w_in, C_B + cbk * 512)
        for jj in range(4):
            J = cbk * 4 + jj
            for (nm, rf, rk, N) in tblocks:
                pa, pak = proj_fm(Wa, jj * 128, rf, rk, N)
                pb, pbk = proj_fm(Wb, jj * 128, rf, rk, N)
                tA, tAk = nxt('tmpA', tmpA)
                B.op('act', lambda e, pb=pb, tA=tA, N=N: e.activation(out=tA[:, 0:N], in_=pb[:, 0:N], func=AF.Sigmoid), R=[pbk], W=[tAk])
                if nm == 'm0':
                    dst = u_p[:, J, 32:544]
                elif nm == 'm1':
                    dst = u_p[:, J, 544:1056]
                elif nm == 'h':
                    dst = u_p[:, J, 0:32]
                else:
                    dst = u_s[:, J, 120:152].rearrange("p (t b) -> p b t", b=4)
                src_a = pa[:, 0:N] if nm != 's' else pa[:, 0:32].rearrange("p (b t) -> p b t", b=4)
                src_s = tA[:, 0:N] if nm != 's' else tA[:, 0:32].rearrange("p (b t) -> p b t", b=4)
                B.op('dve', lambda e, dst=dst, src_a=src_a, src_s=src_s: e.tensor_tensor(out=dst, in0=src_a, in1=src_s, op=ALU.mult),
                     R=[pak, tAk], W=['u_p' if nm != 's' else 'u_s'])
                if nm == 'm1':
                    B.op('dve', lambda e, pa=pa, tA=tA, J=J: e.tensor_tensor(out=utail[:, J, 0:32], in0=pa[:, 480:512], in1=tA[:, 480:512], op=ALU.mult),
                         R=[pak, tAk], W=['utail'])
                if nm == 's':
                    B.op('dve', lambda e, pa=pa, tA=tA, J=J: e.tensor_tensor(out=utail[:, J, 32:64], in0=pa[:, 0:32], in1=tA[:, 0:32], op=ALU.mult),
                         R=[pak, tAk], W=['utail'])
    for half in range(2):
        bk, bkk = nbank()
        for jj in range(4):
            J = half * 4 + jj
            B.op('pe', lambda e, J=J, jj=jj: e.transpose(out=bk[0:64, jj * 128:(jj + 1) * 128], in_=utail[:, J, :], identity=ident_f[:, :]),
                 R=['utail', 'ident_f'], W=[bkk], inc=(jj == 3))
        B.op('act', lambda e, half=half, bk=bk: e.activation(out=uo[:, half * 512:(half + 1) * 512], in_=bk[0:64, :], func=AF.Copy), R=[bkk], W=['uo'])
    B.dma('sp', o_pconv[:, :], uo[0:32, :], R=['uo'])
    for b in range(4):
        B.dma('sp', o_sconv[b, 22:30, :], uo[32 + b * 8:40 + b * 8, :], R=['uo'])
    def build_dg(J):
        dgb = dg_l[J % 2]
        for k in range(31):
            B.op('dve', lambda e, k=k, J=J, dgb=dgb: e.tensor_scalar(out=dgb[:, k, :], in0=ident_b[:, :], scalar1=wdwT[:, J, k:k + 1], scalar2=None, op0=ALU.mult),
                 R=['ident_b', 'wdwT'], W=['dg%d' % (J % 2)])

    build_dg(0)
    for J in range(8):
        if J + 1 < 8:
            build_dg(J + 1)
        dg = dg_l[J % 2]
        dgk = 'dg%d' % (J % 2)
        outs = [(lambda k: u_p[:, J, 2 + k:2 + k + 512], 512, c_f[:, J, 0:512], None),
                (lambda k: u_p[:, J, 514 + k:514 + k + 512], 512, c_f[:, J, 512:1024], None),
                (lambda k: u_p[:, J, k:k + 2], 2, c_f[:, J, 1056:1058], None),
                (lambda k: u_s[:, J, k * 4:k * 4 + 32], 32, c_f[:, J, 1024:1056].rearrange("p (b t) -> p b t", b=4), 's')]
        for (rf, N, dst, kind) in outs:
            bk, bkk = nbank()
            for k in range(31):
                B.op('pe', lambda e, k=k, rf=rf, bk=bk, N=N, dg=dg: e.matmul(bk[:, 0:N], dg[:, k, :], rf(k), start=(k == 0), stop=(k == 30)),
                     R=[dgk, 'u_p', 'u_s'], W=[bkk], inc=(k == 30))
            src = bk[:, 0:N] if kind is None else bk[:, 0:32].rearrange("p (t b) -> p b t", b=4)
            B.op('act', lambda e, dst=dst, src=src, J=J: e.activation(out=dst, in_=src, func=AF.Identity, bias=cpar[:, 0, J:J + 1], scale=1.0),
                 R=[bkk, 'cpar'], W=['c_f'])
    for (c0, N) in TOKE:
        b1, b1k = nbank()
        b2, b2k = nbank()
        for J in range(8):
            tA, tAk = nxt('tmpA', tmpA)
            B.op('act', lambda e, tA=tA, J=J: e.activation(out=tA[:, 0:N], in_=c_f[:, J, c0:c0 + N], func=AF.Square), R=['c_f'], W=[tAk])
            B.op('pe', lambda e, J=J: e.matmul(b1[:, 0:N], ones_f[:, :], c_f[:, J, c0:c0 + N], start=(J == 0), stop=(J == 7)),
                 R=['ones_f', 'c_f'], W=[b1k], inc=(J == 7))
            B.op('pe', lambda e, J=J, tA=tA: e.matmul(b2[:, 0:N], ones_f[:, :], tA[:, 0:N], start=(J == 0), stop=(J == 7)),
                 R=['ones_f', tAk], W=[b2k], inc=True)
        B.op('act', lambda e: e.activation(out=mu_t[:, c0:c0 + N], in_=b1[:, 0:N], func=AF.Copy, scale=1.0 / 1024), R=[b1k], W=['mu_t'])
        tB, tBk = nxt('tmpB', tmpB)
        B.op('dve', lambda e, tB=tB: e.tensor_tensor(out=tB[:, 0:N], in0=mu_t[:, c0:c0 + N], in1=mu_t[:, c0:c0 + N], op=ALU.mult), R=['mu_t'], W=[tBk])
        B.op('dve', lambda e, tB=tB: e.scalar_tensor_tensor(out=tB[:, 0:N], in0=b2[:, 0:N], scalar=1.0 / 1024, in1=tB[:, 0:N],
                                                           op0=ALU.mult, op1=ALU.subtract), R=[b2k, tBk], W=[tBk])
        B.op('act', lambda e, tB=tB: e.activation(out=tB[:, 0:N], in_=tB[:, 0:N], func=AF.Ln, bias=epsc[:, :], scale=1.0), R=[tBk, 'epsc'], W=[tBk])
        B.op('act', lambda e, tB=tB: e.activation(out=rs_t[:, c0:c0 + N], in_=tB[:, 0:N], func=AF.Exp, scale=-0.5), R=[tBk], W=['rs_t'])
    for (c0, N) in TOKE:
        for J in range(8):
            tA, tAk = nxt('tmpA', tmpA)
            B.op('dve', lambda e, tA=tA, J=J: e.tensor_tensor(out=tA[:, 0:N], in0=c_f[:, J, c0:c0 + N], in1=mu_t[:, c0:c0 + N], op=ALU.subtract),
                 R=['c_f', 'mu_t'], W=[tAk])
            B.op('dve', lambda e, tA=tA: e.tensor_tensor(out=tA[:, 0:N], in0=tA[:, 0:N], in1=rs_t[:, c0:c0 + N], op=ALU.mult), R=[tAk, 'rs_t'], W=[tAk])
            B.op('act', lambda e, tA=tA, J=J: e.activation(out=c_act[:, J, c0:c0 + N], in_=tA[:, 0:N], func=AF.Silu,
                                                           bias=cpar[:, 2, J:J + 1], scale=cpar[:, 1, J:J + 1]), R=[tAk, 'cpar'], W=['c_act'])
    B.barrier()
    _ck(21)

    st3 = ExitStack()
    set_ptr(77.5)
    new_ring(st3, 8, "d")
    xt_list = [al("xtd%d" % i, [128, D], F32) for i in range(2)]
    ph['xt_i'] = 0
    prep_bufs = (xt_list, [(psb[6], 'ps6'), (psb[7], 'ps7')])
    hmT = al("hmT", [128, KC, 256], BF16)
    kmT = al("kmT", [128, 8, 256], BF16)
    vmb = al("vmb", [128, 2, 1024], BF16)
    qmT = al("qmT", [128, 8, NT], BF16)
    kf_l = [al("kfd%d" % i, [128, 512], F32) for i in range(2)]
    ko_l = [al("kod%d" % i, [128, 512], F32) for i in range(2)]
    kb_l = [al("kbd%d" % i, [128, 512], BF16) for i in range(2)]
    sqs3 = al("sqs3", [128, 512], F32)
    ph['sqs'] = sqs3
    pm_l = [al("pm%d" % i, [128, 512], BF16) for i in range(4)]
    rd_l = [al("rd%d" % i, [128, 512], F32) for i in range(2)]
    kms = [al("kms%d" % i, [128, 2, 1024], BF16) for i in range(1)]
    vms = [al("vms%d" % i, [128, 2, 1024], BF16) for i in range(1)]
    kmTs = [al("kmTs%d" % i, [128, 8, 256], BF16) for i in range(1)]
    rot.clear()
    pT_bf3 = psb[5][:, :].bitcast(BF16)

    def transpose4b(src_bf, skey, M, half, dst, dkeys):
        pk = 'ps5'
        base = half * 512
        for h in range(4):
            B.op('pe', lambda e, h=h: e.transpose(out=pT_bf3[:, base + h * 128: base + h * 128 + M],
                                                  in_=src_bf[0:M, h * 128:(h + 1) * 128], identity=ident_b[0:M, 0:M]),
                 R=[skey, 'ident_b'], W=[pk], inc=(h == 3))
        src = pT_bf3[:, base:base + 512].rearrange("p (h t) -> p h t", h=4)[:, :, 0:M]
        B.op('act', lambda e: e.activation(out=dst, in_=src, func=AF.Copy), R=[pk], W=dkeys)

    for mt in range(2):
        prep(prep_bufs, mem_p[mt * 128:(mt + 1) * 128, :], 128, gT_mem, 'gT_mem',
             lambda q, mt=mt: hmT[:, q * 4:q * 4 + 4, mt * 128:(mt + 1) * 128], ['hmT'])
    for cb in range(2):
        Wmk = load_w16(w_mem_k, cb * 512)
        Wmv = load_w16(w_mem_v, cb * 512)
        for mt in range(2):
            proj_tm(lambda kc, mt=mt: hmT[:, kc, mt * 128:(mt + 1) * 128], 128, Wmk, psb[0], 'ps0', ['hmT'])
            proj_tm(lambda kc, mt=mt: hmT[:, kc, mt * 128:(mt + 1) * 128], 128, Wmv, psb[1], 'ps1', ['hmT'])
            kf, kfk = nxt('kfd', kf_l); ko, kok = nxt('kod', ko_l); kb, kbk = nxt('kbd', kb_l)
            head_norm(psb[0], 'ps0', 128, 2, 256, gmk_t, 'gmk', kf, kfk, ko, [kok], kb, [kbk])
            B.dma('sp', o_pmk[mt * 128:(mt + 1) * 128, cb * 512:(cb + 1) * 512], ko[:, :], R=[kok])
            transpose4b(kb, kbk, 128, 0, kmT[:, cb * 4:cb * 4 + 4, mt * 128:(mt + 1) * 128], ['kmT'])
            B.op('act', lambda e, mt=mt, cb=cb: e.activation(out=vmb[:, mt, cb * 512:(cb + 1) * 512], in_=psb[1][:, :], func=AF.Copy), R=['ps1'], W=['vmb'])
            ko2, kok2 = nxt('kod', ko_l)
            B.op('dve', lambda e, ko2=ko2: e.tensor_copy(out=ko2[:, :], in_=psb[1][:, :]), R=['ps1'], W=[kok2])
            B.dma('sp', o_pmv[mt * 128:(mt + 1) * 128, cb * 512:(cb + 1) * 512], ko2[:, :], R=[kok2])
    tiles9 = [(t * 128, 128) for t in range(8)] + [(1024, 34)]
    qdef = [None]
    for cb in range(2):
        Wqm = load_w16(w_in, C_QM + cb * 512)
        for (t0, M) in tiles9:
            proj_tm(lambda kc, t0=t0, M=M: hT[:, kc, t0:t0 + M], M, Wqm, psb[2], 'ps2', ['hT'])
            kf, kfk = nxt('kfd', kf_l); kb, kbk = nxt('kbd', kb_l)
            head_norm(psb[2], 'ps2', M, 2, 256, gmq_t, 'gmq', kf, kfk, None, None, kb, [kbk])
            if qdef[0] is not None:
                qdef[0]()
            qdef[0] = (lambda kb=kb, kbk=kbk, M=M, cb=cb, t0=t0: transpose4b(kb, kbk, M, 1, qmT[:, cb * 4:cb * 4 + 4, t0:t0 + M], ['qmT']))
    qdef[0]()
    scale_mem = 1.0 / 16.0

    mdef = [None]

    def mem_attend(kT_, kTk, v_, vk, c0, N):
        for h in range(4):
            pms = []
            for mt in range(2):
                bk, bkk = psb[mt], 'ps%d' % mt
                for dc in range(2):
                    B.op('pe', lambda e, mt=mt, dc=dc, bk=bk: e.matmul(bk[:, 0:N], kT_[:, h * 2 + dc, mt * 128:(mt + 1) * 128],
                                                                      qmT[:, h * 2 + dc, c0:c0 + N], start=(dc == 0), stop=(dc == 1)),
                         R=[kTk, 'qmT'], W=[bkk], inc=(dc == 1))
                pm, pmk = nxt('pm', pm_l)
                B.op('act', lambda e, pm=pm, bk=bk: e.activation(out=pm[:, 0:N], in_=bk[:, 0:N], func=AF.Exp, scale=scale_mem), R=[bkk], W=[pmk])
                pms.append((pm, pmk))
            B.op('pe', lambda e: e.matmul(psb[4][:, 0:N], ones_b[:, :], pms[0][0][:, 0:N], start=True, stop=False), R=['ones_b', pms[0][1]], W=['ps4'], inc=False)
            B.op('pe', lambda e: e.matmul(psb[4][:, 0:N], ones_b[:, :], pms[1][0][:, 0:N], start=False, stop=True), R=['ones_b', pms[1][1]], W=['ps4'], inc=True)
            rd, rdk = nxt('rd', rd_l)
            B.op('act', lambda e, rd=rd: e.activation(out=rd[:, 0:N], in_=psb[4][:, 0:N], func=AF.Ln), R=['ps4'], W=[rdk])
            B.op('act', lambda e, rd=rd: e.activation(out=rd[:, 0:N], in_=rd[:, 0:N], func=AF.Exp, scale=-1.0), R=[rdk], W=[rdk])

            def stage2(h=h, pms=pms, rd=rd, rdk=rdk):
                for dc in range(2):
                    bk, bkk = psb[2 + dc], 'ps%d' % (2 + dc)
                    for mt in range(2):
                        B.op('pe', lambda e, mt=mt, dc=dc, bk=bk: e.matmul(bk[:, 0:N], v_[:, mt, h * 256 + dc * 128:h * 256 + (dc + 1) * 128],
                                                                          pms[mt][0][:, 0:N], start=(mt == 0), stop=(mt == 1)),
                             R=[vk, pms[mt][1]], W=[bkk], inc=(mt == 1))
                    B.op('dve', lambda e, dc=dc, bk=bk: e.tensor_tensor(out=o_mem[:, h * 2 + dc, c0:c0 + N], in0=bk[:, 0:N], in1=rd[:, 0:N], op=ALU.mult),
                         R=[bkk, rdk], W=['o_mem'])
            if mdef[0] is not None:
                mdef[0]()
            mdef[0] = stage2
        mdef[0]()
        mdef[0] = None

    for (c0, N) in [(0, 512), (512, 512), (1056, 2)]:
        mem_attend(kmT, 'kmT', vmb, 'vmb', c0, N)
    for b in range(4):
        km, kmk = nxt('kms', kms); vm, vmk = nxt('vms', vms); kt, ktk = nxt('kmTs', kmTs)
        B.dma('pool', km[:, :, :], cmk[b].rearrange("(mt p) c -> p mt c", p=128), W=[kmk])
        B.dma('pool', vm[:, :, :], cmv[b].rearrange("(mt p) c -> p mt c", p=128), W=[vmk])
        for mt in range(2):
            for cb in range(2):
                transpose4b(km[:, mt, cb * 512:(cb + 1) * 512], kmk, 128, (mt * 2 + cb) % 2, kt[:, cb * 4:cb * 4 + 4, mt * 128:(mt + 1) * 128], [ktk])
        mem_attend(kt, ktk, vm, vmk, 1024 + b * 8, 8)
    if mdef[0] is not None:
        mdef[0]()
        mdef[0] = None
    B.barrier()
    _ck(22)

    st4 = ExitStack()
    st_x2 = ExitStack()
    set_ptr(77.5)
    m_t = al("m_t", [128, KC, NT], BF16, at=162)
    new_ring(st4, 16, "e")
    sg_l = [al("sg%d" % i, [128, 512], F32) for i in range(6)]
    tt_l = [al("tt%d" % i, [128, 512], F32) for i in range(4)]
    for fp in range(8):
        f0 = fp * 256
        Wg = [[load_panel(w_in, 0, 8, cg + f0, 256) + (0,), load_panel(w_in, 8, 8, cg + f0, 256) + (8,)] for cg in (C_GC, C_GA, C_GM)]
        Wco = [load_panel(w_conv_out, 0, 8, f0, 256) + (0,)]
        Wao = [load_panel(w_attn_out, 0, 4, f0, 256) + (0,)]
        Wmo = [load_panel(w_mem_out, 0, 8, f0, 256) + (0,)]
        for fi in range(2):
            f = fp * 2 + fi
            for (c0, N) in TOKE:
                sgs = []
                for gi in range(3):
                    bk, bkk = proj_fm(Wg[gi], fi * 128, lambda kc: hT[:, kc, c0:c0 + N], ['hT'], N)
                    sg, sgk = nxt('sg', sg_l)
                    B.op('act', lambda e, sg=sg, bk=bk: e.activation(out=sg[:, 0:N], in_=bk[:, 0:N], func=AF.Sigmoid), R=[bkk], W=[sgk])
                    sgs.append((sg, sgk))
                brs = [(Wco, lambda kc: c_act[:, kc, c0:c0 + N], ['c_act'], 8),
                       (Wao, lambda kc: o_attn[:, kc, c0:c0 + N], ['o_attn'], 4),
                       (Wmo, lambda kc: o_mem[:, kc, c0:c0 + N], ['o_mem'], 8)]
                tts = []
                for bi, (Wp, rf, rk, nk) in enumerate(brs):
                    bk, bkk = proj_fm(Wp, fi * 128, rf, rk, N, nk_total=nk)
                    tt, ttk = nxt('tt', tt_l)
                    B.op('dve', lambda e, tt=tt, bk=bk, sg=sgs[bi][0]: e.tensor_tensor(out=tt[:, 0:N], in0=bk[:, 0:N], in1=sg[:, 0:N], op=ALU.mult),
                         R=[bkk, sgs[bi][1]], W=[ttk])
                    tts.append((tt, ttk))
                B.op('dve', lambda e: e.tensor_tensor(out=tts[0][0][:, 0:N], in0=tts[0][0][:, 0:N], in1=tts[1][0][:, 0:N], op=ALU.add),
                     R=[tts[0][1], tts[1][1]], W=[tts[0][1]])
                B.op('dve', lambda e, f=f: e.tensor_tensor(out=m_t[:, f, c0:c0 + N], in0=tts[0][0][:, 0:N], in1=tts[2][0][:, 0:N], op=ALU.add),
                     R=[tts[0][1], tts[2][1]], W=['m_t'])
    B.barrier()
    _ck(23)
    x2 = al("x2", [128, 9, D], F32, at=0)
    set_ptr(72)
    new_ring(None, 12, "e2")
    for t in range(8):
        B.dma('sp', x2[:, t, :], x_main[t * 128:(t + 1) * 128, :], W=['x2_%d' % t])
    B.dma('sp', x2[0:34, 8, :], x_misc[:, :], W=['x2_8'])
    for cb in range(4):
        Wo = load_w16(w_o, cb * 512)
        for ti, (t0, M) in enumerate(tiles9):
            bk, bkk = nbank()
            proj_tm(lambda kc, t0=t0, M=M: m_t[:, kc, t0:t0 + M], M, Wo, bk, bkk, ['m_t'])
            B.op('dve', lambda e, bk=bk, ti=ti, M=M, cb=cb: e.tensor_tensor(out=x2[0:M, ti, cb * 512:(cb + 1) * 512], in0=bk[0:M, :],
                                                                         in1=x2[0:M, ti, cb * 512:(cb + 1) * 512], op=ALU.add),
                 R=[bkk, 'x2_%d' % ti], W=['x2_%d' % ti])
    B.barrier()
    _ck(24)

    st5 = ExitStack()
    set_ptr(72)
    copy_thunks = []
    for g, (d, W) in enumerate(GROUPS):
        for b in range(4):
            copy_thunks.append(lambda g=g, b=b, W=W: B.dma('act', o_swk[g][b, 0:W - 8, :].rearrange("(a r) c -> a (r c)", r=8),
                                                          st_wk[g][b, 8:W, :].rearrange("(a r) c -> a (r c)", r=8), sem='copy'))
            copy_thunks.append(lambda g=g, b=b, W=W: B.dma('act', o_swv[g][b, 0:W - 8, :].rearrange("(a r) c -> a (r c)", r=8),
                                                          st_wv[g][b, 8:W, :].rearrange("(a r) c -> a (r c)", r=8), sem='copy'))
    for b in range(4):
        copy_thunks.append(lambda b=b: B.dma('act', o_sconv[b, 0:22, :], st_conv[b, 8:30, :], sem='copy'))

    new_ring(st5, 12, "f")
    h2T = al("h2T", [128, KC, NT], BF16)
    xs_full = al("xs0", [128, 2112], F32)
    xs_l = [xs_full[:, 0:D]]
    ph['xt_i'] = 0
    prep_bufs = (xs_l, [(psb[6], 'ps6'), (psb[7], 'ps7')])
    esc = al("esc", [128, 1], F32)
    wf = al("wf", [128, 88, 3], F32)
    bf_ = al("bf_", [128, 88], F32)
    sfc = al("sfc", [128, 88, 8], F32)
    uptail = al("uptail", [128, 88, 10], F32)
    ldf = al("ldf", [88, 128], F32)
    upraw = al("upraw", [128, 1026], F32)
    ups = al("ups", [128, 40], F32)
    acc_g = al("acc_g", [128, 1056], F32)
    acc_v = al("acc_v", [128, 1056], F32)
    act_t = al("act_t", [128, 4, 1056], BF16)
    act_alt = xs_full[:, 0:2112].bitcast(BF16).rearrange("p (a b) -> p a b", a=4)
    act_l = [act_t, act_alt]
    stg = xs_l[0]
    B.op('dve', lambda e: e.memset(esc[:, :], 1.0), W=['esc'])
    B.op('dve', lambda e: e.tensor_copy(out=esc[32:34, :], in_=eflag[32:34, :]), R=['eflag'], W=['esc'])
    for k3 in range(3):
        pass
    ldw = acc_g
    for part in range(11):
        B.dma('sp', ldw[0:3, 0:1024], w_ffn_dw[:, part * 1024:(part + 1) * 1024], W=['ldw'])
        bk, bkk = nbank()
        for j in range(8):
            B.op('pe', lambda e, j=j, bk=bk: e.transpose(out=bk[:, j * 3:(j + 1) * 3], in_=ldw[0:3, j * 128:(j + 1) * 128], identity=ident_f[0:3, 0:3]),
                 R=['ldw', 'ident_f'], W=[bkk], inc=(j == 7))
        B.op('act', lambda e, bk=bk, part=part: e.activation(out=wf[:, part * 8:(part + 1) * 8, :], in_=bk[:, 0:24].rearrange("p (j k) -> p j k", j=8), func=AF.Copy),
             R=[bkk], W=['wf'])
    B.dma('sp', ldf[:, :], b_ffn_dw.rearrange("(j p) -> j p", p=128), W=['ldf'])
    bk, bkk = nbank()
    B.op('pe', lambda e: e.transpose(out=bk[:, 0:88], in_=ldf[0:88, :], identity=ident_f[0:88, 0:88]), R=['ldf', 'ident_f'], W=[bkk])
    B.op('act', lambda e: e.activation(out=bf_[:, :], in_=bk[:, 0:88], func=AF.Copy), R=[bkk], W=['bf_'])
    for part in range(11):
        B.dma('sp', ldw[0:8, 0:1024], st_ffn[:, part * 1024:(part + 1) * 1024], W=['ldw'])
        bk, bkk = nbank()
        for j in range(8):
            B.op('pe', lambda e, j=j, bk=bk: e.transpose(out=bk[:, j * 8:(j + 1) * 8], in_=ldw[0:8, j * 128:(j + 1) * 128], identity=ident_f[0:8, 0:8]),
                 R=['ldw', 'ident_f'], W=[bkk], inc=(j == 7))
        B.op('act', lambda e, bk=bk, part=part: e.activation(out=sfc[:, part * 8:(part + 1) * 8, :], in_=bk[:, 0:64].rearrange("p (j k) -> p j k", j=8), func=AF.Copy),
             R=[bkk], W=['sfc'])
    B.barrier()
    for ti, (t0, M) in enumerate(tiles9):
        prep(prep_bufs, None, M, gT_ffn, 'gT_ffn', lambda q, t0=t0, M=M: h2T[:, q * 4:q * 4 + 4, t0:t0 + M], ['h2T'],
             from_sbuf=(x2[:, ti, :], 'x2_%d' % ti), extra_scale=(esc if ti == 8 else None))
    B.barrier()

    def up_chunk(j0, jj, Wug, Wuv, act_cur, actk):
            for iv, (Wp, acc) in enumerate([(Wug, acc_g), (Wuv, acc_v)]):
                Jp = iv * 44 + j0 + jj
                for bi, (c0, N) in enumerate(TOKE):
                    bk, bkk = proj_fm(Wp, jj * 128, lambda kc: h2T[:, kc, c0:c0 + N], ['h2T'], N)
                    m1 = min(c0 + N, 1024)
                    if m1 > c0:
                        B.op('act', lambda e, bk=bk, c0=c0, m1=m1: e.activation(out=upraw[:, 2 + c0:2 + m1], in_=bk[:, 0:m1 - c0], func=AF.Copy), R=[bkk], W=['upraw'])
                    if c0 + N > 1024:
                        so = 1024 - c0
                        B.op('act', lambda e, bk=bk, so=so: e.activation(out=ups[:, 8:40].rearrange("p (t b) -> p b t", b=4),
                                                                         in_=bk[:, so:so + 32].rearrange("p (b t) -> p b t", b=4), func=AF.Copy), R=[bkk], W=['ups'])
                        B.op('act', lambda e, bk=bk, so=so: e.activation(out=upraw[:, 0:2], in_=bk[:, so + 32:so + 34], func=AF.Copy), R=[bkk], W=['upraw'])
                B.op('act', lambda e, Jp=Jp: e.activation(out=ups[:, 0:8].rearrange("p (pos b) -> p b pos", b=4),
                                                          in_=sfc[:, Jp, :].rearrange("p (b pos) -> p b pos", b=4), func=AF.Copy), R=['sfc'], W=['ups'])
                B.op('act', lambda e, Jp=Jp: e.activation(out=uptail[:, Jp, 0:2], in_=upraw[:, 1024:1026], func=AF.Copy), R=['upraw'], W=['uptail'])
                B.op('act', lambda e, Jp=Jp: e.activation(out=uptail[:, Jp, 2:10], in_=ups[:, 32:40], func=AF.Copy), R=['ups'], W=['uptail'])
                for is_s in (0, 1):
                    ak = 'acc_g' if iv == 0 else 'acc_v'
                    if not is_s:
                        dst = acc[:, 0:1024]
                        sk = 'upraw'
                        srcs = [upraw[:, 0:1024], upraw[:, 1:1025], upraw[:, 2:1026]]
                    else:
                        dst = acc[:, 1024:1056].rearrange("p (b t) -> p t b", b=4)
                        sk = 'ups'
                        srcs = [ups[:, o_:o_ + 32].rearrange("p (t b) -> p t b", b=4) for o_ in (0, 4, 8)]
                    B.op('act', lambda e, dst=dst, srcs=srcs, Jp=Jp: e.activation(out=dst, in_=srcs[2], func=AF.Identity,
                                                                               bias=bf_[:, Jp:Jp + 1], scale=wf[:, Jp, 2:3]), R=[sk, 'wf', 'bf_'], W=[ak])
                    for tap in (1, 0):
                        B.op('dve', lambda e, dst=dst, srcs=srcs, Jp=Jp, tap=tap: e.scalar_tensor_tensor(
                            out=dst, in0=srcs[tap], scalar=wf[:, Jp, tap:tap + 1], in1=dst, op0=ALU.mult, op1=ALU.add), R=[sk, 'wf', ak], W=[ak])
            B.op('act', lambda e: e.activation(out=acc_g[:, :], in_=acc_g[:, :], func=AF.Silu), R=['acc_g'], W=['acc_g'])
            B.op('dve', lambda e, jj=jj: e.tensor_tensor(out=act_cur[:, jj, :], in0=acc_g[:, :], in1=acc_v[:, :], op=ALU.mult), R=['acc_g', 'acc_v'], W=[actk])


    def down_group(Wd_, act_cur, actk):
        for ti, (t0, M) in enumerate(tiles9):
            tc0 = t0 if ti < 8 else 1024
            MM = M if ti < 8 else 32
            for cb in range(4):
                bk, bkk = nbank()
                view, keys = Wd_[cb]
                for jj in range(4):
                    B.op('pe', lambda e, jj=jj, bk=bk, view=view: e.matmul(bk[0:MM, :], act_cur[:, jj, tc0:tc0 + MM], view[:, jj, :], start=(jj == 0), stop=(jj == 3)),
                         R=[actk] + keys, W=[bkk], inc=(jj == 3))
                B.op('dve', lambda e, bk=bk, ti=ti, cb=cb: e.tensor_tensor(out=x2[0:MM, ti, cb * 512:(cb + 1) * 512], in0=bk[0:MM, :],
                                                                        in1=x2[0:MM, ti, cb * 512:(cb + 1) * 512], op=ALU.add),
                     R=[bkk, 'x2_%d' % ti], W=['x2_%d' % ti])


    prev_d = None
    for G in range(11):
        j0 = G * 4
        Wug = load_w16(w_up, j0 * 128)
        Wuv = load_w16(w_up, DFF + j0 * 128)
        act_cur = act_l[G % 2]
        actk = 'act%d' % (G % 2)
        up_chunk(j0, 0, Wug, Wuv, act_cur, actk)
        for _ in range(3):
            if copy_thunks:
                copy_thunks.pop(0)()
        if prev_d is not None:
            down_group(*prev_d)
        Wd = [load_panel(w_down, j0, 4, cb * 512, 512) for cb in range(4)]
        for jj in range(1, 4):
            up_chunk(j0, jj, Wug, Wuv, act_cur, actk)
        prev_d = (Wd, act_cur, actk)
    down_group(*prev_d)
    while copy_thunks:
        copy_thunks.pop(0)()
    for t in range(8):
        B.dma('sp', y_main[t * 128:(t + 1) * 128, :], x2[:, t, :], R=['x2_%d' % t])
    B.dma('sp', y_misc[:, :], x2[0:34, 8, :], R=['x2_8'])
    for rnd in range(6):
        nch = 16 if rnd < 5 else 8
        banks = []
        for q in range(nch // 4):
            bk, bkk = nbank()
            for j in range(4):
                Jp = rnd * 16 + q * 4 + j
                B.op('pe', lambda e, Jp=Jp, j=j, bk=bk: e.transpose(out=bk[0:10, j * 128:(j + 1) * 128], in_=uptail[:, Jp, :], identity=ident_f[:, :]),
                     R=['uptail', 'ident_f'], W=[bkk], inc=(j == 3))
            banks.append((bk, bkk))
        for q, (bk, bkk) in enumerate(banks):
            B.op('act', lambda e, bk=bk, q=q: e.activation(out=stg[0:10, q * 512:(q + 1) * 512], in_=bk[0:10, :], func=AF.Copy),
                 R=[bkk], W=['xt0'])
        B.dma('sp', o_pffn[:, rnd * 2048:rnd * 2048 + nch * 128], stg[0:10, 0:nch * 128], R=['xt0'])
    B.barrier()


_CACHE = {}


def _consts():
    ident = np.eye(128, dtype=np.float32)
    ki = np.arange(128)[:, None]
    qi = np.arange(128)[None, :]
    mask = np.zeros((128, 256), np.float32)
    mask[:, 0:128] = np.where(ki >= qi, 0.0, NEG)
    mask[:, 128:256] = np.where(ki <= qi, 0.0, NEG)
    smask = np.full((34, 96), NEG, np.float32)
    for g, (d, W) in enumerate(GROUPS):
        ds = min(d, 8)
        QBs = 8 // ds
        for b in range(4):
            for r in range(ds):
                for q in range(QBs):
                    t = r + d * q
                    col = g * 32 + b * 8 + r * QBs + q
                    for t2 in range(8):
                        if t2 <= t and (t - t2) % d == 0:
                            smask[b * 8 + t2, col] = 0.0
    return ident, mask, smask


def _hbias(c):
    P0 = 1024 * c
    hb = np.zeros((128, NHB), np.float32)
    i = np.arange(128)
    for (g, r, kind), col in HT_IDX.items():
        d, W = GROUPS[g]
        pos = P0 - 128 * d * (2 if kind == 1 else 1) + r + d * i
        hb[:, col] = np.where(pos >= 0, 0.0, NEG)
        if kind in (1, 2) and c == 0:
            hb[:, col] = 0.0
    return hb


def kernel(**inp):
    f = lambda a: np.ascontiguousarray(np.asarray(a, dtype=np.float32))
    if 'B' not in _CACHE:
        _CACHE['B'] = build_program()
    B = _CACHE['B']
    ident, mask, smask = _consts()
    xp = f(inp['x_prompt']); xs = f(inp['x_sample'])
    shared = {
        'c_ident': ident, 'c_mask': mask, 'c_smask': smask,
        'g_mix': f(inp['g_mix'][0]), 'g_ffn': f(inp['g_ffn'][0]), 'g_mem': f(inp['g_mem'][0]),
        'w_in': f(inp['w_in'][0]), 'w_dw': f(inp['w_dw'][0]), 'b_dw': f(inp['b_dw'][0]), 'g_cln': f(inp['g_cln'][0]),
        'b_cln': f(inp['b_cln'][0]), 'w_conv_out': f(inp['w_conv_out'][0]), 'g_q': f(inp['g_q'][0]), 'g_k': f(inp['g_k'][0]),
        'w_attn_out': f(inp['w_attn_out'][0]), 'w_mem_k': f(inp['w_mem_k'][0]), 'w_mem_v': f(inp['w_mem_v'][0]),
        'g_mq': f(inp['g_mq'][0]), 'g_mk': f(inp['g_mk'][0]), 'w_mem_out': f(inp['w_mem_out'][0]), 'w_o': f(inp['w_o'][0]),
        'w_up': f(inp['w_up'][0]), 'w_ffn_dw': f(inp['w_ffn_dw'][0]), 'b_ffn_dw': f(inp['b_ffn_dw'][0]), 'w_down': f(inp['w_down'][0]),
    }
    used = set(B.dram.keys())
    in_maps = []
    for core in range(8):
        bp, c = core // 4, core % 4
        P0 = 1024 * c
        m = dict(shared)
        m['x_main'] = xp[bp, P0:P0 + 1024]
        xh = np.zeros((4096, D), np.float32)
        lo = max(0, P0 - 4096)
        if P0 > 0:
            xh[4096 - (P0 - lo):] = xp[bp, lo:P0]
        m['x_halo'] = np.ascontiguousarray(xh[2048:])
        m['x_far'] = np.ascontiguousarray(np.concatenate([xh[14:2048:16], xh[15:2048:16]], axis=0))
        m['x_misc'] = np.ascontiguousarray(np.concatenate([xs[4 * core:4 * core + 4].reshape(32, D), xh[4094:4096]], axis=0))
        m['hbias'] = _hbias(c)
        m['eflag'] = np.full((128, 1), 1.0 if c > 0 else 0.0, np.float32)
        m['mem_p'] = f(inp['mem_prompt'][bp])
        sl = slice(4 * core, 4 * core + 4)
        m['cmk'] = f(inp['cache_mem_k'][0, sl]).reshape(4, 256, 1024)
        m['cmv'] = f(inp['cache_mem_v'][0, sl]).reshape(4, 256, 1024)
        m['st_conv'] = f(inp['state_conv'][0, sl])
        m['st_ffn'] = f(inp['state_ffn_conv'][0, sl]).reshape(8, 2 * DFF)
        swk = [inp['state_win1_k'], inp['state_win2_k'], inp['state_win3_k']]
        swv = [inp['state_win1_v'], inp['state_win2_v'], inp['state_win3_v']]
        for g, (d, W) in enumerate(GROUPS):
            m['st_wk%d' % g] = f(swk[g][0, sl]).reshape(4, W, 512)
            m['st_wv%d' % g] = f(swv[g][0, sl]).reshape(4, W, 512)
        in_maps.append({k: v for k, v in m.items() if k in used})
    res = run_bass_kernel_spmd(B.nc, in_maps, core_ids=list(range(8)))
    R = res.results
    y_p = np.zeros((2, 4096, D), np.float32)
    y_s = np.zeros((32, 8, D), np.float32)
    for core in range(8):
        bp, c = core // 4, core % 4
        y_p[bp, 1024 * c:1024 * (c + 1)] = R[core]['y_main']
        y_s[4 * core:4 * core + 4] = R[core]['y_misc'][0:32].reshape(4, 8, D)
    last = [3, 7]
    p_conv = np.stack([R[k]['o_pconv'][2:32] for k in last])[None]
    pk = []; pv = []
    for g, (d, W) in enumerate(GROUPS):
        if g < 2:
            kk = np.stack([R[k]['o_k%d' % g] for k in last]); vv = np.stack([R[k]['o_v%d' % g] for k in last])
        else:
            kk = np.stack([np.concatenate([R[k - 1]['o_k2'], R[k]['o_k2']], 0) for k in last])
            vv = np.stack([np.concatenate([R[k - 1]['o_v2'], R[k]['o_v2']], 0) for k in last])
        pk.append(kk.reshape(1, 2, W, 4, 128)); pv.append(vv.reshape(1, 2, W, 4, 128))
    p_ffn = np.stack([R[k]['o_pffn'][0:2] for k in last])[None]
    p_mk = np.stack([R[k]['o_pmk'] for k in last]).reshape(1, 2, 256, 4, 256)
    p_mv = np.stack([R[k]['o_pmv'] for k in last]).reshape(1, 2, 256, 4, 256)
    s_conv = np.concatenate([R[k]['o_sconv'] for k in range(8)], 0)[None]
    sk = []; sv = []
    for g, (d, W) in enumerate(GROUPS):
        sk.append(np.concatenate([R[k]['o_swk%d' % g] for k in range(8)], 0).reshape(1, 32, W, 4, 128))
        sv.append(np.concatenate([R[k]['o_swv%d' % g] for k in range(8)], 0).reshape(1, 32, W, 4, 128))
    s_ffn = np.concatenate([R[k]['o_pffn'][2:10].reshape(2, 4, 2 * DFF).transpose(1, 0, 2) for k in range(8)], 0)[None]
    return (y_p, y_s, p_conv, pk[0], pv[0], pk[1], pv[1], pk[2], pv[2], p_ffn, p_mk, p_mv,
            s_conv, sk[0], sv[0], sk[1], sv[1], sk[2], sv[2], s_ffn)
```

```python
import numpy as np
from contextlib import ExitStack
import concourse.bass as bass
import concourse.mybir as mybir
from concourse.bass_utils import run_bass_kernel_spmd

F32 = mybir.dt.float32
BF16 = mybir.dt.bfloat16
AF = mybir.ActivationFunctionType
ALU = mybir.AluOpType
AX = mybir.AxisListType

NEG = -30000.0
EPS = 1e-6
D = 2048
KC = 16
NT = 1058
TOKB = [(0, 512), (512, 512), (1024, 34)]
TOKE = [(0, 352), (352, 354), (706, 352)]
GROUPS = [(1, 128), (4, 512), (16, 2048)]
ERES = {1: [0], 4: [2, 3], 16: [14, 15]}
C_A, C_B, C_Q, C_K, C_V, C_QM, C_GC, C_GA, C_GM = 0, 1024, 2048, 3584, 5120, 6656, 7680, 9728, 11776
DFF = 5632
RING_UNITS = 12
UNIT = 2048


def halo_tiles():
    out = []
    for g, (d, W) in enumerate(GROUPS):
        for r in range(d):
            if r in ERES[d]:
                out.append((g, r, 1))
                out.append((g, r, 2))
            out.append((g, r, 0))
    return out


HT_LIST = halo_tiles()
HT_IDX = {t: i for i, t in enumerate(HT_LIST)}
NHB = len(HT_LIST) + 1


class Builder:
    def __init__(self):
        self.nc = bass.Bass("TRN2", target_bir_lowering=False)
        nc = self.nc
        self.root = ExitStack()
        self.eng = {'pe': nc.tensor, 'act': nc.scalar, 'dve': nc.vector, 'pool': nc.gpsimd, 'sp': nc.sync}
        self.esem = {}
        self.ecount = {}
        for e in self.eng:
            self.esem[e] = self.root.enter_context(nc.semaphore("e_" + e))
            self.ecount[e] = 0
        self.waited = {e: {} for e in self.eng}
        self.last_w = {}
        self.readers = {}
        self.dsems = {}
        self.dram = {}
        self.ring_pos = 0
        self.n_inst = 0

    def din(self, name, shape, dtype=F32):
        t = self.nc.dram_tensor(name, list(shape), dtype, kind="ExternalInput")
        self.dram[name] = t.ap()
        return self.dram[name]

    def dout(self, name, shape, dtype=F32):
        t = self.nc.dram_tensor(name, list(shape), dtype, kind="ExternalOutput")
        self.dram[name] = t.ap()
        return self.dram[name]

    def sb(self, stack, name, shape, dtype):
        return stack.enter_context(self.nc.sbuf_tensor(name, list(shape), dtype))

    def ps(self, stack, name, shape, dtype):
        return stack.enter_context(self.nc.psum_tensor(name, list(shape), dtype))

    def _deps(self, reads, writes):
        toks = []
        for k in list(reads) + list(writes):
            if k in self.last_w:
                toks.append(self.last_w[k])
        for k in writes:
            toks.extend(self.readers.get(k, ()))
        return toks

    def _wait(self, e, toks):
        w = self.waited[e]
        need = {}
        for (sem, val, sid) in toks:
            if w.get(sid, 0) >= val:
                continue
            if sid == 'e_' + e and (e == 'pe' or val > self.ecount[e]):
                continue
            if need.get(sid, (None, 0))[1] < val:
                need[sid] = (sem, val)
        for sid, (sem, val) in need.items():
            self.eng[e].wait_ge(sem, val)
            w[sid] = val

    def _record(self, tok, reads, writes):
        for k in writes:
            self.last_w[k] = tok
            self.readers[k] = []
        for k in reads:
            self.readers.setdefault(k, []).append(tok)
            if len(self.readers[k]) > 24:
                best = {}
                for t in self.readers[k]:
                    if t[2] not in best or best[t[2]][1] < t[1]:
                        best[t[2]] = t
                self.readers[k] = list(best.values())

    def op(self, e, fn, R=(), W=(), inc=True):
        self._wait(e, self._deps(R, W))
        ins = fn(self.eng[e])
        self.n_inst += 1
        if inc:
            ins.then_inc(self.esem[e], 1)
            self.ecount[e] += 1
            tok = (self.esem[e], self.ecount[e], 'e_' + e)
        else:
            tok = (self.esem[e], self.ecount[e] + 1, 'e_' + e)
        self._record(tok, R, W)
        return ins

    def dsem(self, name):
        if name not in self.dsems:
            self.dsems[name] = [self.root.enter_context(self.nc.semaphore("d_" + name)), 0]
        return self.dsems[name]

    def dma(self, q, out, in_, R=(), W=(), sem=None, **kw):
        self._wait(q, self._deps(R, W))
        if sem is None:
            sem = (list(W) + list(R))[0]
        s = self.dsem(sem)
        ins = self.eng[q].dma_start(out=out, in_=in_, **kw)
        ins.then_inc(s[0], 16)
        s[1] += 16
        self.n_inst += 1
        tok = (s[0], s[1], 'd_' + sem)
        self._record(tok, R, W)
        return ins

    def barrier(self):
        toks = [(self.esem[e], self.ecount[e], 'e_' + e) for e in self.eng if self.ecount[e] > 0]
        toks += [(s[0], s[1], 'd_' + n) for n, s in self.dsems.items() if s[1] > 0]
        for e in self.eng:
            self._wait(e, toks)
        self.last_w.clear()
        self.readers.clear()

    def finish(self):
        toks = [(s[0], s[1], 'd_' + n) for n, s in self.dsems.items() if s[1] > 0]
        toks += [(self.esem[e], self.ecount[e], 'e_' + e) for e in self.eng if self.ecount[e] > 0]
        self._wait('sp', toks)


class _Stop(Exception):
    pass


def _ck(n):
    import os
    if int(os.environ.get('KSTOP', '999')) == n:
        raise _Stop()


def build_program():
    B = Builder()
    try:
        _build_body(B)
    except _Stop:
        pass
    B.finish()
    return B


def _build_body(B):
    nc = B.nc
    root = B.root
    x_main = B.din("x_main", [1024, D])
    x_halo = B.din("x_halo", [2048, D])
    x_far = B.din("x_far", [256, D])
    x_misc = B.din("x_misc", [34, D])
    hbias_d = B.din("hbias", [128, NHB])
    eflag_d = B.din("eflag", [128, 1])
    mem_p = B.din("mem_p", [256, D])
    cmk = B.din("cmk", [4, 256, 1024])
    cmv = B.din("cmv", [4, 256, 1024])
    st_conv = B.din("st_conv", [4, 30, 1024])
    st_ffn = B.din("st_ffn", [8, 2 * DFF])
    st_wk = [B.din("st_wk%d" % g, [4, W, 512]) for g, (d, W) in enumerate(GROUPS)]
    st_wv = [B.din("st_wv%d" % g, [4, W, 512]) for g, (d, W) in enumerate(GROUPS)]
    c_ident = B.din("c_ident", [128, 128])
    c_mask = B.din("c_mask", [128, 256])
    c_smask = B.din("c_smask", [34, 96])
    g_mix = B.din("g_mix", [D]); g_ffn = B.din("g_ffn", [D]); g_mem = B.din("g_mem", [D])
    w_in = B.din("w_in", [D, 13824])
    w_dw = B.din("w_dw", [31, 1024]); b_dw = B.din("b_dw", [1024]); g_cln = B.din("g_cln", [1024]); b_cln = B.din("b_cln", [1024])
    w_conv_out = B.din("w_conv_out", [1024, D])
    g_q = B.din("g_q", [128]); g_k = B.din("g_k", [128])
    w_attn_out = B.din("w_attn_out", [512, D])
    w_mem_k = B.din("w_mem_k", [D, 1024]); w_mem_v = B.din("w_mem_v", [D, 1024])
    g_mq = B.din("g_mq", [256]); g_mk = B.din("g_mk", [256])
    w_mem_out = B.din("w_mem_out", [1024, D])
    w_o = B.din("w_o", [D, D])
    w_up = B.din("w_up", [D, 2 * DFF]); w_ffn_dw = B.din("w_ffn_dw", [3, 2 * DFF]); b_ffn_dw = B.din("b_ffn_dw", [2 * DFF])
    w_down = B.din("w_down", [DFF, D])

    y_main = B.dout("y_main", [1024, D])
    y_misc = B.dout("y_misc", [34, D])
    o_pconv = B.dout("o_pconv", [32, 1024])
    o_k = [B.dout("o_k%d" % g, [n, 512]) for g, n in enumerate([128, 512, 1024])]
    o_v = [B.dout("o_v%d" % g, [n, 512]) for g, n in enumerate([128, 512, 1024])]
    o_pffn = B.dout("o_pffn", [10, 2 * DFF])
    o_pmk = B.dout("o_pmk", [256, 1024]); o_pmv = B.dout("o_pmv", [256, 1024])
    o_sconv = B.dout("o_sconv", [4, 30, 1024])
    o_swk = [B.dout("o_swk%d" % g, [4, W, 512]) for g, (d, W) in enumerate(GROUPS)]
    o_swv = [B.dout("o_swv%d" % g, [4, W, 512]) for g, (d, W) in enumerate(GROUPS)]

    ident_f = B.sb(root, "ident_f", [128, 128], F32)
    ident_b = B.sb(root, "ident_b", [128, 128], BF16)
    ones_b = B.sb(root, "ones_b", [128, 128], BF16)
    mask_b = B.sb(root, "mask_b", [128, 256], BF16)
    smask_b = B.sb(root, "smask_b", [34, 96], BF16)
    hbias = B.sb(root, "hbias_s", [128, NHB], F32)
    eflag = B.sb(root, "eflag_s", [128, 1], F32)
    epsc = B.sb(root, "epsc", [128, 1], F32)
    gT_mix = B.sb(root, "gT_mix", [128, KC], F32)
    gT_ffn = B.sb(root, "gT_ffn", [128, KC], F32)
    gT_mem = B.sb(root, "gT_mem", [128, KC], F32)
    gq_t = B.sb(root, "gq_t", [128, 128], F32)
    gk_t = B.sb(root, "gk_t", [128, 128], F32)
    gmq_t = B.sb(root, "gmq_t", [128, 256], F32)
    gmk_t = B.sb(root, "gmk_t", [128, 256], F32)
    stat = B.sb(root, "stat", [128, 192], F32)
    cpar = B.sb(root, "cpar", [128, 3, 8], F32)
    junk = B.sb(root, "junk", [128, 2048], BF16)
    psb = [B.ps(root, "ps%d" % i, [128, 512], F32) for i in range(8)]

    ARENA_KB = 196
    arena = B.sb(root, "arena", [128, ARENA_KB * 256], F32)
    apos = {'p': 0}

    def set_ptr(kb):
        apos['p'] = int(kb * 1024)

    def al(name, shape, dtype, at=None):
        esz = 4 if dtype == F32 else 2
        n = 1
        for d_ in shape[1:]:
            n *= d_
        nbytes = (n * esz + 31) // 32 * 32
        off = apos['p'] if at is None else int(at * 1024)
        if at is None:
            apos['p'] = off + nbytes
        assert off + nbytes <= ARENA_KB * 1024, (name, off, nbytes)
        v = arena[:, off // 4:(off + nbytes) // 4]
        if dtype != F32:
            v = v.bitcast(dtype)
        v = v[0:shape[0], 0:n]
        if len(shape) == 3:
            v = v.rearrange("p (a b) -> p a b", a=shape[1])
        return v

    ph = {'stat_i': 0}

    def stat_slot(n=4):
        i = ph['stat_i']
        ph['stat_i'] = (i + 12) % 192
        return i

    B.dma('sp', ident_f[:], c_ident[:], W=['ident_f'], sem='const')
    B.dma('pool', ident_b[:], c_ident[:], W=['ident_b'], sem='constp')
    B.dma('pool', mask_b[:], c_mask[:], W=['mask_b'], sem='constp')
    B.dma('pool', smask_b[:], c_smask[:], W=['smask_b'], sem='constp')
    B.dma('sp', hbias[:], hbias_d[:], W=['hbias'], sem='const')
    B.dma('sp', eflag[:], eflag_d[:], W=['eflag'], sem='const')
    import os
    SK = os.environ.get('KSKIP', '')
    B.dma('sp', gT_mix[:], g_mix.rearrange("(k p) -> p k", p=128), W=['gT_mix'], sem='const', allow_slow_non_contiguous=True)
    for (t, src, key, n) in [(gq_t, g_q, 'gq', 128), (gk_t, g_k, 'gk', 128), (gmq_t, g_mq, 'gmq', 256), (gmk_t, g_mk, 'gmk', 256)]:
        if 'b' in SK: break
        B.dma('sp', t[:], src.partition_broadcast(128), W=[key], sem='const')
    B.op('dve', lambda e: e.memset(ones_b[:], 1.0), W=['ones_b'])
    B.op('dve', lambda e: e.memset(epsc[:], EPS), W=['epsc'])

    _ck(1)
    def new_ring(stack, units, tag):
        ph['ring'] = al("ring_" + tag, [128, units * UNIT], BF16)
        ph['ring_units'] = units
        B.ring_pos = 0

    def ring_alloc(nunits):
        p = B.ring_pos
        if p + nunits > ph['ring_units']:
            p = 0
        B.ring_pos = p + nunits
        return p, ['ring%d' % u for u in range(p, p + nunits)]

    def load_panel(Wd, k0, nk, c0, ncol):
        nun = -(-(nk * ncol) // UNIT)
        p, keys = ring_alloc(nun)
        view = ph['ring'][:, p * UNIT: p * UNIT + nk * ncol].rearrange("p (k n) -> p k n", k=nk)
        src = Wd[k0 * 128:(k0 + nk) * 128, c0:c0 + ncol].rearrange("(k p) n -> p k n", p=128)
        B.dma('pool', view, src, W=keys, sem=keys[0])
        return view, keys

    def load_w16(Wd, c0, ncol=512):
        return [load_panel(Wd, 0, 8, c0, ncol) + (0,), load_panel(Wd, 8, 8, c0, ncol) + (8,)]

    def prepA(stack_bufs, src, rows, from_sbuf=None, extra_scale=None):
        xt_list, pst = stack_bufs
        slot = ph.get('xt_i', 0)
        ph['xt_i'] = (slot + 1) % len(xt_list)
        xt = xt_list[slot]
        xk = 'xt%d' % slot
        if from_sbuf is None:
            B.dma('sp', xt[0:rows, :], src, W=[xk])
            xin = xt
            rk = [xk]
        else:
            xin, rk0 = from_sbuf
            rk = [rk0]
        si = stat_slot(4)
        sk = 'stat%d' % si
        B.op('act', lambda e: e.activation(out=junk[0:rows, :], in_=xin[0:rows, :], func=AF.Square,
                                           accum_out=stat[0:rows, si:si + 1]), R=rk, W=[sk])
        B.op('act', lambda e: e.activation(out=stat[0:rows, si + 1:si + 2], in_=stat[0:rows, si:si + 1], func=AF.Ln,
                                           bias=epsc[0:rows, :], scale=1.0 / D), R=[sk, 'epsc'], W=[sk])
        B.op('act', lambda e: e.activation(out=stat[0:rows, si + 2:si + 3], in_=stat[0:rows, si + 1:si + 2], func=AF.Exp,
                                           scale=-0.5), R=[sk], W=[sk])
        if extra_scale is not None:
            B.op('dve', lambda e: e.tensor_tensor(out=stat[0:rows, si + 2:si + 3], in0=stat[0:rows, si + 2:si + 3],
                                                  in1=extra_scale[0:rows, :], op=ALU.mult), R=[sk, 'esc'], W=[sk])
        B.op('dve', lambda e: e.tensor_scalar(out=xt[0:rows, :], in0=xin[0:rows, :], scalar1=stat[0:rows, si + 2:si + 3],
                                              scalar2=None, op0=ALU.mult), R=rk + [sk], W=[xk])
        return (xt, xk, rows, pst)

    def prepT(pa, gT, gkey, dst_fn, dst_keys):
        xt, xk, rows, pst = pa
        for q in range(4):
            pi = ph.get('pst_i', 0)
            ph['pst_i'] = (pi + 1) % len(pst)
            bank, bkey = pst[pi]
            for j in range(4):
                kc = q * 4 + j
                B.op('pe', lambda e, kc=kc, j=j: e.transpose(out=bank[:, j * 128:j * 128 + rows],
                                                              in_=xt[0:rows, kc * 128:(kc + 1) * 128],
                                                              identity=ident_f[0:rows, 0:rows]),
                     R=[xk, 'ident_f'], W=[bkey], inc=(j == 3))
            B.op('dve', lambda e, q=q: e.tensor_tensor(
                out=dst_fn(q),
                in0=bank[:, :].rearrange("p (j t) -> p j t", j=4)[:, :, 0:rows],
                in1=gT[:, q * 4:q * 4 + 4].unsqueeze(2).broadcast_to([128, 4, rows]), op=ALU.mult),
                R=[bkey, gkey], W=dst_keys)

    def prep(stack_bufs, src, rows, gT, gkey, dst_fn, dst_keys, from_sbuf=None, extra_scale=None):
        pa = prepA(stack_bufs, src, rows, from_sbuf=from_sbuf, extra_scale=extra_scale)
        prepT(pa, gT, gkey, dst_fn, dst_keys)

    def proj_tm(lhs_fn, M, panels, bank, bkey, rkeys, ncol=512, col_off=0):
        n = 0
        for (view, keys, kb) in panels:
            for k in range(8):
                kc = kb + k
                B.op('pe', lambda e, view=view, k=k, kc=kc, n=n: e.matmul(
                    bank[0:M, col_off:col_off + ncol], lhs_fn(kc), view[:, k, 0:ncol], start=(n == 0), stop=(n == 15)),
                    R=rkeys + keys, W=[bkey], inc=(n == 15))
                n += 1

    def head_norm(bank, bkey, M, nh, hd, g_tile, gkey, kf, kfk, out_f32, of_keys, out_bf, ob_keys):
        w = nh * hd
        B.op('act', lambda e: e.activation(out=kf[0:M, 0:w], in_=bank[0:M, 0:w], func=AF.Copy), R=[bkey], W=[kfk])
        si = stat_slot(12)
        sk = 'stat%d' % si
        sqs = ph['sqs']
        B.op('act', lambda e: e.activation(out=sqs[0:M, 0:w], in_=kf[0:M, 0:w], func=AF.Square), R=[kfk], W=['sqs'])
        B.op('dve', lambda e: e.tensor_reduce(out=stat[0:M, si:si + nh], in_=sqs[0:M, 0:w].rearrange("p (h d) -> p h d", h=nh),
                                              axis=AX.X, op=ALU.add), R=['sqs'], W=[sk])
        B.op('act', lambda e: e.activation(out=stat[0:M, si + 4:si + 4 + nh], in_=stat[0:M, si:si + nh], func=AF.Ln,
                                           bias=epsc[0:M, :], scale=1.0 / hd), R=[sk, 'epsc'], W=[sk])
        B.op('act', lambda e: e.activation(out=stat[0:M, si + 8:si + 8 + nh], in_=stat[0:M, si + 4:si + 4 + nh], func=AF.Exp,
                                           scale=-0.5), R=[sk], W=[sk])
        dst = out_f32 if out_f32 is not None else out_bf
        dk = of_keys if out_f32 is not None else ob_keys
        for h in range(nh):
            B.op('dve', lambda e, h=h: e.scalar_tensor_tensor(
                out=dst[0:M, h * hd:(h + 1) * hd], in0=kf[0:M, h * hd:(h + 1) * hd],
                scalar=stat[0:M, si + 8 + h:si + 9 + h], in1=g_tile[0:M, 0:hd], op0=ALU.mult, op1=ALU.mult),
                R=[kfk, sk, gkey], W=dk)
        if out_f32 is not None and out_bf is not None:
            B.op('act', lambda e: e.activation(out=out_bf[0:M, 0:w], in_=out_f32[0:M, 0:w], func=AF.Copy), R=of_keys, W=ob_keys)

    st_mix = ExitStack()
    st1 = ExitStack()
    hT = al("hT", [128, KC, NT], BF16, at=0)
    o_attn = al("o_attn", [128, 4, NT], BF16, at=34)
    hT_h1 = al("hT_h1", [128, KC, 32], BF16, at=42.5)
    c_act = al("c_act", [128, 8, NT], BF16, at=43.5)
    o_mem = al("o_mem", [128, 8, NT], BF16, at=60.5)
    set_ptr(43.5)
    new_ring(st1, 12, "a")
    xt_list = [al("xt%d" % i, [128, D], F32) for i in range(2)]
    pst = [(psb[6], 'ps6'), (psb[7], 'ps7')]
    prep_bufs = (xt_list, pst)
    hTh = [al("hTh%d" % i, [128, KC, 128], BF16) for i in range(2)]
    acc_O = al("acc_O", [128, 4, NT], F32)
    acc_D = al("acc_D", [128, 4, NT], F32)
    kf_l = [al("kf%d" % i, [128, 512], F32) for i in range(2)]
    ko_l = [al("ko%d" % i, [128, 512], F32) for i in range(3)]
    vo_l = [al("vo%d" % i, [128, 512], F32) for i in range(3)]
    sqs = al("sqs", [128, 512], F32)
    qb_l = [al("qb%d" % i, [128, 512], BF16) for i in range(2)]
    kb_l = [al("kb%d" % i, [128, 512], BF16) for i in range(2)]
    kT_l = [al("kT%d" % i, [128, 4, 128], BF16) for i in range(3)]
    vS_l = [al("vS%d" % i, [128, 512], BF16) for i in range(4)]
    qT_l = [al("qT%d" % i, [128, 4, 128], BF16) for i in range(2)]
    pA_l = [al("pA%d" % i, [128, 512], BF16) for i in range(2)]
    pB_l = [al("pB%d" % i, [128, 512], BF16) for i in range(2)]
    kTs = al("kTs", [128, 4, 34], BF16)
    qTs = al("qTs", [128, 4, 34], BF16)
    vSs = al("vSs", [34, 512], BF16)
    pBs = al("pBs", [34, 4, 32], BF16)
    ksb_l = [al("ksb%d" % i, [128, 512], BF16) for i in range(3)]
    vsb_l = [al("vsb%d" % i, [128, 512], BF16) for i in range(3)]
    kTA_l = [al("kTA%d" % i, [128, 4, 128], BF16) for i in range(2)]
    pAs_l = [al("pAs%d" % i, [128, 4, 8], BF16) for i in range(2)]
    rot = {}
    ph['sqs'] = sqs

    def nxt(name, lst):
        i = rot.get(name, 0)
        rot[name] = (i + 1) % len(lst)
        return lst[i], '%s%d' % (name, i)

    PQ, PK, PV, PT, PSA, PSB, PO, PD = range(8)
    pT_bf = psb[PT][:, :].bitcast(BF16)

    B.barrier()
    a_tiles = [(x_misc[:, :], 34, 1024)] + [(x_main[t * 128:(t + 1) * 128, :], 128, t * 128) for t in range(8)]
    pa_cur = prepA(prep_bufs, a_tiles[0][0], a_tiles[0][1])
    for ti_, (src_, rows_, c0_) in enumerate(a_tiles):
        pa_nxt = prepA(prep_bufs, a_tiles[ti_ + 1][0], a_tiles[ti_ + 1][1]) if ti_ + 1 < len(a_tiles) else None
        prepT(pa_cur, gT_mix, 'gT_mix', lambda q, c0_=c0_, rows_=rows_: hT[:, q * 4:q * 4 + 4, c0_:c0_ + rows_], ['hT'])
        pa_cur = pa_nxt
    for (t_, src_, key_) in [(gT_ffn, g_ffn, 'gT_ffn'), (gT_mem, g_mem, 'gT_mem')]:
        B.dma('sp', t_[:], src_.rearrange("(k p) -> p k", p=128), W=[key_], sem='cparl', allow_slow_non_contiguous=True)
    for i_, src_ in enumerate([b_dw, g_cln, b_cln]):
        B.dma('sp', cpar[:, i_, :], src_.rearrange("(j p) -> p j", p=128), W=['cpar'], sem='cparl', allow_slow_non_contiguous=True)
    _ck(2)

    def transpose4(src_bf, skey, M, half, dst, dkeys, perm=None):
        pk = 'ps%d' % PT
        base = half * 512
        for h in range(4):
            B.op('pe', lambda e, h=h: e.transpose(out=pT_bf[:, base + h * 128: base + h * 128 + M],
                                                  in_=src_bf[0:M, h * 128:(h + 1) * 128], identity=ident_b[0:M, 0:M]),
                 R=[skey, 'ident_b'], W=[pk], inc=(h == 3))
        src = pT_bf[:, base:base + 512].rearrange("p (h t) -> p h t", h=4)[:, :, 0:M]
        if perm is None:
            B.op('act', lambda e: e.activation(out=dst, in_=src, func=AF.Copy), R=[pk], W=dkeys)
        else:
            perm(src, pk)

    scale_att = 1.0 / np.sqrt(128.0)

    def attend_scores(QB, qc0, nq, kTA, kTAk, vA, vAk, kTB, kTBk, vB, vBk, qT, qTk, hb_col, acc_cols_fn, first, hb_colB=None):
        pA, pAk = nxt('pA', pA_l)
        pB, pBk = nxt('pB', pB_l)
        sA, sB = psb[PSA], psb[PSB]
        mA = mask_b[:, qc0:qc0 + nq].unsqueeze(1).broadcast_to([128, 4, nq])
        B.op('pe', lambda e: e.matmul(sA[:, 0:4 * nq], ident_b[:, :], mA, start=True, stop=False),
             R=['ident_b', 'mask_b'], W=['ps%d' % PSA], inc=False)
        for h in range(4):
            B.op('pe', lambda e, h=h: e.matmul(sA[:, h * nq:(h + 1) * nq], kTA[:, h, 0:128], qT[:, h, qc0:qc0 + nq],
                                               start=False, stop=(h == 3)), R=[kTAk, qTk], W=['ps%d' % PSA], inc=(h == 3))
        B.op('act', lambda e: e.activation(out=pA[:, 0:4 * nq], in_=sA[:, 0:4 * nq], func=AF.Exp,
                                           bias=hbias[:, hb_col:hb_col + 1], scale=scale_att),
             R=['ps%d' % PSA, 'hbias'], W=[pAk])
        mB = mask_b[0:QB, 128 + qc0:128 + qc0 + nq].unsqueeze(1).broadcast_to([QB, 4, nq])
        B.op('pe', lambda e: e.matmul(sB[0:QB, 0:4 * nq], ident_b[0:QB, 0:QB], mB, start=True, stop=False),
             R=['ident_b', 'mask_b'], W=['ps%d' % PSB], inc=False)
        for h in range(4):
            B.op('pe', lambda e, h=h: e.matmul(sB[0:QB, h * nq:(h + 1) * nq], kTB[:, h, 0:QB], qT[:, h, qc0:qc0 + nq],
                                               start=False, stop=(h == 3)), R=[kTBk, qTk], W=['ps%d' % PSB], inc=(h == 3))
        cb = NHB - 1 if hb_colB is None else hb_colB
        B.op('act', lambda e: e.activation(out=pB[0:QB, 0:4 * nq], in_=sB[0:QB, 0:4 * nq], func=AF.Exp,
                                           bias=hbias[0:QB, cb:cb + 1], scale=scale_att),
             R=['ps%d' % PSB, 'hbias'], W=[pBk])
        return (pA, pAk, pB, pBk)

    def attend_pv(st, QB, qc0, nq, kTA, kTAk, vA, vAk, kTB, kTBk, vB, vBk, qT, qTk, hb_col, acc_cols_fn, first, hb_colB=None):
        pA, pAk, pB, pBk = st
        oT, dn = psb[PO], psb[PD]
        for h in range(4):
            B.op('pe', lambda e, h=h: e.matmul(oT[:, h * nq:(h + 1) * nq], vA[:, h * 128:(h + 1) * 128], pA[:, h * nq:(h + 1) * nq],
                                               start=True, stop=False), R=[vAk, pAk], W=['ps%d' % PO], inc=False)
            B.op('pe', lambda e, h=h: e.matmul(oT[:, h * nq:(h + 1) * nq], vB[0:QB, h * 128:(h + 1) * 128], pB[0:QB, h * nq:(h + 1) * nq],
                                               start=False, stop=True), R=[vBk, pBk], W=['ps%d' % PO], inc=(h == 3))
        B.op('pe', lambda e: e.matmul(dn[:, 0:4 * nq], ones_b[:, :], pA[:, 0:4 * nq], start=True, stop=False),
             R=['ones_b', pAk], W=['ps%d' % PD], inc=False)
        B.op('pe', lambda e: e.matmul(dn[:, 0:4 * nq], ones_b[0:QB, :], pB[0:QB, 0:4 * nq], start=False, stop=True),
             R=['ones_b', pBk], W=['ps%d' % PD], inc=True)
        for (acc, pbank, pk, ak) in [(acc_O, oT, 'ps%d' % PO, 'acc_O'), (acc_D, dn, 'ps%d' % PD, 'acc_D')]:
            dst = acc_cols_fn(acc)
            src = pbank[:, 0:4 * nq].rearrange("p (h t) -> p h t", h=4)
            if first:
                B.op('act', lambda e, dst=dst, src=src: e.activation(out=dst, in_=src, func=AF.Copy), R=[pk], W=[ak])
            else:
                B.op('dve', lambda e, dst=dst, src=src: e.tensor_tensor(out=dst, in0=src, in1=dst, op=ALU.add), R=[pk, ak], W=[ak])


    def attend(*args, **kw):
        st = attend_scores(*args, **kw)
        attend_pv(st, *args, **kw)

    for g, (d, W) in enumerate(GROUPS):
        QB = min(128, 1024 // d)
        nmb = 1024 // (d * QB)
        Wq = load_w16(w_in, C_Q + g * 512)
        Wk = load_w16(w_in, C_K + g * 512)
        Wv = load_w16(w_in, C_V + g * 512)

        def tile_proj(lhs_fn, lkeys, M, want_q):
            proj_tm(lhs_fn, M, Wk, psb[PK], 'ps%d' % PK, lkeys)
            proj_tm(lhs_fn, M, Wv, psb[PV], 'ps%d' % PV, lkeys)
            if want_q:
                proj_tm(lhs_fn, M, Wq, psb[PQ], 'ps%d' % PQ, lkeys)

        def tile_norm(M, want_q, out_rows):
            res = {}
            kf, kfk = nxt('kf', kf_l)
            kb, kbk = nxt('kb', kb_l)
            if out_rows is not None:
                ko, kok = nxt('ko', ko_l)
                head_norm(psb[PK], 'ps%d' % PK, M, 4, 128, gk_t, 'gk', kf, kfk, ko, [kok], kb, [kbk])
                B.dma('sp', out_rows(o_k[g]), ko[0:M, :], R=[kok])
            else:
                head_norm(psb[PK], 'ps%d' % PK, M, 4, 128, gk_t, 'gk', kf, kfk, None, None, kb, [kbk])
            vS, vSk = nxt('vS', vS_l)
            B.op('act', lambda e: e.activation(out=vS[0:M, :], in_=psb[PV][0:M, :], func=AF.Copy), R=['ps%d' % PV], W=[vSk])
            if out_rows is not None:
                vo, vok = nxt('vo', vo_l)
                B.op('dve', lambda e: e.tensor_copy(out=vo[0:M, :], in_=psb[PV][0:M, :]), R=['ps%d' % PV], W=[vok])
                B.dma('sp', out_rows(o_v[g]), vo[0:M, :], R=[vok])
            res.update(kb=kb, kbk=kbk, vS=vS, vSk=vSk)
            if want_q:
                kf2, kfk2 = nxt('kf', kf_l)
                qb, qbk = nxt('qb', qb_l)
                head_norm(psb[PQ], 'ps%d' % PQ, M, 4, 128, gq_t, 'gq', kf2, kfk2, None, None, qb, [qbk])
                res.update(qb=qb, qbk=qbk)
            return res

        def tile_tr(M, want_q, res):
            kT, kTk = nxt('kT', kT_l)
            transpose4(res['kb'], res['kbk'], M, 0, kT[:, :, 0:M], [kTk])
            res.update(kT=kT, kTk=kTk)
            if want_q:
                qT, qTk = nxt('qT', qT_l)
                transpose4(res['qb'], res['qbk'], M, 1, qT[:, :, 0:M], [qTk])
                res.update(qT=qT, qTk=qTk)
            return res

        proj_tm(lambda kc: hT[:, kc, 1024:1058], 34, Wk, psb[PK], 'ps%d' % PK, ['hT'])
        proj_tm(lambda kc: hT[:, kc, 1024:1058], 34, Wv, psb[PV], 'ps%d' % PV, ['hT'])
        proj_tm(lambda kc: hT[:, kc, 1024:1058], 34, Wq, psb[PQ], 'ps%d' % PQ, ['hT'])
        kf, kfk = nxt('kf', kf_l)
        kb, kbk = nxt('kb', kb_l)
        ko, kok = nxt('ko', ko_l)
        head_norm(psb[PK], 'ps%d' % PK, 34, 4, 128, gk_t, 'gk', kf, kfk, ko, [kok], kb, [kbk])
        for b in range(4):
            B.dma('sp', o_swk[g][b, W - 8:W, :], ko[b * 8:(b + 1) * 8, :], R=[kok])
        transpose4(kb, kbk, 34, 0, kTs[:, :, :], ['kTs'])
        B.op('act', lambda e: e.activation(out=vSs[0:34, :], in_=psb[PV][0:34, :], func=AF.Copy), R=['ps%d' % PV], W=['vSs'])
        vo, vok = nxt('vo', vo_l)
        B.op('dve', lambda e: e.tensor_copy(out=vo[0:34, :], in_=psb[PV][0:34, :]), R=['ps%d' % PV], W=[vok])
        for b in range(4):
            B.dma('sp', o_swv[g][b, W - 8:W, :], vo[b * 8:(b + 1) * 8, :], R=[vok])
        kf2, kfk2 = nxt('kf', kf_l)
        qb, qbk = nxt('qb', qb_l)
        head_norm(psb[PQ], 'ps%d' % PQ, 34, 4, 128, gq_t, 'gq', kf2, kfk2, None, None, qb, [qbk])
        ds = min(d, 8)
        QBs = 8 // ds

        def qperm(src, pk):
            for h in range(4):
                B.op('act', lambda e, h=h: e.activation(
                    out=qTs[:, h, 0:32].rearrange("p (b r q) -> p b q r", b=4, r=ds, q=QBs),
                    in_=src[:, h, 0:32].rearrange("p (b q r) -> p b q r", b=4, q=QBs, r=ds), func=AF.Copy),
                    R=[pk], W=['qTs'])
        transpose4(qb, qbk, 34, 1, None, None, perm=qperm)
        sB = psb[PSB]
        for h in range(4):
            B.op('pe', lambda e, h=h: e.matmul(sB[0:32, h * 32:(h + 1) * 32], kTs[:, h, 0:32], qTs[:, h, 0:32],
                                               start=True, stop=False), R=['kTs', 'qTs'], W=['ps%d' % PSB], inc=False)
            B.op('pe', lambda e, h=h: e.matmul(sB[0:32, h * 32:(h + 1) * 32], ident_b[0:32, 0:32], smask_b[0:32, g * 32:(g + 1) * 32],
                                               start=False, stop=True), R=['ident_b', 'smask_b'], W=['ps%d' % PSB], inc=(h == 3))
        B.op('act', lambda e: e.activation(out=pBs[0:32, :, :], in_=sB[0:32, 0:128].rearrange("p (h t) -> p h t", h=4),
                                           func=AF.Exp, scale=scale_att), R=['ps%d' % PSB], W=['pBs'])
        oTs, dns = psb[PO], psb[PD]
        combos = [(b, r) for b in range(4) for r in range(ds)]

        def s_stageA(b, r):
            ksb, ksbk = nxt('ksb', ksb_l)
            vsb, vsbk = nxt('vsb', vsb_l)
            B.dma('pool', ksb[:, :], st_wk[g][b, r:W:d, :], W=[ksbk])
            B.dma('pool', vsb[:, :], st_wv[g][b, r:W:d, :], W=[vsbk])
            kTA, kTAk = nxt('kTA', kTA_l)
            transpose4(ksb, ksbk, 128, 0, kTA[:, :, :], [kTAk])
            return (kTA, kTAk, vsb, vsbk)

        def s_stageB(b, r, st):
            kTA, kTAk, vsb, vsbk = st
            off = b * 8 + r * QBs
            pAs, pAsk = nxt('pAs', pAs_l)
            sA = psb[PSA]
            for h in range(4):
                B.op('pe', lambda e, h=h: e.matmul(sA[:, h * QBs:(h + 1) * QBs], kTA[:, h, :], qTs[:, h, off:off + QBs],
                                                   start=True, stop=False), R=[kTAk, 'qTs'], W=['ps%d' % PSA], inc=False)
                B.op('pe', lambda e, h=h: e.matmul(sA[:, h * QBs:(h + 1) * QBs], ident_b[:, :], mask_b[:, 0:QBs],
                                                   start=False, stop=True), R=['ident_b', 'mask_b'], W=['ps%d' % PSA], inc=(h == 3))
            B.op('act', lambda e: e.activation(out=pAs[:, :, 0:QBs], in_=sA[:, 0:4 * QBs].rearrange("p (h t) -> p h t", h=4),
                                               func=AF.Exp, scale=scale_att), R=['ps%d' % PSA], W=[pAsk])
            for h in range(4):
                B.op('pe', lambda e, h=h: e.matmul(oTs[:, h * 32 + off:h * 32 + off + QBs], vSs[0:32, h * 128:(h + 1) * 128],
                                                   pBs[0:32, h, off:off + QBs], start=True, stop=False),
                     R=['vSs', 'pBs'], W=['ps%d' % PO], inc=False)
                B.op('pe', lambda e, h=h: e.matmul(oTs[:, h * 32 + off:h * 32 + off + QBs], vsb[:, h * 128:(h + 1) * 128],
                                                   pAs[:, h, 0:QBs], start=False, stop=True),
                     R=[vsbk, pAsk], W=['ps%d' % PO], inc=(h == 3))
            for h in range(4):
                B.op('pe', lambda e, h=h: e.matmul(dns[:, h * 32 + off:h * 32 + off + QBs], ones_b[0:32, :],
                                                   pBs[0:32, h, off:off + QBs], start=True, stop=False),
                     R=['ones_b', 'pBs'], W=['ps%d' % PD], inc=False)
                B.op('pe', lambda e, h=h: e.matmul(dns[:, h * 32 + off:h * 32 + off + QBs], ones_b[:, :],
                                                   pAs[:, h, 0:QBs], start=False, stop=True),
                     R=['ones_b', pAsk], W=['ps%d' % PD], inc=(h == 3))

        st_cur = s_stageA(*combos[0])
        for ci, (b, r) in enumerate(combos):
            st_nxt = s_stageA(*combos[ci + 1]) if ci + 1 < len(combos) else None
            s_stageB(b, r, st_cur)
            st_cur = st_nxt
        for (acc, pbank, pk, ak) in [(acc_O, oTs, 'ps%d' % PO, 'acc_O'), (acc_D, dns, 'ps%d' % PD, 'acc_D')]:
            for h in range(4):
                dst = acc[:, h, 1024:1056].rearrange("p (b q r) -> p b q r", b=4, q=QBs, r=ds)
                src = pbank[:, h * 32:(h + 1) * 32].rearrange("p (b r q) -> p b q r", b=4, r=ds, q=QBs)
                if g == 0:
                    B.op('act', lambda e, dst=dst, src=src: e.activation(out=dst, in_=src, func=AF.Copy), R=[pk], W=[ak])
                else:
                    B.op('dve', lambda e, dst=dst, src=src: e.tensor_tensor(out=dst, in0=src, in1=dst, op=ALU.add), R=[pk, ak], W=[ak])

        _ck(3 + 2 * g)
        jobs = []
        for r in range(d):
            if r in ERES[d]:
                jobs.append(dict(kind='x', r=r, first=True))
            jobs.append(dict(kind='h', r=r, first=(r not in ERES[d])))
            for bm in range(nmb):
                jobs.append(dict(kind='m', r=r, bm=bm, first=False))

        def job_prepA(job):
            r = job['r']
            if job['kind'] == 'h':
                src = x_halo[2048 - 128 * d + r:2048:d, :]
            elif d == 16:
                src = x_far[(r - 14) * 128:(r - 13) * 128, :]
            else:
                src = x_halo[2048 - 256 * d + r:2048 - 128 * d:d, :]
            return prepA(prep_bufs, src, 128)

        def job_prepT(job, pa):
            hh, hhk = nxt('hTh', hTh)
            prepT(pa, gT_mix, 'gT_mix', lambda q, hh=hh: hh[:, q * 4:q * 4 + 4, :], [hhk])
            if job['kind'] == 'h' and d == 1:
                B.op('pool', lambda e, hh=hh: e.tensor_copy(out=hT_h1[:, :, :], in_=hh[:, :, 96:128]), R=[hhk], W=['hT_h1'])
            job['hh'] = (hh, hhk)

        prev = None
        T1 = None
        T2 = None
        if jobs[0]['kind'] != 'm':
            job_prepT(jobs[0], job_prepA(jobs[0]))

        def do_tr(T):
            job_, res_, M_, wq_, t0_ = T
            r_ = job_['r']
            prev_ = None if job_['first'] else do_tr.prev
            cur = tile_tr(M_, wq_, res_)
            args = None
            if job_['kind'] != 'm' and wq_:
                nq = 2 if d == 1 else 1
                qc0 = 128 - nq
                ecol = 1056 + (0 if d == 1 else (r_ - ERES[d][0]))
                args = ((128, qc0, nq, prev_['kT'], prev_['kTk'], prev_['vS'], prev_['vSk'],
                         cur['kT'], cur['kTk'], cur['vS'], cur['vSk'], cur['qT'], cur['qTk'],
                         HT_IDX[(g, r_, 1)], (lambda acc, ecol=ecol, nq=nq: acc[:, :, ecol:ecol + nq]), g == 0),
                        dict(hb_colB=HT_IDX[(g, r_, 2)]))
            elif job_['kind'] == 'm':
                hb = HT_IDX[(g, r_, 0)] if job_['bm'] == 0 else NHB - 1
                args = ((QB, 0, QB, prev_['kT'], prev_['kTk'], prev_['vS'], prev_['vSk'],
                         cur['kT'], cur['kTk'], cur['vS'], cur['vSk'], cur['qT'], cur['qTk'],
                         hb, (lambda acc, t0_=t0_: acc[:, :, t0_:t0_ + d * QB:d]), g == 0), {})
            do_tr.prev = cur
            return args
        do_tr.prev = None

        for i, job in enumerate(jobs):
            nj = jobs[i + 1] if i + 1 < len(jobs) else None
            pa = job_prepA(nj) if (nj is not None and nj['kind'] != 'm') else None
            if T2 is not None:
                dstate = attend_scores(*T2[0], **T2[1])
            r = job['r']
            t0 = None
            if job['kind'] != 'm':
                hh, hhk = job['hh']
                wq = (job['kind'] == 'h' and r in ERES[d])
                M = 128
                tile_proj(lambda kc, hh=hh: hh[:, kc, :], [hhk], M, wq)
                res = tile_norm(M, wq, None)
            else:
                bm = job['bm']
                t0 = r + d * QB * bm
                if g == 0:
                    orow = (lambda o: o[0:128, :]) if bm == 7 else None
                elif g == 1:
                    orow = (lambda o, r=r: o[r:512:4, :]) if bm == 1 else None
                else:
                    orow = (lambda o, r=r: o[r:1024:16, :])
                wq = True
                M = QB
                tile_proj(lambda kc, t0=t0: hT[:, kc, t0:t0 + d * QB:d], ['hT'], M, True)
                res = tile_norm(M, True, orow)
            if T2 is not None:
                attend_pv(dstate, *T2[0], **T2[1])
                T2 = None
            if pa is not None:
                job_prepT(nj, pa)
            if T1 is not None:
                T2 = do_tr(T1)
            T1 = (job, res, M, wq, t0)
        if T2 is not None:
            attend(*T2[0], **T2[1])
            T2 = None
        T2 = do_tr(T1)
        if T2 is not None:
            attend(*T2[0], **T2[1])
            T2 = None
        _ck(4 + 2 * g)
    _ck(10)
    for (c0, n) in TOKB:
        for h in range(4):
            B.op('act', lambda e, h=h, c0=c0, n=n: e.activation(out=acc_D[:, h, c0:c0 + n], in_=acc_D[:, h, c0:c0 + n], func=AF.Ln),
                 R=['acc_D'], W=['acc_D'])
            B.op('act', lambda e, h=h, c0=c0, n=n: e.activation(out=acc_D[:, h, c0:c0 + n], in_=acc_D[:, h, c0:c0 + n], func=AF.Exp, scale=-1.0),
                 R=['acc_D'], W=['acc_D'])
            B.op('dve', lambda e, h=h, c0=c0, n=n: e.tensor_tensor(out=o_attn[:, h, c0:c0 + n], in0=acc_O[:, h, c0:c0 + n],
                                                                  in1=acc_D[:, h, c0:c0 + n], op=ALU.mult),
                 R=['acc_O', 'acc_D'], W=['o_attn'])
    import os
    if os.environ.get('KDEBUG'):
        dbg = B.dout("dbg_oattn", [128, 4 * NT], BF16)
        B.dma('sp', dbg[:, :], o_attn[:, :, :].rearrange("p h t -> p (h t)"), R=['o_attn'], sem='dbg')
    B.barrier()


    _ck(20)
    st2 = ExitStack()
    set_ptr(60.5)
    new_ring(st2, 8, "c")
    u_p = al("u_p", [128, 8, 1056], BF16)
    u_s = al("u_s", [128, 8, 152], BF16)
    utail = al("utail", [128, 8, 64], F32)
    wdwT = al("wdwT", [128, 8, 31], F32)
    dg_l = [al("dg%d" % i, [128, 31, 128], BF16) for i in range(2)]
    c_f = al("c_f", [128, 8, NT], F32)
    tmpA = [al("tmpA%d" % i, [128, 512], F32) for i in range(2)]
    tmpB = [al("tmpB%d" % i, [128, 512], F32) for i in range(2)]
    mu_t = al("mu_t", [128, NT], F32)
    rs_t = al("rs_t", [128, NT], F32)
    ld32 = al("ld32", [128, 1024], F32)
    ones_f = al("ones_f", [128, 128], F32)
    uo = al("uo", [64, 1024], F32)
    bank_i = {'i': 0}

    def nbank():
        i = bank_i['i']
        bank_i['i'] = (i + 1) % 8
        return psb[i], 'ps%d' % i

    B.op('dve', lambda e: e.memset(ones_f[:], 1.0), W=['ones_f'])
    B.dma('sp', ld32[0:31, :], w_dw[:, :], W=['ld32'])
    bk, bkk = nbank()
    for j in range(8):
        B.op('pe', lambda e, j=j: e.transpose(out=bk[:, j * 31:(j + 1) * 31], in_=ld32[0:31, j * 128:(j + 1) * 128],
                                              identity=ident_f[0:31, 0:31]), R=['ld32', 'ident_f'], W=[bkk], inc=(j == 7))
    B.op('act', lambda e: e.activation(out=wdwT[:, :, :], in_=bk[:, 0:248].rearrange("p (j k) -> p j k", j=8), func=AF.Copy),
         R=[bkk], W=['wdwT'])
    B.dma('sp', ld32[0:120, :], st_conv.rearrange("b p c -> (b p) c"), W=['ld32'])
    for half in range(2):
        bk, bkk = nbank()
        for jj in range(4):
            j = half * 4 + jj
            B.op('pe', lambda e, j=j, jj=jj: e.transpose(out=bk[:, jj * 120:(jj + 1) * 120], in_=ld32[0:120, j * 128:(j + 1) * 128],
                                                         identity=ident_f[0:120, 0:120]), R=['ld32', 'ident_f'], W=[bkk], inc=(jj == 3))
        for jj in range(4):
            j = half * 4 + jj
            B.op('act', lambda e, j=j, jj=jj: e.activation(
                out=u_s[:, j, 0:120].rearrange("p (pos b) -> p b pos", b=4),
                in_=bk[:, jj * 120:(jj + 1) * 120].rearrange("p (b pos) -> p b pos", b=4), func=AF.Copy), R=[bkk], W=['u_s'])

    def proj_fm(panels, jcol, rhs_fn, rkeys, N, nk_total=16):
        bk, bkk = nbank()
        n = 0
        for (view, keys, kb) in panels:
            nk = view.shape[1]
            for k in range(nk):
                B.op('pe', lambda e, view=view, k=k, kc=kb + k, n=n: e.matmul(
                    bk[:, 0:N], view[:, k, jcol:jcol + 128], rhs_fn(kc), start=(n == 0), stop=(n == nk_total - 1)),
                    R=rkeys + keys, W=[bkk], inc=(n == nk_total - 1))
                n += 1
        return bk, bkk

    tblocks = [('m0', lambda kc: hT[:, kc, 0:512], ['hT'], 512), ('m1', lambda kc: hT[:, kc, 512:1024], ['hT'], 512),
               ('s', lambda kc: hT[:, kc, 1024:1056], ['hT'], 32), ('h', lambda kc: hT_h1[:, kc, :], ['hT_h1'], 32)]
    for cbk in range(2):
        Wa = load_w16(w_in, C_A + cbk * 512)
        Wb = load_w16(w_in, C_B + cbk * 512)
        for jj in range(4):
            J = cbk * 4 + jj
            for (nm, rf, rk, N) in tblocks:
                pa, pak = proj_fm(Wa, jj * 128, rf, rk, N)
                pb, pbk = proj_fm(Wb, jj * 128, rf, rk, N)
                tA, tAk = nxt('tmpA', tmpA)
                B.op('act', lambda e, pb=pb, tA=tA, N=N: e.activation(out=tA[:, 0:N], in_=pb[:, 0:N], func=AF.Sigmoid), R=[pbk], W=[tAk])
                if nm == 'm0':
                    dst = u_p[:, J, 32:544]
                elif nm == 'm1':
                    dst = u_p[:, J, 544:1056]
                elif nm == 'h':
                    dst = u_p[:, J, 0:32]
                else:
                    dst = u_s[:, J, 120:152].rearrange("p (t b) -> p b t", b=4)
                src_a = pa[:, 0:N] if nm != 's' else pa[:, 0:32].rearrange("p (b t) -> p b t", b=4)
                src_s = tA[:, 0:N] if nm != 's' else tA[:, 0:32].rearrange("p (b t) -> p b t", b=4)
                B.op('dve', lambda e, dst=dst, src_a=src_a, src_s=src_s: e.tensor_tensor(out=dst, in0=src_a, in1=src_s, op=ALU.mult),
                     R=[pak, tAk], W=['u_p' if nm != 's' else 'u_s'])
                if nm == 'm1':
                    B.op('dve', lambda e, pa=pa, tA=tA, J=J: e.tensor_tensor(out=utail[:, J, 0:32], in0=pa[:, 480:512], in1=tA[:, 480:512], op=ALU.mult),
                         R=[pak, tAk], W=['utail'])
                if nm == 's':
                    B.op('dve', lambda e, pa=pa, tA=tA, J=J: e.tensor_tensor(out=utail[:, J, 32:64], in0=pa[:, 0:32], in1=tA[:, 0:32], op=ALU.mult),
                         R=[pak, tAk], W=['utail'])
    for half in range(2):
        bk, bkk = nbank()
        for jj in range(4):
            J = half * 4 + jj
            B.op('pe', lambda e, J=J, jj=jj: e.transpose(out=bk[0:64, jj * 128:(jj + 1) * 128], in_=utail[:, J, :], identity=ident_f[:, :]),
                 R=['utail', 'ident_f'], W=[bkk], inc=(jj == 3))
        B.op('act', lambda e, half=half, bk=bk: e.activation(out=uo[:, half * 512:(half + 1) * 512], in_=bk[0:64, :], func=AF.Copy), R=[bkk], W=['uo'])
    B.dma('sp', o_pconv[:, :], uo[0:32, :], R=['uo'])
    for b in range(4):
        B.dma('sp', o_sconv[b, 22:30, :], uo[32 + b * 8:40 + b * 8, :], R=['uo'])
    def build_dg(J):
        dgb = dg_l[J % 2]
        for k in range(31):
            B.op('dve', lambda e, k=k, J=J, dgb=dgb: e.tensor_scalar(out=dgb[:, k, :], in0=ident_b[:, :], scalar1=wdwT[:, J, k:k + 1], scalar2=None, op0=ALU.mult),
                 R=['ident_b', 'wdwT'], W=['dg%d' % (J % 2)])

    build_dg(0)
    for J in range(8):
        if J + 1 < 8:
            build_dg(J + 1)
        dg = dg_l[J % 2]
        dgk = 'dg%d' % (J % 2)
        outs = [(lambda k: u_p[:, J, 2 + k:2 + k + 512], 512, c_f[:, J, 0:512], None),
                (lambda k: u_p[:, J, 514 + k:514 + k + 512], 512, c_f[:, J, 512:1024], None),
                (lambda k: u_p[:, J, k:k + 2], 2, c_f[:, J, 1056:1058], None),
                (lambda k: u_s[:, J, k * 4:k * 4 + 32], 32, c_f[:, J, 1024:1056].rearrange("p (b t) -> p b t", b=4), 's')]
        for (rf, N, dst, kind) in outs:
            bk, bkk = nbank()
            for k in range(31):
                B.op('pe', lambda e, k=k, rf=rf, bk=bk, N=N, dg=dg: e.matmul(bk[:, 0:N], dg[:, k, :], rf(k), start=(k == 0), stop=(k == 30)),
                     R=[dgk, 'u_p', 'u_s'], W=[bkk], inc=(k == 30))
            src = bk[:, 0:N] if kind is None else bk[:, 0:32].rearrange("p (t b) -> p b t", b=4)
            B.op('act', lambda e, dst=dst, src=src, J=J: e.activation(out=dst, in_=src, func=AF.Identity, bias=cpar[:, 0, J:J + 1], scale=1.0),
                 R=[bkk, 'cpar'], W=['c_f'])
    for (c0, N) in TOKE:
        b1, b1k = nbank()
        b2, b2k = nbank()
        for J in range(8):
            tA, tAk = nxt('tmpA', tmpA)
            B.op('act', lambda e, tA=tA, J=J: e.activation(out=tA[:, 0:N], in_=c_f[:, J, c0:c0 + N], func=AF.Square), R=['c_f'], W=[tAk])
            B.op('pe', lambda e, J=J: e.matmul(b1[:, 0:N], ones_f[:, :], c_f[:, J, c0:c0 + N], start=(J == 0), stop=(J == 7)),
                 R=['ones_f', 'c_f'], W=[b1k], inc=(J == 7))
            B.op('pe', lambda e, J=J, tA=tA: e.matmul(b2[:, 0:N], ones_f[:, :], tA[:, 0:N], start=(J == 0), stop=(J == 7)),
                 R=['ones_f', tAk], W=[b2k], inc=True)
        B.op('act', lambda e: e.activation(out=mu_t[:, c0:c0 + N], in_=b1[:, 0:N], func=AF.Copy, scale=1.0 / 1024), R=[b1k], W=['mu_t'])
        tB, tBk = nxt('tmpB', tmpB)
        B.op('dve', lambda e, tB=tB: e.tensor_tensor(out=tB[:, 0:N], in0=mu_t[:, c0:c0 + N], in1=mu_t[:, c0:c0 + N], op=ALU.mult), R=['mu_t'], W=[tBk])
        B.op('dve', lambda e, tB=tB: e.scalar_tensor_tensor(out=tB[:, 0:N], in0=b2[:, 0:N], scalar=1.0 / 1024, in1=tB[:, 0:N],
                                                           op0=ALU.mult, op1=ALU.subtract), R=[b2k, tBk], W=[tBk])
        B.op('act', lambda e, tB=tB: e.activation(out=tB[:, 0:N], in_=tB[:, 0:N], func=AF.Ln, bias=epsc[:, :], scale=1.0), R=[tBk, 'epsc'], W=[tBk])
        B.op('act', lambda e, tB=tB: e.activation(out=rs_t[:, c0:c0 + N], in_=tB[:, 0:N], func=AF.Exp, scale=-0.5), R=[tBk], W=['rs_t'])
    for (c0, N) in TOKE:
        for J in range(8):
            tA, tAk = nxt('tmpA', tmpA)
            B.op('dve', lambda e, tA=tA, J=J: e.tensor_tensor(out=tA[:, 0:N], in0=c_f[:, J, c0:c0 + N], in1=mu_t[:, c0:c0 + N], op=ALU.subtract),
                 R=['c_f', 'mu_t'], W=[tAk])
            B.op('dve', lambda e, tA=tA: e.tensor_tensor(out=tA[:, 0:N], in0=tA[:, 0:N], in1=rs_t[:, c0:c0 + N], op=ALU.mult), R=[tAk, 'rs_t'], W=[tAk])
            B.op('act', lambda e, tA=tA, J=J: e.activation(out=c_act[:, J, c0:c0 + N], in_=tA[:, 0:N], func=AF.Silu,
                                                           bias=cpar[:, 2, J:J + 1], scale=cpar[:, 1, J:J + 1]), R=[tAk, 'cpar'], W=['c_act'])
    B.barrier()
    _ck(21)

    st3 = ExitStack()
    set_ptr(77.5)
    new_ring(st3, 8, "d")
    xt_list = [al("xtd%d" % i, [128, D], F32) for i in range(2)]
    ph['xt_i'] = 0
    prep_bufs = (xt_list, [(psb[6], 'ps6'), (psb[7], 'ps7')])
    hmT = al("hmT", [128, KC, 256], BF16)
    kmT = al("kmT", [128, 8, 256], BF16)
    vmb = al("vmb", [128, 2, 1024], BF16)
    qmT = al("qmT", [128, 8, NT], BF16)
    kf_l = [al("kfd%d" % i, [128, 512], F32) for i in range(2)]
    ko_l = [al("kod%d" % i, [128, 512], F32) for i in range(2)]
    kb_l = [al("kbd%d" % i, [128, 512], BF16) for i in range(2)]
    sqs3 = al("sqs3", [128, 512], F32)
    ph['sqs'] = sqs3
    pm_l = [al("pm%d" % i, [128, 512], BF16) for i in range(4)]
    rd_l = [al("rd%d" % i, [128, 512], F32) for i in range(2)]
    kms = [al("kms%d" % i, [128, 2, 1024], BF16) for i in range(1)]
    vms = [al("vms%d" % i, [128, 2, 1024], BF16) for i in range(1)]
    kmTs = [al("kmTs%d" % i, [128, 8, 256], BF16) for i in range(1)]
    rot.clear()
    pT_bf3 = psb[5][:, :].bitcast(BF16)

    def transpose4b(src_bf, skey, M, half, dst, dkeys):
        pk = 'ps5'
        base = half * 512
        for h in range(4):
            B.op('pe', lambda e, h=h: e.transpose(out=pT_bf3[:, base + h * 128: base + h * 128 + M],
                                                  in_=src_bf[0:M, h * 128:(h + 1) * 128], identity=ident_b[0:M, 0:M]),
                 R=[skey, 'ident_b'], W=[pk], inc=(h == 3))
        src = pT_bf3[:, base:base + 512].rearrange("p (h t) -> p h t", h=4)[:, :, 0:M]
        B.op('act', lambda e: e.activation(out=dst, in_=src, func=AF.Copy), R=[pk], W=dkeys)

    for mt in range(2):
        prep(prep_bufs, mem_p[mt * 128:(mt + 1) * 128, :], 128, gT_mem, 'gT_mem',
             lambda q, mt=mt: hmT[:, q * 4:q * 4 + 4, mt * 128:(mt + 1) * 128], ['hmT'])
    for cb in range(2):
        Wmk = load_w16(w_mem_k, cb * 512)
        Wmv = load_w16(w_mem_v, cb * 512)
        for mt in range(2):
            proj_tm(lambda kc, mt=mt: hmT[:, kc, mt * 128:(mt + 1) * 128], 128, Wmk, psb[0], 'ps0', ['hmT'])
            proj_tm(lambda kc, mt=mt: hmT[:, kc, mt * 128:(mt + 1) * 128], 128, Wmv, psb[1], 'ps1', ['hmT'])
            kf, kfk = nxt('kfd', kf_l); ko, kok = nxt('kod', ko_l); kb, kbk = nxt('kbd', kb_l)
            head_norm(psb[0], 'ps0', 128, 2, 256, gmk_t, 'gmk', kf, kfk, ko, [kok], kb, [kbk])
            B.dma('sp', o_pmk[mt * 128:(mt + 1) * 128, cb * 512:(cb + 1) * 512], ko[:, :], R=[kok])
            transpose4b(kb, kbk, 128, 0, kmT[:, cb * 4:cb * 4 + 4, mt * 128:(mt + 1) * 128], ['kmT'])
            B.op('act', lambda e, mt=mt, cb=cb: e.activation(out=vmb[:, mt, cb * 512:(cb + 1) * 512], in_=psb[1][:, :], func=AF.Copy), R=['ps1'], W=['vmb'])
            ko2, kok2 = nxt('kod', ko_l)
            B.op('dve', lambda e, ko2=ko2: e.tensor_copy(out=ko2[:, :], in_=psb[1][:, :]), R=['ps1'], W=[kok2])
            B.dma('sp', o_pmv[mt * 128:(mt + 1) * 128, cb * 512:(cb + 1) * 512], ko2[:, :], R=[kok2])
    tiles9 = [(t * 128, 128) for t in range(8)] + [(1024, 34)]
    qdef = [None]
    for cb in range(2):
        Wqm = load_w16(w_in, C_QM + cb * 512)
        for (t0, M) in tiles9:
            proj_tm(lambda kc, t0=t0, M=M: hT[:, kc, t0:t0 + M], M, Wqm, psb[2], 'ps2', ['hT'])
            kf, kfk = nxt('kfd', kf_l); kb, kbk = nxt('kbd', kb_l)
            head_norm(psb[2], 'ps2', M, 2, 256, gmq_t, 'gmq', kf, kfk, None, None, kb, [kbk])
            if qdef[0] is not None:
                qdef[0]()
            qdef[0] = (lambda kb=kb, kbk=kbk, M=M, cb=cb, t0=t0: transpose4b(kb, kbk, M, 1, qmT[:, cb * 4:cb * 4 + 4, t0:t0 + M], ['qmT']))
    qdef[0]()
    scale_mem = 1.0 / 16.0

    mdef = [None]

    def mem_attend(kT_, kTk, v_, vk, c0, N):
        for h in range(4):
            pms = []
            for mt in range(2):
                bk, bkk = psb[mt], 'ps%d' % mt
                for dc in range(2):
                    B.op('pe', lambda e, mt=mt, dc=dc, bk=bk: e.matmul(bk[:, 0:N], kT_[:, h * 2 + dc, mt * 128:(mt + 1) * 128],
                                                                      qmT[:, h * 2 + dc, c0:c0 + N], start=(dc == 0), stop=(dc == 1)),
                         R=[kTk, 'qmT'], W=[bkk], inc=(dc == 1))
                pm, pmk = nxt('pm', pm_l)
                B.op('act', lambda e, pm=pm, bk=bk: e.activation(out=pm[:, 0:N], in_=bk[:, 0:N], func=AF.Exp, scale=scale_mem), R=[bkk], W=[pmk])
                pms.append((pm, pmk))
            B.op('pe', lambda e: e.matmul(psb[4][:, 0:N], ones_b[:, :], pms[0][0][:, 0:N], start=True, stop=False), R=['ones_b', pms[0][1]], W=['ps4'], inc=False)
            B.op('pe', lambda e: e.matmul(psb[4][:, 0:N], ones_b[:, :], pms[1][0][:, 0:N], start=False, stop=True), R=['ones_b', pms[1][1]], W=['ps4'], inc=True)
            rd, rdk = nxt('rd', rd_l)
            B.op('act', lambda e, rd=rd: e.activation(out=rd[:, 0:N], in_=psb[4][:, 0:N], func=AF.Ln), R=['ps4'], W=[rdk])
            B.op('act', lambda e, rd=rd: e.activation(out=rd[:, 0:N], in_=rd[:, 0:N], func=AF.Exp, scale=-1.0), R=[rdk], W=[rdk])

            def stage2(h=h, pms=pms, rd=rd, rdk=rdk):
                for dc in range(2):
                    bk, bkk = psb[2 + dc], 'ps%d' % (2 + dc)
                    for mt in range(2):
                        B.op('pe', lambda e, mt=mt, dc=dc, bk=bk: e.matmul(bk[:, 0:N], v_[:, mt, h * 256 + dc * 128:h * 256 + (dc + 1) * 128],
                                                                          pms[mt][0][:, 0:N], start=(mt == 0), stop=(mt == 1)),
                             R=[vk, pms[mt][1]], W=[bkk], inc=(mt == 1))
                    B.op('dve', lambda e, dc=dc, bk=bk: e.tensor_tensor(out=o_mem[:, h * 2 + dc, c0:c0 + N], in0=bk[:, 0:N], in1=rd[:, 0:N], op=ALU.mult),
                         R=[bkk, rdk], W=['o_mem'])
            if mdef[0] is not None:
                mdef[0]()
            mdef[0] = stage2
        mdef[0]()
        mdef[0] = None

    for (c0, N) in [(0, 512), (512, 512), (1056, 2)]:
        mem_attend(kmT, 'kmT', vmb, 'vmb', c0, N)
    for b in range(4):
        km, kmk = nxt('kms', kms); vm, vmk = nxt('vms', vms); kt, ktk = nxt('kmTs', kmTs)
        B.dma('pool', km[:, :, :], cmk[b].rearrange("(mt p) c -> p mt c", p=128), W=[kmk])
        B.dma('pool', vm[:, :, :], cmv[b].rearrange("(mt p) c -> p mt c", p=128), W=[vmk])
        for mt in range(2):
            for cb in range(2):
                transpose4b(km[:, mt, cb * 512:(cb + 1) * 512], kmk, 128, (mt * 2 + cb) % 2, kt[:, cb * 4:cb * 4 + 4, mt * 128:(mt + 1) * 128], [ktk])
        mem_attend(kt, ktk, vm, vmk, 1024 + b * 8, 8)
    if mdef[0] is not None:
        mdef[0]()
        mdef[0] = None
    B.barrier()
    _ck(22)

    st4 = ExitStack()
    st_x2 = ExitStack()
    set_ptr(77.5)
    m_t = al("m_t", [128, KC, NT], BF16, at=162)
    new_ring(st4, 16, "e")
    sg_l = [al("sg%d" % i, [128, 512], F32) for i in range(6)]
    tt_l = [al("tt%d" % i, [128, 512], F32) for i in range(4)]
    for fp in range(8):
        f0 = fp * 256
        Wg = [[load_panel(w_in, 0, 8, cg + f0, 256) + (0,), load_panel(w_in, 8, 8, cg + f0, 256) + (8,)] for cg in (C_GC, C_GA, C_GM)]
        Wco = [load_panel(w_conv_out, 0, 8, f0, 256) + (0,)]
        Wao = [load_panel(w_attn_out, 0, 4, f0, 256) + (0,)]
        Wmo = [load_panel(w_mem_out, 0, 8, f0, 256) + (0,)]
        for fi in range(2):
            f = fp * 2 + fi
            for (c0, N) in TOKE:
                sgs = []
                for gi in range(3):
                    bk, bkk = proj_fm(Wg[gi], fi * 128, lambda kc: hT[:, kc, c0:c0 + N], ['hT'], N)
                    sg, sgk = nxt('sg', sg_l)
                    B.op('act', lambda e, sg=sg, bk=bk: e.activation(out=sg[:, 0:N], in_=bk[:, 0:N], func=AF.Sigmoid), R=[bkk], W=[sgk])
                    sgs.append((sg, sgk))
                brs = [(Wco, lambda kc: c_act[:, kc, c0:c0 + N], ['c_act'], 8),
                       (Wao, lambda kc: o_attn[:, kc, c0:c0 + N], ['o_attn'], 4),
                       (Wmo, lambda kc: o_mem[:, kc, c0:c0 + N], ['o_mem'], 8)]
                tts = []
                for bi, (Wp, rf, rk, nk) in enumerate(brs):
                    bk, bkk = proj_fm(Wp, fi * 128, rf, rk, N, nk_total=nk)
                    tt, ttk = nxt('tt', tt_l)
                    B.op('dve', lambda e, tt=tt, bk=bk, sg=sgs[bi][0]: e.tensor_tensor(out=tt[:, 0:N], in0=bk[:, 0:N], in1=sg[:, 0:N], op=ALU.mult),
                         R=[bkk, sgs[bi][1]], W=[ttk])
                    tts.append((tt, ttk))
                B.op('dve', lambda e: e.tensor_tensor(out=tts[0][0][:, 0:N], in0=tts[0][0][:, 0:N], in1=tts[1][0][:, 0:N], op=ALU.add),
                     R=[tts[0][1], tts[1][1]], W=[tts[0][1]])
                B.op('dve', lambda e, f=f: e.tensor_tensor(out=m_t[:, f, c0:c0 + N], in0=tts[0][0][:, 0:N], in1=tts[2][0][:, 0:N], op=ALU.add),
                     R=[tts[0][1], tts[2][1]], W=['m_t'])
    B.barrier()
    _ck(23)
    x2 = al("x2", [128, 9, D], F32, at=0)
    set_ptr(72)
    new_ring(None, 12, "e2")
    for t in range(8):
        B.dma('sp', x2[:, t, :], x_main[t * 128:(t + 1) * 128, :], W=['x2_%d' % t])
    B.dma('sp', x2[0:34, 8, :], x_misc[:, :], W=['x2_8'])
    for cb in range(4):
        Wo = load_w16(w_o, cb * 512)
        for ti, (t0, M) in enumerate(tiles9):
            bk, bkk = nbank()
            proj_tm(lambda kc, t0=t0, M=M: m_t[:, kc, t0:t0 + M], M, Wo, bk, bkk, ['m_t'])
            B.op('dve', lambda e, bk=bk, ti=ti, M=M, cb=cb: e.tensor_tensor(out=x2[0:M, ti, cb * 512:(cb + 1) * 512], in0=bk[0:M, :],
                                                                         in1=x2[0:M, ti, cb * 512:(cb + 1) * 512], op=ALU.add),
                 R=[bkk, 'x2_%d' % ti], W=['x2_%d' % ti])
    B.barrier()
    _ck(24)

    st5 = ExitStack()
    set_ptr(72)
    copy_thunks = []
    for g, (d, W) in enumerate(GROUPS):
        for b in range(4):
            copy_thunks.append(lambda g=g, b=b, W=W: B.dma('act', o_swk[g][b, 0:W - 8, :].rearrange("(a r) c -> a (r c)", r=8),
                                                          st_wk[g][b, 8:W, :].rearrange("(a r) c -> a (r c)", r=8), sem='copy'))
            copy_thunks.append(lambda g=g, b=b, W=W: B.dma('act', o_swv[g][b, 0:W - 8, :].rearrange("(a r) c -> a (r c)", r=8),
                                                          st_wv[g][b, 8:W, :].rearrange("(a r) c -> a (r c)", r=8), sem='copy'))
    for b in range(4):
        copy_thunks.append(lambda b=b: B.dma('act', o_sconv[b, 0:22, :], st_conv[b, 8:30, :], sem='copy'))

    new_ring(st5, 12, "f")
    h2T = al("h2T", [128, KC, NT], BF16)
    xs_full = al("xs0", [128, 2112], F32)
    xs_l = [xs_full[:, 0:D]]
    ph['xt_i'] = 0
    prep_bufs = (xs_l, [(psb[6], 'ps6'), (psb[7], 'ps7')])
    esc = al("esc", [128, 1], F32)
    wf = al("wf", [128, 88, 3], F32)
    bf_ = al("bf_", [128, 88], F32)
    sfc = al("sfc", [128, 88, 8], F32)
    uptail = al("uptail", [128, 88, 10], F32)
    ldf = al("ldf", [88, 128], F32)
    upraw = al("upraw", [128, 1026], F32)
    ups = al("ups", [128, 40], F32)
    acc_g = al("acc_g", [128, 1056], F32)
    acc_v = al("acc_v", [128, 1056], F32)
    act_t = al("act_t", [128, 4, 1056], BF16)
    act_alt = xs_full[:, 0:2112].bitcast(BF16).rearrange("p (a b) -> p a b", a=4)
    act_l = [act_t, act_alt]
    stg = xs_l[0]
    B.op('dve', lambda e: e.memset(esc[:, :], 1.0), W=['esc'])
    B.op('dve', lambda e: e.tensor_copy(out=esc[32:34, :], in_=eflag[32:34, :]), R=['eflag'], W=['esc'])
    for k3 in range(3):
        pass
    ldw = acc_g
    for part in range(11):
        B.dma('sp', ldw[0:3, 0:1024], w_ffn_dw[:, part * 1024:(part + 1) * 1024], W=['ldw'])
        bk, bkk = nbank()
        for j in range(8):
            B.op('pe', lambda e, j=j, bk=bk: e.transpose(out=bk[:, j * 3:(j + 1) * 3], in_=ldw[0:3, j * 128:(j + 1) * 128], identity=ident_f[0:3, 0:3]),
                 R=['ldw', 'ident_f'], W=[bkk], inc=(j == 7))
        B.op('act', lambda e, bk=bk, part=part: e.activation(out=wf[:, part * 8:(part + 1) * 8, :], in_=bk[:, 0:24].rearrange("p (j k) -> p j k", j=8), func=AF.Copy),
             R=[bkk], W=['wf'])
    B.dma('sp', ldf[:, :], b_ffn_dw.rearrange("(j p) -> j p", p=128), W=['ldf'])
    bk, bkk = nbank()
    B.op('pe', lambda e: e.transpose(out=bk[:, 0:88], in_=ldf[0:88, :], identity=ident_f[0:88, 0:88]), R=['ldf', 'ident_f'], W=[bkk])
    B.op('act', lambda e: e.activation(out=bf_[:, :], in_=bk[:, 0:88], func=AF.Copy), R=[bkk], W=['bf_'])
    for part in range(11):
        B.dma('sp', ldw[0:8, 0:1024], st_ffn[:, part * 1024:(part + 1) * 1024], W=['ldw'])
        bk, bkk = nbank()
        for j in range(8):
            B.op('pe', lambda e, j=j, bk=bk: e.transpose(out=bk[:, j * 8:(j + 1) * 8], in_=ldw[0:8, j * 128:(j + 1) * 128], identity=ident_f[0:8, 0:8]),
                 R=['ldw', 'ident_f'], W=[bkk], inc=(j == 7))
        B.op('act', lambda e, bk=bk, part=part: e.activation(out=sfc[:, part * 8:(part + 1) * 8, :], in_=bk[:, 0:64].rearrange("p (j k) -> p j k", j=8), func=AF.Copy),
             R=[bkk], W=['sfc'])
    B.barrier()
    for ti, (t0, M) in enumerate(tiles9):
        prep(prep_bufs, None, M, gT_ffn, 'gT_ffn', lambda q, t0=t0, M=M: h2T[:, q * 4:q * 4 + 4, t0:t0 + M], ['h2T'],
             from_sbuf=(x2[:, ti, :], 'x2_%d' % ti), extra_scale=(esc if ti == 8 else None))
    B.barrier()

    def up_chunk(j0, jj, Wug, Wuv, act_cur, actk):
            for iv, (Wp, acc) in enumerate([(Wug, acc_g), (Wuv, acc_v)]):
                Jp = iv * 44 + j0 + jj
                for bi, (c0, N) in enumerate(TOKE):
                    bk, bkk = proj_fm(Wp, jj * 128, lambda kc: h2T[:, kc, c0:c0 + N], ['h2T'], N)
                    m1 = min(c0 + N, 1024)
                    if m1 > c0:
                        B.op('act', lambda e, bk=bk, c0=c0, m1=m1: e.activation(out=upraw[:, 2 + c0:2 + m1], in_=bk[:, 0:m1 - c0], func=AF.Copy), R=[bkk], W=['upraw'])
                    if c0 + N > 1024:
                        so = 1024 - c0
                        B.op('act', lambda e, bk=bk, so=so: e.activation(out=ups[:, 8:40].rearrange("p (t b) -> p b t", b=4),
                                                                         in_=bk[:, so:so + 32].rearrange("p (b t) -> p b t", b=4), func=AF.Copy), R=[bkk], W=['ups'])
                        B.op('act', lambda e, bk=bk, so=so: e.activation(out=upraw[:, 0:2], in_=bk[:, so + 32:so + 34], func=AF.Copy), R=[bkk], W=['upraw'])
                B.op('act', lambda e, Jp=Jp: e.activation(out=ups[:, 0:8].rearrange("p (pos b) -> p b pos", b=4),
                                                          in_=sfc[:, Jp, :].rearrange("p (b pos) -> p b pos", b=4), func=AF.Copy), R=['sfc'], W=['ups'])
                B.op('act', lambda e, Jp=Jp: e.activation(out=uptail[:, Jp, 0:2], in_=upraw[:, 1024:1026], func=AF.Copy), R=['upraw'], W=['uptail'])
                B.op('act', lambda e, Jp=Jp: e.activation(out=uptail[:, Jp, 2:10], in_=ups[:, 32:40], func=AF.Copy), R=['ups'], W=['uptail'])
                for is_s in (0, 1):
                    ak = 'acc_g' if iv == 0 else 'acc_v'
                    if not is_s:
                        dst = acc[:, 0:1024]
                        sk = 'upraw'
                        srcs = [upraw[:, 0:1024], upraw[:, 1:1025], upraw[:, 2:1026]]
                    else:
                        dst = acc[:, 1024:1056].rearrange("p (b t) -> p t b", b=4)
                        sk = 'ups'
                        srcs = [ups[:, o_:o_ + 32].rearrange("p (t b) -> p t b", b=4) for o_ in (0, 4, 8)]
                    B.op('act', lambda e, dst=dst, srcs=srcs, Jp=Jp: e.activation(out=dst, in_=srcs[2], func=AF.Identity,
                                                                               bias=bf_[:, Jp:Jp + 1], scale=wf[:, Jp, 2:3]), R=[sk, 'wf', 'bf_'], W=[ak])
                    for tap in (1, 0):
                        B.op('dve', lambda e, dst=dst, srcs=srcs, Jp=Jp, tap=tap: e.scalar_tensor_tensor(
                            out=dst, in0=srcs[tap], scalar=wf[:, Jp, tap:tap + 1], in1=dst, op0=ALU.mult, op1=ALU.add), R=[sk, 'wf', ak], W=[ak])
            B.op('act', lambda e: e.activation(out=acc_g[:, :], in_=acc_g[:, :], func=AF.Silu), R=['acc_g'], W=['acc_g'])
            B.op('dve', lambda e, jj=jj: e.tensor_tensor(out=act_cur[:, jj, :], in0=acc_g[:, :], in1=acc_v[:, :], op=ALU.mult), R=['acc_g', 'acc_v'], W=[actk])


    def down_group(Wd_, act_cur, actk):
        for ti, (t0, M) in enumerate(tiles9):
            tc0 = t0 if ti < 8 else 1024
            MM = M if ti < 8 else 32
            for cb in range(4):
                bk, bkk = nbank()
                view, keys = Wd_[cb]
                for jj in range(4):
                    B.op('pe', lambda e, jj=jj, bk=bk, view=view: e.matmul(bk[0:MM, :], act_cur[:, jj, tc0:tc0 + MM], view[:, jj, :], start=(jj == 0), stop=(jj == 3)),
                         R=[actk] + keys, W=[bkk], inc=(jj == 3))
                B.op('dve', lambda e, bk=bk, ti=ti, cb=cb: e.tensor_tensor(out=x2[0:MM, ti, cb * 512:(cb + 1) * 512], in0=bk[0:MM, :],
                                                                        in1=x2[0:MM, ti, cb * 512:(cb + 1) * 512], op=ALU.add),
                     R=[bkk, 'x2_%d' % ti], W=['x2_%d' % ti])


    prev_d = None
    for G in range(11):
        j0 = G * 4
        Wug = load_w16(w_up, j0 * 128)
        Wuv = load_w16(w_up, DFF + j0 * 128)
        act_cur = act_l[G % 2]
        actk = 'act%d' % (G % 2)
        up_chunk(j0, 0, Wug, Wuv, act_cur, actk)
        for _ in range(3):
            if copy_thunks:
                copy_thunks.pop(0)()
        if prev_d is not None:
            down_group(*prev_d)
        Wd = [load_panel(w_down, j0, 4, cb * 512, 512) for cb in range(4)]
        for jj in range(1, 4):
            up_chunk(j0, jj, Wug, Wuv, act_cur, actk)
        prev_d = (Wd, act_cur, actk)
    down_group(*prev_d)
    while copy_thunks:
        copy_thunks.pop(0)()
    for t in range(8):
        B.dma('sp', y_main[t * 128:(t + 1) * 128, :], x2[:, t, :], R=['x2_%d' % t])
    B.dma('sp', y_misc[:, :], x2[0:34, 8, :], R=['x2_8'])
    for rnd in range(6):
        nch = 16 if rnd < 5 else 8
        banks = []
        for q in range(nch // 4):
            bk, bkk = nbank()
            for j in range(4):
                Jp = rnd * 16 + q * 4 + j
                B.op('pe', lambda e, Jp=Jp, j=j, bk=bk: e.transpose(out=bk[0:10, j * 128:(j + 1) * 128], in_=uptail[:, Jp, :], identity=ident_f[:, :]),
                     R=['uptail', 'ident_f'], W=[bkk], inc=(j == 3))
            banks.append((bk, bkk))
        for q, (bk, bkk) in enumerate(banks):
            B.op('act', lambda e, bk=bk, q=q: e.activation(out=stg[0:10, q * 512:(q + 1) * 512], in_=bk[0:10, :], func=AF.Copy),
                 R=[bkk], W=['xt0'])
        B.dma('sp', o_pffn[:, rnd * 2048:rnd * 2048 + nch * 128], stg[0:10, 0:nch * 128], R=['xt0'])
    B.barrier()


_CACHE = {}


def _consts():
    ident = np.eye(128, dtype=np.float32)
    ki = np.arange(128)[:, None]
    qi = np.arange(128)[None, :]
    mask = np.zeros((128, 256), np.float32)
    mask[:, 0:128] = np.where(ki >= qi, 0.0, NEG)
    mask[:, 128:256] = np.where(ki <= qi, 0.0, NEG)
    smask = np.full((34, 96), NEG, np.float32)
    for g, (d, W) in enumerate(GROUPS):
        ds = min(d, 8)
        QBs = 8 // ds
        for b in range(4):
            for r in range(ds):
                for q in range(QBs):
                    t = r + d * q
                    col = g * 32 + b * 8 + r * QBs + q
                    for t2 in range(8):
                        if t2 <= t and (t - t2) % d == 0:
                            smask[b * 8 + t2, col] = 0.0
    return ident, mask, smask


def _hbias(c):
    P0 = 1024 * c
    hb = np.zeros((128, NHB), np.float32)
    i = np.arange(128)
    for (g, r, kind), col in HT_IDX.items():
        d, W = GROUPS[g]
        pos = P0 - 128 * d * (2 if kind == 1 else 1) + r + d * i
        hb[:, col] = np.where(pos >= 0, 0.0, NEG)
        if kind in (1, 2) and c == 0:
            hb[:, col] = 0.0
    return hb


def kernel(**inp):
    f = lambda a: np.ascontiguousarray(np.asarray(a, dtype=np.float32))
    if 'B' not in _CACHE:
        _CACHE['B'] = build_program()
    B = _CACHE['B']
    ident, mask, smask = _consts()
    xp = f(inp['x_prompt']); xs = f(inp['x_sample'])
    shared = {
        'c_ident': ident, 'c_mask': mask, 'c_smask': smask,
        'g_mix': f(inp['g_mix'][0]), 'g_ffn': f(inp['g_ffn'][0]), 'g_mem': f(inp['g_mem'][0]),
        'w_in': f(inp['w_in'][0]), 'w_dw': f(inp['w_dw'][0]), 'b_dw': f(inp['b_dw'][0]), 'g_cln': f(inp['g_cln'][0]),
        'b_cln': f(inp['b_cln'][0]), 'w_conv_out': f(inp['w_conv_out'][0]), 'g_q': f(inp['g_q'][0]), 'g_k': f(inp['g_k'][0]),
        'w_attn_out': f(inp['w_attn_out'][0]), 'w_mem_k': f(inp['w_mem_k'][0]), 'w_mem_v': f(inp['w_mem_v'][0]),
        'g_mq': f(inp['g_mq'][0]), 'g_mk': f(inp['g_mk'][0]), 'w_mem_out': f(inp['w_mem_out'][0]), 'w_o': f(inp['w_o'][0]),
        'w_up': f(inp['w_up'][0]), 'w_ffn_dw': f(inp['w_ffn_dw'][0]), 'b_ffn_dw': f(inp['b_ffn_dw'][0]), 'w_down': f(inp['w_down'][0]),
    }
    used = set(B.dram.keys())
    in_maps = []
    for core in range(8):
        bp, c = core // 4, core % 4
        P0 = 1024 * c
        m = dict(shared)
        m['x_main'] = xp[bp, P0:P0 + 1024]
        xh = np.zeros((4096, D), np.float32)
        lo = max(0, P0 - 4096)
        if P0 > 0:
            xh[4096 - (P0 - lo):] = xp[bp, lo:P0]
        m['x_halo'] = np.ascontiguousarray(xh[2048:])
        m['x_far'] = np.ascontiguousarray(np.concatenate([xh[14:2048:16], xh[15:2048:16]], axis=0))
        m['x_misc'] = np.ascontiguousarray(np.concatenate([xs[4 * core:4 * core + 4].reshape(32, D), xh[4094:4096]], axis=0))
        m['hbias'] = _hbias(c)
        m['eflag'] = np.full((128, 1), 1.0 if c > 0 else 0.0, np.float32)
        m['mem_p'] = f(inp['mem_prompt'][bp])
        sl = slice(4 * core, 4 * core + 4)
        m['cmk'] = f(inp['cache_mem_k'][0, sl]).reshape(4, 256, 1024)
        m['cmv'] = f(inp['cache_mem_v'][0, sl]).reshape(4, 256, 1024)
        m['st_conv'] = f(inp['state_conv'][0, sl])
        m['st_ffn'] = f(inp['state_ffn_conv'][0, sl]).reshape(8, 2 * DFF)
        swk = [inp['state_win1_k'], inp['state_win2_k'], inp['state_win3_k']]
        swv = [inp['state_win1_v'], inp['state_win2_v'], inp['state_win3_v']]
        for g, (d, W) in enumerate(GROUPS):
            m['st_wk%d' % g] = f(swk[g][0, sl]).reshape(4, W, 512)
            m['st_wv%d' % g] = f(swv[g][0, sl]).reshape(4, W, 512)
        in_maps.append({k: v for k, v in m.items() if k in used})
    res = run_bass_kernel_spmd(B.nc, in_maps, core_ids=list(range(8)))
    R = res.results
    y_p = np.zeros((2, 4096, D), np.float32)
    y_s = np.zeros((32, 8, D), np.float32)
    for core in range(8):
        bp, c = core // 4, core % 4
        y_p[bp, 1024 * c:1024 * (c + 1)] = R[core]['y_main']
        y_s[4 * core:4 * core + 4] = R[core]['y_misc'][0:32].reshape(4, 8, D)
    last = [3, 7]
    p_conv = np.stack([R[k]['o_pconv'][2:32] for k in last])[None]
    pk = []; pv = []
    for g, (d, W) in enumerate(GROUPS):
        if g < 2:
            kk = np.stack([R[k]['o_k%d' % g] for k in last]); vv = np.stack([R[k]['o_v%d' % g] for k in last])
        else:
            kk = np.stack([np.concatenate([R[k - 1]['o_k2'], R[k]['o_k2']], 0) for k in last])
            vv = np.stack([np.concatenate([R[k - 1]['o_v2'], R[k]['o_v2']], 0) for k in last])
        pk.append(kk.reshape(1, 2, W, 4, 128)); pv.append(vv.reshape(1, 2, W, 4, 128))
    p_ffn = np.stack([R[k]['o_pffn'][0:2] for k in last])[None]
    p_mk = np.stack([R[k]['o_pmk'] for k in last]).reshape(1, 2, 256, 4, 256)
    p_mv = np.stack([R[k]['o_pmv'] for k in last]).reshape(1, 2, 256, 4, 256)
    s_conv = np.concatenate([R[k]['o_sconv'] for k in range(8)], 0)[None]
    sk = []; sv = []
    for g, (d, W) in enumerate(GROUPS):
        sk.append(np.concatenate([R[k]['o_swk%d' % g] for k in range(8)], 0).reshape(1, 32, W, 4, 128))
        sv.append(np.concatenate([R[k]['o_swv%d' % g] for k in range(8)], 0).reshape(1, 32, W, 4, 128))
    s_ffn = np.concatenate([R[k]['o_pffn'][2:10].reshape(2, 4, 2 * DFF).transpose(1, 0, 2) for k in range(8)], 0)[None]
    return (y_p, y_s, p_conv, pk[0], pv[0], pk[1], pv[1], pk[2], pv[2], p_ffn, p_mk, p_mv,
            s_conv, sk[0], sv[0], sk[1], sv[1], sk[2], sv[2], s_ffn)
```

```python
import numpy as np
from contextlib import ExitStack
import concourse.bass as bass
import concourse.mybir as mybir
from concourse.bass_utils import run_bass_kernel_spmd

F32 = mybir.dt.float32
BF16 = mybir.dt.bfloat16
AF = mybir.ActivationFunctionType
ALU = mybir.AluOpType
AX = mybir.AxisListType

NEG = -30000.0
EPS = 1e-6
D = 2048
KC = 16
NT = 1058
TOKB = [(0, 512), (512, 512), (1024, 34)]
TOKE = [(0, 352), (352, 354), (706, 352)]
GROUPS = [(1, 128), (4, 512), (16, 2048)]
ERES = {1: [0], 4: [2, 3], 16: [14, 15]}
C_A, C_B, C_Q, C_K, C_V, C_QM, C_GC, C_GA, C_GM = 0, 1024, 2048, 3584, 5120, 6656, 7680, 9728, 11776
DFF = 5632
RING_UNITS = 12
UNIT = 2048


def halo_tiles():
    out = []
    for g, (d, W) in enumerate(GROUPS):
        for r in range(d):
            if r in ERES[d]:
                out.append((g, r, 1))
                out.append((g, r, 2))
            out.append((g, r, 0))
    return out


HT_LIST = halo_tiles()
HT_IDX = {t: i for i, t in enumerate(HT_LIST)}
NHB = len(HT_LIST) + 1


class Builder:
    def __init__(self):
        self.nc = bass.Bass("TRN2", target_bir_lowering=False)
        nc = self.nc
        self.root = ExitStack()
        self.eng = {'pe': nc.tensor, 'act': nc.scalar, 'dve': nc.vector, 'pool': nc.gpsimd, 'sp': nc.sync}
        self.esem = {}
        self.ecount = {}
        for e in self.eng:
            self.esem[e] = self.root.enter_context(nc.semaphore("e_" + e))
            self.ecount[e] = 0
        self.waited = {e: {} for e in self.eng}
        self.last_w = {}
        self.readers = {}
        self.dsems = {}
        self.dram = {}
        self.ring_pos = 0
        self.n_inst = 0

    def din(self, name, shape, dtype=F32):
        t = self.nc.dram_tensor(name, list(shape), dtype, kind="ExternalInput")
        self.dram[name] = t.ap()
        return self.dram[name]

    def dout(self, name, shape, dtype=F32):
        t = self.nc.dram_tensor(name, list(shape), dtype, kind="ExternalOutput")
        self.dram[name] = t.ap()
        return self.dram[name]

    def sb(self, stack, name, shape, dtype):
        return stack.enter_context(self.nc.sbuf_tensor(name, list(shape), dtype))

    def ps(self, stack, name, shape, dtype):
        return stack.enter_context(self.nc.psum_tensor(name, list(shape), dtype))

    def _deps(self, reads, writes):
        toks = []
        for k in list(reads) + list(writes):
            if k in self.last_w:
                toks.append(self.last_w[k])
        for k in writes:
            toks.extend(self.readers.get(k, ()))
        return toks

    def _wait(self, e, toks):
        w = self.waited[e]
        need = {}
        for (sem, val, sid) in toks:
            if w.get(sid, 0) >= val:
                continue
            if sid == 'e_' + e and (e == 'pe' or val > self.ecount[e]):
                continue
            if need.get(sid, (None, 0))[1] < val:
                need[sid] = (sem, val)
        for sid, (sem, val) in need.items():
            self.eng[e].wait_ge(sem, val)
            w[sid] = val

    def _record(self, tok, reads, writes):
        for k in writes:
            self.last_w[k] = tok
            self.readers[k] = []
        for k in reads:
            self.readers.setdefault(k, []).append(tok)
            if len(self.readers[k]) > 24:
                best = {}
                for t in self.readers[k]:
                    if t[2] not in best or best[t[2]][1] < t[1]:
                        best[t[2]] = t
                self.readers[k] = list(best.values())

    def op(self, e, fn, R=(), W=(), inc=True):
        self._wait(e, self._deps(R, W))
        ins = fn(self.eng[e])
        self.n_inst += 1
        if inc:
            ins.then_inc(self.esem[e], 1)
            self.ecount[e] += 1
            tok = (self.esem[e], self.ecount[e], 'e_' + e)
        else:
            tok = (self.esem[e], self.ecount[e] + 1, 'e_' + e)
        self._record(tok, R, W)
        return ins

    def dsem(self, name):
        if name not in self.dsems:
            self.dsems[name] = [self.root.enter_context(self.nc.semaphore("d_" + name)), 0]
        return self.dsems[name]

    def dma(self, q, out, in_, R=(), W=(), sem=None, **kw):
        self._wait(q, self._deps(R, W))
        if sem is None:
            sem = (list(W) + list(R))[0]
        s = self.dsem(sem)
        ins = self.eng[q].dma_start(out=out, in_=in_, **kw)
        ins.then_inc(s[0], 16)
        s[1] += 16
        self.n_inst += 1
        tok = (s[0], s[1], 'd_' + sem)
        self._record(tok, R, W)
        return ins

    def barrier(self):
        toks = [(self.esem[e], self.ecount[e], 'e_' + e) for e in self.eng if self.ecount[e] > 0]
        toks += [(s[0], s[1], 'd_' + n) for n, s in self.dsems.items() if s[1] > 0]
        for e in self.eng:
            self._wait(e, toks)
        self.last_w.clear()
        self.readers.clear()

    def finish(self):
        toks = [(s[0], s[1], 'd_' + n) for n, s in self.dsems.items() if s[1] > 0]
        toks += [(self.esem[e], self.ecount[e], 'e_' + e) for e in self.eng if self.ecount[e] > 0]
        self._wait('sp', toks)


class _Stop(Exception):
    pass


def _ck(n):
    import os
    if int(os.environ.get('KSTOP', '999')) == n:
        raise _Stop()


def build_program():
    B = Builder()
    try:
        _build_body(B)
    except _Stop:
        pass
    B.finish()
    return B


def _build_body(B):
    nc = B.nc
    root = B.root
    x_main = B.din("x_main", [1024, D])
    x_halo = B.din("x_halo", [2048, D])
    x_far = B.din("x_far", [256, D])
    x_misc = B.din("x_misc", [34, D])
    hbias_d = B.din("hbias", [128, NHB])
    eflag_d = B.din("eflag", [128, 1])
    mem_p = B.din("mem_p", [256, D])
    cmk = B.din("cmk", [4, 256, 1024])
    cmv = B.din("cmv", [4, 256, 1024])
    st_conv = B.din("st_conv", [4, 30, 1024])
    st_ffn = B.din("st_ffn", [8, 2 * DFF])
    st_wk = [B.din("st_wk%d" % g, [4, W, 512]) for g, (d, W) in enumerate(GROUPS)]
    st_wv = [B.din("st_wv%d" % g, [4, W, 512]) for g, (d, W) in enumerate(GROUPS)]
    c_ident = B.din("c_ident", [128, 128])
    c_mask = B.din("c_mask", [128, 256])
    c_smask = B.din("c_smask", [34, 96])
    g_mix = B.din("g_mix", [D]); g_ffn = B.din("g_ffn", [D]); g_mem = B.din("g_mem", [D])
    w_in = B.din("w_in", [D, 13824])
    w_dw = B.din("w_dw", [31, 1024]); b_dw = B.din("b_dw", [1024]); g_cln = B.din("g_cln", [1024]); b_cln = B.din("b_cln", [1024])
    w_conv_out = B.din("w_conv_out", [1024, D])
    g_q = B.din("g_q", [128]); g_k = B.din("g_k", [128])
    w_attn_out = B.din("w_attn_out", [512, D])
    w_mem_k = B.din("w_mem_k", [D, 1024]); w_mem_v = B.din("w_mem_v", [D, 1024])
    g_mq = B.din("g_mq", [256]); g_mk = B.din("g_mk", [256])
    w_mem_out = B.din("w_mem_out", [1024, D])
    w_o = B.din("w_o", [D, D])
    w_up = B.din("w_up", [D, 2 * DFF]); w_ffn_dw = B.din("w_ffn_dw", [3, 2 * DFF]); b_ffn_dw = B.din("b_ffn_dw", [2 * DFF])
    w_down = B.din("w_down", [DFF, D])

    y_main = B.dout("y_main", [1024, D])
    y_misc = B.dout("y_misc", [34, D])
    o_pconv = B.dout("o_pconv", [32, 1024])
    o_k = [B.dout("o_k%d" % g, [n, 512]) for g, n in enumerate([128, 512, 1024])]
    o_v = [B.dout("o_v%d" % g, [n, 512]) for g, n in enumerate([128, 512, 1024])]
    o_pffn = B.dout("o_pffn", [10, 2 * DFF])
    o_pmk = B.dout("o_pmk", [256, 1024]); o_pmv = B.dout("o_pmv", [256, 1024])
    o_sconv = B.dout("o_sconv", [4, 30, 1024])
    o_swk = [B.dout("o_swk%d" % g, [4, W, 512]) for g, (d, W) in enumerate(GROUPS)]
    o_swv = [B.dout("o_swv%d" % g, [4, W, 512]) for g, (d, W) in enumerate(GROUPS)]

    ident_f = B.sb(root, "ident_f", [128, 128], F32)
    ident_b = B.sb(root, "ident_b", [128, 128], BF16)
    ones_b = B.sb(root, "ones_b", [128, 128], BF16)
    mask_b = B.sb(root, "mask_b", [128, 256], BF16)
    smask_b = B.sb(root, "smask_b", [34, 96], BF16)
    hbias = B.sb(root, "hbias_s", [128, NHB], F32)
    eflag = B.sb(root, "eflag_s", [128, 1], F32)
    epsc = B.sb(root, "epsc", [128, 1], F32)
    gT_mix = B.sb(root, "gT_mix", [128, KC], F32)
    gT_ffn = B.sb(root, "gT_ffn", [128, KC], F32)
    gT_mem = B.sb(root, "gT_mem", [128, KC], F32)
    gq_t = B.sb(root, "gq_t", [128, 128], F32)
    gk_t = B.sb(root, "gk_t", [128, 128], F32)
    gmq_t = B.sb(root, "gmq_t", [128, 256], F32)
    gmk_t = B.sb(root, "gmk_t", [128, 256], F32)
    stat = B.sb(root, "stat", [128, 192], F32)
    cpar = B.sb(root, "cpar", [128, 3, 8], F32)
    junk = B.sb(root, "junk", [128, 2048], BF16)
    psb = [B.ps(root, "ps%d" % i, [128, 512], F32) for i in range(8)]

    ARENA_KB = 196
    arena = B.sb(root, "arena", [128, ARENA_KB * 256], F32)
    apos = {'p': 0}

    def set_ptr(kb):
        apos['p'] = int(kb * 1024)

    def al(name, shape, dtype, at=None):
        esz = 4 if dtype == F32 else 2
        n = 1
        for d_ in shape[1:]:
            n *= d_
        nbytes = (n * esz + 31) // 32 * 32
        off = apos['p'] if at is None else int(at * 1024)
        if at is None:
            apos['p'] = off + nbytes
        assert off + nbytes <= ARENA_KB * 1024, (name, off, nbytes)
        v = arena[:, off // 4:(off + nbytes) // 4]
        if dtype != F32:
            v = v.bitcast(dtype)
        v = v[0:shape[0], 0:n]
        if len(shape) == 3:
            v = v.rearrange("p (a b) -> p a b", a=shape[1])
        return v

    ph = {'stat_i': 0}

    def stat_slot(n=4):
        i = ph['stat_i']
        ph['stat_i'] = (i + 12) % 192
        return i

    B.dma('sp', ident_f[:], c_ident[:], W=['ident_f'], sem='const')
    B.dma('pool', ident_b[:], c_ident[:], W=['ident_b'], sem='constp')
    B.dma('pool', mask_b[:], c_mask[:], W=['mask_b'], sem='constp')
    B.dma('pool', smask_b[:], c_smask[:], W=['smask_b'], sem='constp')
    B.dma('sp', hbias[:], hbias_d[:], W=['hbias'], sem='const')
    B.dma('sp', eflag[:], eflag_d[:], W=['eflag'], sem='const')
    import os
    SK = os.environ.get('KSKIP', '')
    B.dma('sp', gT_mix[:], g_mix.rearrange("(k p) -> p k", p=128), W=['gT_mix'], sem='const', allow_slow_non_contiguous=True)
    for (t, src, key, n) in [(gq_t, g_q, 'gq', 128), (gk_t, g_k, 'gk', 128), (gmq_t, g_mq, 'gmq', 256), (gmk_t, g_mk, 'gmk', 256)]:
        if 'b' in SK: break
        B.dma('sp', t[:], src.partition_broadcast(128), W=[key], sem='const')
    B.op('dve', lambda e: e.memset(ones_b[:], 1.0), W=['ones_b'])
    B.op('dve', lambda e: e.memset(epsc[:], EPS), W=['epsc'])

    _ck(1)
    def new_ring(stack, units, tag):
        ph['ring'] = al("ring_" + tag, [128, units * UNIT], BF16)
        ph['ring_units'] = units
        B.ring_pos = 0

    def ring_alloc(nunits):
        p = B.ring_pos
        if p + nunits > ph['ring_units']:
            p = 0
        B.ring_pos = p + nunits
        return p, ['ring%d' % u for u in range(p, p + nunits)]

    def load_panel(Wd, k0, nk, c0, ncol):
        nun = -(-(nk * ncol) // UNIT)
        p, keys = ring_alloc(nun)
        view = ph['ring'][:, p * UNIT: p * UNIT + nk * ncol].rearrange("p (k n) -> p k n", k=nk)
        src = Wd[k0 * 128:(k0 + nk) * 128, c0:c0 + ncol].rearrange("(k p) n -> p k n", p=128)
        B.dma('pool', view, src, W=keys, sem=keys[0])
        return view, keys

    def load_w16(Wd, c0, ncol=512):
        return [load_panel(Wd, 0, 8, c0, ncol) + (0,), load_panel(Wd, 8, 8, c0, ncol) + (8,)]

    def prepA(stack_bufs, src, rows, from_sbuf=None, extra_scale=None):
        xt_list, pst = stack_bufs
        slot = ph.get('xt_i', 0)
        ph['xt_i'] = (slot + 1) % len(xt_list)
        xt = xt_list[slot]
        xk = 'xt%d' % slot
        if from_sbuf is None:
            B.dma('sp', xt[0:rows, :], src, W=[xk])
            xin = xt
            rk = [xk]
        else:
            xin, rk0 = from_sbuf
            rk = [rk0]
        si = stat_slot(4)
        sk = 'stat%d' % si
        B.op('act', lambda e: e.activation(out=junk[0:rows, :], in_=xin[0:rows, :], func=AF.Square,
                                           accum_out=stat[0:rows, si:si + 1]), R=rk, W=[sk])
        B.op('act', lambda e: e.activation(out=stat[0:rows, si + 1:si + 2], in_=stat[0:rows, si:si + 1], func=AF.Ln,
                                           bias=epsc[0:rows, :], scale=1.0 / D), R=[sk, 'epsc'], W=[sk])
        B.op('act', lambda e: e.activation(out=stat[0:rows, si + 2:si + 3], in_=stat[0:rows, si + 1:si + 2], func=AF.Exp,
                                           scale=-0.5), R=[sk], W=[sk])
        if extra_scale is not None:
            B.op('dve', lambda e: e.tensor_tensor(out=stat[0:rows, si + 2:si + 3], in0=stat[0:rows, si + 2:si + 3],
                                                  in1=extra_scale[0:rows, :], op=ALU.mult), R=[sk, 'esc'], W=[sk])
        B.op('dve', lambda e: e.tensor_scalar(out=xt[0:rows, :], in0=xin[0:rows, :], scalar1=stat[0:rows, si + 2:si + 3],
                                              scalar2=None, op0=ALU.mult), R=rk + [sk], W=[xk])
        return (xt, xk, rows, pst)

    def prepT(pa, gT, gkey, dst_fn, dst_keys):
        xt, xk, rows, pst = pa
        for q in range(4):
            pi = ph.get('pst_i', 0)
            ph['pst_i'] = (pi + 1) % len(pst)
            bank, bkey = pst[pi]
            for j in range(4):
                kc = q * 4 + j
                B.op('pe', lambda e, kc=kc, j=j: e.transpose(out=bank[:, j * 128:j * 128 + rows],
                                                              in_=xt[0:rows, kc * 128:(kc + 1) * 128],
                                                              identity=ident_f[0:rows, 0:rows]),
                     R=[xk, 'ident_f'], W=[bkey], inc=(j == 3))
            B.op('dve', lambda e, q=q: e.tensor_tensor(
                out=dst_fn(q),
                in0=bank[:, :].rearrange("p (j t) -> p j t", j=4)[:, :, 0:rows],
                in1=gT[:, q * 4:q * 4 + 4].unsqueeze(2).broadcast_to([128, 4, rows]), op=ALU.mult),
                R=[bkey, gkey], W=dst_keys)

    def prep(stack_bufs, src, rows, gT, gkey, dst_fn, dst_keys, from_sbuf=None, extra_scale=None):
        pa = prepA(stack_bufs, src, rows, from_sbuf=from_sbuf, extra_scale=extra_scale)
        prepT(pa, gT, gkey, dst_fn, dst_keys)

    def proj_tm(lhs_fn, M, panels, bank, bkey, rkeys, ncol=512, col_off=0):
        n = 0
        for (view, keys, kb) in panels:
            for k in range(8):
                kc = kb + k
                B.op('pe', lambda e, view=view, k=k, kc=kc, n=n: e.matmul(
                    bank[0:M, col_off:col_off + ncol], lhs_fn(kc), view[:, k, 0:ncol], start=(n == 0), stop=(n == 15)),
                    R=rkeys + keys, W=[bkey], inc=(n == 15))
                n += 1

    def head_norm(bank, bkey, M, nh, hd, g_tile, gkey, kf, kfk, out_f32, of_keys, out_bf, ob_keys):
        w = nh * hd
        B.op('act', lambda e: e.activation(out=kf[0:M, 0:w], in_=bank[0:M, 0:w], func=AF.Copy), R=[bkey], W=[kfk])
        si = stat_slot(12)
        sk = 'stat%d' % si
        sqs = ph['sqs']
        B.op('act', lambda e: e.activation(out=sqs[0:M, 0:w], in_=kf[0:M, 0:w], func=AF.Square), R=[kfk], W=['sqs'])
        B.op('dve', lambda e: e.tensor_reduce(out=stat[0:M, si:si + nh], in_=sqs[0:M, 0:w].rearrange("p (h d) -> p h d", h=nh),
                                              axis=AX.X, op=ALU.add), R=['sqs'], W=[sk])
        B.op('act', lambda e: e.activation(out=stat[0:M, si + 4:si + 4 + nh], in_=stat[0:M, si:si + nh], func=AF.Ln,
                                           bias=epsc[0:M, :], scale=1.0 / hd), R=[sk, 'epsc'], W=[sk])
        B.op('act', lambda e: e.activation(out=stat[0:M, si + 8:si + 8 + nh], in_=stat[0:M, si + 4:si + 4 + nh], func=AF.Exp,
                                           scale=-0.5), R=[sk], W=[sk])
        dst = out_f32 if out_f32 is not None else out_bf
        dk = of_keys if out_f32 is not None else ob_keys
        for h in range(nh):
            B.op('dve', lambda e, h=h: e.scalar_tensor_tensor(
                out=dst[0:M, h * hd:(h + 1) * hd], in0=kf[0:M, h * hd:(h + 1) * hd],
                scalar=stat[0:M, si + 8 + h:si + 9 + h], in1=g_tile[0:M, 0:hd], op0=ALU.mult, op1=ALU.mult),
                R=[kfk, sk, gkey], W=dk)
        if out_f32 is not None and out_bf is not None:
            B.op('act', lambda e: e.activation(out=out_bf[0:M, 0:w], in_=out_f32[0:M, 0:w], func=AF.Copy), R=of_keys, W=ob_keys)

    st_mix = ExitStack()
    st1 = ExitStack()
    hT = al("hT", [128, KC, NT], BF16, at=0)
    o_attn = al("o_attn", [128, 4, NT], BF16, at=34)
    hT_h1 = al("hT_h1", [128, KC, 32], BF16, at=42.5)
    c_act = al("c_act", [128, 8, NT], BF16, at=43.5)
    o_mem = al("o_mem", [128, 8, NT], BF16, at=60.5)
    set_ptr(43.5)
    new_ring(st1, 12, "a")
    xt_list = [al("xt%d" % i, [128, D], F32) for i in range(2)]
    pst = [(psb[6], 'ps6'), (psb[7], 'ps7')]
    prep_bufs = (xt_list, pst)
    hTh = [al("hTh%d" % i, [128, KC, 128], BF16) for i in range(2)]
    acc_O = al("acc_O", [128, 4, NT], F32)
    acc_D = al("acc_D", [128, 4, NT], F32)
    kf_l = [al("kf%d" % i, [128, 512], F32) for i in range(2)]
    ko_l = [al("ko%d" % i, [128, 512], F32) for i in range(3)]
    vo_l = [al("vo%d" % i, [128, 512], F32) for i in range(3)]
    sqs = al("sqs", [128, 512], F32)
    qb_l = [al("qb%d" % i, [128, 512], BF16) for i in range(2)]
    kb_l = [al("kb%d" % i, [128, 512], BF16) for i in range(2)]
    kT_l = [al("kT%d" % i, [128, 4, 128], BF16) for i in range(3)]
    vS_l = [al("vS%d" % i, [128, 512], BF16) for i in range(4)]
    qT_l = [al("qT%d" % i, [128, 4, 128], BF16) for i in range(2)]
    pA_l = [al("pA%d" % i, [128, 512], BF16) for i in range(2)]
    pB_l = [al("pB%d" % i, [128, 512], BF16) for i in range(2)]
    kTs = al("kTs", [128, 4, 34], BF16)
    qTs = al("qTs", [128, 4, 34], BF16)
    vSs = al("vSs", [34, 512], BF16)
    pBs = al("pBs", [34, 4, 32], BF16)
    ksb_l = [al("ksb%d" % i, [128, 512], BF16) for i in range(3)]
    vsb_l = [al("vsb%d" % i, [128, 512], BF16) for i in range(3)]
    kTA_l = [al("kTA%d" % i, [128, 4, 128], BF16) for i in range(2)]
    pAs_l = [al("pAs%d" % i, [128, 4, 8], BF16) for i in range(2)]
    rot = {}
    ph['sqs'] = sqs

    def nxt(name, lst):
        i = rot.get(name, 0)
        rot[name] = (i + 1) % len(lst)
        return lst[i], '%s%d' % (name, i)

    PQ, PK, PV, PT, PSA, PSB, PO, PD = range(8)
    pT_bf = psb[PT][:, :].bitcast(BF16)

    B.barrier()
    a_tiles = [(x_misc[:, :], 34, 1024)] + [(x_main[t * 128:(t + 1) * 128, :], 128, t * 128) for t in range(8)]
    pa_cur = prepA(prep_bufs, a_tiles[0][0], a_tiles[0][1])
    for ti_, (src_, rows_, c0_) in enumerate(a_tiles):
        pa_nxt = prepA(prep_bufs, a_tiles[ti_ + 1][0], a_tiles[ti_ + 1][1]) if ti_ + 1 < len(a_tiles) else None
        prepT(pa_cur, gT_mix, 'gT_mix', lambda q, c0_=c0_, rows_=rows_: hT[:, q * 4:q * 4 + 4, c0_:c0_ + rows_], ['hT'])
        pa_cur = pa_nxt
    for (t_, src_, key_) in [(gT_ffn, g_ffn, 'gT_ffn'), (gT_mem, g_mem, 'gT_mem')]:
        B.dma('sp', t_[:], src_.rearrange("(k p) -> p k", p=128), W=[key_], sem='cparl', allow_slow_non_contiguous=True)
    for i_, src_ in enumerate([b_dw, g_cln, b_cln]):
        B.dma('sp', cpar[:, i_, :], src_.rearrange("(j p) -> p j", p=128), W=['cpar'], sem='cparl', allow_slow_non_contiguous=True)
    _ck(2)

    def transpose4(src_bf, skey, M, half, dst, dkeys, perm=None):
        pk = 'ps%d' % PT
        base = half * 512
        for h in range(4):
            B.op('pe', lambda e, h=h: e.transpose(out=pT_bf[:, base + h * 128: base + h * 128 + M],
                                                  in_=src_bf[0:M, h * 128:(h + 1) * 128], identity=ident_b[0:M, 0:M]),
                 R=[skey, 'ident_b'], W=[pk], inc=(h == 3))
        src = pT_bf[:, base:base + 512].rearrange("p (h t) -> p h t", h=4)[:, :, 0:M]
        if perm is None:
            B.op('act', lambda e: e.activation(out=dst, in_=src, func=AF.Copy), R=[pk], W=dkeys)
        else:
            perm(src, pk)

    scale_att = 1.0 / np.sqrt(128.0)

    def attend_scores(QB, qc0, nq, kTA, kTAk, vA, vAk, kTB, kTBk, vB, vBk, qT, qTk, hb_col, acc_cols_fn, first, hb_colB=None):
        pA, pAk = nxt('pA', pA_l)
        pB, pBk = nxt('pB', pB_l)
        sA, sB = psb[PSA], psb[PSB]
        mA = mask_b[:, qc0:qc0 + nq].unsqueeze(1).broadcast_to([128, 4, nq])
        B.op('pe', lambda e: e.matmul(sA[:, 0:4 * nq], ident_b[:, :], mA, start=True, stop=False),
             R=['ident_b', 'mask_b'], W=['ps%d' % PSA], inc=False)
        for h in range(4):
            B.op('pe', lambda e, h=h: e.matmul(sA[:, h * nq:(h + 1) * nq], kTA[:, h, 0:128], qT[:, h, qc0:qc0 + nq],
                                               start=False, stop=(h == 3)), R=[kTAk, qTk], W=['ps%d' % PSA], inc=(h == 3))
        B.op('act', lambda e: e.activation(out=pA[:, 0:4 * nq], in_=sA[:, 0:4 * nq], func=AF.Exp,
                                           bias=hbias[:, hb_col:hb_col + 1], scale=scale_att),
             R=['ps%d' % PSA, 'hbias'], W=[pAk])
        mB = mask_b[0:QB, 128 + qc0:128 + qc0 + nq].unsqueeze(1).broadcast_to([QB, 4, nq])
        B.op('pe', lambda e: e.matmul(sB[0:QB, 0:4 * nq], ident_b[0:QB, 0:QB], mB, start=True, stop=False),
             R=['ident_b', 'mask_b'], W=['ps%d' % PSB], inc=False)
        for h in range(4):
            B.op('pe', lambda e, h=h: e.matmul(sB[0:QB, h * nq:(h + 1) * nq], kTB[:, h, 0:QB], qT[:, h, qc0:qc0 + nq],
                                               start=False, stop=(h == 3)), R=[kTBk, qTk], W=['ps%d' % PSB], inc=(h == 3))
        cb = NHB - 1 if hb_colB is None else hb_colB
        B.op('act', lambda e: e.activation(out=pB[0:QB, 0:4 * nq], in_=sB[0:QB, 0:4 * nq], func=AF.Exp,
                                           bias=hbias[0:QB, cb:cb + 1], scale=scale_att),
             R=['ps%d' % PSB, 'hbias'], W=[pBk])
        return (pA, pAk, pB, pBk)

    def attend_pv(st, QB, qc0, nq, kTA, kTAk, vA, vAk, kTB, kTBk, vB, vBk, qT, qTk, hb_col, acc_cols_fn, first, hb_colB=None):
        pA, pAk, pB, pBk = st
        oT, dn = psb[PO], psb[PD]
        for h in range(4):
            B.op('pe', lambda e, h=h: e.matmul(oT[:, h * nq:(h + 1) * nq], vA[:, h * 128:(h + 1) * 128], pA[:, h * nq:(h + 1) * nq],
                                               start=True, stop=False), R=[vAk, pAk], W=['ps%d' % PO], inc=False)
            B.op('pe', lambda e, h=h: e.matmul(oT[:, h * nq:(h + 1) * nq], vB[0:QB, h * 128:(h + 1) * 128], pB[0:QB, h * nq:(h + 1) * nq],
                                               start=False, stop=True), R=[vBk, pBk], W=['ps%d' % PO], inc=(h == 3))
        B.op('pe', lambda e: e.matmul(dn[:, 0:4 * nq], ones_b[:, :], pA[:, 0:4 * nq], start=True, stop=False),
             R=['ones_b', pAk], W=['ps%d' % PD], inc=False)
        B.op('pe', lambda e: e.matmul(dn[:, 0:4 * nq], ones_b[0:QB, :], pB[0:QB, 0:4 * nq], start=False, stop=True),
             R=['ones_b', pBk], W=['ps%d' % PD], inc=True)
        for (acc, pbank, pk, ak) in [(acc_O, oT, 'ps%d' % PO, 'acc_O'), (acc_D, dn, 'ps%d' % PD, 'acc_D')]:
            dst = acc_cols_fn(acc)
            src = pbank[:, 0:4 * nq].rearrange("p (h t) -> p h t", h=4)
            if first:
                B.op('act', lambda e, dst=dst, src=src: e.activation(out=dst, in_=src, func=AF.Copy), R=[pk], W=[ak])
            else:
                B.op('dve', lambda e, dst=dst, src=src: e.tensor_tensor(out=dst, in0=src, in1=dst, op=ALU.add), R=[pk, ak], W=[ak])


    def attend(*args, **kw):
        st = attend_scores(*args, **kw)
        attend_pv(st, *args, **kw)

    for g, (d, W) in enumerate(GROUPS):
        QB = min(128, 1024 // d)
        nmb = 1024 // (d * QB)
        Wq = load_w16(w_in, C_Q + g * 512)
        Wk = load_w16(w_in, C_K + g * 512)
        Wv = load_w16(w_in, C_V + g * 512)

        def tile_proj(lhs_fn, lkeys, M, want_q):
            proj_tm(lhs_fn, M, Wk, psb[PK], 'ps%d' % PK, lkeys)
            proj_tm(lhs_fn, M, Wv, psb[PV], 'ps%d' % PV, lkeys)
            if want_q:
                proj_tm(lhs_fn, M, Wq, psb[PQ], 'ps%d' % PQ, lkeys)

        def tile_norm(M, want_q, out_rows):
            res = {}
            kf, kfk = nxt('kf', kf_l)
            kb, kbk = nxt('kb', kb_l)
            if out_rows is not None:
                ko, kok = nxt('ko', ko_l)
                head_norm(psb[PK], 'ps%d' % PK, M, 4, 128, gk_t, 'gk', kf, kfk, ko, [kok], kb, [kbk])
                B.dma('sp', out_rows(o_k[g]), ko[0:M, :], R=[kok])
            else:
                head_norm(psb[PK], 'ps%d' % PK, M, 4, 128, gk_t, 'gk', kf, kfk, None, None, kb, [kbk])
            vS, vSk = nxt('vS', vS_l)
            B.op('act', lambda e: e.activation(out=vS[0:M, :], in_=psb[PV][0:M, :], func=AF.Copy), R=['ps%d' % PV], W=[vSk])
            if out_rows is not None:
                vo, vok = nxt('vo', vo_l)
                B.op('dve', lambda e: e.tensor_copy(out=vo[0:M, :], in_=psb[PV][0:M, :]), R=['ps%d' % PV], W=[vok])
                B.dma('sp', out_rows(o_v[g]), vo[0:M, :], R=[vok])
            res.update(kb=kb, kbk=kbk, vS=vS, vSk=vSk)
            if want_q:
                kf2, kfk2 = nxt('kf', kf_l)
                qb, qbk = nxt('qb', qb_l)
                head_norm(psb[PQ], 'ps%d' % PQ, M, 4, 128, gq_t, 'gq', kf2, kfk2, None, None, qb, [qbk])
                res.update(qb=qb, qbk=qbk)
            return res

        def tile_tr(M, want_q, res):
            kT, kTk = nxt('kT', kT_l)
            transpose4(res['kb'], res['kbk'], M, 0, kT[:, :, 0:M], [kTk])
            res.update(kT=kT, kTk=kTk)
            if want_q:
                qT, qTk = nxt('qT', qT_l)
                transpose4(res['qb'], res['qbk'], M, 1, qT[:, :, 0:M], [qTk])
                res.update(qT=qT, qTk=qTk)
            return res

        proj_tm(lambda kc: hT[:, kc, 1024:1058], 34, Wk, psb[PK], 'ps%d' % PK, ['hT'])
        proj_tm(lambda kc: hT[:, kc, 1024:1058], 34, Wv, psb[PV], 'ps%d' % PV, ['hT'])
        proj_tm(lambda kc: hT[:, kc, 1024:1058], 34, Wq, psb[PQ], 'ps%d' % PQ, ['hT'])
        kf, kfk = nxt('kf', kf_l)
        kb, kbk = nxt('kb', kb_l)
        ko, kok = nxt('ko', ko_l)
        head_norm(psb[PK], 'ps%d' % PK, 34, 4, 128, gk_t, 'gk', kf, kfk, ko, [kok], kb, [kbk])
        for b in range(4):
            B.dma('sp', o_swk[g][b, W - 8:W, :], ko[b * 8:(b + 1) * 8, :], R=[kok])
        transpose4(kb, kbk, 34, 0, kTs[:, :, :], ['kTs'])
        B.op('act', lambda e: e.activation(out=vSs[0:34, :], in_=psb[PV][0:34, :], func=AF.Copy), R=['ps%d' % PV], W=['vSs'])
        vo, vok = nxt('vo', vo_l)
        B.op('dve', lambda e: e.tensor_copy(out=vo[0:34, :], in_=psb[PV][0:34, :]), R=['ps%d' % PV], W=[vok])
        for b in range(4):
            B.dma('sp', o_swv[g][b, W - 8:W, :], vo[b * 8:(b + 1) * 8, :], R=[vok])
        kf2, kfk2 = nxt('kf', kf_l)
        qb, qbk = nxt('qb', qb_l)
        head_norm(psb[PQ], 'ps%d' % PQ, 34, 4, 128, gq_t, 'gq', kf2, kfk2, None, None, qb, [qbk])
        ds = min(d, 8)
        QBs = 8 // ds

        def qperm(src, pk):
            for h in range(4):
                B.op('act', lambda e, h=h: e.activation(
                    out=qTs[:, h, 0:32].rearrange("p (b r q) -> p b q r", b=4, r=ds, q=QBs),
                    in_=src[:, h, 0:32].rearrange("p (b q r) -> p b q r", b=4, q=QBs, r=ds), func=AF.Copy),
                    R=[pk], W=['qTs'])
        transpose4(qb, qbk, 34, 1, None, None, perm=qperm)
        sB = psb[PSB]
        for h in range(4):
            B.op('pe', lambda e, h=h: e.matmul(sB[0:32, h * 32:(h + 1) * 32], kTs[:, h, 0:32], qTs[:, h, 0:32],
                                               start=True, stop=False), R=['kTs', 'qTs'], W=['ps%d' % PSB], inc=False)
            B.op('pe', lambda e, h=h: e.matmul(sB[0:32, h * 32:(h + 1) * 32], ident_b[0:32, 0:32], smask_b[0:32, g * 32:(g + 1) * 32],
                                               start=False, stop=True), R=['ident_b', 'smask_b'], W=['ps%d' % PSB], inc=(h == 3))
        B.op('act', lambda e: e.activation(out=pBs[0:32, :, :], in_=sB[0:32, 0:128].rearrange("p (h t) -> p h t", h=4),
                                           func=AF.Exp, scale=scale_att), R=['ps%d' % PSB], W=['pBs'])
        oTs, dns = psb[PO], psb[PD]
        combos = [(b, r) for b in range(4) for r in range(ds)]

        def s_stageA(b, r):
            ksb, ksbk = nxt('ksb', ksb_l)
            vsb, vsbk = nxt('vsb', vsb_l)
            B.dma('pool', ksb[:, :], st_wk[g][b, r:W:d, :], W=[ksbk])
            B.dma('pool', vsb[:, :], st_wv[g][b, r:W:d, :], W=[vsbk])
            kTA, kTAk = nxt('kTA', kTA_l)
            transpose4(ksb, ksbk, 128, 0, kTA[:, :, :], [kTAk])
            return (kTA, kTAk, vsb, vsbk)

        def s_stageB(b, r, st):
            kTA, kTAk, vsb, vsbk = st
            off = b * 8 + r * QBs
            pAs, pAsk = nxt('pAs', pAs_l)
            sA = psb[PSA]
            for h in range(4):
                B.op('pe', lambda e, h=h: e.matmul(sA[:, h * QBs:(h + 1) * QBs], kTA[:, h, :], qTs[:, h, off:off + QBs],
                                                   start=True, stop=False), R=[kTAk, 'qTs'], W=['ps%d' % PSA], inc=False)
                B.op('pe', lambda e, h=h: e.matmul(sA[:, h * QBs:(h + 1) * QBs], ident_b[:, :], mask_b[:, 0:QBs],
                                                   start=False, stop=True), R=['ident_b', 'mask_b'], W=['ps%d' % PSA], inc=(h == 3))
            B.op('act', lambda e: e.activation(out=pAs[:, :, 0:QBs], in_=sA[:, 0:4 * QBs].rearrange("p (h t) -> p h t", h=4),
                                               func=AF.Exp, scale=scale_att), R=['ps%d' % PSA], W=[pAsk])
            for h in range(4):
                B.op('pe', lambda e, h=h: e.matmul(oTs[:, h * 32 + off:h * 32 + off + QBs], vSs[0:32, h * 128:(h + 1) * 128],
                                                   pBs[0:32, h, off:off + QBs], start=True, stop=False),
                     R=['vSs', 'pBs'], W=['ps%d' % PO], inc=False)
                B.op('pe', lambda e, h=h: e.matmul(oTs[:, h * 32 + off:h * 32 + off + QBs], vsb[:, h * 128:(h + 1) * 128],
                                                   pAs[:, h, 0:QBs], start=False, stop=True),
                     R=[vsbk, pAsk], W=['ps%d' % PO], inc=(h == 3))
            for h in range(4):
                B.op('pe', lambda e, h=h: e.matmul(dns[:, h * 32 + off:h * 32 + off + QBs], ones_b[0:32, :],
                                                   pBs[0:32, h, off:off + QBs], start=True, stop=False),
                     R=['ones_b', 'pBs'], W=['ps%d' % PD], inc=False)
                B.op('pe', lambda e, h=h: e.matmul(dns[:, h * 32 + off:h * 32 + off + QBs], ones_b[:, :],
                                                   pAs[:, h, 0:QBs], start=False, stop=True),
                     R=['ones_b', pAsk], W=['ps%d' % PD], inc=(h == 3))

        st_cur = s_stageA(*combos[0])
        for ci, (b, r) in enumerate(combos):
            st_nxt = s_stageA(*combos[ci + 1]) if ci + 1 < len(combos) else None
            s_stageB(b, r, st_cur)
            st_cur = st_nxt
        for (acc, pbank, pk, ak) in [(acc_O, oTs, 'ps%d' % PO, 'acc_O'), (acc_D, dns, 'ps%d' % PD, 'acc_D')]:
            for h in range(4):
                dst = acc[:, h, 1024:1056].rearrange("p (b q r) -> p b q r", b=4, q=QBs, r=ds)
                src = pbank[:, h * 32:(h + 1) * 32].rearrange("p (b r q) -> p b q r", b=4, r=ds, q=QBs)
                if g == 0:
                    B.op('act', lambda e, dst=dst, src=src: e.activation(out=dst, in_=src, func=AF.Copy), R=[pk], W=[ak])
                else:
                    B.op('dve', lambda e, dst=dst, src=src: e.tensor_tensor(out=dst, in0=src, in1=dst, op=ALU.add), R=[pk, ak], W=[ak])

        _ck(3 + 2 * g)
        jobs = []
        for r in range(d):
            if r in ERES[d]:
                jobs.append(dict(kind='x', r=r, first=True))
            jobs.append(dict(kind='h', r=r, first=(r not in ERES[d])))
            for bm in range(nmb):
                jobs.append(dict(kind='m', r=r, bm=bm, first=False))

        def job_prepA(job):
            r = job['r']
            if job['kind'] == 'h':
                src = x_halo[2048 - 128 * d + r:2048:d, :]
            elif d == 16:
                src = x_far[(r - 14) * 128:(r - 13) * 128, :]
            else:
                src = x_halo[2048 - 256 * d + r:2048 - 128 * d:d, :]
            return prepA(prep_bufs, src, 128)

        def job_prepT(job, pa):
            hh, hhk = nxt('hTh', hTh)
            prepT(pa, gT_mix, 'gT_mix', lambda q, hh=hh: hh[:, q * 4:q * 4 + 4, :], [hhk])
            if job['kind'] == 'h' and d == 1:
                B.op('pool', lambda e, hh=hh: e.tensor_copy(out=hT_h1[:, :, :], in_=hh[:, :, 96:128]), R=[hhk], W=['hT_h1'])
            job['hh'] = (hh, hhk)

        prev = None
        T1 = None
        T2 = None
        if jobs[0]['kind'] != 'm':
            job_prepT(jobs[0], job_prepA(jobs[0]))

        def do_tr(T):
            job_, res_, M_, wq_, t0_ = T
            r_ = job_['r']
            prev_ = None if job_['first'] else do_tr.prev
            cur = tile_tr(M_, wq_, res_)
            args = None
            if job_['kind'] != 'm' and wq_:
                nq = 2 if d == 1 else 1
                qc0 = 128 - nq
                ecol = 1056 + (0 if d == 1 else (r_ - ERES[d][0]))
                args = ((128, qc0, nq, prev_['kT'], prev_['kTk'], prev_['vS'], prev_['vSk'],
                         cur['kT'], cur['kTk'], cur['vS'], cur['vSk'], cur['qT'], cur['qTk'],
                         HT_IDX[(g, r_, 1)], (lambda acc, ecol=ecol, nq=nq: acc[:, :, ecol:ecol + nq]), g == 0),
                        dict(hb_colB=HT_IDX[(g, r_, 2)]))
            elif job_['kind'] == 'm':
                hb = HT_IDX[(g, r_, 0)] if job_['bm'] == 0 else NHB - 1
                args = ((QB, 0, QB, prev_['kT'], prev_['kTk'], prev_['vS'], prev_['vSk'],
                         cur['kT'], cur['kTk'], cur['vS'], cur['vSk'], cur['qT'], cur['qTk'],
                         hb, (lambda acc, t0_=t0_: acc[:, :, t0_:t0_ + d * QB:d]), g == 0), {})
            do_tr.prev = cur
            return args
        do_tr.prev = None

        for i, job in enumerate(jobs):
            nj = jobs[i + 1] if i + 1 < len(jobs) else None
            pa = job_prepA(nj) if (nj is not None and nj['kind'] != 'm') else None
            if T2 is not None:
                dstate = attend_scores(*T2[0], **T2[1])
            r = job['r']
            t0 = None
            if job['kind'] != 'm':
                hh, hhk = job['hh']
                wq = (job['kind'] == 'h' and r in ERES[d])
                M = 128
                tile_proj(lambda kc, hh=hh: hh[:, kc, :], [hhk], M, wq)
                res = tile_norm(M, wq, None)
            else:
                bm = job['bm']
                t0 = r + d * QB * bm
                if g == 0:
                    orow = (lambda o: o[0:128, :]) if bm == 7 else None
                elif g == 1:
                    orow = (lambda o, r=r: o[r:512:4, :]) if bm == 1 else None
                else:
                    orow = (lambda o, r=r: o[r:1024:16, :])
                wq = True
                M = QB
                tile_proj(lambda kc, t0=t0: hT[:, kc, t0:t0 + d * QB:d], ['hT'], M, True)
                res = tile_norm(M, True, orow)
            if T2 is not None:
                attend_pv(dstate, *T2[0], **T2[1])
                T2 = None
            if pa is not None:
                job_prepT(nj, pa)
            if T1 is not None:
                T2 = do_tr(T1)
            T1 = (job, res, M, wq, t0)
        if T2 is not None:
            attend(*T2[0], **T2[1])
            T2 = None
        T2 = do_tr(T1)
        if T2 is not None:
            attend(*T2[0], **T2[1])
            T2 = None
        _ck(4 + 2 * g)
    _ck(10)
    for (c0, n) in TOKB:
        for h in range(4):
            B.op('act', lambda e, h=h, c0=c0, n=n: e.activation(out=acc_D[:, h, c0:c0 + n], in_=acc_D[:, h, c0:c0 + n], func=AF.Ln),
                 R=['acc_D'], W=['acc_D'])
            B.op('act', lambda e, h=h, c0=c0, n=n: e.activation(out=acc_D[:, h, c0:c0 + n], in_=acc_D[:, h, c0:c0 + n], func=AF.Exp, scale=-1.0),
                 R=['acc_D'], W=['acc_D'])
            B.op('dve', lambda e, h=h, c0=c0, n=n: e.tensor_tensor(out=o_attn[:, h, c0:c0 + n], in0=acc_O[:, h, c0:c0 + n],
                                                                  in1=acc_D[:, h, c0:c0 + n], op=ALU.mult),
                 R=['acc_O', 'acc_D'], W=['o_attn'])
    import os
    if os.environ.get('KDEBUG'):
        dbg = B.dout("dbg_oattn", [128, 4 * NT], BF16)
        B.dma('sp', dbg[:, :], o_attn[:, :, :].rearrange("p h t -> p (h t)"), R=['o_attn'], sem='dbg')
    B.barrier()


    _ck(20)
    st2 = ExitStack()
    set_ptr(60.5)
    new_ring(st2, 8, "c")
    u_p = al("u_p", [128, 8, 1056], BF16)
    u_s = al("u_s", [128, 8, 152], BF16)
    utail = al("utail", [128, 8, 64], F32)
    wdwT = al("wdwT", [128, 8, 31], F32)
    dg_l = [al("dg%d" % i, [128, 31, 128], BF16) for i in range(2)]
    c_f = al("c_f", [128, 8, NT], F32)
    tmpA = [al("tmpA%d" % i, [128, 512], F32) for i in range(2)]
    tmpB = [al("tmpB%d" % i, [128, 512], F32) for i in range(2)]
    mu_t = al("mu_t", [128, NT], F32)
    rs_t = al("rs_t", [128, NT], F32)
    ld32 = al("ld32", [128, 1024], F32)
    ones_f = al("ones_f", [128, 128], F32)
    uo = al("uo", [64, 1024], F32)
    bank_i = {'i': 0}

    def nbank():
        i = bank_i['i']
        bank_i['i'] = (i + 1) % 8
        return psb[i], 'ps%d' % i

    B.op('dve', lambda e: e.memset(ones_f[:], 1.0), W=['ones_f'])
    B.dma('sp', ld32[0:31, :], w_dw[:, :], W=['ld32'])
    bk, bkk = nbank()
    for j in range(8):
        B.op('pe', lambda e, j=j: e.transpose(out=bk[:, j * 31:(j + 1) * 31], in_=ld32[0:31, j * 128:(j + 1) * 128],
                                              identity=ident_f[0:31, 0:31]), R=['ld32', 'ident_f'], W=[bkk], inc=(j == 7))
    B.op('act', lambda e: e.activation(out=wdwT[:, :, :], in_=bk[:, 0:248].rearrange("p (j k) -> p j k", j=8), func=AF.Copy),
         R=[bkk], W=['wdwT'])
    B.dma('sp', ld32[0:120, :], st_conv.rearrange("b p c -> (b p) c"), W=['ld32'])
    for half in range(2):
        bk, bkk = nbank()
        for jj in range(4):
            j = half * 4 + jj
            B.op('pe', lambda e, j=j, jj=jj: e.transpose(out=bk[:, jj * 120:(jj + 1) * 120], in_=ld32[0:120, j * 128:(j + 1) * 128],
                                                         identity=ident_f[0:120, 0:120]), R=['ld32', 'ident_f'], W=[bkk], inc=(jj == 3))
        for jj in range(4):
            j = half * 4 + jj
            B.op('act', lambda e, j=j, jj=jj: e.activation(
                out=u_s[:, j, 0:120].rearrange("p (pos b) -> p b pos", b=4),
                in_=bk[:, jj * 120:(jj + 1) * 120].rearrange("p (b pos) -> p b pos", b=4), func=AF.Copy), R=[bkk], W=['u_s'])

    def proj_fm(panels, jcol, rhs_fn, rkeys, N, nk_total=16):
        bk, bkk = nbank()
        n = 0
        for (view, keys, kb) in panels:
            nk = view.shape[1]
            for k in range(nk):
                B.op('pe', lambda e, view=view, k=k, kc=kb + k, n=n: e.matmul(
                    bk[:, 0:N], view[:, k, jcol:jcol + 128], rhs_fn(kc), start=(n == 0), stop=(n == nk_total - 1)),
                    R=rkeys + keys, W=[bkk], inc=(n == nk_total - 1))
                n += 1
        return bk, bkk

    tblocks = [('m0', lambda kc: hT[:, kc, 0:512], ['hT'], 512), ('m1', lambda kc: hT[:, kc, 512:1024], ['hT'], 512),
               ('s', lambda kc: hT[:, kc, 1024:1056], ['hT'], 32), ('h', lambda kc: hT_h1[:, kc, :], ['hT_h1'], 32)]
    for cbk in range(2):
        Wa = load_w16(w_in, C_A + cbk * 512)
        Wb = load_w16(w_in, C_B + cbk * 512)
        for jj in range(4):
            J = cbk * 4 + jj
            for (nm, rf, rk, N) in tblocks:
                pa, pak = proj_fm(Wa, jj * 128, rf, rk, N)
                pb, pbk = proj_fm(Wb, jj * 128, rf, rk, N)
                tA, tAk = nxt('tmpA', tmpA)
                B.op('act', lambda e, pb=pb, tA=tA, N=N: e.activation(out=tA[:, 0:N], in_=pb[:, 0:N], func=AF.Sigmoid), R=[pbk], W=[tAk])
                if nm == 'm0':
                    dst = u_p[:, J, 32:544]
                elif nm == 'm1':
                    dst = u_p[:, J, 544:1056]
                elif nm == 'h':
                    dst = u_p[:, J, 0:32]
                else:
                    dst = u_s[:, J, 120:152].rearrange("p (t b) -> p b t", b=4)
                src_a = pa[:, 0:N] if nm != 's' else pa[:, 0:32].rearrange("p (b t) -> p b t", b=4)
                src_s = tA[:, 0:N] if nm != 's' else tA[:, 0:32].rearrange("p (b t) -> p b t", b=4)
                B.op('dve', lambda e, dst=dst, src_a=src_a, src_s=src_s: e.tensor_tensor(out=dst, in0=src_a, in1=src_s, op=ALU.mult),
                     R=[pak, tAk], W=['u_p' if nm != 's' else 'u_s'])
                if nm == 'm1':
                    B.op('dve', lambda e, pa=pa, tA=tA, J=J: e.tensor_tensor(out=utail[:, J, 0:32], in0=pa[:, 480:512], in1=tA[:, 480:512], op=ALU.mult),
                         R=[pak, tAk], W=['utail'])
                if nm == 's':
                    B.op('dve', lambda e, pa=pa, tA=tA, J=J: e.tensor_tensor(out=utail[:, J, 32:64], in0=pa[:, 0:32], in1=tA[:, 0:32], op=ALU.mult),
                         R=[pak, tAk], W=['utail'])
    for half in range(2):
        bk, bkk = nbank()
        for jj in range(4):
            J = half * 4 + jj
            B.op('pe', lambda e, J=J, jj=jj: e.transpose(out=bk[0:64, jj * 128:(jj + 1) * 128], in_=utail[:, J, :], identity=ident_f[:, :]),
                 R=['utail', 'ident_f'], W=[bkk], inc=(jj == 3))
        B.op('act', lambda e, half=half, bk=bk: e.activation(out=uo[:, half * 512:(half + 1) * 512], in_=bk[0:64, :], func=AF.Copy), R=[bkk], W=['uo'])
    B.dma('sp', o_pconv[:, :], uo[0:32, :], R=['uo'])
    for b in range(4):
        B.dma('sp', o_sconv[b, 22:30, :], uo[32 + b * 8:40 + b * 8, :], R=['uo'])
    def build_dg(J):
        dgb = dg_l[J % 2]
        for k in range(31):
            B.op('dve', lambda e, k=k, J=J, dgb=dgb: e.tensor_scalar(out=dgb[:, k, :], in0=ident_b[:, :], scalar1=wdwT[:, J, k:k + 1], scalar2=None, op0=ALU.mult),
                 R=['ident_b', 'wdwT'], W=['dg%d' % (J % 2)])

    build_dg(0)
    for J in range(8):
        if J + 1 < 8:
            build_dg(J + 1)
        dg = dg_l[J % 2]
        dgk = 'dg%d' % (J % 2)
        outs = [(lambda k: u_p[:, J, 2 + k:2 + k + 512], 512, c_f[:, J, 0:512], None),
                (lambda k: u_p[:, J, 514 + k:514 + k + 512], 512, c_f[:, J, 512:1024], None),
                (lambda k: u_p[:, J, k:k + 2], 2, c_f[:, J, 1056:1058], None),
                (lambda k: u_s[:, J, k * 4:k * 4 + 32], 32, c_f[:, J, 1024:1056].rearrange("p (b t) -> p b t", b=4), 's')]
        for (rf, N, dst, kind) in outs:
            bk, bkk = nbank()
            for k in range(31):
                B.op('pe', lambda e, k=k, rf=rf, bk=bk, N=N, dg=dg: e.matmul(bk[:, 0:N], dg[:, k, :], rf(k), start=(k == 0), stop=(k == 30)),
                     R=[dgk, 'u_p', 'u_s'], W=[bkk], inc=(k == 30))
            src = bk[:, 0:N] if kind is None else bk[:, 0:32].rearrange("p (t b) -> p b t", b=4)
            B.op('act', lambda e, dst=dst, src=src, J=J: e.activation(out=dst, in_=src, func=AF.Identity, bias=cpar[:, 0, J:J + 1], scale=1.0),
                 R=[bkk, 'cpar'], W=['c_f'])
    for (c0, N) in TOKE:
        b1, b1k = nbank()
        b2, b2k = nbank()
        for J in range(8):
            tA, tAk = nxt('tmpA', tmpA)
            B.op('act', lambda e, tA=tA, J=J: e.activation(out=tA[:, 0:N], in_=c_f[:, J, c0:c0 + N], func=AF.Square), R=['c_f'], W=[tAk])
            B.op('pe', lambda e, J=J: e.matmul(b1[:, 0:N], ones_f[:, :], c_f[:, J, c0:c0 + N], start=(J == 0), stop=(J == 7)),
                 R=['ones_f', 'c_f'], W=[b1k], inc=(J == 7))
            B.op('pe', lambda e, J=J, tA=tA: e.matmul(b2[:, 0:N], ones_f[:, :], tA[:, 0:N], start=(J == 0), stop=(J == 7)),
                 R=['ones_f', tAk], W=[b2k], inc=True)
        B.op('act', lambda e: e.activation(out=mu_t[:, c0:c0 + N], in_=b1[:, 0:N], func=AF.Copy, scale=1.0 / 1024), R=[b1k], W=['mu_t'])
        tB, tBk = nxt('tmpB', tmpB)
        B.op('dve', lambda e, tB=tB: e.tensor_tensor(out=tB[:, 0:N], in0=mu_t[:, c0:c0 + N], in1=mu_t[:, c0:c0 + N], op=ALU.mult), R=['mu_t'], W=[tBk])
        B.op('dve', lambda e, tB=tB: e.scalar_tensor_tensor(out=tB[:, 0:N], in0=b2[:, 0:N], scalar=1.0 / 1024, in1=tB[:, 0:N],
                                                           op0=ALU.mult, op1=ALU.subtract), R=[b2k, tBk], W=[tBk])
        B.op('act', lambda e, tB=tB: e.activation(out=tB[:, 0:N], in_=tB[:, 0:N], func=AF.Ln, bias=epsc[:, :], scale=1.0), R=[tBk, 'epsc'], W=[tBk])
        B.op('act', lambda e, tB=tB: e.activation(out=rs_t[:, c0:c0 + N], in_=tB[:, 0:N], func=AF.Exp, scale=-0.5), R=[tBk], W=['rs_t'])
    for (c0, N) in TOKE:
        for J in range(8):
            tA, tAk = nxt('tmpA', tmpA)
            B.op('dve', lambda e, tA=tA, J=J: e.tensor_tensor(out=tA[:, 0:N], in0=c_f[:, J, c0:c0 + N], in1=mu_t[:, c0:c0 + N], op=ALU.subtract),
                 R=['c_f', 'mu_t'], W=[tAk])
            B.op('dve', lambda e, tA=tA: e.tensor_tensor(out=tA[:, 0:N], in0=tA[:, 0:N], in1=rs_t[:, c0:c0 + N], op=ALU.mult), R=[tAk, 'rs_t'], W=[tAk])
            B.op('act', lambda e, tA=tA, J=J: e.activation(out=c_act[:, J, c0:c0 + N], in_=tA[:, 0:N], func=AF.Silu,
                                                           bias=cpar[:, 2, J:J + 1], scale=cpar[:, 1, J:J + 1]), R=[tAk, 'cpar'], W=['c_act'])
    B.barrier()
    _ck(21)

    st3 = ExitStack()
    set_ptr(77.5)
    new_ring(st3, 8, "d")
    xt_list = [al("xtd%d" % i, [128, D], F32) for i in range(2)]
    ph['xt_i'] = 0
    prep_bufs = (xt_list, [(psb[6], 'ps6'), (psb[7], 'ps7')])
    hmT = al("hmT", [128, KC, 256], BF16)
    kmT = al("kmT", [128, 8, 256], BF16)
    vmb = al("vmb", [128, 2, 1024], BF16)
    qmT = al("qmT", [128, 8, NT], BF16)
    kf_l = [al("kfd%d" % i, [128, 512], F32) for i in range(2)]
    ko_l = [al("kod%d" % i, [128, 512], F32) for i in range(2)]
    kb_l = [al("kbd%d" % i, [128, 512], BF16) for i in range(2)]
    sqs3 = al("sqs3", [128, 512], F32)
    ph['sqs'] = sqs3
    pm_l = [al("pm%d" % i, [128, 512], BF16) for i in range(4)]
    rd_l = [al("rd%d" % i, [128, 512], F32) for i in range(2)]
    kms = [al("kms%d" % i, [128, 2, 1024], BF16) for i in range(1)]
    vms = [al("vms%d" % i, [128, 2, 1024], BF16) for i in range(1)]
    kmTs = [al("kmTs%d" % i, [128, 8, 256], BF16) for i in range(1)]
    rot.clear()
    pT_bf3 = psb[5][:, :].bitcast(BF16)

    def transpose4b(src_bf, skey, M, half, dst, dkeys):
        pk = 'ps5'
        base = half * 512
        for h in range(4):
            B.op('pe', lambda e, h=h: e.transpose(out=pT_bf3[:, base + h * 128: base + h * 128 + M],
                                                  in_=src_bf[0:M, h * 128:(h + 1) * 128], identity=ident_b[0:M, 0:M]),
                 R=[skey, 'ident_b'], W=[pk], inc=(h == 3))
        src = pT_bf3[:, base:base + 512].rearrange("p (h t) -> p h t", h=4)[:, :, 0:M]
        B.op('act', lambda e: e.activation(out=dst, in_=src, func=AF.Copy), R=[pk], W=dkeys)

    for mt in range(2):
        prep(prep_bufs, mem_p[mt * 128:(mt + 1) * 128, :], 128, gT_mem, 'gT_mem',
             lambda q, mt=mt: hmT[:, q * 4:q * 4 + 4, mt * 128:(mt + 1) * 128], ['hmT'])
    for cb in range(2):
        Wmk = load_w16(w_mem_k, cb * 512)
        Wmv = load_w16(w_mem_v, cb * 512)
        for mt in range(2):
            proj_tm(lambda kc, mt=mt: hmT[:, kc, mt * 128:(mt + 1) * 128], 128, Wmk, psb[0], 'ps0', ['hmT'])
            proj_tm(lambda kc, mt=mt: hmT[:, kc, mt * 128:(mt + 1) * 128], 128, Wmv, psb[1], 'ps1', ['hmT'])
            kf, kfk = nxt('kfd', kf_l); ko, kok = nxt('kod', ko_l); kb, kbk = nxt('kbd', kb_l)
            head_norm(psb[0], 'ps0', 128, 2, 256, gmk_t, 'gmk', kf, kfk, ko, [kok], kb, [kbk])
            B.dma('sp', o_pmk[mt * 128:(mt + 1) * 128, cb * 512:(cb + 1) * 512], ko[:, :], R=[kok])
            transpose4b(kb, kbk, 128, 0, kmT[:, cb * 4:cb * 4 + 4, mt * 128:(mt + 1) * 128], ['kmT'])
            B.op('act', lambda e, mt=mt, cb=cb: e.activation(out=vmb[:, mt, cb * 512:(cb + 1) * 512], in_=psb[1][:, :], func=AF.Copy), R=['ps1'], W=['vmb'])
            ko2, kok2 = nxt('kod', ko_l)
            B.op('dve', lambda e, ko2=ko2: e.tensor_copy(out=ko2[:, :], in_=psb[1][:, :]), R=['ps1'], W=[kok2])
            B.dma('sp', o_pmv[mt * 128:(mt + 1) * 128, cb * 512:(cb + 1) * 512], ko2[:, :], R=[kok2])
    tiles9 = [(t * 128, 128) for t in range(8)] + [(1024, 34)]
    qdef = [None]
    for cb in range(2):
        Wqm = load_w16(w_in, C_QM + cb * 512)
        for (t0, M) in tiles9:
            proj_tm(lambda kc, t0=t0, M=M: hT[:, kc, t0:t0 + M], M, Wqm, psb[2], 'ps2', ['hT'])
            kf, kfk = nxt('kfd', kf_l); kb, kbk = nxt('kbd', kb_l)
            head_norm(psb[2], 'ps2', M, 2, 256, gmq_t, 'gmq', kf, kfk, None, None, kb, [kbk])
            if qdef[0] is not None:
                qdef[0]()
            qdef[0] = (lambda kb=kb, kbk=kbk, M=M, cb=cb, t0=t0: transpose4b(kb, kbk, M, 1, qmT[:, cb * 4:cb * 4 + 4, t0:t0 + M], ['qmT']))
    qdef[0]()
    scale_mem = 1.0 / 16.0

    mdef = [None]

    def mem_attend(kT_, kTk, v_, vk, c0, N):
        for h in range(4):
            pms = []
            for mt in range(2):
                bk, bkk = psb[mt], 'ps%d' % mt
                for dc in range(2):
                    B.op('pe', lambda e, mt=mt, dc=dc, bk=bk: e.matmul(bk[:, 0:N], kT_[:, h * 2 + dc, mt * 128:(mt + 1) * 128],
                                                                      qmT[:, h * 2 + dc, c0:c0 + N], start=(dc == 0), stop=(dc == 1)),
                         R=[kTk, 'qmT'], W=[bkk], inc=(dc == 1))
                pm, pmk = nxt('pm', pm_l)
                B.op('act', lambda e, pm=pm, bk=bk: e.activation(out=pm[:, 0:N], in_=bk[:, 0:N], func=AF.Exp, scale=scale_mem), R=[bkk], W=[pmk])
                pms.append((pm, pmk))
            B.op('pe', lambda e: e.matmul(psb[4][:, 0:N], ones_b[:, :], pms[0][0][:, 0:N], start=True, stop=False), R=['ones_b', pms[0][1]], W=['ps4'], inc=False)
            B.op('pe', lambda e: e.matmul(psb[4][:, 0:N], ones_b[:, :], pms[1][0][:, 0:N], start=False, stop=True), R=['ones_b', pms[1][1]], W=['ps4'], inc=True)
            rd, rdk = nxt('rd', rd_l)
            B.op('act', lambda e, rd=rd: e.activation(out=rd[:, 0:N], in_=psb[4][:, 0:N], func=AF.Ln), R=['ps4'], W=[rdk])
            B.op('act', lambda e, rd=rd: e.activation(out=rd[:, 0:N], in_=rd[:, 0:N], func=AF.Exp, scale=-1.0), R=[rdk], W=[rdk])

            def stage2(h=h, pms=pms, rd=rd, rdk=rdk):
                for dc in range(2):
                    bk, bkk = psb[2 + dc], 'ps%d' % (2 + dc)
                    for mt in range(2):
                        B.op('pe', lambda e, mt=mt, dc=dc, bk=bk: e.matmul(bk[:, 0:N], v_[:, mt, h * 256 + dc * 128:h * 256 + (dc + 1) * 128],
                                                                          pms[mt][0][:, 0:N], start=(mt == 0), stop=(mt == 1)),
                             R=[vk, pms[mt][1]], W=[bkk], inc=(mt == 1))
                    B.op('dve', lambda e, dc=dc, bk=bk: e.tensor_tensor(out=o_mem[:, h * 2 + dc, c0:c0 + N], in0=bk[:, 0:N], in1=rd[:, 0:N], op=ALU.mult),
                         R=[bkk, rdk], W=['o_mem'])
            if mdef[0] is not None:
                mdef[0]()
            mdef[0] = stage2
        mdef[0]()
        mdef[0] = None

    for (c0, N) in [(0, 512), (512, 512), (1056, 2)]:
        mem_attend(kmT, 'kmT', vmb, 'vmb', c0, N)
    for b in range(4):
        km, kmk = nxt('kms', kms); vm, vmk = nxt('vms', vms); kt, ktk = nxt('kmTs', kmTs)
        B.dma('pool', km[:, :, :], cmk[b].rearrange("(mt p) c -> p mt c", p=128), W=[kmk])
        B.dma('pool', vm[:, :, :], cmv[b].rearrange("(mt p) c -> p mt c", p=128), W=[vmk])
        for mt in range(2):
            for cb in range(2):
                transpose4b(km[:, mt, cb * 512:(cb + 1) * 512], kmk, 128, (mt * 2 + cb) % 2, kt[:, cb * 4:cb * 4 + 4, mt * 128:(mt + 1) * 128], [ktk])
        mem_attend(kt, ktk, vm, vmk, 1024 + b * 8, 8)
    if mdef[0] is not None:
        mdef[0]()
        mdef[0] = None
    B.barrier()
    _ck(22)

    st4 = ExitStack()
    st_x2 = ExitStack()
    set_ptr(77.5)
    m_t = al("m_t", [128, KC, NT], BF16, at=162)
    new_ring(st4, 16, "e")
    sg_l = [al("sg%d" % i, [128, 512], F32) for i in range(6)]
    tt_l = [al("tt%d" % i, [128, 512], F32) for i in range(4)]
    for fp in range(8):
        f0 = fp * 256
        Wg = [[load_panel(w_in, 0, 8, cg + f0, 256) + (0,), load_panel(w_in, 8, 8, cg + f0, 256) + (8,)] for cg in (C_GC, C_GA, C_GM)]
        Wco = [load_panel(w_conv_out, 0, 8, f0, 256) + (0,)]
        Wao = [load_panel(w_attn_out, 0, 4, f0, 256) + (0,)]
        Wmo = [load_panel(w_mem_out, 0, 8, f0, 256) + (0,)]
        for fi in range(2):
            f = fp * 2 + fi
            for (c0, N) in TOKE:
                sgs = []
                for gi in range(3):
                    bk, bkk = proj_fm(Wg[gi], fi * 128, lambda kc: hT[:, kc, c0:c0 + N], ['hT'], N)
                    sg, sgk = nxt('sg', sg_l)
                    B.op('act', lambda e, sg=sg, bk=bk: e.activation(out=sg[:, 0:N], in_=bk[:, 0:N], func=AF.Sigmoid), R=[bkk], W=[sgk])
                    sgs.append((sg, sgk))
                brs = [(Wco, lambda kc: c_act[:, kc, c0:c0 + N], ['c_act'], 8),
                       (Wao, lambda kc: o_attn[:, kc, c0:c0 + N], ['o_attn'], 4),
                       (Wmo, lambda kc: o_mem[:, kc, c0:c0 + N], ['o_mem'], 8)]
                tts = []
                for bi, (Wp, rf, rk, nk) in enumerate(brs):
                    bk, bkk = proj_fm(Wp, fi * 128, rf, rk, N, nk_total=nk)
                    tt, ttk = nxt('tt', tt_l)
                    B.op('dve', lambda e, tt=tt, bk=bk, sg=sgs[bi][0]: e.tensor_tensor(out=tt[:, 0:N], in0=bk[:, 0:N], in1=sg[:, 0:N], op=ALU.mult),
                         R=[bkk, sgs[bi][1]], W=[ttk])
                    tts.append((tt, ttk))
                B.op('dve', lambda e: e.tensor_tensor(out=tts[0][0][:, 0:N], in0=tts[0][0][:, 0:N], in1=tts[1][0][:, 0:N], op=ALU.add),
                     R=[tts[0][1], tts[1][1]], W=[tts[0][1]])
                B.op('dve', lambda e, f=f: e.tensor_tensor(out=m_t[:, f, c0:c0 + N], in0=tts[0][0][:, 0:N], in1=tts[2][0][:, 0:N], op=ALU.add),
                     R=[tts[0][1], tts[2][1]], W=['m_t'])
    B.barrier()
    _ck(23)
    x2 = al("x2", [128, 9, D], F32, at=0)
    set_ptr(72)
    new_ring(None, 12, "e2")
    for t in range(8):
        B.dma('sp', x2[:, t, :], x_main[t * 128:(t + 1) * 128, :], W=['x2_%d' % t])
    B.dma('sp', x2[0:34, 8, :], x_misc[:, :], W=['x2_8'])
    for cb in range(4):
        Wo = load_w16(w_o, cb * 512)
        for ti, (t0, M) in enumerate(tiles9):
            bk, bkk = nbank()
            proj_tm(lambda kc, t0=t0, M=M: m_t[:, kc, t0:t0 + M], M, Wo, bk, bkk, ['m_t'])
            B.op('dve', lambda e, bk=bk, ti=ti, M=M, cb=cb: e.tensor_tensor(out=x2[0:M, ti, cb * 512:(cb + 1) * 512], in0=bk[0:M, :],
                                                                         in1=x2[0:M, ti, cb * 512:(cb + 1) * 512], op=ALU.add),
                 R=[bkk, 'x2_%d' % ti], W=['x2_%d' % ti])
    B.barrier()
    _ck(24)

    st5 = ExitStack()
    set_ptr(72)
    copy_thunks = []
    for g, (d, W) in enumerate(GROUPS):
        for b in range(4):
            copy_thunks.append(lambda g=g, b=b, W=W: B.dma('act', o_swk[g][b, 0:W - 8, :].rearrange("(a r) c -> a (r c)", r=8),
                                                          st_wk[g][b, 8:W, :].rearrange("(a r) c -> a (r c)", r=8), sem='copy'))
            copy_thunks.append(lambda g=g, b=b, W=W: B.dma('act', o_swv[g][b, 0:W - 8, :].rearrange("(a r) c -> a (r c)", r=8),
                                                          st_wv[g][b, 8:W, :].rearrange("(a r) c -> a (r c)", r=8), sem='copy'))
    for b in range(4):
        copy_thunks.append(lambda b=b: B.dma('act', o_sconv[b, 0:22, :], st_conv[b, 8:30, :], sem='copy'))

    new_ring(st5, 12, "f")
    h2T = al("h2T", [128, KC, NT], BF16)
    xs_full = al("xs0", [128, 2112], F32)
    xs_l = [xs_full[:, 0:D]]
    ph['xt_i'] = 0
    prep_bufs = (xs_l, [(psb[6], 'ps6'), (psb[7], 'ps7')])
    esc = al("esc", [128, 1], F32)
    wf = al("wf", [128, 88, 3], F32)
    bf_ = al("bf_", [128, 88], F32)
    sfc = al("sfc", [128, 88, 8], F32)
    uptail = al("uptail", [128, 88, 10], F32)
    ldf = al("ldf", [88, 128], F32)
    upraw = al("upraw", [128, 1026], F32)
    ups = al("ups", [128, 40], F32)
    acc_g = al("acc_g", [128, 1056], F32)
    acc_v = al("acc_v", [128, 1056], F32)
    act_t = al("act_t", [128, 4, 1056], BF16)
    act_alt = xs_full[:, 0:2112].bitcast(BF16).rearrange("p (a b) -> p a b", a=4)
    act_l = [act_t, act_alt]
    stg = xs_l[0]
    B.op('dve', lambda e: e.memset(esc[:, :], 1.0), W=['esc'])
    B.op('dve', lambda e: e.tensor_copy(out=esc[32:34, :], in_=eflag[32:34, :]), R=['eflag'], W=['esc'])
    for k3 in range(3):
        pass
    ldw = acc_g
    ldw_l = [acc_g, acc_v]
    for part in range(11):
        ldw = ldw_l[part % 2]
        lk = 'ldw%d' % (part % 2)
        B.dma('sp', ldw[0:3, 0:1024], w_ffn_dw[:, part * 1024:(part + 1) * 1024], W=[lk])
        bk, bkk = nbank()
        for j in range(8):
            B.op('pe', lambda e, j=j, bk=bk, ldw=ldw: e.transpose(out=bk[:, j * 3:(j + 1) * 3], in_=ldw[0:3, j * 128:(j + 1) * 128], identity=ident_f[0:3, 0:3]),
                 R=[lk, 'ident_f'], W=[bkk], inc=(j == 7))
        B.op('act', lambda e, bk=bk, part=part: e.activation(out=wf[:, part * 8:(part + 1) * 8, :], in_=bk[:, 0:24].rearrange("p (j k) -> p j k", j=8), func=AF.Copy),
             R=[bkk], W=['wf'])
    B.dma('sp', ldf[:, :], b_ffn_dw.rearrange("(j p) -> j p", p=128), W=['ldf'])
    bk, bkk = nbank()
    B.op('pe', lambda e: e.transpose(out=bk[:, 0:88], in_=ldf[0:88, :], identity=ident_f[0:88, 0:88]), R=['ldf', 'ident_f'], W=[bkk])
    B.op('act', lambda e: e.activation(out=bf_[:, :], in_=bk[:, 0:88], func=AF.Copy), R=[bkk], W=['bf_'])
    for part in range(11):
        ldw = ldw_l[(part + 1) % 2]
        lk = 'ldw%d' % ((part + 1) % 2)
        B.dma('sp', ldw[0:8, 0:1024], st_ffn[:, part * 1024:(part + 1) * 1024], W=[lk])
        bk, bkk = nbank()
        for j in range(8):
            B.op('pe', lambda e, j=j, bk=bk, ldw=ldw: e.transpose(out=bk[:, j * 8:(j + 1) * 8], in_=ldw[0:8, j * 128:(j + 1) * 128], identity=ident_f[0:8, 0:8]),
                 R=[lk, 'ident_f'], W=[bkk], inc=(j == 7))
        B.op('act', lambda e, bk=bk, part=part: e.activation(out=sfc[:, part * 8:(part + 1) * 8, :], in_=bk[:, 0:64].rearrange("p (j k) -> p j k", j=8), func=AF.Copy),
             R=[bkk], W=['sfc'])
    B.barrier()
    for ti, (t0, M) in enumerate(tiles9):
        prep(prep_bufs, None, M, gT_ffn, 'gT_ffn', lambda q, t0=t0, M=M: h2T[:, q * 4:q * 4 + 4, t0:t0 + M], ['h2T'],
             from_sbuf=(x2[:, ti, :], 'x2_%d' % ti), extra_scale=(esc if ti == 8 else None))
    B.barrier()

    def up_chunk(j0, jj, Wug, Wuv, act_cur, actk):
            for iv, (Wp, acc) in enumerate([(Wug, acc_g), (Wuv, acc_v)]):
                Jp = iv * 44 + j0 + jj
                for bi, (c0, N) in enumerate(TOKE):
                    bk, bkk = proj_fm(Wp, jj * 128, lambda kc: h2T[:, kc, c0:c0 + N], ['h2T'], N)
                    m1 = min(c0 + N, 1024)
                    if m1 > c0:
                        B.op('act', lambda e, bk=bk, c0=c0, m1=m1: e.activation(out=upraw[:, 2 + c0:2 + m1], in_=bk[:, 0:m1 - c0], func=AF.Copy), R=[bkk], W=['upraw'])
                    if c0 + N > 1024:
                        so = 1024 - c0
                        B.op('act', lambda e, bk=bk, so=so: e.activation(out=ups[:, 8:40].rearrange("p (t b) -> p b t", b=4),
                                                                         in_=bk[:, so:so + 32].rearrange("p (b t) -> p b t", b=4), func=AF.Copy), R=[bkk], W=['ups'])
                        B.op('act', lambda e, bk=bk, so=so: e.activation(out=upraw[:, 0:2], in_=bk[:, so + 32:so + 34], func=AF.Copy), R=[bkk], W=['upraw'])
                B.op('act', lambda e, Jp=Jp: e.activation(out=ups[:, 0:8].rearrange("p (pos b) -> p b pos", b=4),
                                                          in_=sfc[:, Jp, :].rearrange("p (b pos) -> p b pos", b=4), func=AF.Copy), R=['sfc'], W=['ups'])
                B.op('act', lambda e, Jp=Jp: e.activation(out=uptail[:, Jp, 0:2], in_=upraw[:, 1024:1026], func=AF.Copy), R=['upraw'], W=['uptail'])
                B.op('act', lambda e, Jp=Jp: e.activation(out=uptail[:, Jp, 2:10], in_=ups[:, 32:40], func=AF.Copy), R=['ups'], W=['uptail'])
                for is_s in (0, 1):
                    ak = 'acc_g' if iv == 0 else 'acc_v'
                    if not is_s:
                        dst = acc[:, 0:1024]
                        sk = 'upraw'
                        srcs = [upraw[:, 0:1024], upraw[:, 1:1025], upraw[:, 2:1026]]
                    else:
                        dst = acc[:, 1024:1056].rearrange("p (b t) -> p t b", b=4)
                        sk = 'ups'
                        srcs = [ups[:, o_:o_ + 32].rearrange("p (t b) -> p t b", b=4) for o_ in (0, 4, 8)]
                    B.op('act', lambda e, dst=dst, srcs=srcs, Jp=Jp: e.activation(out=dst, in_=srcs[2], func=AF.Identity,
                                                                               bias=bf_[:, Jp:Jp + 1], scale=wf[:, Jp, 2:3]), R=[sk, 'wf', 'bf_'], W=[ak])
                    for tap in (1, 0):
                        B.op('dve', lambda e, dst=dst, srcs=srcs, Jp=Jp, tap=tap: e.scalar_tensor_tensor(
                            out=dst, in0=srcs[tap], scalar=wf[:, Jp, tap:tap + 1], in1=dst, op0=ALU.mult, op1=ALU.add), R=[sk, 'wf', ak], W=[ak])
            B.op('act', lambda e: e.activation(out=acc_g[:, :], in_=acc_g[:, :], func=AF.Silu), R=['acc_g'], W=['acc_g'])
            B.op('dve', lambda e, jj=jj: e.tensor_tensor(out=act_cur[:, jj, :], in0=acc_g[:, :], in1=acc_v[:, :], op=ALU.mult), R=['acc_g', 'acc_v'], W=[actk])


    def down_group(Wd_, act_cur, actk):
        for ti, (t0, M) in enumerate(tiles9):
            tc0 = t0 if ti < 8 else 1024
            MM = M if ti < 8 else 32
            for cb in range(4):
                bk, bkk = nbank()
                view, keys = Wd_[cb]
                for jj in range(4):
                    B.op('pe', lambda e, jj=jj, bk=bk, view=view: e.matmul(bk[0:MM, :], act_cur[:, jj, tc0:tc0 + MM], view[:, jj, :], start=(jj == 0), stop=(jj == 3)),
                         R=[actk] + keys, W=[bkk], inc=(jj == 3))
                B.op('dve', lambda e, bk=bk, ti=ti, cb=cb: e.tensor_tensor(out=x2[0:MM, ti, cb * 512:(cb + 1) * 512], in0=bk[0:MM, :],
                                                                        in1=x2[0:MM, ti, cb * 512:(cb + 1) * 512], op=ALU.add),
                     R=[bkk, 'x2_%d' % ti], W=['x2_%d' % ti])


    prev_d = None
    for G in range(11):
        j0 = G * 4
        Wug = load_w16(w_up, j0 * 128)
        Wuv = load_w16(w_up, DFF + j0 * 128)
        act_cur = act_l[G % 2]
        actk = 'act%d' % (G % 2)
        up_chunk(j0, 0, Wug, Wuv, act_cur, actk)
        for _ in range(3):
            if copy_thunks:
                copy_thunks.pop(0)()
        if prev_d is not None:
            down_group(*prev_d)
        Wd = [load_panel(w_down, j0, 4, cb * 512, 512) for cb in range(4)]
        for jj in range(1, 4):
            up_chunk(j0, jj, Wug, Wuv, act_cur, actk)
        prev_d = (Wd, act_cur, actk)
    down_group(*prev_d)
    while copy_thunks:
        copy_thunks.pop(0)()
    for t in range(8):
        B.dma('sp', y_main[t * 128:(t + 1) * 128, :], x2[:, t, :], R=['x2_%d' % t])
    B.dma('sp', y_misc[:, :], x2[0:34, 8, :], R=['x2_8'])
    for rnd in range(6):
        nch = 16 if rnd < 5 else 8
        banks = []
        for q in range(nch // 4):
            bk, bkk = nbank()
            for j in range(4):
                Jp = rnd * 16 + q * 4 + j
                B.op('pe', lambda e, Jp=Jp, j=j, bk=bk: e.transpose(out=bk[0:10, j * 128:(j + 1) * 128], in_=uptail[:, Jp, :], identity=ident_f[:, :]),
                     R=['uptail', 'ident_f'], W=[bkk], inc=(j == 3))
            banks.append((bk, bkk))
        for q, (bk, bkk) in enumerate(banks):
            B.op('act', lambda e, bk=bk, q=q: e.activation(out=stg[0:10, q * 512:(q + 1) * 512], in_=bk[0:10, :], func=AF.Copy),
                 R=[bkk], W=['xt0'])
        B.dma('sp', o_pffn[:, rnd * 2048:rnd * 2048 + nch * 128], stg[0:10, 0:nch * 128], R=['xt0'])
    B.barrier()


_CACHE = {}


def _consts():
    ident = np.eye(128, dtype=np.float32)
    ki = np.arange(128)[:, None]
    qi = np.arange(128)[None, :]
    mask = np.zeros((128, 256), np.float32)
    mask[:, 0:128] = np.where(ki >= qi, 0.0, NEG)
    mask[:, 128:256] = np.where(ki <= qi, 0.0, NEG)
    smask = np.full((34, 96), NEG, np.float32)
    for g, (d, W) in enumerate(GROUPS):
        ds = min(d, 8)
        QBs = 8 // ds
        for b in range(4):
            for r in range(ds):
                for q in range(QBs):
                    t = r + d * q
                    col = g * 32 + b * 8 + r * QBs + q
                    for t2 in range(8):
                        if t2 <= t and (t - t2) % d == 0:
                            smask[b * 8 + t2, col] = 0.0
    return ident, mask, smask


def _hbias(c):
    P0 = 1024 * c
    hb = np.zeros((128, NHB), np.float32)
    i = np.arange(128)
    for (g, r, kind), col in HT_IDX.items():
        d, W = GROUPS[g]
        pos = P0 - 128 * d * (2 if kind == 1 else 1) + r + d * i
        hb[:, col] = np.where(pos >= 0, 0.0, NEG)
        if kind in (1, 2) and c == 0:
            hb[:, col] = 0.0
    return hb


def kernel(**inp):
    f = lambda a: np.ascontiguousarray(np.asarray(a, dtype=np.float32))
    if 'B' not in _CACHE:
        _CACHE['B'] = build_program()
    B = _CACHE['B']
    ident, mask, smask = _consts()
    xp = f(inp['x_prompt']); xs = f(inp['x_sample'])
    shared = {
        'c_ident': ident, 'c_mask': mask, 'c_smask': smask,
        'g_mix': f(inp['g_mix'][0]), 'g_ffn': f(inp['g_ffn'][0]), 'g_mem': f(inp['g_mem'][0]),
        'w_in': f(inp['w_in'][0]), 'w_dw': f(inp['w_dw'][0]), 'b_dw': f(inp['b_dw'][0]), 'g_cln': f(inp['g_cln'][0]),
        'b_cln': f(inp['b_cln'][0]), 'w_conv_out': f(inp['w_conv_out'][0]), 'g_q': f(inp['g_q'][0]), 'g_k': f(inp['g_k'][0]),
        'w_attn_out': f(inp['w_attn_out'][0]), 'w_mem_k': f(inp['w_mem_k'][0]), 'w_mem_v': f(inp['w_mem_v'][0]),
        'g_mq': f(inp['g_mq'][0]), 'g_mk': f(inp['g_mk'][0]), 'w_mem_out': f(inp['w_mem_out'][0]), 'w_o': f(inp['w_o'][0]),
        'w_up': f(inp['w_up'][0]), 'w_ffn_dw': f(inp['w_ffn_dw'][0]), 'b_ffn_dw': f(inp['b_ffn_dw'][0]), 'w_down': f(inp['w_down'][0]),
    }
    used = set(B.dram.keys())
    in_maps = []
    for core in range(8):
        bp, c = core // 4, core % 4
        P0 = 1024 * c
        m = dict(shared)
        m['x_main'] = xp[bp, P0:P0 + 1024]
        xh = np.zeros((4096, D), np.float32)
        lo = max(0, P0 - 4096)
        if P0 > 0:
            xh[4096 - (P0 - lo):] = xp[bp, lo:P0]
        m['x_halo'] = np.ascontiguousarray(xh[2048:])
        m['x_far'] = np.ascontiguousarray(np.concatenate([xh[14:2048:16], xh[15:2048:16]], axis=0))
        m['x_misc'] = np.ascontiguousarray(np.concatenate([xs[4 * core:4 * core + 4].reshape(32, D), xh[4094:4096]], axis=0))
        m['hbias'] = _hbias(c)
        m['eflag'] = np.full((128, 1), 1.0 if c > 0 else 0.0, np.float32)
        m['mem_p'] = f(inp['mem_prompt'][bp])
        sl = slice(4 * core, 4 * core + 4)
        m['cmk'] = f(inp['cache_mem_k'][0, sl]).reshape(4, 256, 1024)
        m['cmv'] = f(inp['cache_mem_v'][0, sl]).reshape(4, 256, 1024)
        m['st_conv'] = f(inp['state_conv'][0, sl])
        m['st_ffn'] = f(inp['state_ffn_conv'][0, sl]).reshape(8, 2 * DFF)
        swk = [inp['state_win1_k'], inp['state_win2_k'], inp['state_win3_k']]
        swv = [inp['state_win1_v'], inp['state_win2_v'], inp['state_win3_v']]
        for g, (d, W) in enumerate(GROUPS):
            m['st_wk%d' % g] = f(swk[g][0, sl]).reshape(4, W, 512)
            m['st_wv%d' % g] = f(swv[g][0, sl]).reshape(4, W, 512)
        in_maps.append({k: v for k, v in m.items() if k in used})
    res = run_bass_kernel_spmd(B.nc, in_maps, core_ids=list(range(8)))
    R = res.results
    y_p = np.zeros((2, 4096, D), np.float32)
    y_s = np.zeros((32, 8, D), np.float32)
    for core in range(8):
        bp, c = core // 4, core % 4
        y_p[bp, 1024 * c:1024 * (c + 1)] = R[core]['y_main']
        y_s[4 * core:4 * core + 4] = R[core]['y_misc'][0:32].reshape(4, 8, D)
    last = [3, 7]
    p_conv = np.stack([R[k]['o_pconv'][2:32] for k in last])[None]
    pk = []; pv = []
    for g, (d, W) in enumerate(GROUPS):
        if g < 2:
            kk = np.stack([R[k]['o_k%d' % g] for k in last]); vv = np.stack([R[k]['o_v%d' % g] for k in last])
        else:
            kk = np.stack([np.concatenate([R[k - 1]['o_k2'], R[k]['o_k2']], 0) for k in last])
            vv = np.stack([np.concatenate([R[k - 1]['o_v2'], R[k]['o_v2']], 0) for k in last])
        pk.append(kk.reshape(1, 2, W, 4, 128)); pv.append(vv.reshape(1, 2, W, 4, 128))
    p_ffn = np.stack([R[k]['o_pffn'][0:2] for k in last])[None]
    p_mk = np.stack([R[k]['o_pmk'] for k in last]).reshape(1, 2, 256, 4, 256)
    p_mv = np.stack([R[k]['o_pmv'] for k in last]).reshape(1, 2, 256, 4, 256)
    s_conv = np.concatenate([R[k]['o_sconv'] for k in range(8)], 0)[None]
    sk = []; sv = []
    for g, (d, W) in enumerate(GROUPS):
        sk.append(np.concatenate([R[k]['o_swk%d' % g] for k in range(8)], 0).reshape(1, 32, W, 4, 128))
        sv.append(np.concatenate([R[k]['o_swv%d' % g] for k in range(8)], 0).reshape(1, 32, W, 4, 128))
    s_ffn = np.concatenate([R[k]['o_pffn'][2:10].reshape(2, 4, 2 * DFF).transpose(1, 0, 2) for k in range(8)], 0)[None]
    return (y_p, y_s, p_conv, pk[0], pv[0], pk[1], pv[1], pk[2], pv[2], p_ffn, p_mk, p_mv,
            s_conv, sk[0], sv[0], sk[1], sv[1], sk[2], sv[2], s_ffn)
```
